# Optimizing a Trainium2 kernel written in Bass

```python
import jax, jax.numpy as jnp
from jax import lax
import numpy as np

D_MODEL = 1024
BATCH = 4
SEQ = 4096
DEPTH = 2
DEC_BATCH = 32
DEC_SEQ = 64
PAST_LEN = 2048

CHUNK = 64
N_BACK = 8
BAND = N_BACK * CHUNK
A_HEADS = 4
A_DH = D_MODEL // 8
A_W = A_HEADS * A_DH
B_HEADS = 8
B_DH = D_MODEL // 16
B_W = B_HEADS * B_DH
REL_CLIP = 128
N_REL = CHUNK + REL_CLIP
C_HEADS = 16
C_DH = D_MODEL // 16
C_W = C_HEADS * C_DH
IN_AB = 4 * A_W + 2 * A_HEADS + 3 * B_W
IN_FOX = 3 * C_W + C_HEADS
FFN_HIDDEN = -(-8 * D_MODEL // (3 * 256)) * 256
N_AB_LAYERS = (DEPTH + 1) // 2
N_FOX_LAYERS = DEPTH // 2
Q_BLOCK = 128
RMS_EPS = 1e-6
NEG_INF = -1e30

kernel_name = 'streaming_mlstm_band_fox_encoder_step'


def rmsnorm(x, g):
    xf = x.astype(jnp.float32)
    y = xf * lax.rsqrt(jnp.mean(xf * xf, axis=-1, keepdims=True) + RMS_EPS)
    return (y * g.astype(jnp.float32)).astype(x.dtype)


def swiglu(x, w_in, w_out):
    g, u = jnp.split(x @ w_in, 2, axis=-1)
    return (jax.nn.silu(g) * u) @ w_out


def to_heads(x, n_heads):
    b, t, _ = x.shape
    return x.reshape(b, t, n_heads, -1).transpose(0, 2, 1, 3)


def from_heads(x):
    b, h, t, d = x.shape
    return x.transpose(0, 2, 1, 3).reshape(b, t, h * d)


def head_layernorm(h, g):
    mu = jnp.mean(h, axis=-1, keepdims=True)
    var = jnp.mean(jnp.square(h - mu), axis=-1, keepdims=True)
    return from_heads((h - mu) * lax.rsqrt(var + RMS_EPS)) * g.astype(jnp.float32)


def mlstm_chunk(carry, inp):
    C, n, m = carry
    q, k, v, ig, lf = inp
    L = q.shape[2]
    b = jnp.cumsum(lf, axis=-1)
    causal = jnp.tril(jnp.ones((L, L), dtype=bool))
    D = jnp.where(causal, b[..., :, None] - b[..., None, :] + ig[..., None, :], NEG_INF)
    inter = b + m[..., None]
    m_t = jnp.maximum(inter, jnp.max(D, axis=-1))
    w_intra = jnp.exp(D - m_t[..., None]) * jnp.einsum('bhtd,bhsd->bhts', q, k)
    w_inter = jnp.exp(inter - m_t)
    num = (w_inter[..., None] * jnp.einsum('bhtd,bhde->bhte', q, C)
           + jnp.einsum('bhts,bhse->bhte', w_intra, v))
    den = w_inter * jnp.einsum('bhtd,bhd->bht', q, n) + jnp.sum(w_intra, axis=-1)
    h = num / jnp.maximum(jnp.abs(den), jnp.exp(-m_t))[..., None]
    b_last = b[..., -1]
    w_src = b_last[..., None] - b + ig
    m_new = jnp.maximum(b_last + m, jnp.max(w_src, axis=-1))
    decay = jnp.exp(b_last + m - m_new)
    w_s = jnp.exp(w_src - m_new[..., None])
    C_new = decay[..., None, None] * C + jnp.einsum('bhs,bhsd,bhse->bhde', w_s, k, v)
    n_new = decay[..., None] * n + jnp.einsum('bhs,bhsd->bhd', w_s, k)
    return (C_new, n_new, m_new), h


def mlstm_seq(q, k, v, ig, lf, C0, n0, m0):
    bsz, nh, t_len, _ = q.shape
    L = min(CHUNK, t_len)
    nck = t_len // L

    def chunks(a):
        return jnp.moveaxis(a.reshape(a.shape[:2] + (nck, L) + a.shape[3:]), 2, 0)

    (C, n, m), hs = lax.scan(mlstm_chunk, (C0, n0, m0),
                             (chunks(q), chunks(k), chunks(v), chunks(ig), chunks(lf)))
    h = jnp.moveaxis(hs, 0, 2).reshape(bsz, nh, t_len, -1)
    return h, (C, n, m)


def rel_bias(table, rel):
    idx = jnp.clip(rel, -(CHUNK - 1), REL_CLIP) + (CHUNK - 1)
    return table[:, idx].astype(jnp.float32)


def band_attn_prompt(q, k, v, table):
    bsz, nh, t_len, dh = q.shape
    nck = t_len // CHUNK
    idx = jnp.arange(nck)[:, None] + jnp.arange(N_BACK + 1)[None, :]

    def band(a):
        ap = jnp.pad(a.reshape(bsz, nh, nck, CHUNK, dh),
                     ((0, 0), (0, 0), (N_BACK, 0), (0, 0), (0, 0)))
        return jnp.take(ap, idx, axis=2).reshape(bsz, nh, nck, (N_BACK + 1) * CHUNK, dh)

    qc = q.reshape(bsz, nh, nck, CHUNK, dh).astype(jnp.float32)
    kb = band(k).astype(jnp.float32)
    vb = band(v).astype(jnp.float32)
    valid = jnp.repeat(idx >= N_BACK, CHUNK, axis=1)
    rel = N_BACK * CHUNK + jnp.arange(CHUNK)[:, None] - jnp.arange((N_BACK + 1) * CHUNK)[None, :]
    s = jnp.einsum('bhcqd,bhckd->bhcqk', qc, kb) * dh ** -0.5 + rel_bias(table, rel)[:, None]
    s = jnp.where(valid[:, None, :], s, NEG_INF)
    p = jax.nn.softmax(s, axis=-1)
    return jnp.einsum('bhcqk,bhckd->bhcqd', p, vb).reshape(bsz, nh, t_len, dh)


def band_attn_sample(q, k, v, ck, cv, table):
    w = ck.shape[2]
    t_len, dh = q.shape[2], q.shape[3]
    kk = jnp.concatenate([ck.astype(k.dtype), k], axis=2).astype(jnp.float32)
    vv = jnp.concatenate([cv.astype(v.dtype), v], axis=2).astype(jnp.float32)
    k_off = jnp.concatenate([jnp.arange(w) - w, jnp.arange(t_len)])
    rel = jnp.arange(t_len)[:, None] - k_off[None, :]
    s = jnp.einsum('bhqd,bhkd->bhqk', q.astype(jnp.float32), kk) * dh ** -0.5 + rel_bias(table, rel)
    p = jax.nn.softmax(s, axis=-1)
    return jnp.einsum('bhqk,bhkd->bhqd', p, vv)


def fox_attend(q, k, v, Fq, Fk, q_pos, k_pos):
    s = jnp.einsum('bhqd,bhkd->bhqk', q.astype(jnp.float32), k.astype(jnp.float32)) * C_DH ** -0.5
    s = s + Fq[..., :, None] - Fk[..., None, :]
    s = jnp.where(k_pos[None, :] <= q_pos[:, None], s, NEG_INF)
    p = jax.nn.softmax(s, axis=-1)
    return jnp.einsum('bhqk,bhkd->bhqd', p, v.astype(jnp.float32))


def fox_attn_prompt(q, k, v, lf):
    bsz, nh, t_len, dh = q.shape
    F = jnp.cumsum(lf, axis=-1)
    pos = jnp.arange(t_len)
    nb = t_len // Q_BLOCK
    qb = q.reshape(bsz, nh, nb, Q_BLOCK, dh).transpose(2, 0, 1, 3, 4)
    Fb = F.reshape(bsz, nh, nb, Q_BLOCK).transpose(2, 0, 1, 3)
    pb = pos.reshape(nb, Q_BLOCK)
    o = lax.map(lambda a: fox_attend(a[0], k, v, a[1], F, a[2], pos), (qb, Fb, pb))
    return o.transpose(1, 2, 0, 3, 4).reshape(bsz, nh, t_len, dh)


def fox_attn_sample(q, k, v, lf, ck, cv, clf):
    p_len = ck.shape[2]
    t_len = q.shape[2]
    kk = jnp.concatenate([ck.astype(k.dtype), k], axis=2)
    vv = jnp.concatenate([cv.astype(v.dtype), v], axis=2)
    F = jnp.cumsum(jnp.concatenate([clf.astype(jnp.float32), lf], axis=-1), axis=-1)
    pos = jnp.arange(p_len + t_len)
    return fox_attend(q, kk, vv, F[..., p_len:], F, pos[p_len:], pos)


def mixer_ab(h, w_in, b_gate, gain, table, w_out, mstate, band_cache):
    bsz = h.shape[0]
    splits = [A_W, 2 * A_W, 3 * A_W, 4 * A_W, 4 * A_W + 2 * A_HEADS,
              4 * A_W + 2 * A_HEADS + B_W, 4 * A_W + 2 * A_HEADS + 2 * B_W]
    qa, ka, va, oa, gates, qb, kb, vb = jnp.split(h @ w_in, splits, axis=-1)
    gates = gates.astype(jnp.float32) + b_gate.astype(jnp.float32)
    ig = gates[..., :A_HEADS].transpose(0, 2, 1)
    lf = jax.nn.log_sigmoid(gates[..., A_HEADS:]).transpose(0, 2, 1)
    q = to_heads(qa, A_HEADS).astype(jnp.float32)
    k = to_heads(ka, A_HEADS).astype(jnp.float32) * A_DH ** -0.5
    v = to_heads(va, A_HEADS).astype(jnp.float32)
    if mstate is None:
        C0 = jnp.zeros((bsz, A_HEADS, A_DH, A_DH), jnp.float32)
        n0 = jnp.zeros((bsz, A_HEADS, A_DH), jnp.float32)
        m0 = jnp.zeros((bsz, A_HEADS), jnp.float32)
    else:
        C0, n0, m0 = (s.astype(jnp.float32) for s in mstate)
    hA, mst = mlstm_seq(q, k, v, ig, lf, C0, n0, m0)
    outA = (head_layernorm(hA, gain) * jax.nn.sigmoid(oa.astype(jnp.float32))).astype(h.dtype)
    qB, kB, vB = to_heads(qb, B_HEADS), to_heads(kb, B_HEADS), to_heads(vb, B_HEADS)
    if band_cache is None:
        oB = band_attn_prompt(qB, kB, vB, table)
        t_len = h.shape[1]
        w = min(BAND, t_len)
        rows = (kB[:, :, t_len - w:], vB[:, :, t_len - w:])
    else:
        oB = band_attn_sample(qB, kB, vB, band_cache[0], band_cache[1], table)
        rows = (kB, vB)
    out = jnp.concatenate([outA, from_heads(oB).astype(h.dtype)], axis=-1) @ w_out
    return out, mst, rows


def mixer_fox(h, w_in, b_f, w_out, cache):
    q, k, v, fp = jnp.split(h @ w_in, [C_W, 2 * C_W, 3 * C_W], axis=-1)
    lf = jax.nn.log_sigmoid(fp.astype(jnp.float32) + b_f.astype(jnp.float32)).transpose(0, 2, 1)
    q, k, v = to_heads(q, C_HEADS), to_heads(k, C_HEADS), to_heads(v, C_HEADS)
    if cache is None:
        o = fox_attn_prompt(q, k, v, lf)
    else:
        o = fox_attn_sample(q, k, v, lf, cache[0], cache[1], cache[2])
    return from_heads(o).astype(h.dtype) @ w_out, (k, v, lf)


def trunk(x, caches, norm_mix, norm_ffn, norm_final, w_in_ab, b_gate_ab, mlstm_gain,
          rel_bias_table, w_out_ab, w_in_fox, b_fox_f, w_out_fox, w_ffn_in, w_ffn_out):
    ab_states, fox_states = [], []
    for layer in range(DEPTH):
        h = rmsnorm(x, norm_mix[layer])
        j = layer // 2
        if layer % 2 == 0:
            ms = None if caches is None else (caches[0][j], caches[1][j], caches[2][j])
            bc = None if caches is None else (caches[3][j], caches[4][j])
            out, mst, rows = mixer_ab(h, w_in_ab[j], b_gate_ab[j], mlstm_gain[j],
                                      rel_bias_table[j], w_out_ab[j], ms, bc)
            ab_states.append(mst + rows)
        else:
            fc = None if caches is None else (caches[5][j], caches[6][j], caches[7][j])
            out, rows = mixer_fox(h, w_in_fox[j], b_fox_f[j], w_out_fox[j], fc)
            fox_states.append(rows)
        x = x + out
        x = x + swiglu(rmsnorm(x, norm_ffn[layer]), w_ffn_in[layer], w_ffn_out[layer])
    y = rmsnorm(x, norm_final)
    st_ab = [jnp.stack(s, axis=0) for s in zip(*ab_states)]
    st_fox = [jnp.stack(s, axis=0) for s in zip(*fox_states)]
    return y, st_ab, st_fox


def setup_inputs(seed: int = 0) -> dict:
    key = jax.random.key(seed)
    ks = jax.random.split(key, 26)

    def nrm(k, shape, scale):
        return jax.random.normal(k, shape, jnp.float32) * scale

    w_band = min(BAND, PAST_LEN)
    b_gate_ab = jnp.concatenate(
        [nrm(ks[20], (N_AB_LAYERS, A_HEADS), 0.1),
         jnp.linspace(3.0, 6.0, A_HEADS)[None, :] + nrm(ks[21], (N_AB_LAYERS, A_HEADS), 0.1)], axis=-1)
    return {
        'x_prompt': nrm(ks[0], (BATCH, SEQ, D_MODEL), 1.0),
        'x_sample': nrm(ks[1], (DEC_BATCH, DEC_SEQ, D_MODEL), 1.0),
        'state_mlstm_C': nrm(ks[2], (N_AB_LAYERS, DEC_BATCH, A_HEADS, A_DH, A_DH), 0.5),
        'state_mlstm_n': nrm(ks[3], (N_AB_LAYERS, DEC_BATCH, A_HEADS, A_DH), 1.0),
        'state_mlstm_m': nrm(ks[4], (N_AB_LAYERS, DEC_BATCH, A_HEADS), 0.5),
        'cache_band_k': nrm(ks[5], (N_AB_LAYERS, DEC_BATCH, B_HEADS, w_band, B_DH), 1.0),
        'cache_band_v': nrm(ks[6], (N_AB_LAYERS, DEC_BATCH, B_HEADS, w_band, B_DH), 1.0),
        'cache_fox_k': nrm(ks[7], (N_FOX_LAYERS, DEC_BATCH, C_HEADS, PAST_LEN, C_DH), 1.0),
        'cache_fox_v': nrm(ks[8], (N_FOX_LAYERS, DEC_BATCH, C_HEADS, PAST_LEN, C_DH), 1.0),
        'cache_fox_logf': jax.nn.log_sigmoid(
            2.0 + nrm(ks[9], (N_FOX_LAYERS, DEC_BATCH, C_HEADS, PAST_LEN), 1.0)),
        'norm_mix': 1.0 + nrm(ks[10], (DEPTH, D_MODEL), 0.05),
        'norm_ffn': 1.0 + nrm(ks[11], (DEPTH, D_MODEL), 0.05),
        'norm_final': 1.0 + nrm(ks[12], (D_MODEL,), 0.05),
        'w_in_ab': nrm(ks[13], (N_AB_LAYERS, D_MODEL, IN_AB), D_MODEL ** -0.5),
        'b_gate_ab': b_gate_ab,
        'mlstm_gain': 1.0 + nrm(ks[14], (N_AB_LAYERS, A_W), 0.05),
        'rel_bias_table': nrm(ks[15], (N_AB_LAYERS, B_HEADS, N_REL), 0.2),
        'w_out_ab': nrm(ks[16], (N_AB_LAYERS, A_W + B_W, D_MODEL), (A_W + B_W) ** -0.5),
        'w_in_fox': nrm(ks[17], (N_FOX_LAYERS, D_MODEL, IN_FOX), D_MODEL ** -0.5),
        'b_fox_f': jnp.linspace(0.0, 4.0, C_HEADS)[None, :] + nrm(ks[18], (N_FOX_LAYERS, C_HEADS), 0.1),
        'w_out_fox': nrm(ks[19], (N_FOX_LAYERS, C_W, D_MODEL), C_W ** -0.5),
        'w_ffn_in': nrm(ks[22], (DEPTH, D_MODEL, 2 * FFN_HIDDEN), D_MODEL ** -0.5),
        'w_ffn_out': nrm(ks[23], (DEPTH, FFN_HIDDEN, D_MODEL), FFN_HIDDEN ** -0.5),
    }


def reference(x_prompt, x_sample, state_mlstm_C, state_mlstm_n, state_mlstm_m,
              cache_band_k, cache_band_v, cache_fox_k, cache_fox_v, cache_fox_logf,
              norm_mix, norm_ffn, norm_final, w_in_ab, b_gate_ab, mlstm_gain,
              rel_bias_table, w_out_ab, w_in_fox, b_fox_f, w_out_fox, w_ffn_in, w_ffn_out):
    y_prompt, (p_C, p_n, p_m, p_bk, p_bv), (p_fk, p_fv, p_flf) = trunk(
        x_prompt, None, norm_mix, norm_ffn, norm_final, w_in_ab, b_gate_ab, mlstm_gain,
        rel_bias_table, w_out_ab, w_in_fox, b_fox_f, w_out_fox, w_ffn_in, w_ffn_out)
    caches = (state_mlstm_C, state_mlstm_n, state_mlstm_m, cache_band_k, cache_band_v,
              cache_fox_k, cache_fox_v, cache_fox_logf)
    y_sample, (s_C, s_n, s_m, s_bk, s_bv), (s_fk, s_fv, s_flf) = trunk(
        x_sample, caches, norm_mix, norm_ffn, norm_final, w_in_ab, b_gate_ab, mlstm_gain,
        rel_bias_table, w_out_ab, w_in_fox, b_fox_f, w_out_fox, w_ffn_in, w_ffn_out)
    return (y_prompt, y_sample, p_C, p_n, p_m, p_bk, p_bv, p_fk, p_fv, p_flf,
            s_C, s_n, s_m, s_bk, s_bv, s_fk, s_fv, s_flf)
```

```python
import contextlib
import numpy as np
import concourse.bass as bass
import concourse.mybir as mybir
from concourse.bass_utils import run_bass_kernel_spmd

F32 = mybir.dt.float32
BF16 = mybir.dt.bfloat16
AF = mybir.ActivationFunctionType
ALU = mybir.AluOpType

D = 1024
TP = 4096
NS = 4
TS = 64
R = TP + NS * TS
NT = R // 128
HID = 2816
BIG = 30000.0


class Buf:
    __slots__ = ("w", "r")

    def __init__(self):
        self.w = []
        self.r = {}


class TT:
    def __init__(self, t):
        self.t = t
        self.b = Buf()

    def __getitem__(self, key):
        return self.t[key]


class Eng:
    def __init__(self, name, h, sem):
        self.name = name
        self.h = h
        self.sem = sem
        self.count = 0
        self.seen = {}


class DQ:
    def __init__(self, name, sems):
        self.sems = sems
        self.keys = [f"{name}{i}" for i in range(len(sems))]
        self.uses = [0] * len(sems)
        self.next = 0


class KB:
    def __init__(self, nc, es):
        self.nc = nc
        self.eng = {}
        for name, h in (("pe", nc.tensor), ("act", nc.scalar), ("dve", nc.vector),
                        ("pool", nc.gpsimd), ("sp", nc.sync)):
            self.eng[name] = Eng(name, h, es.enter_context(nc.semaphore("sem_" + name)))
        self.dq = {}
        for q in ("sp", "pool"):
            self.dq[q] = DQ("dq" + q, [es.enter_context(nc.semaphore(f"dq{q}{i}")) for i in range(8)])

    def sb(self, es, name, shape, dt):
        return TT(es.enter_context(self.nc.sbuf_tensor(name, shape, dt)))

    def ps(self, es, name, shape, dt=F32):
        return TT(es.enter_context(self.nc.psum_tensor(name, shape, dt)))

    def _wait(self, e, deps):
        best = {}
        for (key, sem, val) in deps:
            if key == "pe" and e.name == "pe":
                continue
            if e.seen.get(key, 0) >= val:
                continue
            if key not in best or best[key][1] < val:
                best[key] = (sem, val)
        for key, (sem, val) in best.items():
            e.h.wait_ge(sem, val)
            e.seen[key] = val

    def _deps(self, reads, writes, par=(), en=None):
        deps = []
        for t in reads:
            deps.extend(t.b.w)
        for t in writes:
            deps.extend(d for d in t.b.w if d[0] != en)
            deps.extend(d for d in t.b.r.values() if d[0] != en)
        for t in par:
            deps.extend(d for d in t.b.r.values() if d[0] != en)
        return deps

    def _mark(self, tok, reads, writes, par=()):
        for t in par:
            t.b.w.append(tok)
        for t in writes:
            t.b.w = [tok]
            t.b.r = {}
        for t in reads:
            if t in writes:
                continue
            old = t.b.r.get(tok[0])
            if old is None or old[2] < tok[2]:
                t.b.r[tok[0]] = tok

    def op(self, en, fn, reads=(), writes=(), par=(), skip_self=False):
        e = self.eng[en]
        self._wait(e, self._deps(reads, writes, par, en if skip_self else None))
        inst = fn(e.h)
        e.count += 1
        inst.then_inc(e.sem, 1)
        tok = (en, e.sem, e.count)
        self._mark(tok, reads, writes, par)
        return tok

    def dma(self, qn, out, in_, reads=(), writes=(), par=(), **kw):
        e = self.eng[qn]
        q = self.dq[qn]
        self._wait(e, self._deps(reads, writes, par))
        s = q.next % len(q.sems)
        q.next += 1
        if q.uses[s] > 0 and e.seen.get(q.keys[s], 0) < 16 * q.uses[s]:
            e.h.wait_ge(q.sems[s], 16 * q.uses[s])
            e.seen[q.keys[s]] = 16 * q.uses[s]
        inst = e.h.dma_start(out=out, in_=in_, **kw)
        q.uses[s] += 1
        inst.then_inc(q.sems[s], 16)
        tok = (q.keys[s], q.sems[s], 16 * q.uses[s])
        self._mark(tok, reads, writes, par)
        return tok

    def barrier(self):
        toks = [(n, e.sem, e.count) for n, e in self.eng.items() if e.count > 0]
        for q in self.dq.values():
            for i in range(len(q.sems)):
                if q.uses[i] > 0:
                    toks.append((q.keys[i], q.sems[i], 16 * q.uses[i]))
        for e in self.eng.values():
            self._wait(e, [t for t in toks if t[0] != e.name])

    def mm(self, out, lhsT, rhs, start, stop, reads, writes):
        return self.op("pe", lambda h: h.matmul(out, lhsT=lhsT, rhs=rhs, start=start, stop=stop),
                       reads=reads, writes=writes)

    def tr(self, out, in_, ident, reads, writes):
        return self.op("pe", lambda h: h.transpose(out=out, in_=in_, identity=ident),
                       reads=reads, writes=writes)


def load_w(k, dst, w_ap, kc_n, ncols, blk=2048):
    nb = -(-ncols // blk)
    blk = -(-ncols // nb)
    for kc in range(kc_n):
        for c0 in range(0, ncols, blk):
            c1 = min(ncols, c0 + blk)
            k.dma("pool", dst[:, kc, c0:c1], w_ap[kc * 128:(kc + 1) * 128, c0:c1], par=[dst])


def scale_rows(k, es, w, g_ap, kc_n, name, gcol=None):
    if gcol is None:
        gcol = k.sb(es, name, [128, kc_n], F32)
    k.dma("sp", gcol[:, :], g_ap.rearrange("(k p) -> p k", p=128), writes=[gcol],
          allow_slow_non_contiguous=True)
    for kc in range(kc_n):
        eng = "dve" if kc % 2 == 0 else "pool"
        k.op(eng, lambda h, kc=kc: h.tensor_scalar(out=w[:, kc, :], in0=w[:, kc, :],
                                                    scalar1=gcol[:, kc:kc + 1], scalar2=None,
                                                    op0=ALU.mult),
             reads=[gcol], writes=[w])


class Consts:
    pass


def make_consts(k, es):
    c = Consts()
    c.ones = k.sb(es, "c_ones", [128, 128], F32)
    c.ident = k.sb(es, "c_ident", [128, 128], F32)
    c.identb = k.sb(es, "c_identb", [128, 128], BF16)
    c.caus = k.sb(es, "c_caus", [128, 128], F32)
    c.cpos = k.sb(es, "c_cpos", [128, 896], F32)
    c.big = k.sb(es, "c_big", [128, 896], F32)
    c.sel = k.sb(es, "c_sel", [16, 16, 128], F32)
    k.op("pool", lambda h: h.memset(c.ones[:, :], 1.0), writes=[c.ones])
    k.op("pool", lambda h: h.memset(c.big[:, :], BIG), writes=[c.big])
    k.op("pool", lambda h: h.affine_select(out=c.ident[:, :], in_=c.ones[:, :], pattern=[[-1, 128]],
                                           compare_op=ALU.is_equal, fill=0.0, base=0,
                                           channel_multiplier=1),
         reads=[c.ones], writes=[c.ident])
    k.op("pool", lambda h: h.affine_select(out=c.caus[:, :], in_=c.ones[:, :], pattern=[[1, 128]],
                                           compare_op=ALU.is_ge, fill=0.0, base=0,
                                           channel_multiplier=-1),
         reads=[c.ones], writes=[c.caus])
    k.op("pool", lambda h: h.affine_select(out=c.cpos[:, :], in_=c.big[:, :], pattern=[[-1, 896]],
                                           compare_op=ALU.is_gt, fill=0.0, base=384,
                                           channel_multiplier=1),
         reads=[c.big], writes=[c.cpos])
    k.op("pool", lambda h: h.tensor_copy(out=c.identb[:, :], in_=c.ident[:, :]),
         reads=[c.ident], writes=[c.identb])
    for hh in range(16):
        k.op("dve", lambda h, hh=hh: h.tensor_copy(out=c.sel[:, hh, :],
                                                    in_=c.ident[0:16, hh:hh + 1].to_broadcast([16, 128])),
             reads=[c.ident], writes=[c.sel])
    return c


def rms_to_hT(k, c, x_t, hT, col0, st, ti):
    j = ti % 2
    k.op("act", lambda h: h.activation(out=st.junk[:, :], in_=x_t[:, :], func=AF.Square,
                                       accum_out=st.ss[j][:, :]),
         reads=[x_t], writes=[st.junk, st.ss[j]])
    k.op("act", lambda h: h.activation(out=st.sq[j][:, :], in_=st.ss[j][:, :], func=AF.Sqrt,
                                       scale=1.0 / D, bias=1e-6),
         reads=[st.ss[j]], writes=[st.sq[j]])
    k.op("dve", lambda h: h.reciprocal(out=st.rstd[j][:, :], in_=st.sq[j][:, :]),
         reads=[st.sq[j]], writes=[st.rstd[j]])
    k.op("act", lambda h: h.activation(out=st.xn[j][:, :], in_=x_t[:, :], func=AF.Copy,
                                       scale=st.rstd[j][:, :]),
         reads=[x_t, st.rstd[j]], writes=[st.xn[j]])
    for kc in range(8):
        k.tr(st.pT[j][:, kc * 128:(kc + 1) * 128], st.xn[j][:, kc * 128:(kc + 1) * 128],
             c.identb[:, :], reads=[st.xn[j], c.identb], writes=[st.pT[j]])
    k.op("dve", lambda h: h.tensor_copy(out=hT[:, :, col0:col0 + 128],
                                        in_=st.pT[j][:, :].rearrange("p (k n) -> p k n", k=8)),
         reads=[st.pT[j]], writes=[hT])


class NormState:
    def __init__(self, k, es, tag):
        self.junk = k.sb(es, tag + "junk", [128, D], BF16)
        self.ss = [k.sb(es, f"{tag}ss{i}", [128, 1], F32) for i in range(2)]
        self.sq = [k.sb(es, f"{tag}sq{i}", [128, 1], F32) for i in range(2)]
        self.rstd = [k.sb(es, f"{tag}rstd{i}", [128, 1], F32) for i in range(2)]
        self.xn = [k.sb(es, f"{tag}xn{i}", [128, D], BF16) for i in range(2)]
        self.pT = [k.ps(es, f"{tag}pT{i}", [128, D], BF16) for i in range(2)]


def evac(k, i, out, in_, reads, writes, scale=None, func=None):
    if func is not None or (i % 2 == 0):
        f = func if func is not None else AF.Copy
        sc = 1.0 if scale is None else scale
        return k.op("act", lambda h: h.activation(out=out, in_=in_, func=f, scale=sc),
                    reads=reads, writes=writes)
    if scale is None:
        scale = 1.0
    return k.op("dve", lambda h: h.tensor_scalar(out=out, in0=in_, scalar1=scale, scalar2=None,
                                                  op0=ALU.mult), reads=reads, writes=writes)


def groups():
    gs = [(g * 512, 512) for g in range(8)]
    gs.append((TP, NS * TS))
    return gs


def phase1(k, c, io, scr, rows):
    with contextlib.ExitStack() as es:
        stf = [k.sb(es, f"p1sf{i}", [128, 512], F32) for i in range(2)]
        W = k.sb(es, "p1W", [128, 8, 3592], BF16)
        load_w(k, W, io["w_in_ab"], 8, 3592)
        scale_rows(k, es, W, io["norm_mix"][0, :], 8, "p1g")
        st = NormState(k, es, "p1")
        xts = [k.sb(es, f"p1x{i}", [128, D], F32) for i in range(2)]
        hTs = [k.sb(es, f"p1h{i}", [128, 8, 512], BF16) for i in range(2)]
        pfm = [k.ps(es, f"p1pf{i}", [128, 512]) for i in range(2)]
        ptm = [k.ps(es, f"p1pt{i}", [128, 512]) for i in range(2)]
        stb = [k.sb(es, f"p1sb{i}", [128, 512], BF16) for i in range(4)]
        fm = [(0, scr["qTm"], None), (512, scr["kTm"], 128.0 ** -0.5),
              (2056, scr["qTb"], 0.125), (2568, scr["kTb"], None)]
        ev = 0
        sbi = 0
        sfi = 0
        ti = 0
        gl = groups()
        tis = [0]

        def norm_group(gi):
            tok0, n = gl[gi]
            for t in range(n // 128):
                xt = xts[tis[0] % 2]
                k.dma("sp", xt[:, :], io["xin"][tok0 + t * 128: tok0 + (t + 1) * 128, :], writes=[xt])
                rms_to_hT(k, c, xt, hTs[gi % 2], t * 128, st, tis[0])
                tis[0] += 1

        norm_group(0)
        for gi, (tok0, n) in enumerate(gl):
            hT = hTs[gi % 2]
            if gi + 1 < len(gl):
                norm_group(gi + 1)
            for (c0, dst, scale) in ([] if 'nofm' in DBG else fm):
                for mc in range(4):
                    pp = pfm[ev % 2]
                    for kc in range(8):
                        k.mm(pp[:, 0:n], W[:, kc, c0 + mc * 128: c0 + (mc + 1) * 128], hT[:, kc, 0:n],
                             kc == 0, kc == 7, reads=[W, hT], writes=[pp])
                    sbt = stb[sbi % 4]
                    sbi += 1
                    evac(k, ev, sbt[:, 0:n], pp[:, 0:n], [pp], [sbt], scale=scale)
                    ev += 1
                    k.dma("sp", dst[mc * 128:(mc + 1) * 128, tok0:tok0 + n], sbt[:, 0:n], reads=[sbt])
            for (c0, row) in ([] if 'nogate' in DBG else ((2048, rows["ig"]), (2052, rows["fg"]))):
                pp = pfm[ev % 2]
                for kc in range(8):
                    k.mm(pp[0:4, 0:n], W[:, kc, c0:c0 + 4], hT[:, kc, 0:n], kc == 0, kc == 7,
                         reads=[W, hT], writes=[pp])
                evac(k, 1, row[0:4, tok0:tok0 + n], pp[0:4, 0:n], [pp], [row])
                ev += 1
            for t in range(0 if 'notm' in DBG else n // 128):
                r0 = tok0 + t * 128
                is_out = (r0 >= TP - 512)
                blocks = [(512, scr["kmTok"], 128.0 ** -0.5, None, None),
                          (1024, scr["vmTok"], None, None, None),
                          (1536, scr["osTok"], None, (None if 'nosig' in DBG else AF.Sigmoid), None),
                          (3080, scr["vbTok"], None, None, "bv")]
                if is_out:
                    blocks.append((2568, None, None, None, "bk"))
                for (c0, dst, scale, func, outn) in blocks:
                    pp = ptm[ev % 2]
                    for kc in range(8):
                        k.mm(pp[:, :], hT[:, kc, t * 128:(t + 1) * 128], W[:, kc, c0:c0 + 512],
                             kc == 0, kc == 7, reads=[W, hT], writes=[pp])
                    both = (outn is not None and is_out)
                    if both:
                        sft = stf[sfi % 2]
                        sfi += 1
                        evac(k, 1, sft[:, :], pp[:, :], [pp], [sft])
                        ev += 1
                    if dst is not None:
                        sbt = stb[sbi % 4]
                        sbi += 1
                        if both:
                            k.op("act", lambda h, sbt=sbt, sft=sft: h.activation(out=sbt[:, :], in_=sft[:, :], func=AF.Copy),
                                 reads=[sft], writes=[sbt])
                        else:
                            evac(k, ev, sbt[:, :], pp[:, :], [pp], [sbt], scale=scale, func=func)
                        ev += 1
                        k.dma("sp", dst[r0:r0 + 128, :], sbt[:, :], reads=[sbt])
                    if both:
                        for hh in range(0 if ('noodma2' in DBG or 'whole' in DBG) else 8):
                            src = sft[:, hh * 64:(hh + 1) * 64]
                            if r0 < TP:
                                q0 = r0 - (TP - 512)
                                if 'toscr' in DBG:
                                    k.dma("sp", scr["x1"][q0:q0 + 128, hh * 64:(hh + 1) * 64], src, reads=[sft])
                                else:
                                    k.dma("pool" if 'opool' in DBG else "sp", io["p" + outn][hh, q0:q0 + 128, :], src, reads=[sft])
                            else:
                                for s2 in range(2):
                                    sq = (r0 - TP) // TS + s2
                                    k.dma("sp", io["s" + outn][sq, hh, :, :], src[s2 * 64:(s2 + 1) * 64],
                                          reads=[sft])
    k.barrier()


IN_SPECS = [
    ("xin", [R, D]), ("mC0", [NS, 4, 128, 128]), ("mn0", [NS, 4, 128]), ("mm0", [NS, 4]),
    ("cbk", [NS, 8, 512, 64]), ("cbv", [NS, 8, 512, 64]),
    ("cfk", [NS, 16, 2048, 64]), ("cfv", [NS, 16, 2048, 64]), ("cflf", [NS, 16, 2048]),
    ("norm_mix", [2, D]), ("norm_ffn", [2, D]), ("norm_final", [D]),
    ("w_in_ab", [D, 3592]), ("b_gate_ab", [8]), ("mlstm_gain", [512]), ("bandbias", [8, 128, 640]),
    ("bandmask", [128, 640]),
    ("w_out_ab", [D, D]), ("w_in_fox", [D, 3088]), ("b_fox_f", [16]), ("w_out_fox", [D, D]),
    ("w_ffn_in", [2, D, 2 * HID]), ("w_ffn_out", [2, HID, D]),
]
OUT_SPECS = [
    ("y", [R, D]), ("pC", [4, 128, 128]), ("pn", [4, 128]), ("pm", [4, 1]),
    ("pbk", [8, 512, 64]), ("pbv", [8, 512, 64]),
    ("pfk", [16, TP, 64]), ("pfv", [16, TP, 64]), ("pflf", [16, TP]),
    ("sC", [NS, 4, 128, 128]), ("sn", [NS, 4, 128]), ("sm", [NS, 4, 1]),
    ("sbk", [NS, 8, TS, 64]), ("sbv", [NS, 8, TS, 64]),
    ("sfk", [NS, 16, TS, 64]), ("sfv", [NS, 16, TS, 64]), ("sflf", [NS, 16, TS]),
]
SCR_SPECS = [
    ("qTm", [512, R], BF16), ("kTm", [512, R], BF16), ("qTb", [512, R], BF16), ("kTb", [512, R], BF16),
    ("kmTok", [R, 512], BF16), ("vmTok", [R, 512], BF16), ("osTok", [R, 512], BF16),
    ("vbTok", [R, 512], BF16), ("catT", [D, R], BF16),
    ("x1", [R, D], F32), ("x2", [R, D], F32), ("x3", [R, D], F32),
    ("qTf", [D, R], BF16), ("kTf", [D, R], BF16), ("vfTok", [R, D], BF16), ("attT", [D, R], BF16),
    ("hidT", [HID, R], BF16), ("rds", [4, 512], F32),
]

PHASES = 99
DBG = ""


def run_phases(k, c, io, scr, upto=None):
    upto = PHASES if upto is None else upto
    with contextlib.ExitStack() as es1:
        rows = {"ig": k.sb(es1, "row_ig", [4, R], F32), "fg": k.sb(es1, "row_fg", [4, R], F32)}
        pers = {"GW": k.sb(es1, "GW", [4, 2, R], F32), "COLS": k.sb(es1, "COLS", [128, NCH, 3, 4], F32),
                "DECB": k.sb(es1, "DECB", [128, 4, NCH], F32)}
        phase1(k, c, io, scr, rows)
        if upto >= 2:
            phase2(k, c, io, scr, rows, pers)
        if upto >= 3:
            phase3(k, c, io, scr, pers)
    if upto >= 4:
        phase4(k, c, io, scr)
    if upto >= 6:
        with contextlib.ExitStack() as esw:
            Wg = ffn_w_alloc(k, esw, "p6")
            outproj_phase(k, c, io, scr, scr["catT"], io["w_out_ab"], io["xin"], scr["x1"], "p5",
                          prefetch=lambda: ffn_w_load(k, io, 0, Wg))
            ffn_in_phase(k, c, io, scr, 0, scr["x1"], "p6", W=Wg[0])
        ffn_out_phase(k, c, io, scr, 0, scr["x1"], scr["x2"], False, "p7")
    if upto >= 8:
        fox_phases(k, c, io, scr)
    if upto >= 9:
        with contextlib.ExitStack() as esw:
            Wg = ffn_w_alloc(k, esw, "pa")
            outproj_phase(k, c, io, scr, scr["attT"], io["w_out_fox"], scr["x2"], scr["x3"], "p9",
                          prefetch=lambda: ffn_w_load(k, io, 1, Wg))
            ffn_in_phase(k, c, io, scr, 1, scr["x3"], "pa", W=Wg[0])
        ffn_out_phase(k, c, io, scr, 1, scr["x3"], io["y"], True, "pb")


def build():
    nc = bass.Bass("TRN2", target_bir_lowering=False)
    io = {}
    for name, shape in IN_SPECS:
        io[name] = nc.dram_tensor(name, shape, F32, kind="ExternalInput").ap()
    for name, shape in OUT_SPECS:
        io[name] = nc.dram_tensor(name, shape, F32, kind="ExternalOutput").ap()
    scr = {}
    for name, shape, dt in SCR_SPECS:
        scr[name] = nc.dram_tensor("scr_" + name, shape, dt).ap()
    with contextlib.ExitStack() as es:
        k = KB(nc, es)
        c = make_consts(k, es)
        run_phases(k, c, io, scr)
        k.barrier()
    return nc


_NC_CACHE = {}


def kernel(**inp):
    f = lambda a: np.ascontiguousarray(np.asarray(a, dtype=np.float32))
    xp, xs = f(inp["x_prompt"]), f(inp["x_sample"])
    kk = np.arange(128)[:, None]
    qq = np.arange(640)[None, :]
    rel = qq - kk
    idx = np.clip(rel, -63, 128) + 63
    table = f(inp["rel_bias_table"])[0]
    bandbias = np.ascontiguousarray(table[:, idx])
    dch = (qq // 64) - (kk // 64)
    bandmask = np.where((dch >= 0) & (dch <= 8), 0.0, -BIG).astype(np.float32)
    common = {
        "norm_mix": f(inp["norm_mix"]), "norm_ffn": f(inp["norm_ffn"]), "norm_final": f(inp["norm_final"]),
        "w_in_ab": f(inp["w_in_ab"])[0], "b_gate_ab": f(inp["b_gate_ab"])[0],
        "mlstm_gain": f(inp["mlstm_gain"])[0], "bandbias": bandbias, "bandmask": bandmask,
        "w_out_ab": f(inp["w_out_ab"])[0], "w_in_fox": f(inp["w_in_fox"])[0],
        "b_fox_f": f(inp["b_fox_f"])[0], "w_out_fox": f(inp["w_out_fox"])[0],
        "w_ffn_in": f(inp["w_ffn_in"]), "w_ffn_out": f(inp["w_ffn_out"]),
    }
    in_maps = []
    for cid in range(8):
        b = cid % 4
        s0 = 4 * cid
        m = dict(common)
        m["xin"] = np.ascontiguousarray(np.concatenate([xp[b], xs[s0:s0 + 4].reshape(NS * TS, D)], axis=0))
        m["mC0"] = f(inp["state_mlstm_C"])[0, s0:s0 + 4]
        m["mn0"] = f(inp["state_mlstm_n"])[0, s0:s0 + 4]
        m["mm0"] = f(inp["state_mlstm_m"])[0, s0:s0 + 4]
        m["cbk"] = f(inp["cache_band_k"])[0, s0:s0 + 4]
        m["cbv"] = f(inp["cache_band_v"])[0, s0:s0 + 4]
        m["cfk"] = f(inp["cache_fox_k"])[0, s0:s0 + 4]
        m["cfv"] = f(inp["cache_fox_v"])[0, s0:s0 + 4]
        m["cflf"] = f(inp["cache_fox_logf"])[0, s0:s0 + 4]
        in_maps.append({kk_: np.ascontiguousarray(v) for kk_, v in m.items()})
    if "nc" not in _NC_CACHE:
        _NC_CACHE["nc"] = build()
    res = run_bass_kernel_spmd(_NC_CACHE["nc"], in_maps, core_ids=list(range(8)))
    rs = res.results
    P = lambda n: np.stack([rs[b][n] for b in range(4)], axis=0)
    S = lambda n: np.concatenate([rs[cid][n] for cid in range(8)], axis=0)
    y_prompt = np.stack([rs[b]["y"][:TP] for b in range(4)], axis=0)
    y_sample = np.concatenate([rs[cid]["y"][TP:].reshape(NS, TS, D) for cid in range(8)], axis=0)
    outs = (
        y_prompt, y_sample,
        P("pC")[None], P("pn")[None], P("pm")[None, :, :, 0],
        P("pbk")[None], P("pbv")[None], P("pfk")[None], P("pfv")[None], P("pflf")[None],
        S("sC")[None], S("sn")[None], S("sm")[None, :, :, 0],
        S("sbk")[None], S("sbv")[None], S("sfk")[None], S("sfv")[None], S("sflf")[None],
    )
    return tuple(np.ascontiguousarray(o, dtype=np.float32) for o in outs)


def chunks():
    cs = [(c * 128, 128) for c in range(32)]
    cs += [(TP + i * TS, TS) for i in range(NS)]
    return cs


NCH = 36


def phase2(k, c, io, scr, rows, pers):
    GW, COLS, DECB = pers["GW"], pers["COLS"], pers["DECB"]
    ig, fg = rows["ig"], rows["fg"]
    with contextlib.ExitStack() as es:
        negb = k.sb(es, "p2negb", [4, 1], F32)
        bigc = k.sb(es, "p2big", [4, 1], F32)
        m0 = k.sb(es, "p2m0", [4, NS], F32)
        NF = k.sb(es, "p2NF", [4, R], F32)
        WS = k.sb(es, "p2WS", [4, R], F32)
        EM = k.sb(es, "p2EM", [4, R], F32)
        DEC = k.sb(es, "p2DEC", [4, NCH], F32)
        PC = k.ps(es, "p2PC", [128, NCH, 3, 4])
        PD = k.ps(es, "p2PD", [128, 4, NCH])
        bg = io["b_gate_ab"]
        k.dma("sp", negb[:, :], bg[4:8].rearrange("(h o) -> h o", o=1), writes=[negb])
        k.dma("sp", bigc[:, :], bg[0:4].rearrange("(h o) -> h o", o=1), writes=[bigc])
        k.dma("sp", m0[:, :], io["mm0"].rearrange("s h -> h s"), writes=[m0], allow_slow_non_contiguous=True)
        k.op("dve", lambda h: h.tensor_scalar(out=negb[:, :], in0=negb[:, :], scalar1=-1.0, scalar2=None,
                                              op0=ALU.mult), writes=[negb])
        k.op("act", lambda h: h.activation(out=fg[:, :], in_=fg[:, :], func=AF.Exp, scale=-1.0,
                                           bias=negb[:, :]), reads=[negb], writes=[fg])
        k.op("act", lambda h: h.activation(out=fg[:, :], in_=fg[:, :], func=AF.Ln, bias=1.0), writes=[fg])
        segs = [(0, TP, None)] + [(TP + i * TS, TS, i) for i in range(NS)]
        for (s0, sn, si) in segs:
            k.op("dve", lambda h, s0=s0, sn=sn: h.tensor_tensor_scan(
                out=NF[:, s0:s0 + sn], data0=fg[:, s0:s0 + sn], data1=fg[:, s0:s0 + sn], initial=0.0,
                op0=ALU.add, op1=ALU.max), reads=[fg], writes=[NF])
        k.op("dve", lambda h: h.scalar_tensor_tensor(out=ig[:, :], in0=ig[:, :], scalar=bigc[:, :],
                                                     in1=NF[:, :], op0=ALU.add, op1=ALU.add),
             reads=[bigc, NF], writes=[ig])
        for (s0, sn, si) in segs:
            init = 0.0 if si is None else m0[:, si:si + 1]
            k.op("dve", lambda h, s0=s0, sn=sn, init=init: h.tensor_tensor_scan(
                out=GW[:, 0, s0:s0 + sn], data0=ig[:, s0:s0 + sn], data1=ig[:, s0:s0 + sn], initial=init,
                op0=ALU.max, op1=ALU.max), reads=[ig, m0], writes=[GW])
        Gp = GW[:, 0, 0:TP].rearrange("p (c t) -> p c t", t=128)
        k.op("pool", lambda h: h.memset(fg[:, 0:128], 0.0), writes=[fg])
        k.op("dve", lambda h: h.tensor_copy(
            out=fg[:, 128:TP].rearrange("p (c t) -> p c t", t=128),
            in_=Gp[:, 0:31, 127:128].to_broadcast([4, 31, 128])), reads=[GW], writes=[fg])
        for i in range(NS):
            s0 = TP + i * TS
            k.op("dve", lambda h, s0=s0, i=i: h.tensor_copy(out=fg[:, s0:s0 + TS],
                                                          in_=m0[:, i:i + 1].to_broadcast([4, TS])),
                 reads=[m0], writes=[fg])
        k.op("dve", lambda h: h.tensor_tensor(out=WS[:, :], in0=fg[:, :], in1=GW[:, 0, :], op=ALU.subtract),
             reads=[fg, GW], writes=[WS])
        k.op("act", lambda h: h.activation(out=GW[:, 1, :], in_=WS[:, :], func=AF.Exp), reads=[WS], writes=[GW])
        k.op("dve", lambda h: h.tensor_copy(
            out=WS[:, 0:TP].rearrange("p (c t) -> p c t", t=128),
            in_=Gp[:, :, 127:128].to_broadcast([4, 32, 128])), reads=[GW], writes=[WS])
        for i in range(NS):
            s0 = TP + i * TS
            k.op("dve", lambda h, s0=s0: h.tensor_copy(
                out=WS[:, s0:s0 + TS], in_=GW[:, 0, s0 + TS - 1:s0 + TS].to_broadcast([4, TS])),
                reads=[GW], writes=[WS])
        k.op("dve", lambda h: h.tensor_tensor(
            out=DEC[:, 0:32], in0=fg[:, 0:TP].rearrange("p (c t) -> p c t", t=128)[:, :, 0],
            in1=WS[:, 0:TP].rearrange("p (c t) -> p c t", t=128)[:, :, 0], op=ALU.subtract),
            reads=[fg, WS], writes=[DEC])
        for i in range(NS):
            s0 = TP + i * TS
            k.op("dve", lambda h, s0=s0, i=i: h.tensor_tensor(out=DEC[:, 32 + i:33 + i], in0=fg[:, s0:s0 + 1],
                                                            in1=WS[:, s0:s0 + 1], op=ALU.subtract),
                 reads=[fg, WS], writes=[DEC])
        k.op("act", lambda h: h.activation(out=DEC[:, :], in_=DEC[:, :], func=AF.Exp), writes=[DEC])
        k.op("dve", lambda h: h.tensor_tensor(out=WS[:, :], in0=ig[:, :], in1=WS[:, :], op=ALU.subtract),
             reads=[ig], writes=[WS])
        k.op("act", lambda h: h.activation(out=WS[:, :], in_=WS[:, :], func=AF.Exp), writes=[WS])
        k.op("dve", lambda h: h.tensor_tensor(out=EM[:, :], in0=GW[:, 0, :], in1=NF[:, :], op=ALU.subtract),
             reads=[GW, NF], writes=[EM])
        k.dma("sp", io["pm"][:, :], EM[:, TP - 1:TP], reads=[EM])
        for i in range(NS):
            e1 = TP + (i + 1) * TS
            k.dma("sp", io["sm"][i, :, :], EM[:, e1 - 1:e1], reads=[EM])
        k.op("act", lambda h: h.activation(out=EM[:, :], in_=EM[:, :], func=AF.Exp, scale=-1.0), writes=[EM])
        for ci, (r0, n) in enumerate(chunks()):
            for xi, X in enumerate((ig, WS, EM)):
                k.tr(PC[0:n, ci, xi, :], X[0:4, r0:r0 + n], c.ident[0:4, 0:4], reads=[X, c.ident], writes=[PC])
        k.op("dve", lambda h: h.tensor_copy(out=COLS[:, 0:32, :, :], in_=PC[:, 0:32, :, :]), reads=[PC], writes=[COLS])
        k.op("dve", lambda h: h.tensor_copy(out=COLS[0:64, 32:NCH, :, :], in_=PC[0:64, 32:NCH, :, :]), reads=[PC], par=[COLS])
        for hh in range(4):
            k.mm(PD[:, hh, :], c.sel[0:4, hh, :], DEC[0:4, :], True, True, reads=[c.sel, DEC], writes=[PD])
        k.op("dve", lambda h: h.tensor_copy(out=DECB[:, :, :], in_=PD[:, :, :]), reads=[PD], writes=[DECB])
    k.barrier()


def phase3(k, c, io, scr, pers):
    GW, COLS, DECB = pers["GW"], pers["COLS"], pers["DECB"]
    with contextlib.ExitStack() as es:
        gain = k.sb(es, "p3gain", [128, 512], F32)
        k.dma("sp", gain[:, :], io["mlstm_gain"].partition_broadcast(128), writes=[gain])
        qTs = [k.sb(es, f"p3q{i}", [128, 4, 128], BF16) for i in range(2)]
        kTs = [k.sb(es, f"p3k{i}", [128, 4, 128], BF16) for i in range(2)]
        kts = [k.sb(es, f"p3kt{i}", [128, 4, 128], BF16) for i in range(2)]
        vas = [k.sb(es, f"p3va{i}", [128, 4, 129], BF16) for i in range(2)]
        oss = [k.sb(es, f"p3os{i}", [128, 512], BF16) for i in range(3)]
        for i in range(2):
            k.op("pool", lambda h, i=i: h.memset(vas[i][:, :, 128:129], 1.0), writes=[vas[i]])
        Cf = [k.sb(es, f"p3Cf{h}", [128, 129], F32) for h in range(4)]
        Cb = [k.sb(es, f"p3Cb{h}", [128, 129], BF16) for h in range(4)]
        E = k.sb(es, "p3E", [128, 4, 128], F32)
        WT = k.sb(es, "p3WT", [128, 4, 128], BF16)
        QS = k.sb(es, "p3QS", [128, 4, 128], BF16)
        KW = k.sb(es, "p3KW", [128, 4, 128], BF16)
        ONs = [k.sb(es, f"p3ON{i}", [128, 4, 129], F32) for i in range(2)]
        H = k.sb(es, "p3H", [128, 4, 128], F32)
        SQ = k.sb(es, "p3SQ", [128, 4, 128], F32)
        OAs = [k.sb(es, f"p3OA{i}", [128, 512], BF16) for i in range(2)]
        G2s = [k.sb(es, f"p3G2{i}", [128, 512], F32) for i in range(2)]
        junk = k.sb(es, "p3junk", [128, 128], F32)
        sm2 = [k.sb(es, f"p3t{i}", [128, 4], F32) for i in range(4)]
        OT = k.sb(es, "p3OT", [128, 4, 128], BF16)
        sm = [k.sb(es, f"p3s{i}", [128, 4], F32) for i in range(6)]
        PS_S = k.ps(es, "p3PS", [128, 4, 128])
        PS_G = k.ps(es, "p3PG", [128, 4, 128])
        PS_W = k.ps(es, "p3PW", [128, 4, 128])
        PS_O = [k.ps(es, f"p3PO{i}", [128, 2, 129]) for i in range(2)]
        PS_D = [k.ps(es, f"p3PD{i}", [128, 2, 129]) for i in range(2)]
        PS_T = k.ps(es, "p3PT", [128, 4, 128], BF16)
        qv = scr["qTm"].rearrange("(h d) r -> d h r", h=4)
        kv = scr["kTm"].rearrange("(h d) r -> d h r", h=4)
        catv = scr["catT"][0:512, :].rearrange("(h d) r -> d h r", h=4)
        chs = chunks()

        def load(ci):
            r0, n = chs[ci]
            j = ci % 2
            k.dma("sp", qTs[j][:, :, 0:n], qv[:, :, r0:r0 + n], writes=[qTs[j]])
            k.dma("sp", kTs[j][:, :, 0:n], kv[:, :, r0:r0 + n], writes=[kTs[j]])
            k.dma("sp", kts[j][0:n, :, :], scr["kmTok"][r0:r0 + n, :].rearrange("t (h e) -> t h e", h=4),
                  writes=[kts[j]])
            k.dma("sp", vas[j][0:n, :, 0:128], scr["vmTok"][r0:r0 + n, :].rearrange("t (h e) -> t h e", h=4),
                  writes=[vas[j]])
            k.dma("sp", oss[ci % 3][0:n, :], scr["osTok"][r0:r0 + n, :], writes=[oss[ci % 3]])

        load(0)
        pendA, pendA_b, pendB = [], [], []
        for ci, (r0, n) in enumerate(chs):
            j = ci % 2
            qT, kT, kt, va, osg = qTs[j], kTs[j], kts[j], vas[j], oss[ci % 3]
            ON = ONs[ci % 2]
            if ci + 1 < NCH:
                load(ci + 1)
            if ci == 0:
                for hh in range(4):
                    k.op("pool", lambda h, hh=hh: h.memset(Cf[hh][:, :], 0.0), writes=[Cf[hh]])
                    k.op("pool", lambda h, hh=hh: h.memset(Cb[hh][:, :], 0.0), writes=[Cb[hh]])
            elif ci >= 32:
                si = ci - 32
                for hh in range(4):
                    k.dma("sp", Cf[hh][:, 0:128], io["mC0"][si, hh, :, :], writes=[Cf[hh]])
                    k.dma("sp", Cf[hh][:, 128:129], io["mn0"][si, hh, :].rearrange("(d o) -> d o", o=1),
                          par=[Cf[hh]])
                    k.op("act", lambda h, hh=hh: h.activation(out=Cb[hh][:, :], in_=Cf[hh][:, :], func=AF.Copy),
                         reads=[Cf[hh]], writes=[Cb[hh]])
            for hh in range(4):
                k.mm(PS_S[0:n, hh, 0:n], kT[:, hh, 0:n], qT[:, hh, 0:n], True, True, reads=[kT, qT], writes=[PS_S])
            for hh in range(4):
                k.mm(PS_G[0:n, hh, 0:n], c.sel[0:4, hh, 0:n], GW[0:4, 0, r0:r0 + n], True, True,
                     reads=[c.sel, GW], writes=[PS_G])
            for hh in range(4):
                k.mm(PS_W[:, hh, 0:n], c.sel[0:4, hh, :], GW[0:4, 1, r0:r0 + n], True, True,
                     reads=[c.sel, GW], writes=[PS_W])
            for hh in range(4):
                k.op("act", lambda h, hh=hh: h.activation(out=E[0:n, hh, 0:n], in_=PS_G[0:n, hh, 0:n], func=AF.Exp,
                                                        scale=-1.0, bias=COLS[0:n, ci, 0, hh:hh + 1]),
                     reads=[PS_G, COLS], writes=[E])
            for hh in range(4):
                k.op("pool", lambda h, hh=hh: h.tensor_tensor(out=E[0:n, hh, 0:n], in0=E[0:n, hh, 0:n],
                                                            in1=c.caus[0:n, 0:n], op=ALU.mult),
                     reads=[c.caus], writes=[E])
            k.op("dve", lambda h: h.tensor_tensor(out=WT[0:n, :, 0:n], in0=PS_S[0:n, :, 0:n], in1=E[0:n, :, 0:n],
                                                  op=ALU.mult), reads=[PS_S, E], writes=[WT])
            k.op("dve", lambda h: h.tensor_tensor(out=QS[:, :, 0:n], in0=PS_W[:, :, 0:n], in1=qT[:, :, 0:n],
                                                  op=ALU.mult), reads=[PS_W, qT], writes=[QS])
            for hh in range(4):
                po = PS_O[hh // 2]
                k.mm(po[0:n, hh % 2, :], QS[:, hh, 0:n], Cb[hh][:, :], True, False, reads=[QS, Cb[hh]], writes=[po])
                k.mm(po[0:n, hh % 2, :], WT[0:n, hh, 0:n], va[0:n, hh, :], False, True, reads=[WT, va], writes=[po])
            for i2 in range(2):
                k.op("dve", lambda h, i2=i2: h.tensor_copy(out=ON[0:n, 2 * i2:2 * i2 + 2, :], in_=PS_O[i2][0:n, :, :]),
                     reads=[PS_O[i2]], writes=[ON])
            for hh in range(4):
                k.op("act", lambda h, hh=hh: h.activation(out=KW[0:n, hh, :], in_=kt[0:n, hh, :], func=AF.Copy,
                                                        scale=COLS[0:n, ci, 1, hh:hh + 1]),
                     reads=[kt, COLS], writes=[KW])
            for hh in range(4):
                pd = PS_D[hh // 2]
                k.mm(pd[:, hh % 2, :], KW[0:n, hh, :], va[0:n, hh, :], True, True, reads=[KW, va], writes=[pd])
            for hh in range(4):
                pd = PS_D[hh // 2]
                k.op("dve", lambda h, hh=hh, pd=pd: h.scalar_tensor_tensor(
                    out=Cf[hh][:, :], in0=Cf[hh][:, :], scalar=DECB[:, hh, ci:ci + 1], in1=pd[:, hh % 2, :],
                    op0=ALU.mult, op1=ALU.add), reads=[DECB, pd], writes=[Cf[hh]])
                k.op("act", lambda h, hh=hh: h.activation(out=Cb[hh][:, :], in_=Cf[hh][:, :], func=AF.Copy),
                     reads=[Cf[hh]], writes=[Cb[hh]])
            if ci == 31 or ci >= 32:
                for hh in range(4):
                    if ci == 31:
                        oc, on = io["pC"][hh, :, :], io["pn"][hh, :]
                    else:
                        oc, on = io["sC"][ci - 32, hh, :, :], io["sn"][ci - 32, hh, :]
                    k.dma("sp", oc, Cf[hh][:, 0:128], reads=[Cf[hh]])
                    k.dma("sp", on.rearrange("(d o) -> d o", o=1), Cf[hh][:, 128:129], reads=[Cf[hh]])
            G2 = G2s[ci % 2]
            k.op("pool", lambda h, G2=G2, osg=osg: h.tensor_tensor(out=G2[0:n, :], in0=gain[0:n, :], in1=osg[0:n, :],
                                                                   op=ALU.mult), reads=[gain, osg], writes=[G2])

            def epiA(ci=ci, n=n, ON=ON, G2=G2, OAb=OAs[ci % 2]):
                den, dd, rr, mu, var, rs = sm
                s1, s2, aa, bb = sm2
                k.op("dve", lambda h: h.tensor_tensor(out=dd[0:n, :], in0=ON[0:n, :, 128], in1=COLS[0:n, ci, 2, :],
                                                      op=ALU.max), reads=[ON, COLS], writes=[dd])
                k.op("dve", lambda h: h.tensor_scalar(out=den[0:n, :], in0=ON[0:n, :, 128], scalar1=-1.0, scalar2=None,
                                                      op0=ALU.mult), reads=[ON], writes=[den])
                k.op("dve", lambda h: h.tensor_tensor(out=dd[0:n, :], in0=dd[0:n, :], in1=den[0:n, :],
                                                      op=ALU.max), reads=[den], writes=[dd])
                k.op("dve", lambda h: h.reciprocal(out=rr[0:n, :], in_=dd[0:n, :]), reads=[dd], writes=[rr])
                for hh in range(4):
                    k.op("act", lambda h, hh=hh: h.activation(out=junk[0:n, :], in_=ON[0:n, hh, 0:128], func=AF.Copy,
                                                            scale=rr[0:n, hh:hh + 1], accum_out=s1[0:n, hh:hh + 1]),
                         reads=[ON, rr], writes=[junk, s1])
                    k.op("act", lambda h, hh=hh: h.activation(out=junk[0:n, :], in_=ON[0:n, hh, 0:128], func=AF.Square,
                                                            scale=rr[0:n, hh:hh + 1], accum_out=s2[0:n, hh:hh + 1]),
                         reads=[ON, rr], writes=[junk, s2])
                k.op("dve", lambda h: h.tensor_scalar(out=mu[0:n, :], in0=s1[0:n, :], scalar1=1.0 / 128, scalar2=None,
                                                      op0=ALU.mult), reads=[s1], writes=[mu])
                k.op("dve", lambda h: h.tensor_tensor(out=var[0:n, :], in0=mu[0:n, :], in1=mu[0:n, :], op=ALU.mult),
                     reads=[mu], writes=[var])
                k.op("dve", lambda h: h.scalar_tensor_tensor(out=var[0:n, :], in0=s2[0:n, :], scalar=1.0 / 128,
                                                             in1=var[0:n, :], op0=ALU.mult, op1=ALU.subtract),
                     reads=[s2, var], writes=[var])
                k.op("dve", lambda h: h.tensor_scalar(out=var[0:n, :], in0=var[0:n, :], scalar1=0.0, scalar2=None,
                                                      op0=ALU.max), reads=[var], writes=[var])
                k.op("act", lambda h: h.activation(out=rs[0:n, :], in_=var[0:n, :], func=AF.Ln, bias=1e-6),
                     reads=[var], writes=[rs])
                k.op("act", lambda h: h.activation(out=rs[0:n, :], in_=rs[0:n, :], func=AF.Exp, scale=-0.5),
                     reads=[rs], writes=[rs])
                k.op("dve", lambda h: h.tensor_tensor(out=aa[0:n, :], in0=rr[0:n, :], in1=rs[0:n, :], op=ALU.mult),
                     reads=[rr, rs], writes=[aa])
                k.op("dve", lambda h: h.scalar_tensor_tensor(out=bb[0:n, :], in0=mu[0:n, :], scalar=-1.0, in1=rs[0:n, :],
                                                             op0=ALU.mult, op1=ALU.mult), reads=[mu, rs], writes=[bb])
                for hh in range(4):
                    k.op("pool", lambda h, hh=hh: h.tensor_scalar(out=H[0:n, hh, :], in0=ON[0:n, hh, 0:128],
                                                                scalar1=aa[0:n, hh:hh + 1], scalar2=bb[0:n, hh:hh + 1],
                                                                op0=ALU.mult, op1=ALU.add),
                         reads=[ON, aa, bb], writes=[H])
                k.op("dve", lambda h: h.tensor_tensor(out=OAb[0:n, :], in0=H[0:n, :, :].rearrange("p h e -> p (h e)"),
                                                      in1=G2[0:n, :], op=ALU.mult), reads=[H, G2], writes=[OAb])

            def epiB(r0=r0, n=n, OAb=OAs[ci % 2]):
                for hh in range(4):
                    k.tr(PS_T[:, hh, 0:n], OAb[0:n, hh * 128:(hh + 1) * 128], c.identb[0:n, 0:n],
                         reads=[OAb, c.identb], writes=[PS_T])
                k.op("act", lambda h: h.activation(out=OT[:, :, 0:n], in_=PS_T[:, :, 0:n], func=AF.Copy),
                     reads=[PS_T], writes=[OT])
                k.dma("sp", catv[:, :, r0:r0 + n], OT[:, :, 0:n], reads=[OT])
            if pendB:
                pendB.pop(0)()
            if pendA:
                pendA.pop(0)()
                pendB.append(pendA_b.pop(0))
            pendA.append(epiA)
            pendA_b.append(epiB)
        while pendA:
            pendA.pop(0)()
            pendB.append(pendA_b.pop(0))
        while pendB:
            pendB.pop(0)()
    k.barrier()


def attn_epilogue(k, c, PS_OT, nq, OS, RD, PS_BC, out_ap, out_t):
    k.op("dve", lambda h: h.tensor_copy(out=OS[0:65, 0:nq], in_=PS_OT[0:65, 0:nq]), reads=[PS_OT], writes=[OS])
    k.op("act", lambda h: h.activation(out=RD[64:65, 0:nq], in_=OS[64:65, 0:nq], func=AF.Ln), reads=[OS], writes=[RD])
    k.op("act", lambda h: h.activation(out=RD[64:65, 0:nq], in_=RD[64:65, 0:nq], func=AF.Exp, scale=-1.0),
         reads=[RD], writes=[RD])
    k.mm(PS_BC[0:64, 0:nq], c.ones[64:65, 0:64], RD[64:65, 0:nq], True, True, reads=[c.ones, RD], writes=[PS_BC])
    k.op("dve", lambda h: h.tensor_tensor(out=out_ap, in0=OS[0:64, 0:nq], in1=PS_BC[0:64, 0:nq], op=ALU.mult),
         reads=[OS, PS_BC], writes=[out_t])


def phase4(k, c, io, scr):
    with contextlib.ExitStack() as es:
        kTh = [k.sb(es, f"p4k{i}", [128, R], BF16) for i in range(2)]
        qTh = [k.sb(es, f"p4q{i}", [128, R], BF16) for i in range(2)]
        for t_ in kTh + qTh:
            k.op("pool", lambda h, t_=t_: h.memset(t_[64:128, :], 0.0), writes=[t_])
        vbh = [k.sb(es, f"p4v{i}", [128, NT, 65], BF16) for i in range(2)]
        bias = [k.sb(es, f"p4b{i}", [128, 640], F32) for i in range(2)]
        mask = k.sb(es, "p4mask", [128, 640], F32)
        ATT = [k.sb(es, f"p4att{i}", [64, R], BF16) for i in range(2)]
        TMP = [k.sb(es, f"p4tmp{i}", [128, 640], F32) for i in range(2)]
        P = [k.sb(es, f"p4P{i}", [128, 640], BF16) for i in range(2)]
        OS = k.sb(es, "p4OS", [65, 128], F32)
        RD = k.sb(es, "p4RD", [65, 128], F32)
        KC = k.sb(es, "p4KC", [128, 4, 64], BF16)
        KTC = k.sb(es, "p4KTC", [128, 512], BF16)
        k.op("pool", lambda h: h.memset(KTC[64:128, :], 0.0), writes=[KTC])
        VC = k.sb(es, "p4VC", [128, 4, 65], BF16)
        VN = k.sb(es, "p4VN", [64, 65], BF16)
        PS_A = k.ps(es, "p4PA", [128, 512])
        PS_B = k.ps(es, "p4PB", [128, 512])
        PS_OT = [k.ps(es, f"p4PO{i}", [128, 512]) for i in range(2)]
        PS_BC = k.ps(es, "p4PBC", [128, 512])
        PS_KT = k.ps(es, "p4PKT", [64, 4, 128], BF16)
        k.dma("sp", mask[:, :], io["bandmask"][:, :], writes=[mask])
        for i in range(2):
            k.op("pool", lambda h, i=i: h.memset(vbh[i][:, :, 64:65], 1.0), writes=[vbh[i]])
        k.op("pool", lambda h: h.memset(VC[:, :, 64:65], 1.0), writes=[VC])
        k.op("pool", lambda h: h.memset(VN[:, 64:65], 1.0), writes=[VN])
        vview = scr["vbTok"].rearrange("(t p) c -> p t c", p=128)

        def load(hh):
            j = hh % 2
            k.dma("sp", kTh[j][0:64, :], scr["kTb"][hh * 64:(hh + 1) * 64, :], writes=[kTh[j]])
            k.dma("sp", qTh[j][0:64, :], scr["qTb"][hh * 64:(hh + 1) * 64, :], writes=[qTh[j]])
            k.dma("sp", vbh[j][:, :, 0:64], vview[:, :, hh * 64:(hh + 1) * 64], writes=[vbh[j]])
            k.dma("sp", bias[j][:, :], io["bandbias"][hh, :, :], writes=[bias[j]])
            k.op("pool", lambda h: h.tensor_tensor(out=bias[j][:, :], in0=bias[j][:, :], in1=mask[:, :], op=ALU.add),
                 reads=[mask], writes=[bias[j]])

        load(0)
        it = 0
        for hh in range(8):
            j = hh % 2
            kT, qT, vb, bs, att = kTh[j], qTh[j], vbh[j], bias[j], ATT[j]
            if hh + 1 < 8:
                load(hh + 1)
            pendb = []
            for i in range(32):
                nj = min(i, 4) + 1
                tmp, p, pot = TMP[it % 2], P[it % 2], PS_OT[it % 2]
                it += 1
                for s in range(nj):
                    jj = i - s
                    dst = PS_A[:, s * 128:(s + 1) * 128] if s < 4 else PS_B[:, 0:128]
                    k.mm(dst, kT[:, jj * 128:(jj + 1) * 128], qT[:, i * 128:(i + 1) * 128], True, True,
                         reads=[kT, qT], writes=[PS_A if s < 4 else PS_B])
                na = min(nj, 4) * 128
                k.op("dve", lambda h, na=na, tmp=tmp: h.tensor_tensor(out=tmp[:, 0:na], in0=PS_A[:, 0:na],
                                                                     in1=bs[:, 0:na], op=ALU.add),
                     reads=[PS_A, bs], writes=[tmp])
                if nj == 5:
                    k.op("dve", lambda h, tmp=tmp: h.tensor_tensor(out=tmp[:, 512:640], in0=PS_B[:, 0:128],
                                                                  in1=bs[:, 512:640], op=ALU.add),
                         reads=[PS_B, bs], writes=[tmp])
                k.op("act", lambda h, tmp=tmp, p=p, nj=nj: h.activation(out=p[:, 0:nj * 128], in_=tmp[:, 0:nj * 128],
                                                                       func=AF.Exp), reads=[tmp], writes=[p])
                def fin(i=i, nj=nj, p=p, pot=pot):
                    for s in range(nj):
                        jj = i - s
                        k.mm(pot[0:65, 0:128], vb[:, jj, :], p[:, s * 128:(s + 1) * 128], s == 0, s == nj - 1,
                             reads=[vb, p], writes=[pot])
                    attn_epilogue(k, c, pot, 128, OS, RD, PS_BC, att[:, i * 128:(i + 1) * 128], att)
                if pendb:
                    pendb.pop(0)()
                pendb.append(fin)
            while pendb:
                pendb.pop(0)()
            for si in range(NS):
                r0 = TP + si * TS
                tmp, p, pot = TMP[it % 2], P[it % 2], PS_OT[it % 2]
                it += 1
                k.dma("pool", KC[:, :, :], io["cbk"][si, hh, :, :].rearrange("(t p) d -> p t d", p=128), writes=[KC])
                k.dma("pool", VC[:, :, 0:64], io["cbv"][si, hh, :, :].rearrange("(t p) d -> p t d", p=128), writes=[VC])
                k.dma("sp", VN[:, 0:64], scr["vbTok"][r0:r0 + TS, hh * 64:(hh + 1) * 64], writes=[VN])
                for m in range(4):
                    k.tr(PS_KT[:, m, :], KC[:, m, :], c.identb[:, :], reads=[KC, c.identb], writes=[PS_KT])
                k.op("act", lambda h: h.activation(out=KTC[0:64, :], in_=PS_KT[:, :, :].rearrange("p m n -> p (m n)"),
                                                   func=AF.Copy), reads=[PS_KT], writes=[KTC])
                for m in range(4):
                    k.mm(PS_A[:, m * 64:(m + 1) * 64], KTC[:, m * 128:(m + 1) * 128], qT[:, r0:r0 + TS], True, True,
                         reads=[KTC, qT], writes=[PS_A])
                k.mm(PS_A[0:64, 256:320], kT[:, r0:r0 + TS], qT[:, r0:r0 + TS], True, True, reads=[kT, qT], writes=[PS_A])
                for m in range(4):
                    b0 = 512 - 128 * m
                    k.op("dve", lambda h, m=m, b0=b0, tmp=tmp: h.tensor_tensor(
                        out=tmp[:, m * 64:(m + 1) * 64], in0=PS_A[:, m * 64:(m + 1) * 64], in1=bs[:, b0:b0 + 64],
                        op=ALU.add), reads=[PS_A, bs], writes=[tmp])
                k.op("dve", lambda h, tmp=tmp: h.tensor_tensor(out=tmp[0:64, 256:320], in0=PS_A[0:64, 256:320],
                                                              in1=bs[0:64, 0:64], op=ALU.add),
                     reads=[PS_A, bs], writes=[tmp])
                k.op("act", lambda h, tmp=tmp, p=p: h.activation(out=p[:, 0:256], in_=tmp[:, 0:256], func=AF.Exp),
                     reads=[tmp], writes=[p])
                k.op("act", lambda h, tmp=tmp, p=p: h.activation(out=p[0:64, 256:320], in_=tmp[0:64, 256:320],
                                                                func=AF.Exp), reads=[tmp], writes=[p])
                for m in range(4):
                    k.mm(pot[0:65, 0:64], VC[:, m, :], p[:, m * 64:(m + 1) * 64], m == 0, False, reads=[VC, p], writes=[pot])
                k.mm(pot[0:65, 0:64], VN[0:64, :], p[0:64, 256:320], False, True, reads=[VN, p], writes=[pot])
                attn_epilogue(k, c, pot, TS, OS, RD, PS_BC, att[:, r0:r0 + TS], att)
            k.dma("sp", scr["catT"][512 + hh * 64:512 + (hh + 1) * 64, :], att[:, :], reads=[att])
    k.barrier()


def outproj_phase(k, c, io, scr, cat_ap, w_ap, xin_ap, xout_ap, tag, prefetch=None):
    with contextlib.ExitStack() as es:
        W = k.sb(es, tag + "W", [128, 8, D], BF16)
        load_w(k, W, w_ap, 8, D)
        if prefetch is not None:
            prefetch()
        cats = [k.sb(es, f"{tag}c{i}", [128, 8, 512], BF16) for i in range(2)]
        xts = [k.sb(es, f"{tag}x{i}", [128, D], F32) for i in range(2)]
        xos = [k.sb(es, f"{tag}o{i}", [128, D], F32) for i in range(2)]
        pp = [k.ps(es, f"{tag}p{i}", [128, 512]) for i in range(4)]
        cv = cat_ap.rearrange("(k p) r -> p k r", p=128)
        gs = groups()
        k.dma("sp", cats[0][:, :, 0:gs[0][1]], cv[:, :, gs[0][0]:gs[0][0] + gs[0][1]], writes=[cats[0]])
        ti = 0
        pi = 0
        for gi, (tok0, n) in enumerate(gs):
            cat = cats[gi % 2]
            if gi + 1 < len(gs):
                t1, n1 = gs[gi + 1]
                k.dma("sp", cats[(gi + 1) % 2][:, :, 0:n1], cv[:, :, t1:t1 + n1], writes=[cats[(gi + 1) % 2]])
            for t in range(n // 128):
                r0 = tok0 + t * 128
                xt, xo = xts[ti % 2], xos[ti % 2]
                ti += 1
                k.dma("sp", xt[:, :], xin_ap[r0:r0 + 128, :], writes=[xt])
                for blk in range(2):
                    p = pp[pi % 4]
                    pi += 1
                    for kc in range(8):
                        k.mm(p[:, :], cat[:, kc, t * 128:(t + 1) * 128], W[:, kc, blk * 512:(blk + 1) * 512],
                             kc == 0, kc == 7, reads=[cat, W], writes=[p])
                    k.op("dve", lambda h, p=p, blk=blk, xt=xt, xo=xo: h.tensor_tensor(
                        out=xo[:, blk * 512:(blk + 1) * 512], in0=p[:, :], in1=xt[:, blk * 512:(blk + 1) * 512],
                        op=ALU.add), reads=[p, xt], writes=[xo])
                k.dma("sp", xout_ap[r0:r0 + 128, :], xo[:, :], reads=[xo])
    k.barrier()


def ffn_w_alloc(k, es, tag):
    return (k.sb(es, tag + "W", [128, 8, 2 * HID], BF16), k.sb(es, tag + "g", [128, 8], F32))


def ffn_w_load(k, io, layer, Wg):
    W, gcol = Wg
    load_w(k, W, io["w_ffn_in"][layer], 8, 2 * HID)
    scale_rows(k, None, W, io["norm_ffn"][layer, :], 8, None, gcol=gcol)
    return W


def ffn_in_phase(k, c, io, scr, layer, xin_ap, tag, W=None):
    with contextlib.ExitStack() as es:
        if W is None:
            W = ffn_w_load(k, io, layer, ffn_w_alloc(k, es, tag))
        st = NormState(k, es, tag)
        xts = [k.sb(es, f"{tag}x{i}", [128, D], F32) for i in range(2)]
        hTs = [k.sb(es, f"{tag}h{i}", [128, 8, 512], BF16) for i in range(2)]
        SG = [k.sb(es, f"{tag}sg{i}", [128, 512], F32) for i in range(3)]
        HS = [k.sb(es, f"{tag}hs{i}", [128, 512], BF16) for i in range(3)]
        PG = [k.ps(es, f"{tag}pg{i}", [128, 512]) for i in range(3)]
        PU = [k.ps(es, f"{tag}pu{i}", [128, 512]) for i in range(3)]
        ti = 0
        it = 0
        gl = groups()
        tis = [0]

        def norm_group(gi):
            tok0, n = gl[gi]
            for t in range(n // 128):
                xt = xts[tis[0] % 2]
                k.dma("sp", xt[:, :], xin_ap[tok0 + t * 128: tok0 + (t + 1) * 128, :], writes=[xt])
                rms_to_hT(k, c, xt, hTs[gi % 2], t * 128, st, tis[0])
                tis[0] += 1

        norm_group(0)
        for gi, (tok0, n) in enumerate(gl):
            hT = hTs[gi % 2]
            if gi + 1 < len(gl):
                norm_group(gi + 1)
            for cch in range(HID // 128):
                pg, pu, sg, hs = PG[it % 3], PU[it % 3], SG[it % 3], HS[it % 3]
                it += 1
                for kc in range(8):
                    k.mm(pg[:, 0:n], W[:, kc, cch * 128:(cch + 1) * 128], hT[:, kc, 0:n], kc == 0, kc == 7,
                         reads=[W, hT], writes=[pg])
                for kc in range(8):
                    k.mm(pu[:, 0:n], W[:, kc, HID + cch * 128:HID + (cch + 1) * 128], hT[:, kc, 0:n], kc == 0, kc == 7,
                         reads=[W, hT], writes=[pu])
                k.op("act", lambda h, pg=pg, sg=sg: h.activation(out=sg[:, 0:n], in_=pg[:, 0:n], func=AF.Silu),
                     reads=[pg], writes=[sg])
                k.op("dve", lambda h, pu=pu, sg=sg, hs=hs: h.tensor_tensor(out=hs[:, 0:n], in0=pu[:, 0:n], in1=sg[:, 0:n],
                                                                          op=ALU.mult), reads=[pu, sg], writes=[hs])
                k.dma("sp", scr["hidT"][cch * 128:(cch + 1) * 128, tok0:tok0 + n], hs[:, 0:n], reads=[hs])
    k.barrier()


def ffn_out_phase(k, c, io, scr, layer, xin_ap, xout_ap, final, tag):
    with contextlib.ExitStack() as es:
        KC = HID // 128
        W = k.sb(es, tag + "W", [128, KC, D], BF16)
        load_w(k, W, io["w_ffn_out"][layer], KC, D)
        hids = [k.sb(es, f"{tag}hd{i}", [128, KC, 512], BF16) for i in range(2)]
        xts = [k.sb(es, f"{tag}x{i}", [128, D], F32) for i in range(2)]
        xos = [k.sb(es, f"{tag}o{i}", [128, D], F32) for i in range(2)]
        pp = [k.ps(es, f"{tag}p{i}", [128, 512]) for i in range(4)]
        if final:
            gam = k.sb(es, tag + "gam", [128, D], F32)
            k.dma("sp", gam[:, :], io["norm_final"].partition_broadcast(128), writes=[gam])
            junk = k.sb(es, tag + "junk", [128, D], BF16)
            ss = [k.sb(es, f"{tag}ss{i}", [128, 1], F32) for i in range(2)]
            sq = [k.sb(es, f"{tag}sq{i}", [128, 1], F32) for i in range(2)]
            rstd = [k.sb(es, f"{tag}rs{i}", [128, 1], F32) for i in range(2)]
            ys = [k.sb(es, f"{tag}y{i}", [128, D], F32) for i in range(2)]
        hv = scr["hidT"].rearrange("(c p) r -> p c r", p=128)
        gs = groups()
        k.dma("sp", hids[0][:, :, 0:gs[0][1]], hv[:, :, gs[0][0]:gs[0][0] + gs[0][1]], writes=[hids[0]])
        ti = 0
        pi = 0
        for gi, (tok0, n) in enumerate(gs):
            hid = hids[gi % 2]
            if gi + 1 < len(gs):
                t1, n1 = gs[gi + 1]
                k.dma("sp", hids[(gi + 1) % 2][:, :, 0:n1], hv[:, :, t1:t1 + n1], writes=[hids[(gi + 1) % 2]])
            for t in range(n // 128):
                r0 = tok0 + t * 128
                j = ti % 2
                xt, xo = xts[j], xos[j]
                ti += 1
                k.dma("sp", xt[:, :], xin_ap[r0:r0 + 128, :], writes=[xt])
                for blk in range(2):
                    p = pp[pi % 4]
                    pi += 1
                    for kc in range(KC):
                        k.mm(p[:, :], hid[:, kc, t * 128:(t + 1) * 128], W[:, kc, blk * 512:(blk + 1) * 512],
                             kc == 0, kc == KC - 1, reads=[hid, W], writes=[p])
                    k.op("dve", lambda h, p=p, blk=blk, xt=xt, xo=xo: h.tensor_tensor(
                        out=xo[:, blk * 512:(blk + 1) * 512], in0=p[:, :], in1=xt[:, blk * 512:(blk + 1) * 512],
                        op=ALU.add), reads=[p, xt], writes=[xo])
                if not final:
                    k.dma("sp", xout_ap[r0:r0 + 128, :], xo[:, :], reads=[xo])
                else:
                    k.op("act", lambda h, xo=xo, j=j: h.activation(out=junk[:, :], in_=xo[:, :], func=AF.Square,
                                                                  accum_out=ss[j][:, :]),
                         reads=[xo], writes=[junk, ss[j]])
                    k.op("act", lambda h, j=j: h.activation(out=sq[j][:, :], in_=ss[j][:, :], func=AF.Sqrt,
                                                           scale=1.0 / D, bias=1e-6), reads=[ss[j]], writes=[sq[j]])
                    k.op("dve", lambda h, j=j: h.reciprocal(out=rstd[j][:, :], in_=sq[j][:, :]),
                         reads=[sq[j]], writes=[rstd[j]])
                    k.op("act", lambda h, xo=xo, j=j: h.activation(out=ys[j][:, :], in_=xo[:, :], func=AF.Copy,
                                                                  scale=rstd[j][:, :]),
                         reads=[xo, rstd[j]], writes=[ys[j]])
                    k.op("pool", lambda h, j=j: h.tensor_tensor(out=ys[j][:, :], in0=ys[j][:, :], in1=gam[:, :],
                                                               op=ALU.mult), reads=[gam], writes=[ys[j]])
                    k.dma("sp", xout_ap[r0:r0 + 128, :], ys[j][:, :], reads=[ys[j]])
    k.barrier()


def fox_phases(k, c, io, scr):
    with contextlib.ExitStack() as esf:
        frow = k.sb(esf, "fx_frow", [16, R], F32)
        fox_inproj(k, c, io, scr, frow)
        if "fx1" in DBG:
            return
        NFP = k.sb(esf, "fx_NFP", [16, TP], F32)
        FSN = k.sb(esf, "fx_FSN", [16, NS, TS], F32)
        NFcP = k.sb(esf, "fx_NFcP", [128, 32, 16], F32)
        NFcS = k.sb(esf, "fx_NFcS", [128, NS, 17, 16], F32)
        fox_prep(k, c, io, scr, frow, NFP, FSN, NFcP, NFcS)
        if "fx2" in DBG:
            return
        fox_attn(k, c, io, scr, NFP, FSN, NFcP, NFcS)


def fox_inproj(k, c, io, scr, frow):
    with contextlib.ExitStack() as es:
        stf = [k.sb(es, f"f1sf{i}", [128, 512], F32) for i in range(2)]
        W = k.sb(es, "f1W", [128, 8, 3088], BF16)
        load_w(k, W, io["w_in_fox"], 8, 3088)
        scale_rows(k, es, W, io["norm_mix"][1, :], 8, "f1g")
        st = NormState(k, es, "f1")
        xts = [k.sb(es, f"f1x{i}", [128, D], F32) for i in range(2)]
        hTs = [k.sb(es, f"f1h{i}", [128, 8, 512], BF16) for i in range(2)]
        pfm = [k.ps(es, f"f1pf{i}", [128, 512]) for i in range(2)]
        ptm = [k.ps(es, f"f1pt{i}", [128, 512]) for i in range(2)]
        stb = [k.sb(es, f"f1sb{i}", [128, 512], BF16) for i in range(4)]
        ev = 0
        sbi = 0
        sfi = 0
        ti = 0
        gl = groups()
        tis = [0]

        def norm_group(gi):
            tok0, n = gl[gi]
            for t in range(n // 128):
                xt = xts[tis[0] % 2]
                k.dma("sp", xt[:, :], scr["x2"][tok0 + t * 128: tok0 + (t + 1) * 128, :], writes=[xt])
                rms_to_hT(k, c, xt, hTs[gi % 2], t * 128, st, tis[0])
                tis[0] += 1

        norm_group(0)
        for gi, (tok0, n) in enumerate(gl):
            hT = hTs[gi % 2]
            if gi + 1 < len(gl):
                norm_group(gi + 1)
            for (c0, dst, scale) in ([] if "fxnofm" in DBG else ((0, scr["qTf"], 0.125), (1024, scr["kTf"], None))):
                for mc in range(8):
                    pp = pfm[ev % 2]
                    for kc in range(8):
                        k.mm(pp[:, 0:n], W[:, kc, c0 + mc * 128: c0 + (mc + 1) * 128], hT[:, kc, 0:n],
                             kc == 0, kc == 7, reads=[W, hT], writes=[pp])
                    sbt = stb[sbi % 4]
                    sbi += 1
                    evac(k, ev, sbt[:, 0:n], pp[:, 0:n], [pp], [sbt], scale=scale)
                    ev += 1
                    k.dma("sp", dst[mc * 128:(mc + 1) * 128, tok0:tok0 + n], sbt[:, 0:n], reads=[sbt])
            pp = pfm[ev % 2]
            for kc in range(0 if "fxnog" in DBG else 8):
                k.mm(pp[0:16, 0:n], W[:, kc, 3072:3088], hT[:, kc, 0:n], kc == 0, kc == 7, reads=[W, hT], writes=[pp])
            if "fxnog" not in DBG:
                evac(k, 1, frow[0:16, tok0:tok0 + n], pp[0:16, 0:n], [pp], [frow])
            ev += 1
            for t in range(0 if "fxnotm" in DBG else n // 128):
                r0 = tok0 + t * 128
                for (c0, outn, hb, tobf) in ((2048, "fv", 0, True), (2560, "fv", 1, True),
                                             (1024, "fk", 0, False), (1536, "fk", 1, False)):
                    pp = ptm[ev % 2]
                    for kc in range(8):
                        k.mm(pp[:, :], hT[:, kc, t * 128:(t + 1) * 128], W[:, kc, c0:c0 + 512],
                             kc == 0, kc == 7, reads=[W, hT], writes=[pp])
                    ev += 1
                    sft = stf[sfi % 2]
                    sfi += 1
                    evac(k, 1, sft[:, :], pp[:, :], [pp], [sft])
                    if tobf:
                        sbt = stb[sbi % 4]
                        sbi += 1
                        k.op("act", lambda h, sbt=sbt, sft=sft: h.activation(out=sbt[:, :], in_=sft[:, :], func=AF.Copy),
                             reads=[sft], writes=[sbt])
                        k.dma("sp", scr["vfTok"][r0:r0 + 128, hb * 512:(hb + 1) * 512], sbt[:, :], reads=[sbt])
                    for h8 in range(0 if "fxnoout" in DBG else 8):
                        src = sft[:, h8 * 64:(h8 + 1) * 64]
                        if r0 < TP:
                            k.dma("sp", io["p" + outn][hb * 8 + h8, r0:r0 + 128, :], src, reads=[sft])
                        else:
                            for s2 in range(2):
                                sq = (r0 - TP) // TS + s2
                                k.dma("sp", io["s" + outn][sq, hb * 8 + h8, :, :], src[s2 * 64:(s2 + 1) * 64], reads=[sft])
    k.barrier()


def fox_prep(k, c, io, scr, frow, NFP, FSN, NFcP, NFcS):
    with contextlib.ExitStack() as es:
        negb = k.sb(es, "f2negb", [16, 1], F32)
        LF = k.sb(es, "f2LF", [16, R], F32)
        CL = [k.sb(es, f"f2CL{i}", [16, 2048], F32) for i in range(2)]
        FC = [k.sb(es, f"f2FC{i}", [16, 2048], F32) for i in range(2)]
        PCp = k.ps(es, "f2PCp", [128, 32, 16])
        PCs = [k.ps(es, f"f2PCs{i}", [128, 17, 16]) for i in range(2)]
        k.dma("sp", negb[:, :], io["b_fox_f"].rearrange("(h o) -> h o", o=1), writes=[negb])
        k.op("dve", lambda h: h.tensor_scalar(out=negb[:, :], in0=negb[:, :], scalar1=-1.0, scalar2=None, op0=ALU.mult),
             writes=[negb])
        k.op("act", lambda h: h.activation(out=frow[:, :], in_=frow[:, :], func=AF.Exp, scale=-1.0, bias=negb[:, :]),
             reads=[negb], writes=[frow])
        k.op("act", lambda h: h.activation(out=frow[:, :], in_=frow[:, :], func=AF.Ln, bias=1.0), writes=[frow])
        k.op("dve", lambda h: h.tensor_scalar(out=LF[:, :], in0=frow[:, :], scalar1=-1.0, scalar2=None, op0=ALU.mult),
             reads=[frow], writes=[LF])
        k.dma("sp", io["pflf"][:, :], LF[:, 0:TP], reads=[LF])
        for i in range(NS):
            k.dma("sp", io["sflf"][i, :, :], LF[:, TP + i * TS:TP + (i + 1) * TS], reads=[LF])
        k.op("dve", lambda h: h.tensor_tensor_scan(out=NFP[:, :], data0=frow[:, 0:TP], data1=frow[:, 0:TP], initial=0.0,
                                                   op0=ALU.add, op1=ALU.max), reads=[frow], writes=[NFP])
        for t in range(32):
            k.tr(PCp[:, t, :], NFP[0:16, t * 128:(t + 1) * 128], c.ident[0:16, 0:16], reads=[NFP, c.ident], writes=[PCp])
        k.op("dve", lambda h: h.tensor_copy(out=NFcP[:, :, :], in_=PCp[:, :, :]), reads=[PCp], writes=[NFcP])
        for i in range(NS):
            cl, fc, pcs = CL[i % 2], FC[i % 2], PCs[i % 2]
            s0 = TP + i * TS
            k.dma("sp", cl[:, :], io["cflf"][i, :, :], writes=[cl])
            k.op("dve", lambda h, cl=cl: h.tensor_scalar(out=cl[:, :], in0=cl[:, :], scalar1=-1.0, scalar2=None,
                                                        op0=ALU.mult), writes=[cl])
            k.op("dve", lambda h, cl=cl, fc=fc: h.tensor_tensor_scan(out=fc[:, :], data0=cl[:, :], data1=cl[:, :],
                                                                    initial=0.0, op0=ALU.add, op1=ALU.max),
                 reads=[cl], writes=[fc])
            k.op("dve", lambda h, fc=fc, i=i, s0=s0: h.tensor_tensor_scan(
                out=FSN[:, i, :], data0=frow[:, s0:s0 + TS], data1=frow[:, s0:s0 + TS], initial=fc[:, 2047:2048],
                op0=ALU.add, op1=ALU.max), reads=[fc, frow], writes=[FSN])
            for m in range(16):
                k.tr(pcs[:, m, :], fc[0:16, m * 128:(m + 1) * 128], c.ident[0:16, 0:16], reads=[fc, c.ident], writes=[pcs])
            k.tr(pcs[0:64, 16, :], FSN[0:16, i, :], c.ident[0:16, 0:16], reads=[FSN, c.ident], writes=[pcs])
            k.op("dve", lambda h, pcs=pcs, i=i: h.tensor_copy(out=NFcS[:, i, 0:16, :], in_=pcs[:, 0:16, :]),
                 reads=[pcs], writes=[NFcS])
            k.op("dve", lambda h, pcs=pcs, i=i: h.tensor_copy(out=NFcS[0:64, i, 16, :], in_=pcs[0:64, 16, :]),
                 reads=[pcs], par=[NFcS])
    k.barrier()


def fox_attn(k, c, io, scr, NFP, FSN, NFcP, NFcS):
    with contextlib.ExitStack() as es:
        kTh = [k.sb(es, f"f3k{i}", [128, R], BF16) for i in range(2)]
        qTh = [k.sb(es, f"f3q{i}", [128, R], BF16) for i in range(2)]
        for t_ in kTh + qTh:
            k.op("pool", lambda h, t_=t_: h.memset(t_[64:128, :], 0.0), writes=[t_])
        vfh = [k.sb(es, f"f3v{i}", [128, NT, 65], BF16) for i in range(2)]
        ATT = k.sb(es, "f3att", [64, R], BF16)
        FQ = [k.sb(es, f"f3fq{i}", [128, 512], F32) for i in range(2)]
        BD = [[k.sb(es, f"f3bd{b}_{i}", [128, 512], F32) for i in range(4)] for b in range(2)]
        TMP = [k.sb(es, f"f3tmp{i}", [128, 512], F32) for i in range(4)]
        P = [k.sb(es, f"f3P{i}", [128, 512], BF16) for i in range(4)]
        OS = k.sb(es, "f3OS", [65, 512], F32)
        RD = k.sb(es, "f3RD", [65, 512], F32)
        KC = [k.sb(es, f"f3KC{i}", [128, 16, 64], BF16) for i in range(2)]
        KTC = k.sb(es, "f3KTC", [128, 2048], BF16)
        k.op("pool", lambda h: h.memset(KTC[64:128, :], 0.0), writes=[KTC])
        VC = [k.sb(es, f"f3VC{i}", [128, 16, 65], BF16) for i in range(2)]
        VN = [k.sb(es, f"f3VN{i}", [64, 65], BF16) for i in range(2)]
        FQs = k.sb(es, "f3FQs", [128, TS], F32)
        BDs = k.sb(es, "f3BDs", [64, TS], F32)
        TMPs = k.sb(es, "f3TMPs", [128, 1088], F32)
        Ps = k.sb(es, "f3Ps", [128, 1088], BF16)
        PS_S = [k.ps(es, f"f3PS{i}", [128, 512]) for i in range(3)]
        PS_O = [k.ps(es, f"f3PO{i}", [128, 512]) for i in range(2)]
        PS_F = k.ps(es, "f3PF", [128, 512])
        PS_BC = k.ps(es, "f3PBC", [128, 512])
        PS_KT = k.ps(es, "f3PKT", [64, 1024], BF16)
        LN = k.sb(es, "f3LN", [65, 512], F32)
        RDr = [k.sb(es, f"f3RDr{i}", [65, 512], F32) for i in range(2)]
        RB = [k.sb(es, f"f3RB{i}", [64, 512], F32) for i in range(2)]
        rdbuf = [TT(None) for _ in range(2)]
        for i in range(2):
            k.op("pool", lambda h, i=i: h.memset(vfh[i][:, :, 64:65], 1.0), writes=[vfh[i]])
            k.op("pool", lambda h, i=i: h.memset(VC[i][:, :, 64:65], 1.0), writes=[VC[i]])
            k.op("pool", lambda h, i=i: h.memset(VN[i][:, 64:65], 1.0), writes=[VN[i]])
        vview = scr["vfTok"].rearrange("(t p) c -> p t c", p=128)

        def load(hh):
            j = hh % 2
            k.dma("sp", kTh[j][0:64, :], scr["kTf"][hh * 64:(hh + 1) * 64, :], writes=[kTh[j]])
            k.dma("sp", qTh[j][0:64, :], scr["qTf"][hh * 64:(hh + 1) * 64, :], writes=[qTh[j]])
            k.dma("sp", vfh[j][:, :, 0:64], vview[:, :, hh * 64:(hh + 1) * 64], writes=[vfh[j]])

        def prep(hh, g, b):
            fq = FQ[b]
            k.mm(PS_F[:, 0:512], c.sel[0:16, hh, :], NFP[0:16, g * 512:(g + 1) * 512], True, True,
                 reads=[c.sel, NFP], writes=[PS_F])
            k.op("dve", lambda h: h.tensor_copy(out=fq[:, :], in_=PS_F[:, 0:512]), reads=[PS_F], writes=[fq])
            for r in range(4):
                b0 = 384 - 128 * r
                k.op("pool", lambda h, r=r, b0=b0: h.tensor_tensor(out=BD[b][r][:, :], in0=fq[:, :],
                                                                 in1=c.cpos[:, b0:b0 + 512], op=ALU.add),
                     reads=[fq, c.cpos], writes=[BD[b][r]])

        def sload(hh, si, b):
            r0 = TP + si * TS
            k.dma("pool", KC[b][:, :, :], io["cfk"][si, hh, :, :].rearrange("(t p) d -> p t d", p=128), writes=[KC[b]])
            k.dma("pool", VC[b][:, :, 0:64], io["cfv"][si, hh, :, :].rearrange("(t p) d -> p t d", p=128), writes=[VC[b]])
            k.dma("sp", VN[b][:, 0:64], scr["vfTok"][r0:r0 + TS, hh * 64:(hh + 1) * 64], writes=[VN[b]])

        load(0)
        it = 0
        gi = 0
        sli = 0
        pend_epiA = []
        pend_epiB = []
        sload(0, 0, 0)
        for hh in range(16):
            j = hh % 2
            kT, qT, vf = kTh[j], qTh[j], vfh[j]
            if hh + 1 < 16:
                load(hh + 1)
            prep(hh, 0, gi % 2)
            for g in range(8):
                b = gi % 2
                fq, bd, pso = FQ[b], BD[b], PS_O[b]
                gi += 1
                if g + 1 < 8:
                    prep(hh, g + 1, gi % 2)
                last = 4 * g + 3
                pend = []
                for jj in range(last + 1):
                    ps, tmp, p = PS_S[it % 3], TMP[it % 4], P[it % 4]
                    it += 1
                    k.mm(ps[:, 0:512], kT[:, jj * 128:(jj + 1) * 128], qT[:, g * 512:(g + 1) * 512], True, True,
                         reads=[kT, qT], writes=[ps])
                    bsrc = bd[jj - 4 * g] if jj >= 4 * g else fq
                    k.op("dve", lambda h, ps=ps, tmp=tmp, bsrc=bsrc: h.tensor_tensor(out=tmp[:, :], in0=ps[:, 0:512],
                                                                                    in1=bsrc[:, :], op=ALU.subtract),
                         reads=[ps, bsrc], writes=[tmp], skip_self=True)
                    k.op("act", lambda h, tmp=tmp, p=p, jj=jj: h.activation(out=p[:, :], in_=tmp[:, :], func=AF.Exp,
                                                                           bias=NFcP[:, jj, hh:hh + 1]),
                         reads=[tmp, NFcP], writes=[p], skip_self=True)
                    pend.append(lambda jj=jj, p=p, pso=pso, last=last: k.mm(
                        pso[0:65, 0:512], vf[:, jj, :], p[:, :], jj == 0, jj == last, reads=[vf, p], writes=[pso]))
                    if len(pend) > 2:
                        pend.pop(0)()
                    if jj == 1 and pend_epiA:
                        pend_epiA.pop(0)()
                    if jj == last - 1 and pend_epiB:
                        pend_epiB.pop(0)()
                while pend:
                    pend.pop(0)()
                while pend_epiA:
                    pend_epiA.pop(0)()
                while pend_epiB:
                    pend_epiB.pop(0)()

                def epiA(pso=pso, b=b):
                    k.op("act", lambda h: h.activation(out=LN[64:65, :], in_=pso[64:65, 0:512], func=AF.Ln),
                         reads=[pso], writes=[LN])
                    k.op("act", lambda h: h.activation(out=RDr[b][64:65, :], in_=LN[64:65, :], func=AF.Exp, scale=-1.0),
                         reads=[LN], writes=[RDr[b]])
                    k.dma("sp", scr["rds"][b:b + 1, :], RDr[b][64:65, :], reads=[RDr[b]], writes=[rdbuf[b]])
                    k.dma("sp", RB[b][:, :], scr["rds"][b, :].partition_broadcast(64), reads=[rdbuf[b]], writes=[RB[b]])

                def epiB(pso=pso, b=b, g=g):
                    k.op("dve", lambda h: h.tensor_tensor(out=ATT[:, g * 512:(g + 1) * 512], in0=pso[0:64, 0:512],
                                                          in1=RB[b][:, :], op=ALU.mult),
                         reads=[pso, RB[b]], writes=[ATT])
                pend_epiA.append(epiA)
                pend_epiB.append(epiB)
            while pend_epiA:
                pend_epiA.pop(0)()
            while pend_epiB:
                pend_epiB.pop(0)()
            for si in range(NS):
                r0 = TP + si * TS
                b = sli % 2
                sli += 1
                kc, vc, vn = KC[b], VC[b], VN[b]
                pso = PS_O[sli % 2]
                if si + 1 < NS:
                    sload(hh, si + 1, sli % 2)
                elif hh + 1 < 16:
                    sload(hh + 1, 0, sli % 2)
                for half in range(2):
                    for m in range(8):
                        k.tr(PS_KT[:, m * 128:(m + 1) * 128], kc[:, half * 8 + m, :], c.identb[:, :],
                             reads=[kc, c.identb], writes=[PS_KT])
                    k.op("act", lambda h, half=half: h.activation(out=KTC[0:64, half * 1024:(half + 1) * 1024],
                                                                in_=PS_KT[:, :], func=AF.Copy),
                         reads=[PS_KT], writes=[KTC])
                k.mm(PS_F[:, 0:TS], c.sel[0:16, hh, :], FSN[0:16, si, :], True, True, reads=[c.sel, FSN], writes=[PS_F])
                k.op("dve", lambda h: h.tensor_copy(out=FQs[:, :], in_=PS_F[:, 0:TS]), reads=[PS_F], writes=[FQs])
                k.op("pool", lambda h: h.tensor_tensor(out=BDs[:, :], in0=FQs[0:64, :], in1=c.cpos[0:64, 384:384 + TS],
                                                       op=ALU.add), reads=[FQs, c.cpos], writes=[BDs])
                for m in range(16):
                    ps = PS_S[m // 8]
                    k.mm(ps[:, (m % 8) * 64:(m % 8 + 1) * 64], KTC[:, m * 128:(m + 1) * 128], qT[:, r0:r0 + TS], True, True,
                         reads=[KTC, qT], writes=[ps])
                k.mm(PS_S[2][0:64, 0:TS], kT[:, r0:r0 + TS], qT[:, r0:r0 + TS], True, True, reads=[kT, qT], writes=[PS_S[2]])
                for m in range(16):
                    ps = PS_S[m // 8]
                    k.op("dve", lambda h, m=m, ps=ps: h.scalar_tensor_tensor(
                        out=TMPs[:, m * 64:(m + 1) * 64], in0=ps[:, (m % 8) * 64:(m % 8 + 1) * 64],
                        scalar=NFcS[:, si, m, hh:hh + 1], in1=FQs[:, :], op0=ALU.add, op1=ALU.subtract),
                        reads=[ps, NFcS, FQs], writes=[TMPs])
                k.op("dve", lambda h: h.scalar_tensor_tensor(
                    out=TMPs[0:64, 1024:1088], in0=PS_S[2][0:64, 0:TS], scalar=NFcS[0:64, si, 16, hh:hh + 1],
                    in1=BDs[:, :], op0=ALU.add, op1=ALU.subtract), reads=[PS_S[2], NFcS, BDs], writes=[TMPs])
                k.op("act", lambda h: h.activation(out=Ps[:, 0:1024], in_=TMPs[:, 0:1024], func=AF.Exp),
                     reads=[TMPs], writes=[Ps])
                k.op("act", lambda h: h.activation(out=Ps[0:64, 1024:1088], in_=TMPs[0:64, 1024:1088], func=AF.Exp),
                     reads=[TMPs], writes=[Ps])
                for m in range(16):
                    k.mm(pso[0:65, 0:TS], vc[:, m, :], Ps[:, m * 64:(m + 1) * 64], m == 0, False, reads=[vc, Ps], writes=[pso])
                k.mm(pso[0:65, 0:TS], vn[0:64, :], Ps[0:64, 1024:1088], False, True, reads=[vn, Ps], writes=[pso])
                attn_epilogue(k, c, pso, TS, OS, RD, PS_BC, ATT[:, r0:r0 + TS], ATT)
            k.dma("sp", scr["attT"][hh * 64:(hh + 1) * 64, :], ATT[:, :], reads=[ATT])
    k.barrier()
```

```python
import contextlib
import numpy as np
import concourse.bass as bass
import concourse.mybir as mybir
from concourse.bass_utils import run_bass_kernel_spmd

F32 = mybir.dt.float32
BF16 = mybir.dt.bfloat16
AF = mybir.ActivationFunctionType
ALU = mybir.AluOpType

D = 1024
TP = 4096
NS = 4
TS = 64
R = TP + NS * TS
NT = R // 128
HID = 2816
BIG = 30000.0


class Buf:
    __slots__ = ("w", "r")

    def __init__(self):
        self.w = []
        self.r = {}


class TT:
    def __init__(self, t):
        self.t = t
        self.b = Buf()

    def __getitem__(self, key):
        return self.t[key]


class Eng:
    def __init__(self, name, h, sem):
        self.name = name
        self.h = h
        self.sem = sem
        self.count = 0
        self.seen = {}


class DQ:
    def __init__(self, name, sems):
        self.sems = sems
        self.keys = [f"{name}{i}" for i in range(len(sems))]
        self.uses = [0] * len(sems)
        self.next = 0


class KB:
    def __init__(self, nc, es):
        self.nc = nc
        self.eng = {}
        for name, h in (("pe", nc.tensor), ("act", nc.scalar), ("dve", nc.vector),
                        ("pool", nc.gpsimd), ("sp", nc.sync)):
            self.eng[name] = Eng(name, h, es.enter_context(nc.semaphore("sem_" + name)))
        self.dq = {}
        for q in ("sp", "pool"):
            self.dq[q] = DQ("dq" + q, [es.enter_context(nc.semaphore(f"dq{q}{i}")) for i in range(8)])

    def sb(self, es, name, shape, dt):
        return TT(es.enter_context(self.nc.sbuf_tensor(name, shape, dt)))

    def ps(self, es, name, shape, dt=F32):
        return TT(es.enter_context(self.nc.psum_tensor(name, shape, dt)))

    def _wait(self, e, deps):
        best = {}
        for (key, sem, val) in deps:
            if key == "pe" and e.name == "pe":
                continue
            if e.seen.get(key, 0) >= val:
                continue
            if key not in best or best[key][1] < val:
                best[key] = (sem, val)
        for key, (sem, val) in best.items():
            e.h.wait_ge(sem, val)
            e.seen[key] = val

    def _deps(self, reads, writes, par=(), en=None):
        deps = []
        for t in reads:
            deps.extend(t.b.w)
        for t in writes:
            deps.extend(d for d in t.b.w if d[0] != en)
            deps.extend(d for d in t.b.r.values() if d[0] != en)
        for t in par:
            deps.extend(d for d in t.b.r.values() if d[0] != en)
        return deps

    def _mark(self, tok, reads, writes, par=()):
        for t in par:
            t.b.w.append(tok)
        for t in writes:
            t.b.w = [tok]
            t.b.r = {}
        for t in reads:
            if t in writes:
                continue
            old = t.b.r.get(tok[0])
            if old is None or old[2] < tok[2]:
                t.b.r[tok[0]] = tok

    def op(self, en, fn, reads=(), writes=(), par=(), skip_self=False):
        e = self.eng[en]
        self._wait(e, self._deps(reads, writes, par, en if skip_self else None))
        inst = fn(e.h)
        e.count += 1
        inst.then_inc(e.sem, 1)
        tok = (en, e.sem, e.count)
        self._mark(tok, reads, writes, par)
        return tok

    def dma(self, qn, out, in_, reads=(), writes=(), par=(), **kw):
        e = self.eng[qn]
        q = self.dq[qn]
        self._wait(e, self._deps(reads, writes, par))
        s = q.next % len(q.sems)
        q.next += 1
        if q.uses[s] > 0 and e.seen.get(q.keys[s], 0) < 16 * q.uses[s]:
            e.h.wait_ge(q.sems[s], 16 * q.uses[s])
            e.seen[q.keys[s]] = 16 * q.uses[s]
        inst = e.h.dma_start(out=out, in_=in_, **kw)
        q.uses[s] += 1
        inst.then_inc(q.sems[s], 16)
        tok = (q.keys[s], q.sems[s], 16 * q.uses[s])
        self._mark(tok, reads, writes, par)
        return tok

    def barrier(self):
        toks = [(n, e.sem, e.count) for n, e in self.eng.items() if e.count > 0]
        for q in self.dq.values():
            for i in range(len(q.sems)):
                if q.uses[i] > 0:
                    toks.append((q.keys[i], q.sems[i], 16 * q.uses[i]))
        for e in self.eng.values():
            self._wait(e, [t for t in toks if t[0] != e.name])

    def mm(self, out, lhsT, rhs, start, stop, reads, writes):
        return self.op("pe", lambda h: h.matmul(out, lhsT=lhsT, rhs=rhs, start=start, stop=stop),
                       reads=reads, writes=writes)

    def tr(self, out, in_, ident, reads, writes):
        return self.op("pe", lambda h: h.transpose(out=out, in_=in_, identity=ident),
                       reads=reads, writes=writes)


def load_w(k, dst, w_ap, kc_n, ncols, blk=2048):
    nb = -(-ncols // blk)
    blk = -(-ncols // nb)
    for kc in range(kc_n):
        for c0 in range(0, ncols, blk):
            c1 = min(ncols, c0 + blk)
            k.dma("pool", dst[:, kc, c0:c1], w_ap[kc * 128:(kc + 1) * 128, c0:c1], par=[dst])


def scale_rows(k, es, w, g_ap, kc_n, name, gcol=None):
    if gcol is None:
        gcol = k.sb(es, name, [128, kc_n], F32)
    k.dma("sp", gcol[:, :], g_ap.rearrange("(k p) -> p k", p=128), writes=[gcol],
          allow_slow_non_contiguous=True)
    for kc in range(kc_n):
        eng = "dve" if kc % 2 == 0 else "pool"
        k.op(eng, lambda h, kc=kc: h.tensor_scalar(out=w[:, kc, :], in0=w[:, kc, :],
                                                    scalar1=gcol[:, kc:kc + 1], scalar2=None,
                                                    op0=ALU.mult),
             reads=[gcol], writes=[w])


class Consts:
    pass


def make_consts(k, es):
    c = Consts()
    c.ones = k.sb(es, "c_ones", [128, 128], F32)
    c.ident = k.sb(es, "c_ident", [128, 128], F32)
    c.identb = k.sb(es, "c_identb", [128, 128], BF16)
    c.caus = k.sb(es, "c_caus", [128, 128], F32)
    c.cpos = k.sb(es, "c_cpos", [128, 896], F32)
    c.big = k.sb(es, "c_big", [128, 896], F32)
    c.sel = k.sb(es, "c_sel", [16, 16, 128], F32)
    k.op("pool", lambda h: h.memset(c.ones[:, :], 1.0), writes=[c.ones])
    k.op("pool", lambda h: h.memset(c.big[:, :], BIG), writes=[c.big])
    k.op("pool", lambda h: h.affine_select(out=c.ident[:, :], in_=c.ones[:, :], pattern=[[-1, 128]],
                                           compare_op=ALU.is_equal, fill=0.0, base=0,
                                           channel_multiplier=1),
         reads=[c.ones], writes=[c.ident])
    k.op("pool", lambda h: h.affine_select(out=c.caus[:, :], in_=c.ones[:, :], pattern=[[1, 128]],
                                           compare_op=ALU.is_ge, fill=0.0, base=0,
                                           channel_multiplier=-1),
         reads=[c.ones], writes=[c.caus])
    k.op("pool", lambda h: h.affine_select(out=c.cpos[:, :], in_=c.big[:, :], pattern=[[-1, 896]],
                                           compare_op=ALU.is_gt, fill=0.0, base=384,
                                           channel_multiplier=1),
         reads=[c.big], writes=[c.cpos])
    k.op("pool", lambda h: h.tensor_copy(out=c.identb[:, :], in_=c.ident[:, :]),
         reads=[c.ident], writes=[c.identb])
    for hh in range(16):
        k.op("dve", lambda h, hh=hh: h.tensor_copy(out=c.sel[:, hh, :],
                                                    in_=c.ident[0:16, hh:hh + 1].to_broadcast([16, 128])),
             reads=[c.ident], writes=[c.sel])
    return c


def rms_to_hT(k, c, x_t, hT, col0, st, ti):
    j = ti % 2
    k.op("act", lambda h: h.activation(out=st.junk[:, :], in_=x_t[:, :], func=AF.Square,
                                       accum_out=st.ss[j][:, :]),
         reads=[x_t], writes=[st.junk, st.ss[j]])
    k.op("act", lambda h: h.activation(out=st.sq[j][:, :], in_=st.ss[j][:, :], func=AF.Sqrt,
                                       scale=1.0 / D, bias=1e-6),
         reads=[st.ss[j]], writes=[st.sq[j]])
    k.op("dve", lambda h: h.reciprocal(out=st.rstd[j][:, :], in_=st.sq[j][:, :]),
         reads=[st.sq[j]], writes=[st.rstd[j]])
    k.op("act", lambda h: h.activation(out=st.xn[j][:, :], in_=x_t[:, :], func=AF.Copy,
                                       scale=st.rstd[j][:, :]),
         reads=[x_t, st.rstd[j]], writes=[st.xn[j]])
    for kc in range(8):
        k.tr(st.pT[j][:, kc * 128:(kc + 1) * 128], st.xn[j][:, kc * 128:(kc + 1) * 128],
             c.identb[:, :], reads=[st.xn[j], c.identb], writes=[st.pT[j]])
    k.op("dve", lambda h: h.tensor_copy(out=hT[:, :, col0:col0 + 128],
                                        in_=st.pT[j][:, :].rearrange("p (k n) -> p k n", k=8)),
         reads=[st.pT[j]], writes=[hT])


class NormState:
    def __init__(self, k, es, tag):
        self.junk = k.sb(es, tag + "junk", [128, D], BF16)
        self.ss = [k.sb(es, f"{tag}ss{i}", [128, 1], F32) for i in range(2)]
        self.sq = [k.sb(es, f"{tag}sq{i}", [128, 1], F32) for i in range(2)]
        self.rstd = [k.sb(es, f"{tag}rstd{i}", [128, 1], F32) for i in range(2)]
        self.xn = [k.sb(es, f"{tag}xn{i}", [128, D], BF16) for i in range(2)]
        self.pT = [k.ps(es, f"{tag}pT{i}", [128, D], BF16) for i in range(2)]


def evac(k, i, out, in_, reads, writes, scale=None, func=None):
    if func is not None or (i % 2 == 0):
        f = func if func is not None else AF.Copy
        sc = 1.0 if scale is None else scale
        return k.op("act", lambda h: h.activation(out=out, in_=in_, func=f, scale=sc),
                    reads=reads, writes=writes)
    if scale is None:
        scale = 1.0
    return k.op("dve", lambda h: h.tensor_scalar(out=out, in0=in_, scalar1=scale, scalar2=None,
                                                  op0=ALU.mult), reads=reads, writes=writes)


def groups():
    gs = [(g * 512, 512) for g in range(8)]
    gs.append((TP, NS * TS))
    return gs


def phase1(k, c, io, scr, rows):
    with contextlib.ExitStack() as es:
        stf = [k.sb(es, f"p1sf{i}", [128, 512], F32) for i in range(2)]
        W = k.sb(es, "p1W", [128, 8, 3592], BF16)
        load_w(k, W, io["w_in_ab"], 8, 3592)
        scale_rows(k, es, W, io["norm_mix"][0, :], 8, "p1g")
        st = NormState(k, es, "p1")
        xts = [k.sb(es, f"p1x{i}", [128, D], F32) for i in range(2)]
        hTs = [k.sb(es, f"p1h{i}", [128, 8, 512], BF16) for i in range(2)]
        pfm = [k.ps(es, f"p1pf{i}", [128, 512]) for i in range(2)]
        ptm = [k.ps(es, f"p1pt{i}", [128, 512]) for i in range(2)]
        stb = [k.sb(es, f"p1sb{i}", [128, 512], BF16) for i in range(4)]
        fm = [(0, scr["qTm"], None), (512, scr["kTm"], 128.0 ** -0.5),
              (2056, scr["qTb"], 0.125), (2568, scr["kTb"], None)]
        ev = 0
        sbi = 0
        sfi = 0
        ti = 0
        gl = groups()
        tis = [0]

        def norm_group(gi):
            tok0, n = gl[gi]
            for t in range(n // 128):
                xt = xts[tis[0] % 2]
                k.dma("sp", xt[:, :], io["xin"][tok0 + t * 128: tok0 + (t + 1) * 128, :], writes=[xt])
                rms_to_hT(k, c, xt, hTs[gi % 2], t * 128, st, tis[0])
                tis[0] += 1

        norm_group(0)
        for gi, (tok0, n) in enumerate(gl):
            hT = hTs[gi % 2]
            if gi + 1 < len(gl):
                norm_group(gi + 1)
            for (c0, dst, scale) in ([] if 'nofm' in DBG else fm):
                for mc in range(4):
                    pp = pfm[ev % 2]
                    for kc in range(8):
                        k.mm(pp[:, 0:n], W[:, kc, c0 + mc * 128: c0 + (mc + 1) * 128], hT[:, kc, 0:n],
                             kc == 0, kc == 7, reads=[W, hT], writes=[pp])
                    sbt = stb[sbi % 4]
                    sbi += 1
                    evac(k, ev, sbt[:, 0:n], pp[:, 0:n], [pp], [sbt], scale=scale)
                    ev += 1
                    k.dma("sp", dst[mc * 128:(mc + 1) * 128, tok0:tok0 + n], sbt[:, 0:n], reads=[sbt])
            for (c0, row) in ([] if 'nogate' in DBG else ((2048, rows["ig"]), (2052, rows["fg"]))):
                pp = pfm[ev % 2]
                for kc in range(8):
                    k.mm(pp[0:4, 0:n], W[:, kc, c0:c0 + 4], hT[:, kc, 0:n], kc == 0, kc == 7,
                         reads=[W, hT], writes=[pp])
                evac(k, 1, row[0:4, tok0:tok0 + n], pp[0:4, 0:n], [pp], [row])
                ev += 1
            for t in range(0 if 'notm' in DBG else n // 128):
                r0 = tok0 + t * 128
                is_out = (r0 >= TP - 512)
                blocks = [(512, scr["kmTok"], 128.0 ** -0.5, None, None),
                          (1024, scr["vmTok"], None, None, None),
                          (1536, scr["osTok"], None, (None if 'nosig' in DBG else AF.Sigmoid), None),
                          (3080, scr["vbTok"], None, None, "bv")]
                if is_out:
                    blocks.append((2568, None, None, None, "bk"))
                for (c0, dst, scale, func, outn) in blocks:
                    pp = ptm[ev % 2]
                    for kc in range(8):
                        k.mm(pp[:, :], hT[:, kc, t * 128:(t + 1) * 128], W[:, kc, c0:c0 + 512],
                             kc == 0, kc == 7, reads=[W, hT], writes=[pp])
                    both = (outn is not None and is_out)
                    if both:
                        sft = stf[sfi % 2]
                        sfi += 1
                        evac(k, 1, sft[:, :], pp[:, :], [pp], [sft])
                        ev += 1
                    if dst is not None:
                        sbt = stb[sbi % 4]
                        sbi += 1
                        if both:
                            k.op("act", lambda h, sbt=sbt, sft=sft: h.activation(out=sbt[:, :], in_=sft[:, :], func=AF.Copy),
                                 reads=[sft], writes=[sbt])
                        else:
                            evac(k, ev, sbt[:, :], pp[:, :], [pp], [sbt], scale=scale, func=func)
                        ev += 1
                        k.dma("sp", dst[r0:r0 + 128, :], sbt[:, :], reads=[sbt])
                    if both:
                        for hh in range(0 if ('noodma2' in DBG or 'whole' in DBG) else 8):
                            src = sft[:, hh * 64:(hh + 1) * 64]
                            if r0 < TP:
                                q0 = r0 - (TP - 512)
                                if 'toscr' in DBG:
                                    k.dma("sp", scr["x1"][q0:q0 + 128, hh * 64:(hh + 1) * 64], src, reads=[sft])
                                else:
                                    k.dma("pool" if 'opool' in DBG else "sp", io["p" + outn][hh, q0:q0 + 128, :], src, reads=[sft])
                            else:
                                for s2 in range(2):
                                    sq = (r0 - TP) // TS + s2
                                    k.dma("sp", io["s" + outn][sq, hh, :, :], src[s2 * 64:(s2 + 1) * 64],
                                          reads=[sft])
    k.barrier()


IN_SPECS = [
    ("xin", [R, D]), ("mC0", [NS, 4, 128, 128]), ("mn0", [NS, 4, 128]), ("mm0", [NS, 4]),
    ("cbk", [NS, 8, 512, 64]), ("cbv", [NS, 8, 512, 64]),
    ("cfk", [NS, 16, 2048, 64]), ("cfv", [NS, 16, 2048, 64]), ("cflf", [NS, 16, 2048]),
    ("norm_mix", [2, D]), ("norm_ffn", [2, D]), ("norm_final", [D]),
    ("w_in_ab", [D, 3592]), ("b_gate_ab", [8]), ("mlstm_gain", [512]), ("bandbias", [8, 128, 640]),
    ("bandmask", [128, 640]),
    ("w_out_ab", [D, D]), ("w_in_fox", [D, 3088]), ("b_fox_f", [16]), ("w_out_fox", [D, D]),
    ("w_ffn_in", [2, D, 2 * HID]), ("w_ffn_out", [2, HID, D]),
]
OUT_SPECS = [
    ("y", [R, D]), ("pC", [4, 128, 128]), ("pn", [4, 128]), ("pm", [4, 1]),
    ("pbk", [8, 512, 64]), ("pbv", [8, 512, 64]),
    ("pfk", [16, TP, 64]), ("pfv", [16, TP, 64]), ("pflf", [16, TP]),
    ("sC", [NS, 4, 128, 128]), ("sn", [NS, 4, 128]), ("sm", [NS, 4, 1]),
    ("sbk", [NS, 8, TS, 64]), ("sbv", [NS, 8, TS, 64]),
    ("sfk", [NS, 16, TS, 64]), ("sfv", [NS, 16, TS, 64]), ("sflf", [NS, 16, TS]),
]
SCR_SPECS = [
    ("qTm", [512, R], BF16), ("kTm", [512, R], BF16), ("qTb", [512, R], BF16), ("kTb", [512, R], BF16),
    ("kmTok", [R, 512], BF16), ("vmTok", [R, 512], BF16), ("osTok", [R, 512], BF16),
    ("vbTok", [R, 512], BF16), ("catT", [D, R], BF16),
    ("x1", [R, D], F32), ("x2", [R, D], F32), ("x3", [R, D], F32),
    ("qTf", [D, R], BF16), ("kTf", [D, R], BF16), ("vfTok", [R, D], BF16), ("attT", [D, R], BF16),
    ("hidT", [HID, R], BF16), ("rds", [4, 512], F32),
]

PHASES = 99
DBG = ""


def run_phases(k, c, io, scr, upto=None):
    upto = PHASES if upto is None else upto
    with contextlib.ExitStack() as es1:
        rows = {"ig": k.sb(es1, "row_ig", [4, R], F32), "fg": k.sb(es1, "row_fg", [4, R], F32)}
        pers = {"GW": k.sb(es1, "GW", [4, 2, R], F32), "COLS": k.sb(es1, "COLS", [128, NCH, 3, 4], F32),
                "DECB": k.sb(es1, "DECB", [128, 4, NCH], F32)}
        phase1(k, c, io, scr, rows)
        if upto >= 2:
            phase2(k, c, io, scr, rows, pers)
        if upto >= 3:
            phase3(k, c, io, scr, pers)
    if upto >= 4:
        phase4(k, c, io, scr)
    if upto >= 6:
        with contextlib.ExitStack() as esw:
            Wg = ffn_w_alloc(k, esw, "p6")
            outproj_phase(k, c, io, scr, scr["catT"], io["w_out_ab"], io["xin"], scr["x1"], "p5",
                          prefetch=lambda: ffn_w_load(k, io, 0, Wg))
            ffn_in_phase(k, c, io, scr, 0, scr["x1"], "p6", W=Wg[0])
        ffn_out_phase(k, c, io, scr, 0, scr["x1"], scr["x2"], False, "p7")
    if upto >= 8:
        fox_phases(k, c, io, scr)
    if upto >= 9:
        with contextlib.ExitStack() as esw:
            Wg = ffn_w_alloc(k, esw, "pa")
            outproj_phase(k, c, io, scr, scr["attT"], io["w_out_fox"], scr["x2"], scr["x3"], "p9",
                          prefetch=lambda: ffn_w_load(k, io, 1, Wg))
            ffn_in_phase(k, c, io, scr, 1, scr["x3"], "pa", W=Wg[0])
        ffn_out_phase(k, c, io, scr, 1, scr["x3"], io["y"], True, "pb")


def build():
    nc = bass.Bass("TRN2", target_bir_lowering=False)
    io = {}
    for name, shape in IN_SPECS:
        io[name] = nc.dram_tensor(name, shape, F32, kind="ExternalInput").ap()
    for name, shape in OUT_SPECS:
        io[name] = nc.dram_tensor(name, shape, F32, kind="ExternalOutput").ap()
    scr = {}
    for name, shape, dt in SCR_SPECS:
        scr[name] = nc.dram_tensor("scr_" + name, shape, dt).ap()
    with contextlib.ExitStack() as es:
        k = KB(nc, es)
        c = make_consts(k, es)
        run_phases(k, c, io, scr)
        k.barrier()
    return nc


_NC_CACHE = {}


def kernel(**inp):
    f = lambda a: np.ascontiguousarray(np.asarray(a, dtype=np.float32))
    xp, xs = f(inp["x_prompt"]), f(inp["x_sample"])
    kk = np.arange(128)[:, None]
    qq = np.arange(640)[None, :]
    rel = qq - kk
    idx = np.clip(rel, -63, 128) + 63
    table = f(inp["rel_bias_table"])[0]
    bandbias = np.ascontiguousarray(table[:, idx])
    dch = (qq // 64) - (kk // 64)
    bandmask = np.where((dch >= 0) & (dch <= 8), 0.0, -BIG).astype(np.float32)
    common = {
        "norm_mix": f(inp["norm_mix"]), "norm_ffn": f(inp["norm_ffn"]), "norm_final": f(inp["norm_final"]),
        "w_in_ab": f(inp["w_in_ab"])[0], "b_gate_ab": f(inp["b_gate_ab"])[0],
        "mlstm_gain": f(inp["mlstm_gain"])[0], "bandbias": bandbias, "bandmask": bandmask,
        "w_out_ab": f(inp["w_out_ab"])[0], "w_in_fox": f(inp["w_in_fox"])[0],
        "b_fox_f": f(inp["b_fox_f"])[0], "w_out_fox": f(inp["w_out_fox"])[0],
        "w_ffn_in": f(inp["w_ffn_in"]), "w_ffn_out": f(inp["w_ffn_out"]),
    }
    in_maps = []
    for cid in range(8):
        b = cid % 4
        s0 = 4 * cid
        m = dict(common)
        m["xin"] = np.ascontiguousarray(np.concatenate([xp[b], xs[s0:s0 + 4].reshape(NS * TS, D)], axis=0))
        m["mC0"] = f(inp["state_mlstm_C"])[0, s0:s0 + 4]
        m["mn0"] = f(inp["state_mlstm_n"])[0, s0:s0 + 4]
        m["mm0"] = f(inp["state_mlstm_m"])[0, s0:s0 + 4]
        m["cbk"] = f(inp["cache_band_k"])[0, s0:s0 + 4]
        m["cbv"] = f(inp["cache_band_v"])[0, s0:s0 + 4]
        m["cfk"] = f(inp["cache_fox_k"])[0, s0:s0 + 4]
        m["cfv"] = f(inp["cache_fox_v"])[0, s0:s0 + 4]
        m["cflf"] = f(inp["cache_fox_logf"])[0, s0:s0 + 4]
        in_maps.append({kk_: np.ascontiguousarray(v) for kk_, v in m.items()})
    if "nc" not in _NC_CACHE:
        _NC_CACHE["nc"] = build()
    res = run_bass_kernel_spmd(_NC_CACHE["nc"], in_maps, core_ids=list(range(8)))
    rs = res.results
    P = lambda n: np.stack([rs[b][n] for b in range(4)], axis=0)
    S = lambda n: np.concatenate([rs[cid][n] for cid in range(8)], axis=0)
    y_prompt = np.stack([rs[b]["y"][:TP] for b in range(4)], axis=0)
    y_sample = np.concatenate([rs[cid]["y"][TP:].reshape(NS, TS, D) for cid in range(8)], axis=0)
    outs = (
        y_prompt, y_sample,
        P("pC")[None], P("pn")[None], P("pm")[None, :, :, 0],
        P("pbk")[None], P("pbv")[None], P("pfk")[None], P("pfv")[None], P("pflf")[None],
        S("sC")[None], S("sn")[None], S("sm")[None, :, :, 0],
        S("sbk")[None], S("sbv")[None], S("sfk")[None], S("sfv")[None], S("sflf")[None],
    )
    return tuple(np.ascontiguousarray(o, dtype=np.float32) for o in outs)


def chunks():
    cs = [(c * 128, 128) for c in range(32)]
    cs += [(TP + i * TS, TS) for i in range(NS)]
    return cs


NCH = 36


def phase2(k, c, io, scr, rows, pers):
    GW, COLS, DECB = pers["GW"], pers["COLS"], pers["DECB"]
    ig, fg = rows["ig"], rows["fg"]
    with contextlib.ExitStack() as es:
        negb = k.sb(es, "p2negb", [4, 1], F32)
        bigc = k.sb(es, "p2big", [4, 1], F32)
        m0 = k.sb(es, "p2m0", [4, NS], F32)
        NF = k.sb(es, "p2NF", [4, R], F32)
        WS = k.sb(es, "p2WS", [4, R], F32)
        EM = k.sb(es, "p2EM", [4, R], F32)
        DEC = k.sb(es, "p2DEC", [4, NCH], F32)
        PC = k.ps(es, "p2PC", [128, NCH, 3, 4])
        PD = k.ps(es, "p2PD", [128, 4, NCH])
        bg = io["b_gate_ab"]
        k.dma("sp", negb[:, :], bg[4:8].rearrange("(h o) -> h o", o=1), writes=[negb])
        k.dma("sp", bigc[:, :], bg[0:4].rearrange("(h o) -> h o", o=1), writes=[bigc])
        k.dma("sp", m0[:, :], io["mm0"].rearrange("s h -> h s"), writes=[m0], allow_slow_non_contiguous=True)
        k.op("dve", lambda h: h.tensor_scalar(out=negb[:, :], in0=negb[:, :], scalar1=-1.0, scalar2=None,
                                              op0=ALU.mult), writes=[negb])
        k.op("act", lambda h: h.activation(out=fg[:, :], in_=fg[:, :], func=AF.Exp, scale=-1.0,
                                           bias=negb[:, :]), reads=[negb], writes=[fg])
        k.op("act", lambda h: h.activation(out=fg[:, :], in_=fg[:, :], func=AF.Ln, bias=1.0), writes=[fg])
        segs = [(0, TP, None)] + [(TP + i * TS, TS, i) for i in range(NS)]
        for (s0, sn, si) in segs:
            k.op("dve", lambda h, s0=s0, sn=sn: h.tensor_tensor_scan(
                out=NF[:, s0:s0 + sn], data0=fg[:, s0:s0 + sn], data1=fg[:, s0:s0 + sn], initial=0.0,
                op0=ALU.add, op1=ALU.max), reads=[fg], writes=[NF])
        k.op("dve", lambda h: h.scalar_tensor_tensor(out=ig[:, :], in0=ig[:, :], scalar=bigc[:, :],
                                                     in1=NF[:, :], op0=ALU.add, op1=ALU.add),
             reads=[bigc, NF], writes=[ig])
        for (s0, sn, si) in segs:
            init = 0.0 if si is None else m0[:, si:si + 1]
            k.op("dve", lambda h, s0=s0, sn=sn, init=init: h.tensor_tensor_scan(
                out=GW[:, 0, s0:s0 + sn], data0=ig[:, s0:s0 + sn], data1=ig[:, s0:s0 + sn], initial=init,
                op0=ALU.max, op1=ALU.max), reads=[ig, m0], writes=[GW])
        Gp = GW[:, 0, 0:TP].rearrange("p (c t) -> p c t", t=128)
        k.op("pool", lambda h: h.memset(fg[:, 0:128], 0.0), writes=[fg])
        k.op("dve", lambda h: h.tensor_copy(
            out=fg[:, 128:TP].rearrange("p (c t) -> p c t", t=128),
            in_=Gp[:, 0:31, 127:128].to_broadcast([4, 31, 128])), reads=[GW], writes=[fg])
        for i in range(NS):
            s0 = TP + i * TS
            k.op("dve", lambda h, s0=s0, i=i: h.tensor_copy(out=fg[:, s0:s0 + TS],
                                                          in_=m0[:, i:i + 1].to_broadcast([4, TS])),
                 reads=[m0], writes=[fg])
        k.op("dve", lambda h: h.tensor_tensor(out=WS[:, :], in0=fg[:, :], in1=GW[:, 0, :], op=ALU.subtract),
             reads=[fg, GW], writes=[WS])
        k.op("act", lambda h: h.activation(out=GW[:, 1, :], in_=WS[:, :], func=AF.Exp), reads=[WS], writes=[GW])
        k.op("dve", lambda h: h.tensor_copy(
            out=WS[:, 0:TP].rearrange("p (c t) -> p c t", t=128),
            in_=Gp[:, :, 127:128].to_broadcast([4, 32, 128])), reads=[GW], writes=[WS])
        for i in range(NS):
            s0 = TP + i * TS
            k.op("dve", lambda h, s0=s0: h.tensor_copy(
                out=WS[:, s0:s0 + TS], in_=GW[:, 0, s0 + TS - 1:s0 + TS].to_broadcast([4, TS])),
                reads=[GW], writes=[WS])
        k.op("dve", lambda h: h.tensor_tensor(
            out=DEC[:, 0:32], in0=fg[:, 0:TP].rearrange("p (c t) -> p c t", t=128)[:, :, 0],
            in1=WS[:, 0:TP].rearrange("p (c t) -> p c t", t=128)[:, :, 0], op=ALU.subtract),
            reads=[fg, WS], writes=[DEC])
        for i in range(NS):
            s0 = TP + i * TS
            k.op("dve", lambda h, s0=s0, i=i: h.tensor_tensor(out=DEC[:, 32 + i:33 + i], in0=fg[:, s0:s0 + 1],
                                                            in1=WS[:, s0:s0 + 1], op=ALU.subtract),
                 reads=[fg, WS], writes=[DEC])
        k.op("act", lambda h: h.activation(out=DEC[:, :], in_=DEC[:, :], func=AF.Exp), writes=[DEC])
        k.op("dve", lambda h: h.tensor_tensor(out=WS[:, :], in0=ig[:, :], in1=WS[:, :], op=ALU.subtract),
             reads=[ig], writes=[WS])
        k.op("act", lambda h: h.activation(out=WS[:, :], in_=WS[:, :], func=AF.Exp), writes=[WS])
        k.op("dve", lambda h: h.tensor_tensor(out=EM[:, :], in0=GW[:, 0, :], in1=NF[:, :], op=ALU.subtract),
             reads=[GW, NF], writes=[EM])
        k.dma("sp", io["pm"][:, :], EM[:, TP - 1:TP], reads=[EM])
        for i in range(NS):
            e1 = TP + (i + 1) * TS
            k.dma("sp", io["sm"][i, :, :], EM[:, e1 - 1:e1], reads=[EM])
        k.op("act", lambda h: h.activation(out=EM[:, :], in_=EM[:, :], func=AF.Exp, scale=-1.0), writes=[EM])
        for ci, (r0, n) in enumerate(chunks()):
            for xi, X in enumerate((ig, WS, EM)):
                k.tr(PC[0:n, ci, xi, :], X[0:4, r0:r0 + n], c.ident[0:4, 0:4], reads=[X, c.ident], writes=[PC])
        k.op("dve", lambda h: h.tensor_copy(out=COLS[:, 0:32, :, :], in_=PC[:, 0:32, :, :]), reads=[PC], writes=[COLS])
        k.op("dve", lambda h: h.tensor_copy(out=COLS[0:64, 32:NCH, :, :], in_=PC[0:64, 32:NCH, :, :]), reads=[PC], par=[COLS])
        for hh in range(4):
            k.mm(PD[:, hh, :], c.sel[0:4, hh, :], DEC[0:4, :], True, True, reads=[c.sel, DEC], writes=[PD])
        k.op("dve", lambda h: h.tensor_copy(out=DECB[:, :, :], in_=PD[:, :, :]), reads=[PD], writes=[DECB])
    k.barrier()


def phase3(k, c, io, scr, pers):
    GW, COLS, DECB = pers["GW"], pers["COLS"], pers["DECB"]
    with contextlib.ExitStack() as es:
        gain = k.sb(es, "p3gain", [128, 512], F32)
        k.dma("sp", gain[:, :], io["mlstm_gain"].partition_broadcast(128), writes=[gain])
        qTs = [k.sb(es, f"p3q{i}", [128, 4, 128], BF16) for i in range(2)]
        kTs = [k.sb(es, f"p3k{i}", [128, 4, 128], BF16) for i in range(2)]
        kts = [k.sb(es, f"p3kt{i}", [128, 4, 128], BF16) for i in range(2)]
        vas = [k.sb(es, f"p3va{i}", [128, 4, 129], BF16) for i in range(2)]
        oss = [k.sb(es, f"p3os{i}", [128, 512], BF16) for i in range(3)]
        for i in range(2):
            k.op("pool", lambda h, i=i: h.memset(vas[i][:, :, 128:129], 1.0), writes=[vas[i]])
        Cf = [k.sb(es, f"p3Cf{h}", [128, 129], F32) for h in range(4)]
        Cb = [k.sb(es, f"p3Cb{h}", [128, 129], BF16) for h in range(4)]
        E = k.sb(es, "p3E", [128, 4, 128], F32)
        WT = k.sb(es, "p3WT", [128, 4, 128], BF16)
        QS = k.sb(es, "p3QS", [128, 4, 128], BF16)
        KW = k.sb(es, "p3KW", [128, 4, 128], BF16)
        ONs = [k.sb(es, f"p3ON{i}", [128, 4, 129], F32) for i in range(2)]
        H = k.sb(es, "p3H", [128, 4, 128], F32)
        SQ = k.sb(es, "p3SQ", [128, 4, 128], F32)
        OAs = [k.sb(es, f"p3OA{i}", [128, 512], BF16) for i in range(2)]
        G2s = [k.sb(es, f"p3G2{i}", [128, 512], F32) for i in range(2)]
        junk = k.sb(es, "p3junk", [128, 128], F32)
        sm2 = [k.sb(es, f"p3t{i}", [128, 4], F32) for i in range(4)]
        OT = k.sb(es, "p3OT", [128, 4, 128], BF16)
        sm = [k.sb(es, f"p3s{i}", [128, 4], F32) for i in range(6)]
        PS_S = k.ps(es, "p3PS", [128, 4, 128])
        PS_G = k.ps(es, "p3PG", [128, 4, 128])
        PS_W = k.ps(es, "p3PW", [128, 4, 128])
        PS_O = [k.ps(es, f"p3PO{i}", [128, 2, 129]) for i in range(2)]
        PS_D = [k.ps(es, f"p3PD{i}", [128, 2, 129]) for i in range(2)]
        PS_T = k.ps(es, "p3PT", [128, 4, 128], BF16)
        qv = scr["qTm"].rearrange("(h d) r -> d h r", h=4)
        kv = scr["kTm"].rearrange("(h d) r -> d h r", h=4)
        catv = scr["catT"][0:512, :].rearrange("(h d) r -> d h r", h=4)
        chs = chunks()

        def load(ci):
            r0, n = chs[ci]
            j = ci % 2
            k.dma("sp", qTs[j][:, :, 0:n], qv[:, :, r0:r0 + n], writes=[qTs[j]])
            k.dma("sp", kTs[j][:, :, 0:n], kv[:, :, r0:r0 + n], writes=[kTs[j]])
            k.dma("sp", kts[j][0:n, :, :], scr["kmTok"][r0:r0 + n, :].rearrange("t (h e) -> t h e", h=4),
                  writes=[kts[j]])
            k.dma("sp", vas[j][0:n, :, 0:128], scr["vmTok"][r0:r0 + n, :].rearrange("t (h e) -> t h e", h=4),
                  writes=[vas[j]])
            k.dma("sp", oss[ci % 3][0:n, :], scr["osTok"][r0:r0 + n, :], writes=[oss[ci % 3]])

        load(0)
        pendA, pendA_b, pendB = [], [], []
        for ci, (r0, n) in enumerate(chs):
            j = ci % 2
            qT, kT, kt, va, osg = qTs[j], kTs[j], kts[j], vas[j], oss[ci % 3]
            ON = ONs[ci % 2]
            if ci + 1 < NCH:
                load(ci + 1)
            if ci == 0:
                for hh in range(4):
                    k.op("pool", lambda h, hh=hh: h.memset(Cf[hh][:, :], 0.0), writes=[Cf[hh]])
                    k.op("pool", lambda h, hh=hh: h.memset(Cb[hh][:, :], 0.0), writes=[Cb[hh]])
            elif ci >= 32:
                si = ci - 32
                for hh in range(4):
                    k.dma("sp", Cf[hh][:, 0:128], io["mC0"][si, hh, :, :], writes=[Cf[hh]])
                    k.dma("sp", Cf[hh][:, 128:129], io["mn0"][si, hh, :].rearrange("(d o) -> d o", o=1),
                          par=[Cf[hh]])
                    k.op("act", lambda h, hh=hh: h.activation(out=Cb[hh][:, :], in_=Cf[hh][:, :], func=AF.Copy),
                         reads=[Cf[hh]], writes=[Cb[hh]])
            for hh in range(4):
                k.mm(PS_S[0:n, hh, 0:n], kT[:, hh, 0:n], qT[:, hh, 0:n], True, True, reads=[kT, qT], writes=[PS_S])
            for hh in range(4):
                k.mm(PS_G[0:n, hh, 0:n], c.sel[0:4, hh, 0:n], GW[0:4, 0, r0:r0 + n], True, True,
                     reads=[c.sel, GW], writes=[PS_G])
            for hh in range(4):
                k.mm(PS_W[:, hh, 0:n], c.sel[0:4, hh, :], GW[0:4, 1, r0:r0 + n], True, True,
                     reads=[c.sel, GW], writes=[PS_W])
            for hh in range(4):
                k.op("act", lambda h, hh=hh: h.activation(out=E[0:n, hh, 0:n], in_=PS_G[0:n, hh, 0:n], func=AF.Exp,
                                                        scale=-1.0, bias=COLS[0:n, ci, 0, hh:hh + 1]),
                     reads=[PS_G, COLS], writes=[E])
            for hh in range(4):
                k.op("pool", lambda h, hh=hh: h.tensor_tensor(out=E[0:n, hh, 0:n], in0=E[0:n, hh, 0:n],
                                                            in1=c.caus[0:n, 0:n], op=ALU.mult),
                     reads=[c.caus], writes=[E])
            k.op("dve", lambda h: h.tensor_tensor(out=WT[0:n, :, 0:n], in0=PS_S[0:n, :, 0:n], in1=E[0:n, :, 0:n],
                                                  op=ALU.mult), reads=[PS_S, E], writes=[WT])
            k.op("dve", lambda h: h.tensor_tensor(out=QS[:, :, 0:n], in0=PS_W[:, :, 0:n], in1=qT[:, :, 0:n],
                                                  op=ALU.mult), reads=[PS_W, qT], writes=[QS])
            for hh in range(4):
                po = PS_O[hh // 2]
                k.mm(po[0:n, hh % 2, :], QS[:, hh, 0:n], Cb[hh][:, :], True, False, reads=[QS, Cb[hh]], writes=[po])
                k.mm(po[0:n, hh % 2, :], WT[0:n, hh, 0:n], va[0:n, hh, :], False, True, reads=[WT, va], writes=[po])
            for i2 in range(2):
                k.op("dve", lambda h, i2=i2: h.tensor_copy(out=ON[0:n, 2 * i2:2 * i2 + 2, :], in_=PS_O[i2][0:n, :, :]),
                     reads=[PS_O[i2]], writes=[ON])
            for hh in range(4):
                k.op("act", lambda h, hh=hh: h.activation(out=KW[0:n, hh, :], in_=kt[0:n, hh, :], func=AF.Copy,
                                                        scale=COLS[0:n, ci, 1, hh:hh + 1]),
                     reads=[kt, COLS], writes=[KW])
            for hh in range(4):
                pd = PS_D[hh // 2]
                k.mm(pd[:, hh % 2, :], KW[0:n, hh, :], va[0:n, hh, :], True, True, reads=[KW, va], writes=[pd])
            for hh in range(4):
                pd = PS_D[hh // 2]
                k.op("dve", lambda h, hh=hh, pd=pd: h.scalar_tensor_tensor(
                    out=Cf[hh][:, :], in0=Cf[hh][:, :], scalar=DECB[:, hh, ci:ci + 1], in1=pd[:, hh % 2, :],
                    op0=ALU.mult, op1=ALU.add), reads=[DECB, pd], writes=[Cf[hh]])
                k.op("act", lambda h, hh=hh: h.activation(out=Cb[hh][:, :], in_=Cf[hh][:, :], func=AF.Copy),
                     reads=[Cf[hh]], writes=[Cb[hh]])
            if ci == 31 or ci >= 32:
                for hh in range(4):
                    if ci == 31:
                        oc, on = io["pC"][hh, :, :], io["pn"][hh, :]
                    else:
                        oc, on = io["sC"][ci - 32, hh, :, :], io["sn"][ci - 32, hh, :]
                    k.dma("sp", oc, Cf[hh][:, 0:128], reads=[Cf[hh]])
                    k.dma("sp", on.rearrange("(d o) -> d o", o=1), Cf[hh][:, 128:129], reads=[Cf[hh]])
            G2 = G2s[ci % 2]
            k.op("pool", lambda h, G2=G2, osg=osg: h.tensor_tensor(out=G2[0:n, :], in0=gain[0:n, :], in1=osg[0:n, :],
                                                                   op=ALU.mult), reads=[gain, osg], writes=[G2])

            def epiA(ci=ci, n=n, ON=ON, G2=G2, OAb=OAs[ci % 2]):
                den, dd, rr, mu, var, rs = sm
                s1, s2, aa, bb = sm2
                k.op("dve", lambda h: h.tensor_tensor(out=dd[0:n, :], in0=ON[0:n, :, 128], in1=COLS[0:n, ci, 2, :],
                                                      op=ALU.max), reads=[ON, COLS], writes=[dd])
                k.op("dve", lambda h: h.tensor_scalar(out=den[0:n, :], in0=ON[0:n, :, 128], scalar1=-1.0, scalar2=None,
                                                      op0=ALU.mult), reads=[ON], writes=[den])
                k.op("dve", lambda h: h.tensor_tensor(out=dd[0:n, :], in0=dd[0:n, :], in1=den[0:n, :],
                                                      op=ALU.max), reads=[den], writes=[dd])
                k.op("dve", lambda h: h.reciprocal(out=rr[0:n, :], in_=dd[0:n, :]), reads=[dd], writes=[rr])
                for hh in range(4):
                    k.op("act", lambda h, hh=hh: h.activation(out=junk[0:n, :], in_=ON[0:n, hh, 0:128], func=AF.Copy,
                                                            scale=rr[0:n, hh:hh + 1], accum_out=s1[0:n, hh:hh + 1]),
                         reads=[ON, rr], writes=[junk, s1])
                    k.op("act", lambda h, hh=hh: h.activation(out=junk[0:n, :], in_=ON[0:n, hh, 0:128], func=AF.Square,
                                                            scale=rr[0:n, hh:hh + 1], accum_out=s2[0:n, hh:hh + 1]),
                         reads=[ON, rr], writes=[junk, s2])
                k.op("dve", lambda h: h.tensor_scalar(out=mu[0:n, :], in0=s1[0:n, :], scalar1=1.0 / 128, scalar2=None,
                                                      op0=ALU.mult), reads=[s1], writes=[mu])
                k.op("dve", lambda h: h.tensor_tensor(out=var[0:n, :], in0=mu[0:n, :], in1=mu[0:n, :], op=ALU.mult),
                     reads=[mu], writes=[var])
                k.op("dve", lambda h: h.scalar_tensor_tensor(out=var[0:n, :], in0=s2[0:n, :], scalar=1.0 / 128,
                                                             in1=var[0:n, :], op0=ALU.mult, op1=ALU.subtract),
                     reads=[s2, var], writes=[var])
                k.op("dve", lambda h: h.tensor_scalar(out=var[0:n, :], in0=var[0:n, :], scalar1=0.0, scalar2=None,
                                                      op0=ALU.max), reads=[var], writes=[var])
                k.op("act", lambda h: h.activation(out=rs[0:n, :], in_=var[0:n, :], func=AF.Ln, bias=1e-6),
                     reads=[var], writes=[rs])
                k.op("act", lambda h: h.activation(out=rs[0:n, :], in_=rs[0:n, :], func=AF.Exp, scale=-0.5),
                     reads=[rs], writes=[rs])
                k.op("dve", lambda h: h.tensor_tensor(out=aa[0:n, :], in0=rr[0:n, :], in1=rs[0:n, :], op=ALU.mult),
                     reads=[rr, rs], writes=[aa])
                k.op("dve", lambda h: h.scalar_tensor_tensor(out=bb[0:n, :], in0=mu[0:n, :], scalar=-1.0, in1=rs[0:n, :],
                                                             op0=ALU.mult, op1=ALU.mult), reads=[mu, rs], writes=[bb])
                for hh in range(4):
                    k.op("pool", lambda h, hh=hh: h.tensor_scalar(out=H[0:n, hh, :], in0=ON[0:n, hh, 0:128],
                                                                scalar1=aa[0:n, hh:hh + 1], scalar2=bb[0:n, hh:hh + 1],
                                                                op0=ALU.mult, op1=ALU.add),
                         reads=[ON, aa, bb], writes=[H])
                k.op("dve", lambda h: h.tensor_tensor(out=OAb[0:n, :], in0=H[0:n, :, :].rearrange("p h e -> p (h e)"),
                                                      in1=G2[0:n, :], op=ALU.mult), reads=[H, G2], writes=[OAb])

            def epiB(r0=r0, n=n, OAb=OAs[ci % 2]):
                for hh in range(4):
                    k.tr(PS_T[:, hh, 0:n], OAb[0:n, hh * 128:(hh + 1) * 128], c.identb[0:n, 0:n],
                         reads=[OAb, c.identb], writes=[PS_T])
                k.op("act", lambda h: h.activation(out=OT[:, :, 0:n], in_=PS_T[:, :, 0:n], func=AF.Copy),
                     reads=[PS_T], writes=[OT])
                k.dma("sp", catv[:, :, r0:r0 + n], OT[:, :, 0:n], reads=[OT])
            if pendB:
                pendB.pop(0)()
            if pendA:
                pendA.pop(0)()
                pendB.append(pendA_b.pop(0))
            pendA.append(epiA)
            pendA_b.append(epiB)
        while pendA:
            pendA.pop(0)()
            pendB.append(pendA_b.pop(0))
        while pendB:
            pendB.pop(0)()
    k.barrier()


def attn_epilogue(k, c, PS_OT, nq, OS, RD, PS_BC, out_ap, out_t):
    k.op("dve", lambda h: h.tensor_copy(out=OS[0:65, 0:nq], in_=PS_OT[0:65, 0:nq]), reads=[PS_OT], writes=[OS])
    k.op("act", lambda h: h.activation(out=RD[64:65, 0:nq], in_=OS[64:65, 0:nq], func=AF.Ln), reads=[OS], writes=[RD])
    k.op("act", lambda h: h.activation(out=RD[64:65, 0:nq], in_=RD[64:65, 0:nq], func=AF.Exp, scale=-1.0),
         reads=[RD], writes=[RD])
    k.mm(PS_BC[0:64, 0:nq], c.ones[64:65, 0:64], RD[64:65, 0:nq], True, True, reads=[c.ones, RD], writes=[PS_BC])
    k.op("dve", lambda h: h.tensor_tensor(out=out_ap, in0=OS[0:64, 0:nq], in1=PS_BC[0:64, 0:nq], op=ALU.mult),
         reads=[OS, PS_BC], writes=[out_t])


def phase4(k, c, io, scr):
    with contextlib.ExitStack() as es:
        kTh = [k.sb(es, f"p4k{i}", [128, R], BF16) for i in range(2)]
        qTh = [k.sb(es, f"p4q{i}", [128, R], BF16) for i in range(2)]
        for t_ in kTh + qTh:
            k.op("pool", lambda h, t_=t_: h.memset(t_[64:128, :], 0.0), writes=[t_])
        vbh = [k.sb(es, f"p4v{i}", [128, NT, 65], BF16) for i in range(2)]
        bias = [k.sb(es, f"p4b{i}", [128, 640], F32) for i in range(2)]
        mask = k.sb(es, "p4mask", [128, 640], F32)
        ATT = [k.sb(es, f"p4att{i}", [64, R], BF16) for i in range(2)]
        TMP = [k.sb(es, f"p4tmp{i}", [128, 640], F32) for i in range(2)]
        P = [k.sb(es, f"p4P{i}", [128, 640], BF16) for i in range(2)]
        OS = k.sb(es, "p4OS", [65, 128], F32)
        RD = k.sb(es, "p4RD", [65, 128], F32)
        KC = k.sb(es, "p4KC", [128, 4, 64], BF16)
        KTC = k.sb(es, "p4KTC", [128, 512], BF16)
        k.op("pool", lambda h: h.memset(KTC[64:128, :], 0.0), writes=[KTC])
        VC = k.sb(es, "p4VC", [128, 4, 65], BF16)
        VN = k.sb(es, "p4VN", [64, 65], BF16)
        PS_A = k.ps(es, "p4PA", [128, 512])
        PS_B = k.ps(es, "p4PB", [128, 512])
        PS_OT = [k.ps(es, f"p4PO{i}", [128, 512]) for i in range(2)]
        PS_BC = k.ps(es, "p4PBC", [128, 512])
        PS_KT = k.ps(es, "p4PKT", [64, 4, 128], BF16)
        k.dma("sp", mask[:, :], io["bandmask"][:, :], writes=[mask])
        for i in range(2):
            k.op("pool", lambda h, i=i: h.memset(vbh[i][:, :, 64:65], 1.0), writes=[vbh[i]])
        k.op("pool", lambda h: h.memset(VC[:, :, 64:65], 1.0), writes=[VC])
        k.op("pool", lambda h: h.memset(VN[:, 64:65], 1.0), writes=[VN])
        vview = scr["vbTok"].rearrange("(t p) c -> p t c", p=128)

        def load(hh):
            j = hh % 2
            k.dma("sp", kTh[j][0:64, :], scr["kTb"][hh * 64:(hh + 1) * 64, :], writes=[kTh[j]])
            k.dma("sp", qTh[j][0:64, :], scr["qTb"][hh * 64:(hh + 1) * 64, :], writes=[qTh[j]])
            k.dma("sp", vbh[j][:, :, 0:64], vview[:, :, hh * 64:(hh + 1) * 64], writes=[vbh[j]])
            k.dma("sp", bias[j][:, :], io["bandbias"][hh, :, :], writes=[bias[j]])
            k.op("pool", lambda h: h.tensor_tensor(out=bias[j][:, :], in0=bias[j][:, :], in1=mask[:, :], op=ALU.add),
                 reads=[mask], writes=[bias[j]])

        load(0)
        it = 0
        for hh in range(8):
            j = hh % 2
            kT, qT, vb, bs, att = kTh[j], qTh[j], vbh[j], bias[j], ATT[j]
            if hh + 1 < 8:
                load(hh + 1)
            pendb = []
            for i in range(32):
                nj = min(i, 4) + 1
                tmp, p, pot = TMP[it % 2], P[it % 2], PS_OT[it % 2]
                it += 1
                for s in range(nj):
                    jj = i - s
                    dst = PS_A[:, s * 128:(s + 1) * 128] if s < 4 else PS_B[:, 0:128]
                    k.mm(dst, kT[:, jj * 128:(jj + 1) * 128], qT[:, i * 128:(i + 1) * 128], True, True,
                         reads=[kT, qT], writes=[PS_A if s < 4 else PS_B])
                na = min(nj, 4) * 128
                k.op("dve", lambda h, na=na, tmp=tmp: h.tensor_tensor(out=tmp[:, 0:na], in0=PS_A[:, 0:na],
                                                                     in1=bs[:, 0:na], op=ALU.add),
                     reads=[PS_A, bs], writes=[tmp])
                if nj == 5:
                    k.op("dve", lambda h, tmp=tmp: h.tensor_tensor(out=tmp[:, 512:640], in0=PS_B[:, 0:128],
                                                                  in1=bs[:, 512:640], op=ALU.add),
                         reads=[PS_B, bs], writes=[tmp])
                k.op("act", lambda h, tmp=tmp, p=p, nj=nj: h.activation(out=p[:, 0:nj * 128], in_=tmp[:, 0:nj * 128],
                                                                       func=AF.Exp), reads=[tmp], writes=[p])
                def fin(i=i, nj=nj, p=p, pot=pot):
                    for s in range(nj):
                        jj = i - s
                        k.mm(pot[0:65, 0:128], vb[:, jj, :], p[:, s * 128:(s + 1) * 128], s == 0, s == nj - 1,
                             reads=[vb, p], writes=[pot])
                    attn_epilogue(k, c, pot, 128, OS, RD, PS_BC, att[:, i * 128:(i + 1) * 128], att)
                if pendb:
                    pendb.pop(0)()
                pendb.append(fin)
            while pendb:
                pendb.pop(0)()
            for si in range(NS):
                r0 = TP + si * TS
                tmp, p, pot = TMP[it % 2], P[it % 2], PS_OT[it % 2]
                it += 1
                k.dma("pool", KC[:, :, :], io["cbk"][si, hh, :, :].rearrange("(t p) d -> p t d", p=128), writes=[KC])
                k.dma("pool", VC[:, :, 0:64], io["cbv"][si, hh, :, :].rearrange("(t p) d -> p t d", p=128), writes=[VC])
                k.dma("sp", VN[:, 0:64], scr["vbTok"][r0:r0 + TS, hh * 64:(hh + 1) * 64], writes=[VN])
                for m in range(4):
                    k.tr(PS_KT[:, m, :], KC[:, m, :], c.identb[:, :], reads=[KC, c.identb], writes=[PS_KT])
                k.op("act", lambda h: h.activation(out=KTC[0:64, :], in_=PS_KT[:, :, :].rearrange("p m n -> p (m n)"),
                                                   func=AF.Copy), reads=[PS_KT], writes=[KTC])
                for m in range(4):
                    k.mm(PS_A[:, m * 64:(m + 1) * 64], KTC[:, m * 128:(m + 1) * 128], qT[:, r0:r0 + TS], True, True,
                         reads=[KTC, qT], writes=[PS_A])
                k.mm(PS_A[0:64, 256:320], kT[:, r0:r0 + TS], qT[:, r0:r0 + TS], True, True, reads=[kT, qT], writes=[PS_A])
                for m in range(4):
                    b0 = 512 - 128 * m
                    k.op("dve", lambda h, m=m, b0=b0, tmp=tmp: h.tensor_tensor(
                        out=tmp[:, m * 64:(m + 1) * 64], in0=PS_A[:, m * 64:(m + 1) * 64], in1=bs[:, b0:b0 + 64],
                        op=ALU.add), reads=[PS_A, bs], writes=[tmp])
                k.op("dve", lambda h, tmp=tmp: h.tensor_tensor(out=tmp[0:64, 256:320], in0=PS_A[0:64, 256:320],
                                                              in1=bs[0:64, 0:64], op=ALU.add),
                     reads=[PS_A, bs], writes=[tmp])
                k.op("act", lambda h, tmp=tmp, p=p: h.activation(out=p[:, 0:256], in_=tmp[:, 0:256], func=AF.Exp),
                     reads=[tmp], writes=[p])
                k.op("act", lambda h, tmp=tmp, p=p: h.activation(out=p[0:64, 256:320], in_=tmp[0:64, 256:320],
                                                                func=AF.Exp), reads=[tmp], writes=[p])
                for m in range(4):
                    k.mm(pot[0:65, 0:64], VC[:, m, :], p[:, m * 64:(m + 1) * 64], m == 0, False, reads=[VC, p], writes=[pot])
                k.mm(pot[0:65, 0:64], VN[0:64, :], p[0:64, 256:320], False, True, reads=[VN, p], writes=[pot])
                attn_epilogue(k, c, pot, TS, OS, RD, PS_BC, att[:, r0:r0 + TS], att)
            k.dma("sp", scr["catT"][512 + hh * 64:512 + (hh + 1) * 64, :], att[:, :], reads=[att])
    k.barrier()


def outproj_phase(k, c, io, scr, cat_ap, w_ap, xin_ap, xout_ap, tag, prefetch=None):
    with contextlib.ExitStack() as es:
        W = k.sb(es, tag + "W", [128, 8, D], BF16)
        load_w(k, W, w_ap, 8, D)
        if prefetch is not None:
            prefetch()
        cats = [k.sb(es, f"{tag}c{i}", [128, 8, 512], BF16) for i in range(2)]
        xts = [k.sb(es, f"{tag}x{i}", [128, D], F32) for i in range(2)]
        xos = [k.sb(es, f"{tag}o{i}", [128, D], F32) for i in range(2)]
        pp = [k.ps(es, f"{tag}p{i}", [128, 512]) for i in range(4)]
        cv = cat_ap.rearrange("(k p) r -> p k r", p=128)
        gs = groups()
        k.dma("sp", cats[0][:, :, 0:gs[0][1]], cv[:, :, gs[0][0]:gs[0][0] + gs[0][1]], writes=[cats[0]])
        ti = 0
        pi = 0
        for gi, (tok0, n) in enumerate(gs):
            cat = cats[gi % 2]
            if gi + 1 < len(gs):
                t1, n1 = gs[gi + 1]
                k.dma("sp", cats[(gi + 1) % 2][:, :, 0:n1], cv[:, :, t1:t1 + n1], writes=[cats[(gi + 1) % 2]])
            for t in range(n // 128):
                r0 = tok0 + t * 128
                xt, xo = xts[ti % 2], xos[ti % 2]
                ti += 1
                k.dma("sp", xt[:, :], xin_ap[r0:r0 + 128, :], writes=[xt])
                for blk in range(2):
                    p = pp[pi % 4]
                    pi += 1
                    for kc in range(8):
                        k.mm(p[:, :], cat[:, kc, t * 128:(t + 1) * 128], W[:, kc, blk * 512:(blk + 1) * 512],
                             kc == 0, kc == 7, reads=[cat, W], writes=[p])
                    k.op("dve", lambda h, p=p, blk=blk, xt=xt, xo=xo: h.tensor_tensor(
                        out=xo[:, blk * 512:(blk + 1) * 512], in0=p[:, :], in1=xt[:, blk * 512:(blk + 1) * 512],
                        op=ALU.add), reads=[p, xt], writes=[xo])
                k.dma("sp", xout_ap[r0:r0 + 128, :], xo[:, :], reads=[xo])
    k.barrier()


def ffn_w_alloc(k, es, tag):
    return (k.sb(es, tag + "W", [128, 8, 2 * HID], BF16), k.sb(es, tag + "g", [128, 8], F32))


def ffn_w_load(k, io, layer, Wg):
    W, gcol = Wg
    load_w(k, W, io["w_ffn_in"][layer], 8, 2 * HID)
    scale_rows(k, None, W, io["norm_ffn"][layer, :], 8, None, gcol=gcol)
    return W


def ffn_in_phase(k, c, io, scr, layer, xin_ap, tag, W=None):
    with contextlib.ExitStack() as es:
        if W is None:
            W = ffn_w_load(k, io, layer, ffn_w_alloc(k, es, tag))
        st = NormState(k, es, tag)
        xts = [k.sb(es, f"{tag}x{i}", [128, D], F32) for i in range(2)]
        hTs = [k.sb(es, f"{tag}h{i}", [128, 8, 512], BF16) for i in range(2)]
        SG = [k.sb(es, f"{tag}sg{i}", [128, 512], F32) for i in range(2)]
        HS = [k.sb(es, f"{tag}hs{i}", [128, 512], BF16) for i in range(3)]
        PG = [k.ps(es, f"{tag}pg{i}", [128, 512]) for i in range(2)]
        PU = [k.ps(es, f"{tag}pu{i}", [128, 512]) for i in range(2)]
        ti = 0
        it = 0
        gl = groups()
        tis = [0]

        def norm_group(gi):
            tok0, n = gl[gi]
            for t in range(n // 128):
                xt = xts[tis[0] % 2]
                k.dma("sp", xt[:, :], xin_ap[tok0 + t * 128: tok0 + (t + 1) * 128, :], writes=[xt])
                rms_to_hT(k, c, xt, hTs[gi % 2], t * 128, st, tis[0])
                tis[0] += 1

        norm_group(0)
        for gi, (tok0, n) in enumerate(gl):
            hT = hTs[gi % 2]
            if gi + 1 < len(gl):
                norm_group(gi + 1)
            for cch in range(HID // 128):
                pg, pu, sg, hs = PG[it % 2], PU[it % 2], SG[it % 2], HS[it % 3]
                it += 1
                for kc in range(8):
                    k.mm(pg[:, 0:n], W[:, kc, cch * 128:(cch + 1) * 128], hT[:, kc, 0:n], kc == 0, kc == 7,
                         reads=[W, hT], writes=[pg])
                for kc in range(8):
                    k.mm(pu[:, 0:n], W[:, kc, HID + cch * 128:HID + (cch + 1) * 128], hT[:, kc, 0:n], kc == 0, kc == 7,
                         reads=[W, hT], writes=[pu])
                k.op("act", lambda h, pg=pg, sg=sg: h.activation(out=sg[:, 0:n], in_=pg[:, 0:n], func=AF.Silu),
                     reads=[pg], writes=[sg])
                k.op("dve", lambda h, pu=pu, sg=sg, hs=hs: h.tensor_tensor(out=hs[:, 0:n], in0=pu[:, 0:n], in1=sg[:, 0:n],
                                                                          op=ALU.mult), reads=[pu, sg], writes=[hs])
                k.dma("sp", scr["hidT"][cch * 128:(cch + 1) * 128, tok0:tok0 + n], hs[:, 0:n], reads=[hs])
    k.barrier()


def ffn_out_phase(k, c, io, scr, layer, xin_ap, xout_ap, final, tag):
    with contextlib.ExitStack() as es:
        KC = HID // 128
        W = k.sb(es, tag + "W", [128, KC, D], BF16)
        load_w(k, W, io["w_ffn_out"][layer], KC, D)
        hids = [k.sb(es, f"{tag}hd{i}", [128, KC, 512], BF16) for i in range(2)]
        xts = [k.sb(es, f"{tag}x{i}", [128, D], F32) for i in range(2)]
        xos = [k.sb(es, f"{tag}o{i}", [128, D], F32) for i in range(2)]
        pp = [k.ps(es, f"{tag}p{i}", [128, 512]) for i in range(4)]
        if final:
            gam = k.sb(es, tag + "gam", [128, D], F32)
            k.dma("sp", gam[:, :], io["norm_final"].partition_broadcast(128), writes=[gam])
            junk = k.sb(es, tag + "junk", [128, D], BF16)
            ss = [k.sb(es, f"{tag}ss{i}", [128, 1], F32) for i in range(2)]
            sq = [k.sb(es, f"{tag}sq{i}", [128, 1], F32) for i in range(2)]
            rstd = [k.sb(es, f"{tag}rs{i}", [128, 1], F32) for i in range(2)]
            ys = [k.sb(es, f"{tag}y{i}", [128, D], F32) for i in range(2)]
        hv = scr["hidT"].rearrange("(c p) r -> p c r", p=128)
        gs = groups()
        k.dma("sp", hids[0][:, :, 0:gs[0][1]], hv[:, :, gs[0][0]:gs[0][0] + gs[0][1]], writes=[hids[0]])
        ti = 0
        pi = 0
        for gi, (tok0, n) in enumerate(gs):
            hid = hids[gi % 2]
            if gi + 1 < len(gs):
                t1, n1 = gs[gi + 1]
                k.dma("sp", hids[(gi + 1) % 2][:, :, 0:n1], hv[:, :, t1:t1 + n1], writes=[hids[(gi + 1) % 2]])
            for t in range(n // 128):
                r0 = tok0 + t * 128
                j = ti % 2
                xt, xo = xts[j], xos[j]
                ti += 1
                k.dma("sp", xt[:, :], xin_ap[r0:r0 + 128, :], writes=[xt])
                for blk in range(2):
                    p = pp[pi % 4]
                    pi += 1
                    for kc in range(KC):
                        k.mm(p[:, :], hid[:, kc, t * 128:(t + 1) * 128], W[:, kc, blk * 512:(blk + 1) * 512],
                             kc == 0, kc == KC - 1, reads=[hid, W], writes=[p])
                    k.op("dve", lambda h, p=p, blk=blk, xt=xt, xo=xo: h.tensor_tensor(
                        out=xo[:, blk * 512:(blk + 1) * 512], in0=p[:, :], in1=xt[:, blk * 512:(blk + 1) * 512],
                        op=ALU.add), reads=[p, xt], writes=[xo])
                if not final:
                    k.dma("sp", xout_ap[r0:r0 + 128, :], xo[:, :], reads=[xo])
                else:
                    k.op("act", lambda h, xo=xo, j=j: h.activation(out=junk[:, :], in_=xo[:, :], func=AF.Square,
                                                                  accum_out=ss[j][:, :]),
                         reads=[xo], writes=[junk, ss[j]])
                    k.op("act", lambda h, j=j: h.activation(out=sq[j][:, :], in_=ss[j][:, :], func=AF.Sqrt,
                                                           scale=1.0 / D, bias=1e-6), reads=[ss[j]], writes=[sq[j]])
                    k.op("dve", lambda h, j=j: h.reciprocal(out=rstd[j][:, :], in_=sq[j][:, :]),
                         reads=[sq[j]], writes=[rstd[j]])
                    k.op("act", lambda h, xo=xo, j=j: h.activation(out=ys[j][:, :], in_=xo[:, :], func=AF.Copy,
                                                                  scale=rstd[j][:, :]),
                         reads=[xo, rstd[j]], writes=[ys[j]])
                    k.op("pool", lambda h, j=j: h.tensor_tensor(out=ys[j][:, :], in0=ys[j][:, :], in1=gam[:, :],
                                                               op=ALU.mult), reads=[gam], writes=[ys[j]])
                    k.dma("sp", xout_ap[r0:r0 + 128, :], ys[j][:, :], reads=[ys[j]])
    k.barrier()


def fox_phases(k, c, io, scr):
    with contextlib.ExitStack() as esf:
        frow = k.sb(esf, "fx_frow", [16, R], F32)
        fox_inproj(k, c, io, scr, frow)
        if "fx1" in DBG:
            return
        NFP = k.sb(esf, "fx_NFP", [16, TP], F32)
        FSN = k.sb(esf, "fx_FSN", [16, NS, TS], F32)
        NFcP = k.sb(esf, "fx_NFcP", [128, 32, 16], F32)
        NFcS = k.sb(esf, "fx_NFcS", [128, NS, 17, 16], F32)
        fox_prep(k, c, io, scr, frow, NFP, FSN, NFcP, NFcS)
        if "fx2" in DBG:
            return
        fox_attn(k, c, io, scr, NFP, FSN, NFcP, NFcS)


def fox_inproj(k, c, io, scr, frow):
    with contextlib.ExitStack() as es:
        stf = [k.sb(es, f"f1sf{i}", [128, 512], F32) for i in range(2)]
        W = k.sb(es, "f1W", [128, 8, 3088], BF16)
        load_w(k, W, io["w_in_fox"], 8, 3088)
        scale_rows(k, es, W, io["norm_mix"][1, :], 8, "f1g")
        st = NormState(k, es, "f1")
        xts = [k.sb(es, f"f1x{i}", [128, D], F32) for i in range(2)]
        hTs = [k.sb(es, f"f1h{i}", [128, 8, 512], BF16) for i in range(2)]
        pfm = [k.ps(es, f"f1pf{i}", [128, 512]) for i in range(2)]
        ptm = [k.ps(es, f"f1pt{i}", [128, 512]) for i in range(2)]
        stb = [k.sb(es, f"f1sb{i}", [128, 512], BF16) for i in range(4)]
        ev = 0
        sbi = 0
        sfi = 0
        ti = 0
        gl = groups()
        tis = [0]

        def norm_group(gi):
            tok0, n = gl[gi]
            for t in range(n // 128):
                xt = xts[tis[0] % 2]
                k.dma("sp", xt[:, :], scr["x2"][tok0 + t * 128: tok0 + (t + 1) * 128, :], writes=[xt])
                rms_to_hT(k, c, xt, hTs[gi % 2], t * 128, st, tis[0])
                tis[0] += 1

        norm_group(0)
        for gi, (tok0, n) in enumerate(gl):
            hT = hTs[gi % 2]
            if gi + 1 < len(gl):
                norm_group(gi + 1)
            for (c0, dst, scale) in ([] if "fxnofm" in DBG else ((0, scr["qTf"], 0.125), (1024, scr["kTf"], None))):
                for mc in range(8):
                    pp = pfm[ev % 2]
                    for kc in range(8):
                        k.mm(pp[:, 0:n], W[:, kc, c0 + mc * 128: c0 + (mc + 1) * 128], hT[:, kc, 0:n],
                             kc == 0, kc == 7, reads=[W, hT], writes=[pp])
                    sbt = stb[sbi % 4]
                    sbi += 1
                    evac(k, ev, sbt[:, 0:n], pp[:, 0:n], [pp], [sbt], scale=scale)
                    ev += 1
                    k.dma("sp", dst[mc * 128:(mc + 1) * 128, tok0:tok0 + n], sbt[:, 0:n], reads=[sbt])
            pp = pfm[ev % 2]
            for kc in range(0 if "fxnog" in DBG else 8):
                k.mm(pp[0:16, 0:n], W[:, kc, 3072:3088], hT[:, kc, 0:n], kc == 0, kc == 7, reads=[W, hT], writes=[pp])
            if "fxnog" not in DBG:
                evac(k, 1, frow[0:16, tok0:tok0 + n], pp[0:16, 0:n], [pp], [frow])
            ev += 1
            for t in range(0 if "fxnotm" in DBG else n // 128):
                r0 = tok0 + t * 128
                for (c0, outn, hb, tobf) in ((2048, "fv", 0, True), (2560, "fv", 1, True),
                                             (1024, "fk", 0, False), (1536, "fk", 1, False)):
                    pp = ptm[ev % 2]
                    for kc in range(8):
                        k.mm(pp[:, :], hT[:, kc, t * 128:(t + 1) * 128], W[:, kc, c0:c0 + 512],
                             kc == 0, kc == 7, reads=[W, hT], writes=[pp])
                    ev += 1
                    sft = stf[sfi % 2]
                    sfi += 1
                    evac(k, 1, sft[:, :], pp[:, :], [pp], [sft])
                    if tobf:
                        sbt = stb[sbi % 4]
                        sbi += 1
                        k.op("act", lambda h, sbt=sbt, sft=sft: h.activation(out=sbt[:, :], in_=sft[:, :], func=AF.Copy),
                             reads=[sft], writes=[sbt])
                        k.dma("sp", scr["vfTok"][r0:r0 + 128, hb * 512:(hb + 1) * 512], sbt[:, :], reads=[sbt])
                    for h8 in range(0 if "fxnoout" in DBG else 8):
                        src = sft[:, h8 * 64:(h8 + 1) * 64]
                        if r0 < TP:
                            k.dma("sp", io["p" + outn][hb * 8 + h8, r0:r0 + 128, :], src, reads=[sft])
                        else:
                            for s2 in range(2):
                                sq = (r0 - TP) // TS + s2
                                k.dma("sp", io["s" + outn][sq, hb * 8 + h8, :, :], src[s2 * 64:(s2 + 1) * 64], reads=[sft])
    k.barrier()


def fox_prep(k, c, io, scr, frow, NFP, FSN, NFcP, NFcS):
    with contextlib.ExitStack() as es:
        negb = k.sb(es, "f2negb", [16, 1], F32)
        LF = k.sb(es, "f2LF", [16, R], F32)
        CL = [k.sb(es, f"f2CL{i}", [16, 2048], F32) for i in range(2)]
        FC = [k.sb(es, f"f2FC{i}", [16, 2048], F32) for i in range(2)]
        PCp = k.ps(es, "f2PCp", [128, 32, 16])
        PCs = [k.ps(es, f"f2PCs{i}", [128, 17, 16]) for i in range(2)]
        k.dma("sp", negb[:, :], io["b_fox_f"].rearrange("(h o) -> h o", o=1), writes=[negb])
        k.op("dve", lambda h: h.tensor_scalar(out=negb[:, :], in0=negb[:, :], scalar1=-1.0, scalar2=None, op0=ALU.mult),
             writes=[negb])
        k.op("act", lambda h: h.activation(out=frow[:, :], in_=frow[:, :], func=AF.Exp, scale=-1.0, bias=negb[:, :]),
             reads=[negb], writes=[frow])
        k.op("act", lambda h: h.activation(out=frow[:, :], in_=frow[:, :], func=AF.Ln, bias=1.0), writes=[frow])
        k.op("dve", lambda h: h.tensor_scalar(out=LF[:, :], in0=frow[:, :], scalar1=-1.0, scalar2=None, op0=ALU.mult),
             reads=[frow], writes=[LF])
        k.dma("sp", io["pflf"][:, :], LF[:, 0:TP], reads=[LF])
        for i in range(NS):
            k.dma("sp", io["sflf"][i, :, :], LF[:, TP + i * TS:TP + (i + 1) * TS], reads=[LF])
        k.op("dve", lambda h: h.tensor_tensor_scan(out=NFP[:, :], data0=frow[:, 0:TP], data1=frow[:, 0:TP], initial=0.0,
                                                   op0=ALU.add, op1=ALU.max), reads=[frow], writes=[NFP])
        for t in range(32):
            k.tr(PCp[:, t, :], NFP[0:16, t * 128:(t + 1) * 128], c.ident[0:16, 0:16], reads=[NFP, c.ident], writes=[PCp])
        k.op("dve", lambda h: h.tensor_copy(out=NFcP[:, :, :], in_=PCp[:, :, :]), reads=[PCp], writes=[NFcP])
        for i in range(NS):
            cl, fc, pcs = CL[i % 2], FC[i % 2], PCs[i % 2]
            s0 = TP + i * TS
            k.dma("sp", cl[:, :], io["cflf"][i, :, :], writes=[cl])
            k.op("dve", lambda h, cl=cl: h.tensor_scalar(out=cl[:, :], in0=cl[:, :], scalar1=-1.0, scalar2=None,
                                                        op0=ALU.mult), writes=[cl])
            k.op("dve", lambda h, cl=cl, fc=fc: h.tensor_tensor_scan(out=fc[:, :], data0=cl[:, :], data1=cl[:, :],
                                                                    initial=0.0, op0=ALU.add, op1=ALU.max),
                 reads=[cl], writes=[fc])
            k.op("dve", lambda h, fc=fc, i=i, s0=s0: h.tensor_tensor_scan(
                out=FSN[:, i, :], data0=frow[:, s0:s0 + TS], data1=frow[:, s0:s0 + TS], initial=fc[:, 2047:2048],
                op0=ALU.add, op1=ALU.max), reads=[fc, frow], writes=[FSN])
            for m in range(16):
                k.tr(pcs[:, m, :], fc[0:16, m * 128:(m + 1) * 128], c.ident[0:16, 0:16], reads=[fc, c.ident], writes=[pcs])
            k.tr(pcs[0:64, 16, :], FSN[0:16, i, :], c.ident[0:16, 0:16], reads=[FSN, c.ident], writes=[pcs])
            k.op("dve", lambda h, pcs=pcs, i=i: h.tensor_copy(out=NFcS[:, i, 0:16, :], in_=pcs[:, 0:16, :]),
                 reads=[pcs], writes=[NFcS])
            k.op("dve", lambda h, pcs=pcs, i=i: h.tensor_copy(out=NFcS[0:64, i, 16, :], in_=pcs[0:64, 16, :]),
                 reads=[pcs], par=[NFcS])
    k.barrier()


def fox_attn(k, c, io, scr, NFP, FSN, NFcP, NFcS):
    with contextlib.ExitStack() as es:
        kTh = [k.sb(es, f"f3k{i}", [128, R], BF16) for i in range(2)]
        qTh = [k.sb(es, f"f3q{i}", [128, R], BF16) for i in range(2)]
        for t_ in kTh + qTh:
            k.op("pool", lambda h, t_=t_: h.memset(t_[64:128, :], 0.0), writes=[t_])
        vfh = [k.sb(es, f"f3v{i}", [128, NT, 65], BF16) for i in range(2)]
        ATT = k.sb(es, "f3att", [64, R], BF16)
        FQ = [k.sb(es, f"f3fq{i}", [128, 512], F32) for i in range(2)]
        BD = [[k.sb(es, f"f3bd{b}_{i}", [128, 512], F32) for i in range(4)] for b in range(2)]
        TMP = [k.sb(es, f"f3tmp{i}", [128, 512], F32) for i in range(4)]
        P = [k.sb(es, f"f3P{i}", [128, 512], BF16) for i in range(4)]
        OS = k.sb(es, "f3OS", [65, 512], F32)
        RD = k.sb(es, "f3RD", [65, 512], F32)
        KC = [k.sb(es, f"f3KC{i}", [128, 16, 64], BF16) for i in range(2)]
        KTC = k.sb(es, "f3KTC", [128, 2048], BF16)
        k.op("pool", lambda h: h.memset(KTC[64:128, :], 0.0), writes=[KTC])
        VC = [k.sb(es, f"f3VC{i}", [128, 16, 65], BF16) for i in range(2)]
        VN = [k.sb(es, f"f3VN{i}", [64, 65], BF16) for i in range(2)]
        FQs = k.sb(es, "f3FQs", [128, TS], F32)
        BDs = k.sb(es, "f3BDs", [64, TS], F32)
        TMPs = k.sb(es, "f3TMPs", [128, 1088], F32)
        Ps = k.sb(es, "f3Ps", [128, 1088], BF16)
        PS_S = [k.ps(es, f"f3PS{i}", [128, 512]) for i in range(3)]
        PS_O = [k.ps(es, f"f3PO{i}", [128, 512]) for i in range(2)]
        PS_F = k.ps(es, "f3PF", [128, 512])
        PS_BC = k.ps(es, "f3PBC", [128, 512])
        PS_KT = k.ps(es, "f3PKT", [64, 1024], BF16)
        LN = k.sb(es, "f3LN", [65, 512], F32)
        RDr = [k.sb(es, f"f3RDr{i}", [65, 512], F32) for i in range(2)]
        RB = [k.sb(es, f"f3RB{i}", [64, 512], F32) for i in range(2)]
        rdbuf = [TT(None) for _ in range(2)]
        for i in range(2):
            k.op("pool", lambda h, i=i: h.memset(vfh[i][:, :, 64:65], 1.0), writes=[vfh[i]])
            k.op("pool", lambda h, i=i: h.memset(VC[i][:, :, 64:65], 1.0), writes=[VC[i]])
            k.op("pool", lambda h, i=i: h.memset(VN[i][:, 64:65], 1.0), writes=[VN[i]])
        vview = scr["vfTok"].rearrange("(t p) c -> p t c", p=128)

        def load(hh):
            j = hh % 2
            k.dma("sp", kTh[j][0:64, :], scr["kTf"][hh * 64:(hh + 1) * 64, :], writes=[kTh[j]])
            k.dma("sp", qTh[j][0:64, :], scr["qTf"][hh * 64:(hh + 1) * 64, :], writes=[qTh[j]])
            k.dma("sp", vfh[j][:, :, 0:64], vview[:, :, hh * 64:(hh + 1) * 64], writes=[vfh[j]])

        def prep(hh, g, b):
            fq = FQ[b]
            k.mm(PS_F[:, 0:512], c.sel[0:16, hh, :], NFP[0:16, g * 512:(g + 1) * 512], True, True,
                 reads=[c.sel, NFP], writes=[PS_F])
            k.op("dve", lambda h: h.tensor_copy(out=fq[:, :], in_=PS_F[:, 0:512]), reads=[PS_F], writes=[fq])
            for r in range(4):
                b0 = 384 - 128 * r
                k.op("pool", lambda h, r=r, b0=b0: h.tensor_tensor(out=BD[b][r][:, :], in0=fq[:, :],
                                                                 in1=c.cpos[:, b0:b0 + 512], op=ALU.add),
                     reads=[fq, c.cpos], writes=[BD[b][r]])

        def sload(hh, si, b):
            r0 = TP + si * TS
            k.dma("pool", KC[b][:, :, :], io["cfk"][si, hh, :, :].rearrange("(t p) d -> p t d", p=128), writes=[KC[b]])
            k.dma("pool", VC[b][:, :, 0:64], io["cfv"][si, hh, :, :].rearrange("(t p) d -> p t d", p=128), writes=[VC[b]])
            k.dma("sp", VN[b][:, 0:64], scr["vfTok"][r0:r0 + TS, hh * 64:(hh + 1) * 64], writes=[VN[b]])

        load(0)
        it = 0
        gi = 0
        sli = 0
        pend_epiA = []
        pend_epiB = []
        sload(0, 0, 0)
        for hh in range(16):
            j = hh % 2
            kT, qT, vf = kTh[j], qTh[j], vfh[j]
            if hh + 1 < 16:
                load(hh + 1)
            prep(hh, 0, gi % 2)
            pend = []
            for g in range(8):
                b = gi % 2
                fq, bd, pso = FQ[b], BD[b], PS_O[b]
                gi += 1
                if g + 1 < 8:
                    prep(hh, g + 1, gi % 2)
                last = 4 * g + 3
                for jj in range(last + 1):
                    ps, tmp, p = PS_S[it % 3], TMP[it % 4], P[it % 4]
                    it += 1
                    k.mm(ps[:, 0:512], kT[:, jj * 128:(jj + 1) * 128], qT[:, g * 512:(g + 1) * 512], True, True,
                         reads=[kT, qT], writes=[ps])
                    bsrc = bd[jj - 4 * g] if jj >= 4 * g else fq
                    k.op("dve", lambda h, ps=ps, tmp=tmp, bsrc=bsrc: h.tensor_tensor(out=tmp[:, :], in0=ps[:, 0:512],
                                                                                    in1=bsrc[:, :], op=ALU.subtract),
                         reads=[ps, bsrc], writes=[tmp], skip_self=True)
                    k.op("act", lambda h, tmp=tmp, p=p, jj=jj: h.activation(out=p[:, :], in_=tmp[:, :], func=AF.Exp,
                                                                           bias=NFcP[:, jj, hh:hh + 1]),
                         reads=[tmp, NFcP], writes=[p], skip_self=True)
                    pend.append(lambda jj=jj, p=p, pso=pso, last=last: k.mm(
                        pso[0:65, 0:512], vf[:, jj, :], p[:, :], jj == 0, jj == last, reads=[vf, p], writes=[pso]))
                    if len(pend) > 2:
                        pend.pop(0)()
                    if jj == 1 and pend_epiA:
                        pend_epiA.pop(0)()
                    if jj == last - 1 and pend_epiB:
                        pend_epiB.pop(0)()
                def epiA(pso=pso, b=b):
                    k.op("act", lambda h: h.activation(out=LN[64:65, :], in_=pso[64:65, 0:512], func=AF.Ln),
                         reads=[pso], writes=[LN])
                    k.op("act", lambda h: h.activation(out=RDr[b][64:65, :], in_=LN[64:65, :], func=AF.Exp, scale=-1.0),
                         reads=[LN], writes=[RDr[b]])
                    k.dma("sp", scr["rds"][b:b + 1, :], RDr[b][64:65, :], reads=[RDr[b]], writes=[rdbuf[b]])
                    k.dma("sp", RB[b][:, :], scr["rds"][b, :].partition_broadcast(64), reads=[rdbuf[b]], writes=[RB[b]])

                def epiB(pso=pso, b=b, g=g):
                    k.op("dve", lambda h: h.tensor_tensor(out=ATT[:, g * 512:(g + 1) * 512], in0=pso[0:64, 0:512],
                                                          in1=RB[b][:, :], op=ALU.mult),
                         reads=[pso, RB[b]], writes=[ATT])
                pend_epiA.append(epiA)
                pend_epiB.append(epiB)
            while pend:
                pend.pop(0)()
            while pend_epiA:
                pend_epiA.pop(0)()
            while pend_epiB:
                pend_epiB.pop(0)()
            for si in range(NS):
                r0 = TP + si * TS
                b = sli % 2
                sli += 1
                kc, vc, vn = KC[b], VC[b], VN[b]
                pso = PS_O[sli % 2]
                if si + 1 < NS:
                    sload(hh, si + 1, sli % 2)
                elif hh + 1 < 16:
                    sload(hh + 1, 0, sli % 2)
                for half in range(2):
                    for m in range(8):
                        k.tr(PS_KT[:, m * 128:(m + 1) * 128], kc[:, half * 8 + m, :], c.identb[:, :],
                             reads=[kc, c.identb], writes=[PS_KT])
                    k.op("act", lambda h, half=half: h.activation(out=KTC[0:64, half * 1024:(half + 1) * 1024],
                                                                in_=PS_KT[:, :], func=AF.Copy),
                         reads=[PS_KT], writes=[KTC])
                k.mm(PS_F[:, 0:TS], c.sel[0:16, hh, :], FSN[0:16, si, :], True, True, reads=[c.sel, FSN], writes=[PS_F])
                k.op("dve", lambda h: h.tensor_copy(out=FQs[:, :], in_=PS_F[:, 0:TS]), reads=[PS_F], writes=[FQs])
                k.op("pool", lambda h: h.tensor_tensor(out=BDs[:, :], in0=FQs[0:64, :], in1=c.cpos[0:64, 384:384 + TS],
                                                       op=ALU.add), reads=[FQs, c.cpos], writes=[BDs])
                for m in range(16):
                    ps = PS_S[m // 8]
                    k.mm(ps[:, (m % 8) * 64:(m % 8 + 1) * 64], KTC[:, m * 128:(m + 1) * 128], qT[:, r0:r0 + TS], True, True,
                         reads=[KTC, qT], writes=[ps])
                k.mm(PS_S[2][0:64, 0:TS], kT[:, r0:r0 + TS], qT[:, r0:r0 + TS], True, True, reads=[kT, qT], writes=[PS_S[2]])
                for m in range(16):
                    ps = PS_S[m // 8]
                    k.op("dve", lambda h, m=m, ps=ps: h.scalar_tensor_tensor(
                        out=TMPs[:, m * 64:(m + 1) * 64], in0=ps[:, (m % 8) * 64:(m % 8 + 1) * 64],
                        scalar=NFcS[:, si, m, hh:hh + 1], in1=FQs[:, :], op0=ALU.add, op1=ALU.subtract),
                        reads=[ps, NFcS, FQs], writes=[TMPs])
                k.op("dve", lambda h: h.scalar_tensor_tensor(
                    out=TMPs[0:64, 1024:1088], in0=PS_S[2][0:64, 0:TS], scalar=NFcS[0:64, si, 16, hh:hh + 1],
                    in1=BDs[:, :], op0=ALU.add, op1=ALU.subtract), reads=[PS_S[2], NFcS, BDs], writes=[TMPs])
                k.op("act", lambda h: h.activation(out=Ps[:, 0:1024], in_=TMPs[:, 0:1024], func=AF.Exp),
                     reads=[TMPs], writes=[Ps])
                k.op("act", lambda h: h.activation(out=Ps[0:64, 1024:1088], in_=TMPs[0:64, 1024:1088], func=AF.Exp),
                     reads=[TMPs], writes=[Ps])
                for m in range(16):
                    k.mm(pso[0:65, 0:TS], vc[:, m, :], Ps[:, m * 64:(m + 1) * 64], m == 0, False, reads=[vc, Ps], writes=[pso])
                k.mm(pso[0:65, 0:TS], vn[0:64, :], Ps[0:64, 1024:1088], False, True, reads=[vn, Ps], writes=[pso])
                attn_epilogue(k, c, pso, TS, OS, RD, PS_BC, ATT[:, r0:r0 + TS], ATT)
            k.dma("sp", scr["attT"][hh * 64:(hh + 1) * 64, :], ATT[:, :], reads=[ATT])
    k.barrier()
```

```python
import contextlib
import numpy as np
import concourse.bass as bass
import concourse.mybir as mybir
from concourse.bass_utils import run_bass_kernel_spmd

F32 = mybir.dt.float32
BF16 = mybir.dt.bfloat16
AF = mybir.ActivationFunctionType
ALU = mybir.AluOpType

D = 1024
TP = 4096
NS = 4
TS = 64
R = TP + NS * TS
NT = R // 128
HID = 2816
BIG = 30000.0


class Buf:
    __slots__ = ("w", "r")

    def __init__(self):
        self.w = []
        self.r = {}


class TT:
    def __init__(self, t):
        self.t = t
        self.b = Buf()

    def __getitem__(self, key):
        return self.t[key]


class Eng:
    def __init__(self, name, h, sem):
        self.name = name
        self.h = h
        self.sem = sem
        self.count = 0
        self.seen = {}


class DQ:
    def __init__(self, name, sems):
        self.sems = sems
        self.keys = [f"{name}{i}" for i in range(len(sems))]
        self.uses = [0] * len(sems)
        self.next = 0


class KB:
    def __init__(self, nc, es):
        self.nc = nc
        self.eng = {}
        for name, h in (("pe", nc.tensor), ("act", nc.scalar), ("dve", nc.vector),
                        ("pool", nc.gpsimd), ("sp", nc.sync)):
            self.eng[name] = Eng(name, h, es.enter_context(nc.semaphore("sem_" + name)))
        self.dq = {}
        for q in ("sp", "pool"):
            self.dq[q] = DQ("dq" + q, [es.enter_context(nc.semaphore(f"dq{q}{i}")) for i in range(8)])

    def sb(self, es, name, shape, dt):
        return TT(es.enter_context(self.nc.sbuf_tensor(name, shape, dt)))

    def ps(self, es, name, shape, dt=F32):
        return TT(es.enter_context(self.nc.psum_tensor(name, shape, dt)))

    def _wait(self, e, deps):
        best = {}
        for (key, sem, val) in deps:
            if key == "pe" and e.name == "pe":
                continue
            if e.seen.get(key, 0) >= val:
                continue
            if key not in best or best[key][1] < val:
                best[key] = (sem, val)
        for key, (sem, val) in best.items():
            e.h.wait_ge(sem, val)
            e.seen[key] = val

    def _deps(self, reads, writes, par=(), en=None, skip_war=()):
        deps = []
        for t in reads:
            deps.extend(t.b.w)
        for t in writes:
            deps.extend(d for d in t.b.w if d[0] != en)
            deps.extend(d for d in t.b.r.values() if d[0] != en and d[0] not in skip_war)
        for t in par:
            deps.extend(d for d in t.b.r.values() if d[0] != en)
        return deps

    def _mark(self, tok, reads, writes, par=()):
        for t in par:
            t.b.w.append(tok)
        for t in writes:
            t.b.w = [tok]
            t.b.r = {}
        for t in reads:
            if t in writes:
                continue
            old = t.b.r.get(tok[0])
            if old is None or old[2] < tok[2]:
                t.b.r[tok[0]] = tok

    def op(self, en, fn, reads=(), writes=(), par=(), skip_self=False, skip_war=()):
        e = self.eng[en]
        self._wait(e, self._deps(reads, writes, par, en if skip_self else None, skip_war))
        inst = fn(e.h)
        e.count += 1
        inst.then_inc(e.sem, 1)
        tok = (en, e.sem, e.count)
        self._mark(tok, reads, writes, par)
        return tok

    def dma(self, qn, out, in_, reads=(), writes=(), par=(), **kw):
        e = self.eng[qn]
        q = self.dq[qn]
        self._wait(e, self._deps(reads, writes, par))
        s = q.next % len(q.sems)
        q.next += 1
        if q.uses[s] > 0 and e.seen.get(q.keys[s], 0) < 16 * q.uses[s]:
            e.h.wait_ge(q.sems[s], 16 * q.uses[s])
            e.seen[q.keys[s]] = 16 * q.uses[s]
        inst = e.h.dma_start(out=out, in_=in_, **kw)
        q.uses[s] += 1
        inst.then_inc(q.sems[s], 16)
        tok = (q.keys[s], q.sems[s], 16 * q.uses[s])
        self._mark(tok, reads, writes, par)
        return tok

    def barrier(self):
        toks = [(n, e.sem, e.count) for n, e in self.eng.items() if e.count > 0]
        for q in self.dq.values():
            for i in range(len(q.sems)):
                if q.uses[i] > 0:
                    toks.append((q.keys[i], q.sems[i], 16 * q.uses[i]))
        for e in self.eng.values():
            self._wait(e, [t for t in toks if t[0] != e.name])

    def mm(self, out, lhsT, rhs, start, stop, reads, writes):
        return self.op("pe", lambda h: h.matmul(out, lhsT=lhsT, rhs=rhs, start=start, stop=stop),
                       reads=reads, writes=writes)

    def tr(self, out, in_, ident, reads, writes):
        return self.op("pe", lambda h: h.transpose(out=out, in_=in_, identity=ident),
                       reads=reads, writes=writes)


def load_w(k, dst, w_ap, kc_n, ncols, blk=2048):
    nb = -(-ncols // blk)
    blk = -(-ncols // nb)
    for kc in range(kc_n):
        for c0 in range(0, ncols, blk):
            c1 = min(ncols, c0 + blk)
            k.dma("pool", dst[:, kc, c0:c1], w_ap[kc * 128:(kc + 1) * 128, c0:c1], par=[dst])


def scale_rows(k, es, w, g_ap, kc_n, name, gcol=None):
    if gcol is None:
        gcol = k.sb(es, name, [128, kc_n], F32)
    k.dma("sp", gcol[:, :], g_ap.rearrange("(k p) -> p k", p=128), writes=[gcol],
          allow_slow_non_contiguous=True)
    for kc in range(kc_n):
        eng = "dve" if kc % 2 == 0 else "pool"
        k.op(eng, lambda h, kc=kc: h.tensor_scalar(out=w[:, kc, :], in0=w[:, kc, :],
                                                    scalar1=gcol[:, kc:kc + 1], scalar2=None,
                                                    op0=ALU.mult),
             reads=[gcol], writes=[w])


class Consts:
    pass


def make_consts(k, es):
    c = Consts()
    c.ones = k.sb(es, "c_ones", [128, 128], F32)
    c.ident = k.sb(es, "c_ident", [128, 128], F32)
    c.identb = k.sb(es, "c_identb", [128, 128], BF16)
    c.caus = k.sb(es, "c_caus", [128, 128], F32)
    c.cpos = k.sb(es, "c_cpos", [128, 896], F32)
    c.big = k.sb(es, "c_big", [128, 896], F32)
    c.sel = k.sb(es, "c_sel", [16, 16, 128], F32)
    k.op("pool", lambda h: h.memset(c.ones[:, :], 1.0), writes=[c.ones])
    k.op("pool", lambda h: h.memset(c.big[:, :], BIG), writes=[c.big])
    k.op("pool", lambda h: h.affine_select(out=c.ident[:, :], in_=c.ones[:, :], pattern=[[-1, 128]],
                                           compare_op=ALU.is_equal, fill=0.0, base=0,
                                           channel_multiplier=1),
         reads=[c.ones], writes=[c.ident])
    k.op("pool", lambda h: h.affine_select(out=c.caus[:, :], in_=c.ones[:, :], pattern=[[1, 128]],
                                           compare_op=ALU.is_ge, fill=0.0, base=0,
                                           channel_multiplier=-1),
         reads=[c.ones], writes=[c.caus])
    k.op("pool", lambda h: h.affine_select(out=c.cpos[:, :], in_=c.big[:, :], pattern=[[-1, 896]],
                                           compare_op=ALU.is_gt, fill=0.0, base=384,
                                           channel_multiplier=1),
         reads=[c.big], writes=[c.cpos])
    k.op("pool", lambda h: h.tensor_copy(out=c.identb[:, :], in_=c.ident[:, :]),
         reads=[c.ident], writes=[c.identb])
    for hh in range(16):
        k.op("dve", lambda h, hh=hh: h.tensor_copy(out=c.sel[:, hh, :],
                                                    in_=c.ident[0:16, hh:hh + 1].to_broadcast([16, 128])),
             reads=[c.ident], writes=[c.sel])
    return c


def rms_to_hT(k, c, x_t, hT, col0, st, ti):
    j = ti % 2
    k.op("act", lambda h: h.activation(out=st.junk[:, :], in_=x_t[:, :], func=AF.Square,
                                       accum_out=st.ss[j][:, :]),
         reads=[x_t], writes=[st.junk, st.ss[j]])
    k.op("act", lambda h: h.activation(out=st.sq[j][:, :], in_=st.ss[j][:, :], func=AF.Sqrt,
                                       scale=1.0 / D, bias=1e-6),
         reads=[st.ss[j]], writes=[st.sq[j]])
    k.op("dve", lambda h: h.reciprocal(out=st.rstd[j][:, :], in_=st.sq[j][:, :]),
         reads=[st.sq[j]], writes=[st.rstd[j]])
    k.op("act", lambda h: h.activation(out=st.xn[j][:, :], in_=x_t[:, :], func=AF.Copy,
                                       scale=st.rstd[j][:, :]),
         reads=[x_t, st.rstd[j]], writes=[st.xn[j]])
    for kc in range(8):
        k.tr(st.pT[j][:, kc * 128:(kc + 1) * 128], st.xn[j][:, kc * 128:(kc + 1) * 128],
             c.identb[:, :], reads=[st.xn[j], c.identb], writes=[st.pT[j]])
    k.op("dve", lambda h: h.tensor_copy(out=hT[:, :, col0:col0 + 128],
                                        in_=st.pT[j][:, :].rearrange("p (k n) -> p k n", k=8)),
         reads=[st.pT[j]], writes=[hT])


class NormState:
    def __init__(self, k, es, tag):
        self.junk = k.sb(es, tag + "junk", [128, D], BF16)
        self.ss = [k.sb(es, f"{tag}ss{i}", [128, 1], F32) for i in range(2)]
        self.sq = [k.sb(es, f"{tag}sq{i}", [128, 1], F32) for i in range(2)]
        self.rstd = [k.sb(es, f"{tag}rstd{i}", [128, 1], F32) for i in range(2)]
        self.xn = [k.sb(es, f"{tag}xn{i}", [128, D], BF16) for i in range(2)]
        self.pT = [k.ps(es, f"{tag}pT{i}", [128, D], BF16) for i in range(2)]


def evac(k, i, out, in_, reads, writes, scale=None, func=None):
    if func is not None or (i % 2 == 0):
        f = func if func is not None else AF.Copy
        sc = 1.0 if scale is None else scale
        return k.op("act", lambda h: h.activation(out=out, in_=in_, func=f, scale=sc),
                    reads=reads, writes=writes)
    if scale is None:
        scale = 1.0
    return k.op("dve", lambda h: h.tensor_scalar(out=out, in0=in_, scalar1=scale, scalar2=None,
                                                  op0=ALU.mult), reads=reads, writes=writes)


def groups():
    gs = [(g * 512, 512) for g in range(8)]
    gs.append((TP, NS * TS))
    return gs


def phase1(k, c, io, scr, rows):
    with contextlib.ExitStack() as es:
        stf = [k.sb(es, f"p1sf{i}", [128, 512], F32) for i in range(2)]
        W = k.sb(es, "p1W", [128, 8, 3592], BF16)
        load_w(k, W, io["w_in_ab"], 8, 3592)
        scale_rows(k, es, W, io["norm_mix"][0, :], 8, "p1g")
        st = NormState(k, es, "p1")
        xts = [k.sb(es, f"p1x{i}", [128, D], F32) for i in range(2)]
        hTs = [k.sb(es, f"p1h{i}", [128, 8, 512], BF16) for i in range(2)]
        pfm = [k.ps(es, f"p1pf{i}", [128, 512]) for i in range(2)]
        ptm = [k.ps(es, f"p1pt{i}", [128, 512]) for i in range(2)]
        stb = [k.sb(es, f"p1sb{i}", [128, 512], BF16) for i in range(4)]
        fm = [(0, scr["qTm"], None), (512, scr["kTm"], 128.0 ** -0.5),
              (2056, scr["qTb"], 0.125), (2568, scr["kTb"], None)]
        ev = 0
        sbi = 0
        sfi = 0
        ti = 0
        gl = groups()
        tis = [0]

        def norm_group(gi):
            tok0, n = gl[gi]
            for t in range(n // 128):
                xt = xts[tis[0] % 2]
                k.dma("sp", xt[:, :], io["xin"][tok0 + t * 128: tok0 + (t + 1) * 128, :], writes=[xt])
                rms_to_hT(k, c, xt, hTs[gi % 2], t * 128, st, tis[0])
                tis[0] += 1

        norm_group(0)
        for gi, (tok0, n) in enumerate(gl):
            hT = hTs[gi % 2]
            if gi + 1 < len(gl):
                norm_group(gi + 1)
            for (c0, dst, scale) in ([] if 'nofm' in DBG else fm):
                for mc in range(4):
                    pp = pfm[ev % 2]
                    for kc in range(8):
                        k.mm(pp[:, 0:n], W[:, kc, c0 + mc * 128: c0 + (mc + 1) * 128], hT[:, kc, 0:n],
                             kc == 0, kc == 7, reads=[W, hT], writes=[pp])
                    sbt = stb[sbi % 4]
                    sbi += 1
                    evac(k, ev, sbt[:, 0:n], pp[:, 0:n], [pp], [sbt], scale=scale)
                    ev += 1
                    k.dma("sp", dst[mc * 128:(mc + 1) * 128, tok0:tok0 + n], sbt[:, 0:n], reads=[sbt])
            for (c0, row) in ([] if 'nogate' in DBG else ((2048, rows["ig"]), (2052, rows["fg"]))):
                pp = pfm[ev % 2]
                for kc in range(8):
                    k.mm(pp[0:4, 0:n], W[:, kc, c0:c0 + 4], hT[:, kc, 0:n], kc == 0, kc == 7,
                         reads=[W, hT], writes=[pp])
                evac(k, 1, row[0:4, tok0:tok0 + n], pp[0:4, 0:n], [pp], [row])
                ev += 1
            for t in range(0 if 'notm' in DBG else n // 128):
                r0 = tok0 + t * 128
                is_out = (r0 >= TP - 512)
                blocks = [(512, scr["kmTok"], 128.0 ** -0.5, None, None),
                          (1024, scr["vmTok"], None, None, None),
                          (1536, scr["osTok"], None, (None if 'nosig' in DBG else AF.Sigmoid), None),
                          (3080, scr["vbTok"], None, None, "bv")]
                if is_out:
                    blocks.append((2568, None, None, None, "bk"))
                for (c0, dst, scale, func, outn) in blocks:
                    pp = ptm[ev % 2]
                    for kc in range(8):
                        k.mm(pp[:, :], hT[:, kc, t * 128:(t + 1) * 128], W[:, kc, c0:c0 + 512],
                             kc == 0, kc == 7, reads=[W, hT], writes=[pp])
                    both = (outn is not None and is_out)
                    if both:
                        sft = stf[sfi % 2]
                        sfi += 1
                        evac(k, 1, sft[:, :], pp[:, :], [pp], [sft])
                        ev += 1
                    if dst is not None:
                        sbt = stb[sbi % 4]
                        sbi += 1
                        if both:
                            k.op("act", lambda h, sbt=sbt, sft=sft: h.activation(out=sbt[:, :], in_=sft[:, :], func=AF.Copy),
                                 reads=[sft], writes=[sbt])
                        else:
                            evac(k, ev, sbt[:, :], pp[:, :], [pp], [sbt], scale=scale, func=func)
                        ev += 1
                        k.dma("sp", dst[r0:r0 + 128, :], sbt[:, :], reads=[sbt])
                    if both:
                        for hh in range(0 if ('noodma2' in DBG or 'whole' in DBG) else 8):
                            src = sft[:, hh * 64:(hh + 1) * 64]
                            if r0 < TP:
                                q0 = r0 - (TP - 512)
                                if 'toscr' in DBG:
                                    k.dma("sp", scr["x1"][q0:q0 + 128, hh * 64:(hh + 1) * 64], src, reads=[sft])
                                else:
                                    k.dma("pool" if 'opool' in DBG else "sp", io["p" + outn][hh, q0:q0 + 128, :], src, reads=[sft])
                            else:
                                for s2 in range(2):
                                    sq = (r0 - TP) // TS + s2
                                    k.dma("sp", io["s" + outn][sq, hh, :, :], src[s2 * 64:(s2 + 1) * 64],
                                          reads=[sft])
    k.barrier()


IN_SPECS = [
    ("xin", [R, D]), ("mC0", [NS, 4, 128, 128]), ("mn0", [NS, 4, 128]), ("mm0", [NS, 4]),
    ("cbk", [NS, 8, 512, 64]), ("cbv", [NS, 8, 512, 64]),
    ("cfk", [NS, 16, 2048, 64]), ("cfv", [NS, 16, 2048, 64]), ("cflf", [NS, 16, 2048]),
    ("norm_mix", [2, D]), ("norm_ffn", [2, D]), ("norm_final", [D]),
    ("w_in_ab", [D, 3592]), ("b_gate_ab", [8]), ("mlstm_gain", [512]), ("bandbias", [8, 128, 640]),
    ("bandmask", [128, 640]),
    ("w_out_ab", [D, D]), ("w_in_fox", [D, 3088]), ("b_fox_f", [16]), ("w_out_fox", [D, D]),
    ("w_ffn_in", [2, D, 2 * HID]), ("w_ffn_out", [2, HID, D]),
]
OUT_SPECS = [
    ("y", [R, D]), ("pC", [4, 128, 128]), ("pn", [4, 128]), ("pm", [4, 1]),
    ("pbk", [8, 512, 64]), ("pbv", [8, 512, 64]),
    ("pfk", [16, TP, 64]), ("pfv", [16, TP, 64]), ("pflf", [16, TP]),
    ("sC", [NS, 4, 128, 128]), ("sn", [NS, 4, 128]), ("sm", [NS, 4, 1]),
    ("sbk", [NS, 8, TS, 64]), ("sbv", [NS, 8, TS, 64]),
    ("sfk", [NS, 16, TS, 64]), ("sfv", [NS, 16, TS, 64]), ("sflf", [NS, 16, TS]),
]
SCR_SPECS = [
    ("qTm", [512, R], BF16), ("kTm", [512, R], BF16), ("qTb", [512, R], BF16), ("kTb", [512, R], BF16),
    ("kmTok", [R, 512], BF16), ("vmTok", [R, 512], BF16), ("osTok", [R, 512], BF16),
    ("vbTok", [R, 512], BF16), ("catT", [D, R], BF16),
    ("x1", [R, D], F32), ("x2", [R, D], F32), ("x3", [R, D], F32),
    ("qTf", [D, R], BF16), ("kTf", [D, R], BF16), ("vfTok", [R, D], BF16), ("attT", [D, R], BF16),
    ("hidT", [HID, R], BF16), ("rds", [4, 512], F32),
]

PHASES = 99
DBG = ""


def run_phases(k, c, io, scr, upto=None):
    upto = PHASES if upto is None else upto
    with contextlib.ExitStack() as es1:
        rows = {"ig": k.sb(es1, "row_ig", [4, R], F32), "fg": k.sb(es1, "row_fg", [4, R], F32)}
        pers = {"GW": k.sb(es1, "GW", [4, 2, R], F32), "COLS": k.sb(es1, "COLS", [128, NCH, 3, 4], F32),
                "DECB": k.sb(es1, "DECB", [128, 4, NCH], F32)}
        phase1(k, c, io, scr, rows)
        if upto >= 2:
            phase2(k, c, io, scr, rows, pers)
        if upto >= 3:
            phase3(k, c, io, scr, pers)
    if upto >= 4:
        phase4(k, c, io, scr)
    if upto >= 6:
        with contextlib.ExitStack() as esw:
            Wg = ffn_w_alloc(k, esw, "p6")
            outproj_phase(k, c, io, scr, scr["catT"], io["w_out_ab"], io["xin"], scr["x1"], "p5",
                          prefetch=lambda: ffn_w_load(k, io, 0, Wg))
            ffn_in_phase(k, c, io, scr, 0, scr["x1"], "p6", W=Wg[0])
        ffn_out_phase(k, c, io, scr, 0, scr["x1"], scr["x2"], False, "p7")
    if upto >= 8:
        fox_phases(k, c, io, scr)
    if upto >= 9:
        with contextlib.ExitStack() as esw:
            Wg = ffn_w_alloc(k, esw, "pa")
            outproj_phase(k, c, io, scr, scr["attT"], io["w_out_fox"], scr["x2"], scr["x3"], "p9",
                          prefetch=lambda: ffn_w_load(k, io, 1, Wg))
            ffn_in_phase(k, c, io, scr, 1, scr["x3"], "pa", W=Wg[0])
        ffn_out_phase(k, c, io, scr, 1, scr["x3"], io["y"], True, "pb")


def build():
    nc = bass.Bass("TRN2", target_bir_lowering=False)
    io = {}
    for name, shape in IN_SPECS:
        io[name] = nc.dram_tensor(name, shape, F32, kind="ExternalInput").ap()
    for name, shape in OUT_SPECS:
        io[name] = nc.dram_tensor(name, shape, F32, kind="ExternalOutput").ap()
    scr = {}
    for name, shape, dt in SCR_SPECS:
        scr[name] = nc.dram_tensor("scr_" + name, shape, dt).ap()
    with contextlib.ExitStack() as es:
        k = KB(nc, es)
        c = make_consts(k, es)
        run_phases(k, c, io, scr)
        k.barrier()
    return nc


_NC_CACHE = {}


def kernel(**inp):
    f = lambda a: np.ascontiguousarray(np.asarray(a, dtype=np.float32))
    xp, xs = f(inp["x_prompt"]), f(inp["x_sample"])
    kk = np.arange(128)[:, None]
    qq = np.arange(640)[None, :]
    rel = qq - kk
    idx = np.clip(rel, -63, 128) + 63
    table = f(inp["rel_bias_table"])[0]
    bandbias = np.ascontiguousarray(table[:, idx])
    dch = (qq // 64) - (kk // 64)
    bandmask = np.where((dch >= 0) & (dch <= 8), 0.0, -BIG).astype(np.float32)
    common = {
        "norm_mix": f(inp["norm_mix"]), "norm_ffn": f(inp["norm_ffn"]), "norm_final": f(inp["norm_final"]),
        "w_in_ab": f(inp["w_in_ab"])[0], "b_gate_ab": f(inp["b_gate_ab"])[0],
        "mlstm_gain": f(inp["mlstm_gain"])[0], "bandbias": bandbias, "bandmask": bandmask,
        "w_out_ab": f(inp["w_out_ab"])[0], "w_in_fox": f(inp["w_in_fox"])[0],
        "b_fox_f": f(inp["b_fox_f"])[0], "w_out_fox": f(inp["w_out_fox"])[0],
        "w_ffn_in": f(inp["w_ffn_in"]), "w_ffn_out": f(inp["w_ffn_out"]),
    }
    in_maps = []
    for cid in range(8):
        b = cid % 4
        s0 = 4 * cid
        m = dict(common)
        m["xin"] = np.ascontiguousarray(np.concatenate([xp[b], xs[s0:s0 + 4].reshape(NS * TS, D)], axis=0))
        m["mC0"] = f(inp["state_mlstm_C"])[0, s0:s0 + 4]
        m["mn0"] = f(inp["state_mlstm_n"])[0, s0:s0 + 4]
        m["mm0"] = f(inp["state_mlstm_m"])[0, s0:s0 + 4]
        m["cbk"] = f(inp["cache_band_k"])[0, s0:s0 + 4]
        m["cbv"] = f(inp["cache_band_v"])[0, s0:s0 + 4]
        m["cfk"] = f(inp["cache_fox_k"])[0, s0:s0 + 4]
        m["cfv"] = f(inp["cache_fox_v"])[0, s0:s0 + 4]
        m["cflf"] = f(inp["cache_fox_logf"])[0, s0:s0 + 4]
        in_maps.append({kk_: np.ascontiguousarray(v) for kk_, v in m.items()})
    if "nc" not in _NC_CACHE:
        _NC_CACHE["nc"] = build()
    res = run_bass_kernel_spmd(_NC_CACHE["nc"], in_maps, core_ids=list(range(8)))
    rs = res.results
    P = lambda n: np.stack([rs[b][n] for b in range(4)], axis=0)
    S = lambda n: np.concatenate([rs[cid][n] for cid in range(8)], axis=0)
    y_prompt = np.stack([rs[b]["y"][:TP] for b in range(4)], axis=0)
    y_sample = np.concatenate([rs[cid]["y"][TP:].reshape(NS, TS, D) for cid in range(8)], axis=0)
    outs = (
        y_prompt, y_sample,
        P("pC")[None], P("pn")[None], P("pm")[None, :, :, 0],
        P("pbk")[None], P("pbv")[None], P("pfk")[None], P("pfv")[None], P("pflf")[None],
        S("sC")[None], S("sn")[None], S("sm")[None, :, :, 0],
        S("sbk")[None], S("sbv")[None], S("sfk")[None], S("sfv")[None], S("sflf")[None],
    )
    return tuple(np.ascontiguousarray(o, dtype=np.float32) for o in outs)


def chunks():
    cs = [(c * 128, 128) for c in range(32)]
    cs += [(TP + i * TS, TS) for i in range(NS)]
    return cs


NCH = 36


def phase2(k, c, io, scr, rows, pers):
    GW, COLS, DECB = pers["GW"], pers["COLS"], pers["DECB"]
    ig, fg = rows["ig"], rows["fg"]
    with contextlib.ExitStack() as es:
        negb = k.sb(es, "p2negb", [4, 1], F32)
        bigc = k.sb(es, "p2big", [4, 1], F32)
        m0 = k.sb(es, "p2m0", [4, NS], F32)
        NF = k.sb(es, "p2NF", [4, R], F32)
        WS = k.sb(es, "p2WS", [4, R], F32)
        EM = k.sb(es, "p2EM", [4, R], F32)
        DEC = k.sb(es, "p2DEC", [4, NCH], F32)
        PC = k.ps(es, "p2PC", [128, NCH, 3, 4])
        PD = k.ps(es, "p2PD", [128, 4, NCH])
        bg = io["b_gate_ab"]
        k.dma("sp", negb[:, :], bg[4:8].rearrange("(h o) -> h o", o=1), writes=[negb])
        k.dma("sp", bigc[:, :], bg[0:4].rearrange("(h o) -> h o", o=1), writes=[bigc])
        k.dma("sp", m0[:, :], io["mm0"].rearrange("s h -> h s"), writes=[m0], allow_slow_non_contiguous=True)
        k.op("dve", lambda h: h.tensor_scalar(out=negb[:, :], in0=negb[:, :], scalar1=-1.0, scalar2=None,
                                              op0=ALU.mult), writes=[negb])
        k.op("act", lambda h: h.activation(out=fg[:, :], in_=fg[:, :], func=AF.Exp, scale=-1.0,
                                           bias=negb[:, :]), reads=[negb], writes=[fg])
        k.op("act", lambda h: h.activation(out=fg[:, :], in_=fg[:, :], func=AF.Ln, bias=1.0), writes=[fg])
        segs = [(0, TP, None)] + [(TP + i * TS, TS, i) for i in range(NS)]
        for (s0, sn, si) in segs:
            k.op("dve", lambda h, s0=s0, sn=sn: h.tensor_tensor_scan(
                out=NF[:, s0:s0 + sn], data0=fg[:, s0:s0 + sn], data1=fg[:, s0:s0 + sn], initial=0.0,
                op0=ALU.add, op1=ALU.max), reads=[fg], writes=[NF])
        k.op("dve", lambda h: h.scalar_tensor_tensor(out=ig[:, :], in0=ig[:, :], scalar=bigc[:, :],
                                                     in1=NF[:, :], op0=ALU.add, op1=ALU.add),
             reads=[bigc, NF], writes=[ig])
        for (s0, sn, si) in segs:
            init = 0.0 if si is None else m0[:, si:si + 1]
            k.op("dve", lambda h, s0=s0, sn=sn, init=init: h.tensor_tensor_scan(
                out=GW[:, 0, s0:s0 + sn], data0=ig[:, s0:s0 + sn], data1=ig[:, s0:s0 + sn], initial=init,
                op0=ALU.max, op1=ALU.max), reads=[ig, m0], writes=[GW])
        Gp = GW[:, 0, 0:TP].rearrange("p (c t) -> p c t", t=128)
        k.op("pool", lambda h: h.memset(fg[:, 0:128], 0.0), writes=[fg])
        k.op("dve", lambda h: h.tensor_copy(
            out=fg[:, 128:TP].rearrange("p (c t) -> p c t", t=128),
            in_=Gp[:, 0:31, 127:128].to_broadcast([4, 31, 128])), reads=[GW], writes=[fg])
        for i in range(NS):
            s0 = TP + i * TS
            k.op("dve", lambda h, s0=s0, i=i: h.tensor_copy(out=fg[:, s0:s0 + TS],
                                                          in_=m0[:, i:i + 1].to_broadcast([4, TS])),
                 reads=[m0], writes=[fg])
        k.op("dve", lambda h: h.tensor_tensor(out=WS[:, :], in0=fg[:, :], in1=GW[:, 0, :], op=ALU.subtract),
             reads=[fg, GW], writes=[WS])
        k.op("act", lambda h: h.activation(out=GW[:, 1, :], in_=WS[:, :], func=AF.Exp), reads=[WS], writes=[GW])
        k.op("dve", lambda h: h.tensor_copy(
            out=WS[:, 0:TP].rearrange("p (c t) -> p c t", t=128),
            in_=Gp[:, :, 127:128].to_broadcast([4, 32, 128])), reads=[GW], writes=[WS])
        for i in range(NS):
            s0 = TP + i * TS
            k.op("dve", lambda h, s0=s0: h.tensor_copy(
                out=WS[:, s0:s0 + TS], in_=GW[:, 0, s0 + TS - 1:s0 + TS].to_broadcast([4, TS])),
                reads=[GW], writes=[WS])
        k.op("dve", lambda h: h.tensor_tensor(
            out=DEC[:, 0:32], in0=fg[:, 0:TP].rearrange("p (c t) -> p c t", t=128)[:, :, 0],
            in1=WS[:, 0:TP].rearrange("p (c t) -> p c t", t=128)[:, :, 0], op=ALU.subtract),
            reads=[fg, WS], writes=[DEC])
        for i in range(NS):
            s0 = TP + i * TS
            k.op("dve", lambda h, s0=s0, i=i: h.tensor_tensor(out=DEC[:, 32 + i:33 + i], in0=fg[:, s0:s0 + 1],
                                                            in1=WS[:, s0:s0 + 1], op=ALU.subtract),
                 reads=[fg, WS], writes=[DEC])
        k.op("act", lambda h: h.activation(out=DEC[:, :], in_=DEC[:, :], func=AF.Exp), writes=[DEC])
        k.op("dve", lambda h: h.tensor_tensor(out=WS[:, :], in0=ig[:, :], in1=WS[:, :], op=ALU.subtract),
             reads=[ig], writes=[WS])
        k.op("act", lambda h: h.activation(out=WS[:, :], in_=WS[:, :], func=AF.Exp), writes=[WS])
        k.op("dve", lambda h: h.tensor_tensor(out=EM[:, :], in0=GW[:, 0, :], in1=NF[:, :], op=ALU.subtract),
             reads=[GW, NF], writes=[EM])
        k.dma("sp", io["pm"][:, :], EM[:, TP - 1:TP], reads=[EM])
        for i in range(NS):
            e1 = TP + (i + 1) * TS
            k.dma("sp", io["sm"][i, :, :], EM[:, e1 - 1:e1], reads=[EM])
        k.op("act", lambda h: h.activation(out=EM[:, :], in_=EM[:, :], func=AF.Exp, scale=-1.0), writes=[EM])
        for ci, (r0, n) in enumerate(chunks()):
            for xi, X in enumerate((ig, WS, EM)):
                k.tr(PC[0:n, ci, xi, :], X[0:4, r0:r0 + n], c.ident[0:4, 0:4], reads=[X, c.ident], writes=[PC])
        k.op("dve", lambda h: h.tensor_copy(out=COLS[:, 0:32, :, :], in_=PC[:, 0:32, :, :]), reads=[PC], writes=[COLS])
        k.op("dve", lambda h: h.tensor_copy(out=COLS[0:64, 32:NCH, :, :], in_=PC[0:64, 32:NCH, :, :]), reads=[PC], par=[COLS])
        for hh in range(4):
            k.mm(PD[:, hh, :], c.sel[0:4, hh, :], DEC[0:4, :], True, True, reads=[c.sel, DEC], writes=[PD])
        k.op("dve", lambda h: h.tensor_copy(out=DECB[:, :, :], in_=PD[:, :, :]), reads=[PD], writes=[DECB])
    k.barrier()


def phase3(k, c, io, scr, pers):
    GW, COLS, DECB = pers["GW"], pers["COLS"], pers["DECB"]
    with contextlib.ExitStack() as es:
        gain = k.sb(es, "p3gain", [128, 512], F32)
        k.dma("sp", gain[:, :], io["mlstm_gain"].partition_broadcast(128), writes=[gain])
        qTs = [k.sb(es, f"p3q{i}", [128, 4, 128], BF16) for i in range(2)]
        kTs = [k.sb(es, f"p3k{i}", [128, 4, 128], BF16) for i in range(2)]
        kts = [k.sb(es, f"p3kt{i}", [128, 4, 128], BF16) for i in range(2)]
        vas = [k.sb(es, f"p3va{i}", [128, 4, 129], BF16) for i in range(2)]
        oss = [k.sb(es, f"p3os{i}", [128, 512], BF16) for i in range(3)]
        for i in range(2):
            k.op("pool", lambda h, i=i: h.memset(vas[i][:, :, 128:129], 1.0), writes=[vas[i]])
        Cf = [k.sb(es, f"p3Cf{h}", [128, 129], F32) for h in range(4)]
        Cb = [k.sb(es, f"p3Cb{h}", [128, 129], BF16) for h in range(4)]
        E = k.sb(es, "p3E", [128, 4, 128], F32)
        WT = k.sb(es, "p3WT", [128, 4, 128], BF16)
        QS = k.sb(es, "p3QS", [128, 4, 128], BF16)
        KW = k.sb(es, "p3KW", [128, 4, 128], BF16)
        ONs = [k.sb(es, f"p3ON{i}", [128, 4, 129], F32) for i in range(2)]
        H = k.sb(es, "p3H", [128, 4, 128], F32)
        SQ = k.sb(es, "p3SQ", [128, 4, 128], F32)
        OAs = [k.sb(es, f"p3OA{i}", [128, 512], BF16) for i in range(2)]
        G2s = [k.sb(es, f"p3G2{i}", [128, 512], F32) for i in range(2)]
        junk = k.sb(es, "p3junk", [128, 128], F32)
        sm2 = [k.sb(es, f"p3t{i}", [128, 4], F32) for i in range(4)]
        OT = k.sb(es, "p3OT", [128, 4, 128], BF16)
        sm = [k.sb(es, f"p3s{i}", [128, 4], F32) for i in range(6)]
        PS_S = k.ps(es, "p3PS", [128, 4, 128])
        PS_G = k.ps(es, "p3PG", [128, 4, 128])
        PS_W = k.ps(es, "p3PW", [128, 4, 128])
        PS_O = [k.ps(es, f"p3PO{i}", [128, 2, 129]) for i in range(2)]
        PS_D = [k.ps(es, f"p3PD{i}", [128, 2, 129]) for i in range(2)]
        PS_T = k.ps(es, "p3PT", [128, 4, 128], BF16)
        qv = scr["qTm"].rearrange("(h d) r -> d h r", h=4)
        kv = scr["kTm"].rearrange("(h d) r -> d h r", h=4)
        catv = scr["catT"][0:512, :].rearrange("(h d) r -> d h r", h=4)
        chs = chunks()

        def load(ci):
            r0, n = chs[ci]
            j = ci % 2
            k.dma("sp", qTs[j][:, :, 0:n], qv[:, :, r0:r0 + n], writes=[qTs[j]])
            k.dma("sp", kTs[j][:, :, 0:n], kv[:, :, r0:r0 + n], writes=[kTs[j]])
            k.dma("sp", kts[j][0:n, :, :], scr["kmTok"][r0:r0 + n, :].rearrange("t (h e) -> t h e", h=4),
                  writes=[kts[j]])
            k.dma("sp", vas[j][0:n, :, 0:128], scr["vmTok"][r0:r0 + n, :].rearrange("t (h e) -> t h e", h=4),
                  writes=[vas[j]])
            k.dma("sp", oss[ci % 3][0:n, :], scr["osTok"][r0:r0 + n, :], writes=[oss[ci % 3]])

        load(0)
        pendA, pendA_b, pendB = [], [], []
        for ci, (r0, n) in enumerate(chs):
            j = ci % 2
            qT, kT, kt, va, osg = qTs[j], kTs[j], kts[j], vas[j], oss[ci % 3]
            ON = ONs[ci % 2]
            if ci + 1 < NCH:
                load(ci + 1)
            if ci == 0:
                for hh in range(4):
                    k.op("pool", lambda h, hh=hh: h.memset(Cf[hh][:, :], 0.0), writes=[Cf[hh]])
                    k.op("pool", lambda h, hh=hh: h.memset(Cb[hh][:, :], 0.0), writes=[Cb[hh]])
            elif ci >= 32:
                si = ci - 32
                for hh in range(4):
                    k.dma("sp", Cf[hh][:, 0:128], io["mC0"][si, hh, :, :], writes=[Cf[hh]])
                    k.dma("sp", Cf[hh][:, 128:129], io["mn0"][si, hh, :].rearrange("(d o) -> d o", o=1),
                          par=[Cf[hh]])
                    k.op("act", lambda h, hh=hh: h.activation(out=Cb[hh][:, :], in_=Cf[hh][:, :], func=AF.Copy),
                         reads=[Cf[hh]], writes=[Cb[hh]])
            for hh in range(4):
                k.mm(PS_S[0:n, hh, 0:n], kT[:, hh, 0:n], qT[:, hh, 0:n], True, True, reads=[kT, qT], writes=[PS_S])
            for hh in range(4):
                k.mm(PS_G[0:n, hh, 0:n], c.sel[0:4, hh, 0:n], GW[0:4, 0, r0:r0 + n], True, True,
                     reads=[c.sel, GW], writes=[PS_G])
            for hh in range(4):
                k.mm(PS_W[:, hh, 0:n], c.sel[0:4, hh, :], GW[0:4, 1, r0:r0 + n], True, True,
                     reads=[c.sel, GW], writes=[PS_W])
            for hh in range(4):
                k.op("act", lambda h, hh=hh: h.activation(out=E[0:n, hh, 0:n], in_=PS_G[0:n, hh, 0:n], func=AF.Exp,
                                                        scale=-1.0, bias=COLS[0:n, ci, 0, hh:hh + 1]),
                     reads=[PS_G, COLS], writes=[E])
            for hh in range(4):
                k.op("pool", lambda h, hh=hh: h.tensor_tensor(out=E[0:n, hh, 0:n], in0=E[0:n, hh, 0:n],
                                                            in1=c.caus[0:n, 0:n], op=ALU.mult),
                     reads=[c.caus], writes=[E])
            k.op("dve", lambda h: h.tensor_tensor(out=WT[0:n, :, 0:n], in0=PS_S[0:n, :, 0:n], in1=E[0:n, :, 0:n],
                                                  op=ALU.mult), reads=[PS_S, E], writes=[WT])
            k.op("dve", lambda h: h.tensor_tensor(out=QS[:, :, 0:n], in0=PS_W[:, :, 0:n], in1=qT[:, :, 0:n],
                                                  op=ALU.mult), reads=[PS_W, qT], writes=[QS])
            for hh in range(4):
                po = PS_O[hh // 2]
                k.mm(po[0:n, hh % 2, :], QS[:, hh, 0:n], Cb[hh][:, :], True, False, reads=[QS, Cb[hh]], writes=[po])
                k.mm(po[0:n, hh % 2, :], WT[0:n, hh, 0:n], va[0:n, hh, :], False, True, reads=[WT, va], writes=[po])
            for i2 in range(2):
                k.op("dve", lambda h, i2=i2: h.tensor_copy(out=ON[0:n, 2 * i2:2 * i2 + 2, :], in_=PS_O[i2][0:n, :, :]),
                     reads=[PS_O[i2]], writes=[ON])
            for hh in range(4):
                k.op("act", lambda h, hh=hh: h.activation(out=KW[0:n, hh, :], in_=kt[0:n, hh, :], func=AF.Copy,
                                                        scale=COLS[0:n, ci, 1, hh:hh + 1]),
                     reads=[kt, COLS], writes=[KW])
            for hh in range(4):
                pd = PS_D[hh // 2]
                k.mm(pd[:, hh % 2, :], KW[0:n, hh, :], va[0:n, hh, :], True, True, reads=[KW, va], writes=[pd])
            for hh in range(4):
                pd = PS_D[hh // 2]
                k.op("dve", lambda h, hh=hh, pd=pd: h.scalar_tensor_tensor(
                    out=Cf[hh][:, :], in0=Cf[hh][:, :], scalar=DECB[:, hh, ci:ci + 1], in1=pd[:, hh % 2, :],
                    op0=ALU.mult, op1=ALU.add), reads=[DECB, pd], writes=[Cf[hh]])
                k.op("act", lambda h, hh=hh: h.activation(out=Cb[hh][:, :], in_=Cf[hh][:, :], func=AF.Copy),
                     reads=[Cf[hh]], writes=[Cb[hh]])
            if ci == 31 or ci >= 32:
                for hh in range(4):
                    if ci == 31:
                        oc, on = io["pC"][hh, :, :], io["pn"][hh, :]
                    else:
                        oc, on = io["sC"][ci - 32, hh, :, :], io["sn"][ci - 32, hh, :]
                    k.dma("sp", oc, Cf[hh][:, 0:128], reads=[Cf[hh]])
                    k.dma("sp", on.rearrange("(d o) -> d o", o=1), Cf[hh][:, 128:129], reads=[Cf[hh]])
            G2 = G2s[ci % 2]
            k.op("pool", lambda h, G2=G2, osg=osg: h.tensor_tensor(out=G2[0:n, :], in0=gain[0:n, :], in1=osg[0:n, :],
                                                                   op=ALU.mult), reads=[gain, osg], writes=[G2])

            def epiA(ci=ci, n=n, ON=ON, G2=G2, OAb=OAs[ci % 2]):
                den, dd, rr, mu, var, rs = sm
                s1, s2, aa, bb = sm2
                k.op("dve", lambda h: h.tensor_tensor(out=dd[0:n, :], in0=ON[0:n, :, 128], in1=COLS[0:n, ci, 2, :],
                                                      op=ALU.max), reads=[ON, COLS], writes=[dd])
                k.op("dve", lambda h: h.tensor_scalar(out=den[0:n, :], in0=ON[0:n, :, 128], scalar1=-1.0, scalar2=None,
                                                      op0=ALU.mult), reads=[ON], writes=[den])
                k.op("dve", lambda h: h.tensor_tensor(out=dd[0:n, :], in0=dd[0:n, :], in1=den[0:n, :],
                                                      op=ALU.max), reads=[den], writes=[dd])
                k.op("dve", lambda h: h.reciprocal(out=rr[0:n, :], in_=dd[0:n, :]), reads=[dd], writes=[rr])
                for hh in range(4):
                    k.op("act", lambda h, hh=hh: h.activation(out=junk[0:n, :], in_=ON[0:n, hh, 0:128], func=AF.Copy,
                                                            scale=rr[0:n, hh:hh + 1], accum_out=s1[0:n, hh:hh + 1]),
                         reads=[ON, rr], writes=[junk, s1])
                    k.op("act", lambda h, hh=hh: h.activation(out=junk[0:n, :], in_=ON[0:n, hh, 0:128], func=AF.Square,
                                                            scale=rr[0:n, hh:hh + 1], accum_out=s2[0:n, hh:hh + 1]),
                         reads=[ON, rr], writes=[junk, s2])
                k.op("dve", lambda h: h.tensor_scalar(out=mu[0:n, :], in0=s1[0:n, :], scalar1=1.0 / 128, scalar2=None,
                                                      op0=ALU.mult), reads=[s1], writes=[mu])
                k.op("dve", lambda h: h.tensor_tensor(out=var[0:n, :], in0=mu[0:n, :], in1=mu[0:n, :], op=ALU.mult),
                     reads=[mu], writes=[var])
                k.op("dve", lambda h: h.scalar_tensor_tensor(out=var[0:n, :], in0=s2[0:n, :], scalar=1.0 / 128,
                                                             in1=var[0:n, :], op0=ALU.mult, op1=ALU.subtract),
                     reads=[s2, var], writes=[var])
                k.op("dve", lambda h: h.tensor_scalar(out=var[0:n, :], in0=var[0:n, :], scalar1=0.0, scalar2=None,
                                                      op0=ALU.max), reads=[var], writes=[var])
                k.op("act", lambda h: h.activation(out=rs[0:n, :], in_=var[0:n, :], func=AF.Ln, bias=1e-6),
                     reads=[var], writes=[rs])
                k.op("act", lambda h: h.activation(out=rs[0:n, :], in_=rs[0:n, :], func=AF.Exp, scale=-0.5),
                     reads=[rs], writes=[rs])
                k.op("dve", lambda h: h.tensor_tensor(out=aa[0:n, :], in0=rr[0:n, :], in1=rs[0:n, :], op=ALU.mult),
                     reads=[rr, rs], writes=[aa])
                k.op("dve", lambda h: h.scalar_tensor_tensor(out=bb[0:n, :], in0=mu[0:n, :], scalar=-1.0, in1=rs[0:n, :],
                                                             op0=ALU.mult, op1=ALU.mult), reads=[mu, rs], writes=[bb])
                for hh in range(4):
                    k.op("pool", lambda h, hh=hh: h.tensor_scalar(out=H[0:n, hh, :], in0=ON[0:n, hh, 0:128],
                                                                scalar1=aa[0:n, hh:hh + 1], scalar2=bb[0:n, hh:hh + 1],
                                                                op0=ALU.mult, op1=ALU.add),
                         reads=[ON, aa, bb], writes=[H])
                k.op("dve", lambda h: h.tensor_tensor(out=OAb[0:n, :], in0=H[0:n, :, :].rearrange("p h e -> p (h e)"),
                                                      in1=G2[0:n, :], op=ALU.mult), reads=[H, G2], writes=[OAb])

            def epiB(r0=r0, n=n, OAb=OAs[ci % 2]):
                for hh in range(4):
                    k.tr(PS_T[:, hh, 0:n], OAb[0:n, hh * 128:(hh + 1) * 128], c.identb[0:n, 0:n],
                         reads=[OAb, c.identb], writes=[PS_T])
                k.op("act", lambda h: h.activation(out=OT[:, :, 0:n], in_=PS_T[:, :, 0:n], func=AF.Copy),
                     reads=[PS_T], writes=[OT])
                k.dma("sp", catv[:, :, r0:r0 + n], OT[:, :, 0:n], reads=[OT])
            if pendB:
                pendB.pop(0)()
            if pendA:
                pendA.pop(0)()
                pendB.append(pendA_b.pop(0))
            pendA.append(epiA)
            pendA_b.append(epiB)
        while pendA:
            pendA.pop(0)()
            pendB.append(pendA_b.pop(0))
        while pendB:
            pendB.pop(0)()
    k.barrier()


def attn_epilogue(k, c, PS_OT, nq, OS, RD, PS_BC, out_ap, out_t):
    k.op("dve", lambda h: h.tensor_copy(out=OS[0:65, 0:nq], in_=PS_OT[0:65, 0:nq]), reads=[PS_OT], writes=[OS])
    k.op("act", lambda h: h.activation(out=RD[64:65, 0:nq], in_=OS[64:65, 0:nq], func=AF.Ln), reads=[OS], writes=[RD])
    k.op("act", lambda h: h.activation(out=RD[64:65, 0:nq], in_=RD[64:65, 0:nq], func=AF.Exp, scale=-1.0),
         reads=[RD], writes=[RD])
    k.mm(PS_BC[0:64, 0:nq], c.ones[64:65, 0:64], RD[64:65, 0:nq], True, True, reads=[c.ones, RD], writes=[PS_BC])
    k.op("dve", lambda h: h.tensor_tensor(out=out_ap, in0=OS[0:64, 0:nq], in1=PS_BC[0:64, 0:nq], op=ALU.mult),
         reads=[OS, PS_BC], writes=[out_t])


def phase4(k, c, io, scr):
    with contextlib.ExitStack() as es:
        kTh = [k.sb(es, f"p4k{i}", [128, R], BF16) for i in range(2)]
        qTh = [k.sb(es, f"p4q{i}", [128, R], BF16) for i in range(2)]
        for t_ in kTh + qTh:
            k.op("pool", lambda h, t_=t_: h.memset(t_[64:128, :], 0.0), writes=[t_])
        vbh = [k.sb(es, f"p4v{i}", [128, NT, 65], BF16) for i in range(2)]
        bias = [k.sb(es, f"p4b{i}", [128, 640], F32) for i in range(2)]
        mask = k.sb(es, "p4mask", [128, 640], F32)
        ATT = [k.sb(es, f"p4att{i}", [64, R], BF16) for i in range(2)]
        TMP = [k.sb(es, f"p4tmp{i}", [128, 640], F32) for i in range(2)]
        P = [k.sb(es, f"p4P{i}", [128, 640], BF16) for i in range(2)]
        OS = k.sb(es, "p4OS", [65, 128], F32)
        RD = k.sb(es, "p4RD", [65, 128], F32)
        KC = k.sb(es, "p4KC", [128, 4, 64], BF16)
        KTC = k.sb(es, "p4KTC", [128, 512], BF16)
        k.op("pool", lambda h: h.memset(KTC[64:128, :], 0.0), writes=[KTC])
        VC = k.sb(es, "p4VC", [128, 4, 65], BF16)
        VN = k.sb(es, "p4VN", [64, 65], BF16)
        PS_A = k.ps(es, "p4PA", [128, 512])
        PS_B = k.ps(es, "p4PB", [128, 512])
        PS_OT = [k.ps(es, f"p4PO{i}", [128, 512]) for i in range(2)]
        PS_BC = k.ps(es, "p4PBC", [128, 512])
        PS_KT = k.ps(es, "p4PKT", [64, 4, 128], BF16)
        k.dma("sp", mask[:, :], io["bandmask"][:, :], writes=[mask])
        for i in range(2):
            k.op("pool", lambda h, i=i: h.memset(vbh[i][:, :, 64:65], 1.0), writes=[vbh[i]])
        k.op("pool", lambda h: h.memset(VC[:, :, 64:65], 1.0), writes=[VC])
        k.op("pool", lambda h: h.memset(VN[:, 64:65], 1.0), writes=[VN])
        vview = scr["vbTok"].rearrange("(t p) c -> p t c", p=128)

        def load(hh):
            j = hh % 2
            k.dma("sp", kTh[j][0:64, :], scr["kTb"][hh * 64:(hh + 1) * 64, :], writes=[kTh[j]])
            k.dma("sp", qTh[j][0:64, :], scr["qTb"][hh * 64:(hh + 1) * 64, :], writes=[qTh[j]])
            k.dma("sp", vbh[j][:, :, 0:64], vview[:, :, hh * 64:(hh + 1) * 64], writes=[vbh[j]])
            k.dma("sp", bias[j][:, :], io["bandbias"][hh, :, :], writes=[bias[j]])
            k.op("pool", lambda h: h.tensor_tensor(out=bias[j][:, :], in0=bias[j][:, :], in1=mask[:, :], op=ALU.add),
                 reads=[mask], writes=[bias[j]])

        load(0)
        it = 0
        for hh in range(8):
            j = hh % 2
            kT, qT, vb, bs, att = kTh[j], qTh[j], vbh[j], bias[j], ATT[j]
            if hh + 1 < 8:
                load(hh + 1)
            pendb = []
            for i in range(32):
                nj = min(i, 4) + 1
                tmp, p, pot = TMP[it % 2], P[it % 2], PS_OT[it % 2]
                it += 1
                for s in range(nj):
                    jj = i - s
                    dst = PS_A[:, s * 128:(s + 1) * 128] if s < 4 else PS_B[:, 0:128]
                    k.mm(dst, kT[:, jj * 128:(jj + 1) * 128], qT[:, i * 128:(i + 1) * 128], True, True,
                         reads=[kT, qT], writes=[PS_A if s < 4 else PS_B])
                na = min(nj, 4) * 128
                k.op("dve", lambda h, na=na, tmp=tmp: h.tensor_tensor(out=tmp[:, 0:na], in0=PS_A[:, 0:na],
                                                                     in1=bs[:, 0:na], op=ALU.add),
                     reads=[PS_A, bs], writes=[tmp])
                if nj == 5:
                    k.op("dve", lambda h, tmp=tmp: h.tensor_tensor(out=tmp[:, 512:640], in0=PS_B[:, 0:128],
                                                                  in1=bs[:, 512:640], op=ALU.add),
                         reads=[PS_B, bs], writes=[tmp])
                k.op("act", lambda h, tmp=tmp, p=p, nj=nj: h.activation(out=p[:, 0:nj * 128], in_=tmp[:, 0:nj * 128],
                                                                       func=AF.Exp), reads=[tmp], writes=[p])
                def fin(i=i, nj=nj, p=p, pot=pot):
                    for s in range(nj):
                        jj = i - s
                        k.mm(pot[0:65, 0:128], vb[:, jj, :], p[:, s * 128:(s + 1) * 128], s == 0, s == nj - 1,
                             reads=[vb, p], writes=[pot])
                    attn_epilogue(k, c, pot, 128, OS, RD, PS_BC, att[:, i * 128:(i + 1) * 128], att)
                if pendb:
                    pendb.pop(0)()
                pendb.append(fin)
            while pendb:
                pendb.pop(0)()
            for si in range(NS):
                r0 = TP + si * TS
                tmp, p, pot = TMP[it % 2], P[it % 2], PS_OT[it % 2]
                it += 1
                k.dma("pool", KC[:, :, :], io["cbk"][si, hh, :, :].rearrange("(t p) d -> p t d", p=128), writes=[KC])
                k.dma("pool", VC[:, :, 0:64], io["cbv"][si, hh, :, :].rearrange("(t p) d -> p t d", p=128), writes=[VC])
                k.dma("sp", VN[:, 0:64], scr["vbTok"][r0:r0 + TS, hh * 64:(hh + 1) * 64], writes=[VN])
                for m in range(4):
                    k.tr(PS_KT[:, m, :], KC[:, m, :], c.identb[:, :], reads=[KC, c.identb], writes=[PS_KT])
                k.op("act", lambda h: h.activation(out=KTC[0:64, :], in_=PS_KT[:, :, :].rearrange("p m n -> p (m n)"),
                                                   func=AF.Copy), reads=[PS_KT], writes=[KTC])
                for m in range(4):
                    k.mm(PS_A[:, m * 64:(m + 1) * 64], KTC[:, m * 128:(m + 1) * 128], qT[:, r0:r0 + TS], True, True,
                         reads=[KTC, qT], writes=[PS_A])
                k.mm(PS_A[0:64, 256:320], kT[:, r0:r0 + TS], qT[:, r0:r0 + TS], True, True, reads=[kT, qT], writes=[PS_A])
                for m in range(4):
                    b0 = 512 - 128 * m
                    k.op("dve", lambda h, m=m, b0=b0, tmp=tmp: h.tensor_tensor(
                        out=tmp[:, m * 64:(m + 1) * 64], in0=PS_A[:, m * 64:(m + 1) * 64], in1=bs[:, b0:b0 + 64],
                        op=ALU.add), reads=[PS_A, bs], writes=[tmp])
                k.op("dve", lambda h, tmp=tmp: h.tensor_tensor(out=tmp[0:64, 256:320], in0=PS_A[0:64, 256:320],
                                                              in1=bs[0:64, 0:64], op=ALU.add),
                     reads=[PS_A, bs], writes=[tmp])
                k.op("act", lambda h, tmp=tmp, p=p: h.activation(out=p[:, 0:256], in_=tmp[:, 0:256], func=AF.Exp),
                     reads=[tmp], writes=[p])
                k.op("act", lambda h, tmp=tmp, p=p: h.activation(out=p[0:64, 256:320], in_=tmp[0:64, 256:320],
                                                                func=AF.Exp), reads=[tmp], writes=[p])
                for m in range(4):
                    k.mm(pot[0:65, 0:64], VC[:, m, :], p[:, m * 64:(m + 1) * 64], m == 0, False, reads=[VC, p], writes=[pot])
                k.mm(pot[0:65, 0:64], VN[0:64, :], p[0:64, 256:320], False, True, reads=[VN, p], writes=[pot])
                attn_epilogue(k, c, pot, TS, OS, RD, PS_BC, att[:, r0:r0 + TS], att)
            k.dma("sp", scr["catT"][512 + hh * 64:512 + (hh + 1) * 64, :], att[:, :], reads=[att])
    k.barrier()


def outproj_phase(k, c, io, scr, cat_ap, w_ap, xin_ap, xout_ap, tag, prefetch=None):
    with contextlib.ExitStack() as es:
        W = k.sb(es, tag + "W", [128, 8, D], BF16)
        load_w(k, W, w_ap, 8, D)
        if prefetch is not None:
            prefetch()
        cats = [k.sb(es, f"{tag}c{i}", [128, 8, 512], BF16) for i in range(2)]
        xts = [k.sb(es, f"{tag}x{i}", [128, D], F32) for i in range(2)]
        xos = [k.sb(es, f"{tag}o{i}", [128, D], F32) for i in range(2)]
        pp = [k.ps(es, f"{tag}p{i}", [128, 512]) for i in range(4)]
        cv = cat_ap.rearrange("(k p) r -> p k r", p=128)
        gs = groups()
        k.dma("sp", cats[0][:, :, 0:gs[0][1]], cv[:, :, gs[0][0]:gs[0][0] + gs[0][1]], writes=[cats[0]])
        ti = 0
        pi = 0
        for gi, (tok0, n) in enumerate(gs):
            cat = cats[gi % 2]
            if gi + 1 < len(gs):
                t1, n1 = gs[gi + 1]
                k.dma("sp", cats[(gi + 1) % 2][:, :, 0:n1], cv[:, :, t1:t1 + n1], writes=[cats[(gi + 1) % 2]])
            for t in range(n // 128):
                r0 = tok0 + t * 128
                xt, xo = xts[ti % 2], xos[ti % 2]
                ti += 1
                k.dma("sp", xt[:, :], xin_ap[r0:r0 + 128, :], writes=[xt])
                for blk in range(2):
                    p = pp[pi % 4]
                    pi += 1
                    for kc in range(8):
                        k.mm(p[:, :], cat[:, kc, t * 128:(t + 1) * 128], W[:, kc, blk * 512:(blk + 1) * 512],
                             kc == 0, kc == 7, reads=[cat, W], writes=[p])
                    k.op("dve", lambda h, p=p, blk=blk, xt=xt, xo=xo: h.tensor_tensor(
                        out=xo[:, blk * 512:(blk + 1) * 512], in0=p[:, :], in1=xt[:, blk * 512:(blk + 1) * 512],
                        op=ALU.add), reads=[p, xt], writes=[xo])
                k.dma("sp", xout_ap[r0:r0 + 128, :], xo[:, :], reads=[xo])
    k.barrier()


def ffn_w_alloc(k, es, tag):
    return (k.sb(es, tag + "W", [128, 8, 2 * HID], BF16), k.sb(es, tag + "g", [128, 8], F32))


def ffn_w_load(k, io, layer, Wg):
    W, gcol = Wg
    load_w(k, W, io["w_ffn_in"][layer], 8, 2 * HID)
    scale_rows(k, None, W, io["norm_ffn"][layer, :], 8, None, gcol=gcol)
    return W


def ffn_in_phase(k, c, io, scr, layer, xin_ap, tag, W=None):
    with contextlib.ExitStack() as es:
        if W is None:
            W = ffn_w_load(k, io, layer, ffn_w_alloc(k, es, tag))
        st = NormState(k, es, tag)
        xts = [k.sb(es, f"{tag}x{i}", [128, D], F32) for i in range(2)]
        hTs = [k.sb(es, f"{tag}h{i}", [128, 8, 512], BF16) for i in range(2)]
        SG = [k.sb(es, f"{tag}sg{i}", [128, 512], F32) for i in range(2)]
        HS = [k.sb(es, f"{tag}hs{i}", [128, 512], BF16) for i in range(3)]
        PG = [k.ps(es, f"{tag}pg{i}", [128, 512]) for i in range(2)]
        PU = [k.ps(es, f"{tag}pu{i}", [128, 512]) for i in range(2)]
        ti = 0
        it = 0
        gl = groups()
        tis = [0]

        def norm_group(gi):
            tok0, n = gl[gi]
            for t in range(n // 128):
                xt = xts[tis[0] % 2]
                k.dma("sp", xt[:, :], xin_ap[tok0 + t * 128: tok0 + (t + 1) * 128, :], writes=[xt])
                rms_to_hT(k, c, xt, hTs[gi % 2], t * 128, st, tis[0])
                tis[0] += 1

        norm_group(0)
        for gi, (tok0, n) in enumerate(gl):
            hT = hTs[gi % 2]
            if gi + 1 < len(gl):
                norm_group(gi + 1)
            for cch in range(HID // 128):
                pg, pu, sg, hs = PG[it % 2], PU[it % 2], SG[it % 2], HS[it % 3]
                it += 1
                for kc in range(8):
                    k.mm(pg[:, 0:n], W[:, kc, cch * 128:(cch + 1) * 128], hT[:, kc, 0:n], kc == 0, kc == 7,
                         reads=[W, hT], writes=[pg])
                for kc in range(8):
                    k.mm(pu[:, 0:n], W[:, kc, HID + cch * 128:HID + (cch + 1) * 128], hT[:, kc, 0:n], kc == 0, kc == 7,
                         reads=[W, hT], writes=[pu])
                k.op("act", lambda h, pg=pg, sg=sg: h.activation(out=sg[:, 0:n], in_=pg[:, 0:n], func=AF.Silu),
                     reads=[pg], writes=[sg])
                k.op("dve", lambda h, pu=pu, sg=sg, hs=hs: h.tensor_tensor(out=hs[:, 0:n], in0=pu[:, 0:n], in1=sg[:, 0:n],
                                                                          op=ALU.mult), reads=[pu, sg], writes=[hs])
                k.dma("sp", scr["hidT"][cch * 128:(cch + 1) * 128, tok0:tok0 + n], hs[:, 0:n], reads=[hs])
    k.barrier()


def ffn_out_phase(k, c, io, scr, layer, xin_ap, xout_ap, final, tag):
    with contextlib.ExitStack() as es:
        KC = HID // 128
        W = k.sb(es, tag + "W", [128, KC, D], BF16)
        load_w(k, W, io["w_ffn_out"][layer], KC, D)
        hids = [k.sb(es, f"{tag}hd{i}", [128, KC, 512], BF16) for i in range(2)]
        xts = [k.sb(es, f"{tag}x{i}", [128, D], F32) for i in range(2)]
        xos = [k.sb(es, f"{tag}o{i}", [128, D], F32) for i in range(2)]
        pp = [k.ps(es, f"{tag}p{i}", [128, 512]) for i in range(4)]
        if final:
            gam = k.sb(es, tag + "gam", [128, D], F32)
            k.dma("sp", gam[:, :], io["norm_final"].partition_broadcast(128), writes=[gam])
            junk = k.sb(es, tag + "junk", [128, D], BF16)
            ss = [k.sb(es, f"{tag}ss{i}", [128, 1], F32) for i in range(2)]
            sq = [k.sb(es, f"{tag}sq{i}", [128, 1], F32) for i in range(2)]
            rstd = [k.sb(es, f"{tag}rs{i}", [128, 1], F32) for i in range(2)]
            ys = [k.sb(es, f"{tag}y{i}", [128, D], F32) for i in range(2)]
        hv = scr["hidT"].rearrange("(c p) r -> p c r", p=128)
        gs = groups()
        k.dma("sp", hids[0][:, :, 0:gs[0][1]], hv[:, :, gs[0][0]:gs[0][0] + gs[0][1]], writes=[hids[0]])
        ti = 0
        pi = 0
        for gi, (tok0, n) in enumerate(gs):
            hid = hids[gi % 2]
            if gi + 1 < len(gs):
                t1, n1 = gs[gi + 1]
                k.dma("sp", hids[(gi + 1) % 2][:, :, 0:n1], hv[:, :, t1:t1 + n1], writes=[hids[(gi + 1) % 2]])
            for t in range(n // 128):
                r0 = tok0 + t * 128
                j = ti % 2
                xt, xo = xts[j], xos[j]
                ti += 1
                k.dma("sp", xt[:, :], xin_ap[r0:r0 + 128, :], writes=[xt])
                for blk in range(2):
                    p = pp[pi % 4]
                    pi += 1
                    for kc in range(KC):
                        k.mm(p[:, :], hid[:, kc, t * 128:(t + 1) * 128], W[:, kc, blk * 512:(blk + 1) * 512],
                             kc == 0, kc == KC - 1, reads=[hid, W], writes=[p])
                    k.op("dve", lambda h, p=p, blk=blk, xt=xt, xo=xo: h.tensor_tensor(
                        out=xo[:, blk * 512:(blk + 1) * 512], in0=p[:, :], in1=xt[:, blk * 512:(blk + 1) * 512],
                        op=ALU.add), reads=[p, xt], writes=[xo])
                if not final:
                    k.dma("sp", xout_ap[r0:r0 + 128, :], xo[:, :], reads=[xo])
                else:
                    k.op("act", lambda h, xo=xo, j=j: h.activation(out=junk[:, :], in_=xo[:, :], func=AF.Square,
                                                                  accum_out=ss[j][:, :]),
                         reads=[xo], writes=[junk, ss[j]])
                    k.op("act", lambda h, j=j: h.activation(out=sq[j][:, :], in_=ss[j][:, :], func=AF.Sqrt,
                                                           scale=1.0 / D, bias=1e-6), reads=[ss[j]], writes=[sq[j]])
                    k.op("dve", lambda h, j=j: h.reciprocal(out=rstd[j][:, :], in_=sq[j][:, :]),
                         reads=[sq[j]], writes=[rstd[j]])
                    k.op("act", lambda h, xo=xo, j=j: h.activation(out=ys[j][:, :], in_=xo[:, :], func=AF.Copy,
                                                                  scale=rstd[j][:, :]),
                         reads=[xo, rstd[j]], writes=[ys[j]])
                    k.op("pool", lambda h, j=j: h.tensor_tensor(out=ys[j][:, :], in0=ys[j][:, :], in1=gam[:, :],
                                                               op=ALU.mult), reads=[gam], writes=[ys[j]])
                    k.dma("sp", xout_ap[r0:r0 + 128, :], ys[j][:, :], reads=[ys[j]])
    k.barrier()


def fox_phases(k, c, io, scr):
    with contextlib.ExitStack() as esf:
        frow = k.sb(esf, "fx_frow", [16, R], F32)
        fox_inproj(k, c, io, scr, frow)
        if "fx1" in DBG:
            return
        NFP = k.sb(esf, "fx_NFP", [16, TP], F32)
        FSN = k.sb(esf, "fx_FSN", [16, NS, TS], F32)
        NFcP = k.sb(esf, "fx_NFcP", [128, 32, 16], F32)
        NFcS = k.sb(esf, "fx_NFcS", [128, NS, 17, 16], F32)
        fox_prep(k, c, io, scr, frow, NFP, FSN, NFcP, NFcS)
        if "fx2" in DBG:
            return
        fox_attn(k, c, io, scr, NFP, FSN, NFcP, NFcS)


def fox_inproj(k, c, io, scr, frow):
    with contextlib.ExitStack() as es:
        stf = [k.sb(es, f"f1sf{i}", [128, 512], F32) for i in range(2)]
        W = k.sb(es, "f1W", [128, 8, 3088], BF16)
        load_w(k, W, io["w_in_fox"], 8, 3088)
        scale_rows(k, es, W, io["norm_mix"][1, :], 8, "f1g")
        st = NormState(k, es, "f1")
        xts = [k.sb(es, f"f1x{i}", [128, D], F32) for i in range(2)]
        hTs = [k.sb(es, f"f1h{i}", [128, 8, 512], BF16) for i in range(2)]
        pfm = [k.ps(es, f"f1pf{i}", [128, 512]) for i in range(2)]
        ptm = [k.ps(es, f"f1pt{i}", [128, 512]) for i in range(2)]
        stb = [k.sb(es, f"f1sb{i}", [128, 512], BF16) for i in range(4)]
        ev = 0
        sbi = 0
        sfi = 0
        ti = 0
        gl = groups()
        tis = [0]

        def norm_group(gi):
            tok0, n = gl[gi]
            for t in range(n // 128):
                xt = xts[tis[0] % 2]
                k.dma("sp", xt[:, :], scr["x2"][tok0 + t * 128: tok0 + (t + 1) * 128, :], writes=[xt])
                rms_to_hT(k, c, xt, hTs[gi % 2], t * 128, st, tis[0])
                tis[0] += 1

        norm_group(0)
        for gi, (tok0, n) in enumerate(gl):
            hT = hTs[gi % 2]
            if gi + 1 < len(gl):
                norm_group(gi + 1)
            for (c0, dst, scale) in ([] if "fxnofm" in DBG else ((0, scr["qTf"], 0.125), (1024, scr["kTf"], None))):
                for mc in range(8):
                    pp = pfm[ev % 2]
                    for kc in range(8):
                        k.mm(pp[:, 0:n], W[:, kc, c0 + mc * 128: c0 + (mc + 1) * 128], hT[:, kc, 0:n],
                             kc == 0, kc == 7, reads=[W, hT], writes=[pp])
                    sbt = stb[sbi % 4]
                    sbi += 1
                    evac(k, ev, sbt[:, 0:n], pp[:, 0:n], [pp], [sbt], scale=scale)
                    ev += 1
                    k.dma("sp", dst[mc * 128:(mc + 1) * 128, tok0:tok0 + n], sbt[:, 0:n], reads=[sbt])
            pp = pfm[ev % 2]
            for kc in range(0 if "fxnog" in DBG else 8):
                k.mm(pp[0:16, 0:n], W[:, kc, 3072:3088], hT[:, kc, 0:n], kc == 0, kc == 7, reads=[W, hT], writes=[pp])
            if "fxnog" not in DBG:
                evac(k, 1, frow[0:16, tok0:tok0 + n], pp[0:16, 0:n], [pp], [frow])
            ev += 1
            for t in range(0 if "fxnotm" in DBG else n // 128):
                r0 = tok0 + t * 128
                for (c0, outn, hb, tobf) in ((2048, "fv", 0, True), (2560, "fv", 1, True),
                                             (1024, "fk", 0, False), (1536, "fk", 1, False)):
                    pp = ptm[ev % 2]
                    for kc in range(8):
                        k.mm(pp[:, :], hT[:, kc, t * 128:(t + 1) * 128], W[:, kc, c0:c0 + 512],
                             kc == 0, kc == 7, reads=[W, hT], writes=[pp])
                    ev += 1
                    sft = stf[sfi % 2]
                    sfi += 1
                    evac(k, 1, sft[:, :], pp[:, :], [pp], [sft])
                    if tobf:
                        sbt = stb[sbi % 4]
                        sbi += 1
                        k.op("act", lambda h, sbt=sbt, sft=sft: h.activation(out=sbt[:, :], in_=sft[:, :], func=AF.Copy),
                             reads=[sft], writes=[sbt])
                        k.dma("sp", scr["vfTok"][r0:r0 + 128, hb * 512:(hb + 1) * 512], sbt[:, :], reads=[sbt])
                    for h8 in range(0 if "fxnoout" in DBG else 8):
                        src = sft[:, h8 * 64:(h8 + 1) * 64]
                        if r0 < TP:
                            k.dma("sp", io["p" + outn][hb * 8 + h8, r0:r0 + 128, :], src, reads=[sft])
                        else:
                            for s2 in range(2):
                                sq = (r0 - TP) // TS + s2
                                k.dma("sp", io["s" + outn][sq, hb * 8 + h8, :, :], src[s2 * 64:(s2 + 1) * 64], reads=[sft])
    k.barrier()


def fox_prep(k, c, io, scr, frow, NFP, FSN, NFcP, NFcS):
    with contextlib.ExitStack() as es:
        negb = k.sb(es, "f2negb", [16, 1], F32)
        LF = k.sb(es, "f2LF", [16, R], F32)
        CL = [k.sb(es, f"f2CL{i}", [16, 2048], F32) for i in range(2)]
        FC = [k.sb(es, f"f2FC{i}", [16, 2048], F32) for i in range(2)]
        PCp = k.ps(es, "f2PCp", [128, 32, 16])
        PCs = [k.ps(es, f"f2PCs{i}", [128, 17, 16]) for i in range(2)]
        k.dma("sp", negb[:, :], io["b_fox_f"].rearrange("(h o) -> h o", o=1), writes=[negb])
        k.op("dve", lambda h: h.tensor_scalar(out=negb[:, :], in0=negb[:, :], scalar1=-1.0, scalar2=None, op0=ALU.mult),
             writes=[negb])
        k.op("act", lambda h: h.activation(out=frow[:, :], in_=frow[:, :], func=AF.Exp, scale=-1.0, bias=negb[:, :]),
             reads=[negb], writes=[frow])
        k.op("act", lambda h: h.activation(out=frow[:, :], in_=frow[:, :], func=AF.Ln, bias=1.0), writes=[frow])
        k.op("dve", lambda h: h.tensor_scalar(out=LF[:, :], in0=frow[:, :], scalar1=-1.0, scalar2=None, op0=ALU.mult),
             reads=[frow], writes=[LF])
        k.dma("sp", io["pflf"][:, :], LF[:, 0:TP], reads=[LF])
        for i in range(NS):
            k.dma("sp", io["sflf"][i, :, :], LF[:, TP + i * TS:TP + (i + 1) * TS], reads=[LF])
        k.op("dve", lambda h: h.tensor_tensor_scan(out=NFP[:, :], data0=frow[:, 0:TP], data1=frow[:, 0:TP], initial=0.0,
                                                   op0=ALU.add, op1=ALU.max), reads=[frow], writes=[NFP])
        for t in range(32):
            k.tr(PCp[:, t, :], NFP[0:16, t * 128:(t + 1) * 128], c.ident[0:16, 0:16], reads=[NFP, c.ident], writes=[PCp])
        k.op("dve", lambda h: h.tensor_copy(out=NFcP[:, :, :], in_=PCp[:, :, :]), reads=[PCp], writes=[NFcP])
        for i in range(NS):
            cl, fc, pcs = CL[i % 2], FC[i % 2], PCs[i % 2]
            s0 = TP + i * TS
            k.dma("sp", cl[:, :], io["cflf"][i, :, :], writes=[cl])
            k.op("dve", lambda h, cl=cl: h.tensor_scalar(out=cl[:, :], in0=cl[:, :], scalar1=-1.0, scalar2=None,
                                                        op0=ALU.mult), writes=[cl])
            k.op("dve", lambda h, cl=cl, fc=fc: h.tensor_tensor_scan(out=fc[:, :], data0=cl[:, :], data1=cl[:, :],
                                                                    initial=0.0, op0=ALU.add, op1=ALU.max),
                 reads=[cl], writes=[fc])
            k.op("dve", lambda h, fc=fc, i=i, s0=s0: h.tensor_tensor_scan(
                out=FSN[:, i, :], data0=frow[:, s0:s0 + TS], data1=frow[:, s0:s0 + TS], initial=fc[:, 2047:2048],
                op0=ALU.add, op1=ALU.max), reads=[fc, frow], writes=[FSN])
            for m in range(16):
                k.tr(pcs[:, m, :], fc[0:16, m * 128:(m + 1) * 128], c.ident[0:16, 0:16], reads=[fc, c.ident], writes=[pcs])
            k.tr(pcs[0:64, 16, :], FSN[0:16, i, :], c.ident[0:16, 0:16], reads=[FSN, c.ident], writes=[pcs])
            k.op("dve", lambda h, pcs=pcs, i=i: h.tensor_copy(out=NFcS[:, i, 0:16, :], in_=pcs[:, 0:16, :]),
                 reads=[pcs], writes=[NFcS])
            k.op("dve", lambda h, pcs=pcs, i=i: h.tensor_copy(out=NFcS[0:64, i, 16, :], in_=pcs[0:64, 16, :]),
                 reads=[pcs], par=[NFcS])
    k.barrier()


def fox_attn(k, c, io, scr, NFP, FSN, NFcP, NFcS):
    with contextlib.ExitStack() as es:
        kTh = [k.sb(es, f"f3k{i}", [128, R], BF16) for i in range(2)]
        qTh = [k.sb(es, f"f3q{i}", [128, R], BF16) for i in range(2)]
        for t_ in kTh + qTh:
            k.op("pool", lambda h, t_=t_: h.memset(t_[64:128, :], 0.0), writes=[t_])
        vfh = [k.sb(es, f"f3v{i}", [128, NT, 65], BF16) for i in range(2)]
        ATT = k.sb(es, "f3att", [64, R], BF16)
        FQ = [k.sb(es, f"f3fq{i}", [128, 512], F32) for i in range(2)]
        BD = [[k.sb(es, f"f3bd{b}_{i}", [128, 512], F32) for i in range(4)] for b in range(2)]
        TMP = [k.sb(es, f"f3tmp{i}", [128, 512], F32) for i in range(4)]
        P = [k.sb(es, f"f3P{i}", [128, 512], BF16) for i in range(4)]
        OS = k.sb(es, "f3OS", [65, 512], F32)
        RD = k.sb(es, "f3RD", [65, 512], F32)
        KC = [k.sb(es, f"f3KC{i}", [128, 16, 64], BF16) for i in range(2)]
        KTC = k.sb(es, "f3KTC", [128, 2048], BF16)
        k.op("pool", lambda h: h.memset(KTC[64:128, :], 0.0), writes=[KTC])
        VC = [k.sb(es, f"f3VC{i}", [128, 16, 65], BF16) for i in range(2)]
        VN = [k.sb(es, f"f3VN{i}", [64, 65], BF16) for i in range(2)]
        FQs = k.sb(es, "f3FQs", [128, TS], F32)
        BDs = k.sb(es, "f3BDs", [64, TS], F32)
        TMPs = k.sb(es, "f3TMPs", [128, 1088], F32)
        Ps = k.sb(es, "f3Ps", [128, 1088], BF16)
        PS_S = [k.ps(es, f"f3PS{i}", [128, 512]) for i in range(3)]
        PS_O = [k.ps(es, f"f3PO{i}", [128, 512]) for i in range(2)]
        PS_F = k.ps(es, "f3PF", [128, 512])
        PS_BC = k.ps(es, "f3PBC", [128, 512])
        PS_KT = k.ps(es, "f3PKT", [64, 1024], BF16)
        LN = k.sb(es, "f3LN", [65, 512], F32)
        RDr = [k.sb(es, f"f3RDr{i}", [65, 512], F32) for i in range(2)]
        RB = [k.sb(es, f"f3RB{i}", [64, 512], F32) for i in range(2)]
        rdbuf = [TT(None) for _ in range(2)]
        for i in range(2):
            k.op("pool", lambda h, i=i: h.memset(vfh[i][:, :, 64:65], 1.0), writes=[vfh[i]])
            k.op("pool", lambda h, i=i: h.memset(VC[i][:, :, 64:65], 1.0), writes=[VC[i]])
            k.op("pool", lambda h, i=i: h.memset(VN[i][:, 64:65], 1.0), writes=[VN[i]])
        vview = scr["vfTok"].rearrange("(t p) c -> p t c", p=128)

        def load(hh):
            j = hh % 2
            k.dma("sp", kTh[j][0:64, :], scr["kTf"][hh * 64:(hh + 1) * 64, :], writes=[kTh[j]])
            k.dma("sp", qTh[j][0:64, :], scr["qTf"][hh * 64:(hh + 1) * 64, :], writes=[qTh[j]])
            k.dma("sp", vfh[j][:, :, 0:64], vview[:, :, hh * 64:(hh + 1) * 64], writes=[vfh[j]])

        def prep(hh, g, b):
            fq = FQ[b]
            k.mm(PS_F[:, 0:512], c.sel[0:16, hh, :], NFP[0:16, g * 512:(g + 1) * 512], True, True,
                 reads=[c.sel, NFP], writes=[PS_F])
            k.op("dve", lambda h: h.tensor_copy(out=fq[:, :], in_=PS_F[:, 0:512]), reads=[PS_F], writes=[fq])
            for r in range(4):
                b0 = 384 - 128 * r
                k.op("pool", lambda h, r=r, b0=b0: h.tensor_tensor(out=BD[b][r][:, :], in0=fq[:, :],
                                                                 in1=c.cpos[:, b0:b0 + 512], op=ALU.add),
                     reads=[fq, c.cpos], writes=[BD[b][r]])

        def sload(hh, si, b):
            r0 = TP + si * TS
            k.dma("pool", KC[b][:, :, :], io["cfk"][si, hh, :, :].rearrange("(t p) d -> p t d", p=128), writes=[KC[b]])
            k.dma("pool", VC[b][:, :, 0:64], io["cfv"][si, hh, :, :].rearrange("(t p) d -> p t d", p=128), writes=[VC[b]])
            k.dma("sp", VN[b][:, 0:64], scr["vfTok"][r0:r0 + TS, hh * 64:(hh + 1) * 64], writes=[VN[b]])

        load(0)
        it = 0
        gi = 0
        sli = 0
        pend_epiA = []
        pend_epiB = []
        sload(0, 0, 0)
        for hh in range(16):
            j = hh % 2
            kT, qT, vf = kTh[j], qTh[j], vfh[j]
            if hh + 1 < 16:
                load(hh + 1)
            prep(hh, 0, gi % 2)
            pend = []
            for g in range(8):
                b = gi % 2
                fq, bd, pso = FQ[b], BD[b], PS_O[b]
                gi += 1
                if g + 1 < 8:
                    prep(hh, g + 1, gi % 2)
                last = 4 * g + 3
                for jj in range(last + 1):
                    ps, tmp, p = PS_S[it % 3], TMP[it % 4], P[it % 4]
                    it += 1
                    k.mm(ps[:, 0:512], kT[:, jj * 128:(jj + 1) * 128], qT[:, g * 512:(g + 1) * 512], True, True,
                         reads=[kT, qT], writes=[ps])
                    bsrc = bd[jj - 4 * g] if jj >= 4 * g else fq
                    k.op("dve", lambda h, ps=ps, tmp=tmp, bsrc=bsrc: h.tensor_tensor(out=tmp[:, :], in0=ps[:, 0:512],
                                                                                    in1=bsrc[:, :], op=ALU.subtract),
                         reads=[ps, bsrc], writes=[tmp], skip_self=True, skip_war=("act",))
                    k.op("act", lambda h, tmp=tmp, p=p, jj=jj: h.activation(out=p[:, :], in_=tmp[:, :], func=AF.Exp,
                                                                           bias=NFcP[:, jj, hh:hh + 1]),
                         reads=[tmp, NFcP], writes=[p], skip_self=True, skip_war=("pe",))
                    pend.append(lambda jj=jj, p=p, pso=pso, last=last: k.mm(
                        pso[0:65, 0:512], vf[:, jj, :], p[:, :], jj == 0, jj == last, reads=[vf, p], writes=[pso]))
                    if len(pend) > 2:
                        pend.pop(0)()
                    if jj == 1 and pend_epiA:
                        pend_epiA.pop(0)()
                    if jj == last - 1 and pend_epiB:
                        pend_epiB.pop(0)()
                def epiA(pso=pso, b=b):
                    k.op("act", lambda h: h.activation(out=LN[64:65, :], in_=pso[64:65, 0:512], func=AF.Ln),
                         reads=[pso], writes=[LN])
                    k.op("act", lambda h: h.activation(out=RDr[b][64:65, :], in_=LN[64:65, :], func=AF.Exp, scale=-1.0),
                         reads=[LN], writes=[RDr[b]])
                    k.dma("sp", scr["rds"][b:b + 1, :], RDr[b][64:65, :], reads=[RDr[b]], writes=[rdbuf[b]])
                    k.dma("sp", RB[b][:, :], scr["rds"][b, :].partition_broadcast(64), reads=[rdbuf[b]], writes=[RB[b]])

                def epiB(pso=pso, b=b, g=g):
                    k.op("dve", lambda h: h.tensor_tensor(out=ATT[:, g * 512:(g + 1) * 512], in0=pso[0:64, 0:512],
                                                          in1=RB[b][:, :], op=ALU.mult),
                         reads=[pso, RB[b]], writes=[ATT])
                pend_epiA.append(epiA)
                pend_epiB.append(epiB)
            while pend:
                pend.pop(0)()
            while pend_epiA:
                pend_epiA.pop(0)()
            while pend_epiB:
                pend_epiB.pop(0)()
            for si in range(NS):
                r0 = TP + si * TS
                b = sli % 2
                sli += 1
                kc, vc, vn = KC[b], VC[b], VN[b]
                pso = PS_O[sli % 2]
                if si + 1 < NS:
                    sload(hh, si + 1, sli % 2)
                elif hh + 1 < 16:
                    sload(hh + 1, 0, sli % 2)
                for half in range(2):
                    for m in range(8):
                        k.tr(PS_KT[:, m * 128:(m + 1) * 128], kc[:, half * 8 + m, :], c.identb[:, :],
                             reads=[kc, c.identb], writes=[PS_KT])
                    k.op("act", lambda h, half=half: h.activation(out=KTC[0:64, half * 1024:(half + 1) * 1024],
                                                                in_=PS_KT[:, :], func=AF.Copy),
                         reads=[PS_KT], writes=[KTC])
                k.mm(PS_F[:, 0:TS], c.sel[0:16, hh, :], FSN[0:16, si, :], True, True, reads=[c.sel, FSN], writes=[PS_F])
                k.op("dve", lambda h: h.tensor_copy(out=FQs[:, :], in_=PS_F[:, 0:TS]), reads=[PS_F], writes=[FQs])
                k.op("pool", lambda h: h.tensor_tensor(out=BDs[:, :], in0=FQs[0:64, :], in1=c.cpos[0:64, 384:384 + TS],
                                                       op=ALU.add), reads=[FQs, c.cpos], writes=[BDs])
                for m in range(16):
                    ps = PS_S[m // 8]
                    k.mm(ps[:, (m % 8) * 64:(m % 8 + 1) * 64], KTC[:, m * 128:(m + 1) * 128], qT[:, r0:r0 + TS], True, True,
                         reads=[KTC, qT], writes=[ps])
                k.mm(PS_S[2][0:64, 0:TS], kT[:, r0:r0 + TS], qT[:, r0:r0 + TS], True, True, reads=[kT, qT], writes=[PS_S[2]])
                for m in range(16):
                    ps = PS_S[m // 8]
                    k.op("dve", lambda h, m=m, ps=ps: h.scalar_tensor_tensor(
                        out=TMPs[:, m * 64:(m + 1) * 64], in0=ps[:, (m % 8) * 64:(m % 8 + 1) * 64],
                        scalar=NFcS[:, si, m, hh:hh + 1], in1=FQs[:, :], op0=ALU.add, op1=ALU.subtract),
                        reads=[ps, NFcS, FQs], writes=[TMPs])
                k.op("dve", lambda h: h.scalar_tensor_tensor(
                    out=TMPs[0:64, 1024:1088], in0=PS_S[2][0:64, 0:TS], scalar=NFcS[0:64, si, 16, hh:hh + 1],
                    in1=BDs[:, :], op0=ALU.add, op1=ALU.subtract), reads=[PS_S[2], NFcS, BDs], writes=[TMPs])
                k.op("act", lambda h: h.activation(out=Ps[:, 0:1024], in_=TMPs[:, 0:1024], func=AF.Exp),
                     reads=[TMPs], writes=[Ps])
                k.op("act", lambda h: h.activation(out=Ps[0:64, 1024:1088], in_=TMPs[0:64, 1024:1088], func=AF.Exp),
                     reads=[TMPs], writes=[Ps])
                for m in range(16):
                    k.mm(pso[0:65, 0:TS], vc[:, m, :], Ps[:, m * 64:(m + 1) * 64], m == 0, False, reads=[vc, Ps], writes=[pso])
                k.mm(pso[0:65, 0:TS], vn[0:64, :], Ps[0:64, 1024:1088], False, True, reads=[vn, Ps], writes=[pso])
                attn_epilogue(k, c, pso, TS, OS, RD, PS_BC, ATT[:, r0:r0 + TS], ATT)
            k.dma("sp", scr["attT"][hh * 64:(hh + 1) * 64, :], ATT[:, :], reads=[ATT])
    k.barrier()
```

```python
import contextlib
import numpy as np
import concourse.bass as bass
import concourse.mybir as mybir
from concourse.bass_utils import run_bass_kernel_spmd

F32 = mybir.dt.float32
BF16 = mybir.dt.bfloat16
AF = mybir.ActivationFunctionType
ALU = mybir.AluOpType

D = 1024
TP = 4096
NS = 4
TS = 64
R = TP + NS * TS
NT = R // 128
HID = 2816
BIG = 30000.0


class Buf:
    __slots__ = ("w", "r")

    def __init__(self):
        self.w = []
        self.r = {}


class TT:
    def __init__(self, t):
        self.t = t
        self.b = Buf()

    def __getitem__(self, key):
        return self.t[key]


class Eng:
    def __init__(self, name, h, sem):
        self.name = name
        self.h = h
        self.sem = sem
        self.count = 0
        self.seen = {}


class DQ:
    def __init__(self, name, sems):
        self.sems = sems
        self.keys = [f"{name}{i}" for i in range(len(sems))]
        self.uses = [0] * len(sems)
        self.next = 0


class KB:
    def __init__(self, nc, es):
        self.nc = nc
        self.eng = {}
        for name, h in (("pe", nc.tensor), ("act", nc.scalar), ("dve", nc.vector),
                        ("pool", nc.gpsimd), ("sp", nc.sync)):
            self.eng[name] = Eng(name, h, es.enter_context(nc.semaphore("sem_" + name)))
        self.dq = {}
        for q in ("sp", "pool"):
            self.dq[q] = DQ("dq" + q, [es.enter_context(nc.semaphore(f"dq{q}{i}")) for i in range(8)])

    def sb(self, es, name, shape, dt):
        return TT(es.enter_context(self.nc.sbuf_tensor(name, shape, dt)))

    def ps(self, es, name, shape, dt=F32):
        return TT(es.enter_context(self.nc.psum_tensor(name, shape, dt)))

    def _wait(self, e, deps):
        best = {}
        for (key, sem, val) in deps:
            if key == "pe" and e.name == "pe":
                continue
            if e.seen.get(key, 0) >= val:
                continue
            if key not in best or best[key][1] < val:
                best[key] = (sem, val)
        for key, (sem, val) in best.items():
            e.h.wait_ge(sem, val)
            e.seen[key] = val

    def _deps(self, reads, writes, par=(), en=None, skip_war=()):
        deps = []
        for t in reads:
            deps.extend(t.b.w)
        for t in writes:
            deps.extend(d for d in t.b.w if d[0] != en)
            deps.extend(d for d in t.b.r.values() if d[0] != en and d[0] not in skip_war)
        for t in par:
            deps.extend(d for d in t.b.r.values() if d[0] != en)
        return deps

    def _mark(self, tok, reads, writes, par=()):
        for t in par:
            t.b.w.append(tok)
        for t in writes:
            t.b.w = [tok]
            t.b.r = {}
        for t in reads:
            if t in writes:
                continue
            old = t.b.r.get(tok[0])
            if old is None or old[2] < tok[2]:
                t.b.r[tok[0]] = tok

    def op(self, en, fn, reads=(), writes=(), par=(), skip_self=False, skip_war=()):
        e = self.eng[en]
        self._wait(e, self._deps(reads, writes, par, en if skip_self else None, skip_war))
        inst = fn(e.h)
        e.count += 1
        inst.then_inc(e.sem, 1)
        tok = (en, e.sem, e.count)
        self._mark(tok, reads, writes, par)
        return tok

    def dma(self, qn, out, in_, reads=(), writes=(), par=(), **kw):
        e = self.eng[qn]
        q = self.dq[qn]
        self._wait(e, self._deps(reads, writes, par))
        s = q.next % len(q.sems)
        q.next += 1
        if q.uses[s] > 0 and e.seen.get(q.keys[s], 0) < 16 * q.uses[s]:
            e.h.wait_ge(q.sems[s], 16 * q.uses[s])
            e.seen[q.keys[s]] = 16 * q.uses[s]
        inst = e.h.dma_start(out=out, in_=in_, **kw)
        q.uses[s] += 1
        inst.then_inc(q.sems[s], 16)
        tok = (q.keys[s], q.sems[s], 16 * q.uses[s])
        self._mark(tok, reads, writes, par)
        return tok

    def barrier(self):
        toks = [(n, e.sem, e.count) for n, e in self.eng.items() if e.count > 0]
        for q in self.dq.values():
            for i in range(len(q.sems)):
                if q.uses[i] > 0:
                    toks.append((q.keys[i], q.sems[i], 16 * q.uses[i]))
        for e in self.eng.values():
            self._wait(e, [t for t in toks if t[0] != e.name])

    def mm(self, out, lhsT, rhs, start, stop, reads, writes):
        return self.op("pe", lambda h: h.matmul(out, lhsT=lhsT, rhs=rhs, start=start, stop=stop),
                       reads=reads, writes=writes)

    def tr(self, out, in_, ident, reads, writes):
        return self.op("pe", lambda h: h.transpose(out=out, in_=in_, identity=ident),
                       reads=reads, writes=writes)


def load_w(k, dst, w_ap, kc_n, ncols, blk=2048):
    nb = -(-ncols // blk)
    blk = -(-ncols // nb)
    for kc in range(kc_n):
        for c0 in range(0, ncols, blk):
            c1 = min(ncols, c0 + blk)
            k.dma("pool", dst[:, kc, c0:c1], w_ap[kc * 128:(kc + 1) * 128, c0:c1], par=[dst])


def scale_rows(k, es, w, g_ap, kc_n, name, gcol=None):
    if gcol is None:
        gcol = k.sb(es, name, [128, kc_n], F32)
    k.dma("sp", gcol[:, :], g_ap.rearrange("(k p) -> p k", p=128), writes=[gcol],
          allow_slow_non_contiguous=True)
    for kc in range(kc_n):
        eng = "dve" if kc % 2 == 0 else "pool"
        k.op(eng, lambda h, kc=kc: h.tensor_scalar(out=w[:, kc, :], in0=w[:, kc, :],
                                                    scalar1=gcol[:, kc:kc + 1], scalar2=None,
                                                    op0=ALU.mult),
             reads=[gcol], writes=[w])


class Consts:
    pass


def make_consts(k, es):
    c = Consts()
    c.ones = k.sb(es, "c_ones", [128, 128], F32)
    c.ident = k.sb(es, "c_ident", [128, 128], F32)
    c.identb = k.sb(es, "c_identb", [128, 128], BF16)
    c.caus = k.sb(es, "c_caus", [128, 128], F32)
    c.cpos = k.sb(es, "c_cpos", [128, 896], F32)
    c.big = k.sb(es, "c_big", [128, 896], F32)
    c.sel = k.sb(es, "c_sel", [16, 16, 128], F32)
    k.op("pool", lambda h: h.memset(c.ones[:, :], 1.0), writes=[c.ones])
    k.op("pool", lambda h: h.memset(c.big[:, :], BIG), writes=[c.big])
    k.op("pool", lambda h: h.affine_select(out=c.ident[:, :], in_=c.ones[:, :], pattern=[[-1, 128]],
                                           compare_op=ALU.is_equal, fill=0.0, base=0,
                                           channel_multiplier=1),
         reads=[c.ones], writes=[c.ident])
    k.op("pool", lambda h: h.affine_select(out=c.caus[:, :], in_=c.ones[:, :], pattern=[[1, 128]],
                                           compare_op=ALU.is_ge, fill=0.0, base=0,
                                           channel_multiplier=-1),
         reads=[c.ones], writes=[c.caus])
    k.op("pool", lambda h: h.affine_select(out=c.cpos[:, :], in_=c.big[:, :], pattern=[[-1, 896]],
                                           compare_op=ALU.is_gt, fill=0.0, base=384,
                                           channel_multiplier=1),
         reads=[c.big], writes=[c.cpos])
    k.op("pool", lambda h: h.tensor_copy(out=c.identb[:, :], in_=c.ident[:, :]),
         reads=[c.ident], writes=[c.identb])
    for hh in range(16):
        k.op("dve", lambda h, hh=hh: h.tensor_copy(out=c.sel[:, hh, :],
                                                    in_=c.ident[0:16, hh:hh + 1].to_broadcast([16, 128])),
             reads=[c.ident], writes=[c.sel])
    return c


def rms_to_hT(k, c, x_t, hT, col0, st, ti):
    j = ti % 2
    k.op("act", lambda h: h.activation(out=st.junk[:, :], in_=x_t[:, :], func=AF.Square,
                                       accum_out=st.ss[j][:, :]),
         reads=[x_t], writes=[st.junk, st.ss[j]])
    k.op("act", lambda h: h.activation(out=st.sq[j][:, :], in_=st.ss[j][:, :], func=AF.Sqrt,
                                       scale=1.0 / D, bias=1e-6),
         reads=[st.ss[j]], writes=[st.sq[j]])
    k.op("dve", lambda h: h.reciprocal(out=st.rstd[j][:, :], in_=st.sq[j][:, :]),
         reads=[st.sq[j]], writes=[st.rstd[j]])
    k.op("act", lambda h: h.activation(out=st.xn[j][:, :], in_=x_t[:, :], func=AF.Copy,
                                       scale=st.rstd[j][:, :]),
         reads=[x_t, st.rstd[j]], writes=[st.xn[j]])
    for kc in range(8):
        k.tr(st.pT[j][:, kc * 128:(kc + 1) * 128], st.xn[j][:, kc * 128:(kc + 1) * 128],
             c.identb[:, :], reads=[st.xn[j], c.identb], writes=[st.pT[j]])
    k.op("dve", lambda h: h.tensor_copy(out=hT[:, :, col0:col0 + 128],
                                        in_=st.pT[j][:, :].rearrange("p (k n) -> p k n", k=8)),
         reads=[st.pT[j]], writes=[hT])


class NormState:
    def __init__(self, k, es, tag):
        self.junk = k.sb(es, tag + "junk", [128, D], BF16)
        self.ss = [k.sb(es, f"{tag}ss{i}", [128, 1], F32) for i in range(2)]
        self.sq = [k.sb(es, f"{tag}sq{i}", [128, 1], F32) for i in range(2)]
        self.rstd = [k.sb(es, f"{tag}rstd{i}", [128, 1], F32) for i in range(2)]
        self.xn = [k.sb(es, f"{tag}xn{i}", [128, D], BF16) for i in range(2)]
        self.pT = [k.ps(es, f"{tag}pT{i}", [128, D], BF16) for i in range(2)]


def evac(k, i, out, in_, reads, writes, scale=None, func=None):
    if func is not None or (i % 2 == 0):
        f = func if func is not None else AF.Copy
        sc = 1.0 if scale is None else scale
        return k.op("act", lambda h: h.activation(out=out, in_=in_, func=f, scale=sc),
                    reads=reads, writes=writes)
    if scale is None:
        scale = 1.0
    return k.op("dve", lambda h: h.tensor_scalar(out=out, in0=in_, scalar1=scale, scalar2=None,
                                                  op0=ALU.mult), reads=reads, writes=writes)


def groups():
    gs = [(g * 512, 512) for g in range(8)]
    gs.append((TP, NS * TS))
    return gs


def phase1(k, c, io, scr, rows):
    with contextlib.ExitStack() as es:
        stf = [k.sb(es, f"p1sf{i}", [128, 512], F32) for i in range(2)]
        W = k.sb(es, "p1W", [128, 8, 3592], BF16)
        load_w(k, W, io["w_in_ab"], 8, 3592)
        scale_rows(k, es, W, io["norm_mix"][0, :], 8, "p1g")
        st = NormState(k, es, "p1")
        xts = [k.sb(es, f"p1x{i}", [128, D], F32) for i in range(2)]
        hTs = [k.sb(es, f"p1h{i}", [128, 8, 512], BF16) for i in range(2)]
        pfm = [k.ps(es, f"p1pf{i}", [128, 512]) for i in range(2)]
        ptm = [k.ps(es, f"p1pt{i}", [128, 512]) for i in range(2)]
        stb = [k.sb(es, f"p1sb{i}", [128, 512], BF16) for i in range(4)]
        fm = [(0, scr["qTm"], None), (512, scr["kTm"], 128.0 ** -0.5),
              (2056, scr["qTb"], 0.125), (2568, scr["kTb"], None)]
        ev = 0
        sbi = 0
        sfi = 0
        ti = 0
        gl = groups()
        tis = [0]

        def norm_group(gi):
            tok0, n = gl[gi]
            for t in range(n // 128):
                xt = xts[tis[0] % 2]
                k.dma("sp", xt[:, :], io["xin"][tok0 + t * 128: tok0 + (t + 1) * 128, :], writes=[xt])
                rms_to_hT(k, c, xt, hTs[gi % 2], t * 128, st, tis[0])
                tis[0] += 1

        norm_group(0)
        for gi, (tok0, n) in enumerate(gl):
            hT = hTs[gi % 2]
            if gi + 1 < len(gl):
                norm_group(gi + 1)
            for (c0, dst, scale) in ([] if 'nofm' in DBG else fm):
                for mc in range(4):
                    pp = pfm[ev % 2]
                    for kc in range(8):
                        k.mm(pp[:, 0:n], W[:, kc, c0 + mc * 128: c0 + (mc + 1) * 128], hT[:, kc, 0:n],
                             kc == 0, kc == 7, reads=[W, hT], writes=[pp])
                    sbt = stb[sbi % 4]
                    sbi += 1
                    evac(k, ev, sbt[:, 0:n], pp[:, 0:n], [pp], [sbt], scale=scale)
                    ev += 1
                    k.dma("sp", dst[mc * 128:(mc + 1) * 128, tok0:tok0 + n], sbt[:, 0:n], reads=[sbt])
            for (c0, row) in ([] if 'nogate' in DBG else ((2048, rows["ig"]), (2052, rows["fg"]))):
                pp = pfm[ev % 2]
                for kc in range(8):
                    k.mm(pp[0:4, 0:n], W[:, kc, c0:c0 + 4], hT[:, kc, 0:n], kc == 0, kc == 7,
                         reads=[W, hT], writes=[pp])
                evac(k, 1, row[0:4, tok0:tok0 + n], pp[0:4, 0:n], [pp], [row])
                ev += 1
            for t in range(0 if 'notm' in DBG else n // 128):
                r0 = tok0 + t * 128
                is_out = (r0 >= TP - 512)
                blocks = [(512, scr["kmTok"], 128.0 ** -0.5, None, None),
                          (1024, scr["vmTok"], None, None, None),
                          (1536, scr["osTok"], None, (None if 'nosig' in DBG else AF.Sigmoid), None),
                          (3080, scr["vbTok"], None, None, "bv")]
                if is_out:
                    blocks.append((2568, None, None, None, "bk"))
                for (c0, dst, scale, func, outn) in blocks:
                    pp = ptm[ev % 2]
                    for kc in range(8):
                        k.mm(pp[:, :], hT[:, kc, t * 128:(t + 1) * 128], W[:, kc, c0:c0 + 512],
                             kc == 0, kc == 7, reads=[W, hT], writes=[pp])
                    both = (outn is not None and is_out)
                    if both:
                        sft = stf[sfi % 2]
                        sfi += 1
                        evac(k, 1, sft[:, :], pp[:, :], [pp], [sft])
                        ev += 1
                    if dst is not None:
                        sbt = stb[sbi % 4]
                        sbi += 1
                        if both:
                            k.op("act", lambda h, sbt=sbt, sft=sft: h.activation(out=sbt[:, :], in_=sft[:, :], func=AF.Copy),
                                 reads=[sft], writes=[sbt])
                        else:
                            evac(k, ev, sbt[:, :], pp[:, :], [pp], [sbt], scale=scale, func=func)
                        ev += 1
                        k.dma("sp", dst[r0:r0 + 128, :], sbt[:, :], reads=[sbt])
                    if both:
                        for hh in range(0 if ('noodma2' in DBG or 'whole' in DBG) else 8):
                            src = sft[:, hh * 64:(hh + 1) * 64]
                            if r0 < TP:
                                q0 = r0 - (TP - 512)
                                if 'toscr' in DBG:
                                    k.dma("sp", scr["x1"][q0:q0 + 128, hh * 64:(hh + 1) * 64], src, reads=[sft])
                                else:
                                    k.dma("pool" if 'opool' in DBG else "sp", io["p" + outn][hh, q0:q0 + 128, :], src, reads=[sft])
                            else:
                                for s2 in range(2):
                                    sq = (r0 - TP) // TS + s2
                                    k.dma("sp", io["s" + outn][sq, hh, :, :], src[s2 * 64:(s2 + 1) * 64],
                                          reads=[sft])
    k.barrier()


IN_SPECS = [
    ("xin", [R, D]), ("mC0", [NS, 4, 128, 128]), ("mn0", [NS, 4, 128]), ("mm0", [NS, 4]),
    ("cbk", [NS, 8, 512, 64]), ("cbv", [NS, 8, 512, 64]),
    ("cfk", [NS, 16, 2048, 64]), ("cfv", [NS, 16, 2048, 64]), ("cflf", [NS, 16, 2048]),
    ("norm_mix", [2, D]), ("norm_ffn", [2, D]), ("norm_final", [D]),
    ("w_in_ab", [D, 3592]), ("b_gate_ab", [8]), ("mlstm_gain", [512]), ("bandbias", [8, 128, 640]),
    ("bandmask", [128, 640]),
    ("w_out_ab", [D, D]), ("w_in_fox", [D, 3088]), ("b_fox_f", [16]), ("w_out_fox", [D, D]),
    ("w_ffn_in", [2, D, 2 * HID]), ("w_ffn_out", [2, HID, D]),
]
OUT_SPECS = [
    ("y", [R, D]), ("pC", [4, 128, 128]), ("pn", [4, 128]), ("pm", [4, 1]),
    ("pbk", [8, 512, 64]), ("pbv", [8, 512, 64]),
    ("pfk", [16, TP, 64]), ("pfv", [16, TP, 64]), ("pflf", [16, TP]),
    ("sC", [NS, 4, 128, 128]), ("sn", [NS, 4, 128]), ("sm", [NS, 4, 1]),
    ("sbk", [NS, 8, TS, 64]), ("sbv", [NS, 8, TS, 64]),
    ("sfk", [NS, 16, TS, 64]), ("sfv", [NS, 16, TS, 64]), ("sflf", [NS, 16, TS]),
]
SCR_SPECS = [
    ("qTm", [512, R], BF16), ("kTm", [512, R], BF16), ("qTb", [512, R], BF16), ("kTb", [512, R], BF16),
    ("kmTok", [R, 512], BF16), ("vmTok", [R, 512], BF16), ("osTok", [R, 512], BF16),
    ("vbTok", [R, 512], BF16), ("catT", [D, R], BF16),
    ("x1", [R, D], F32), ("x2", [R, D], F32), ("x3", [R, D], F32),
    ("qTf", [D, R], BF16), ("kTf", [D, R], BF16), ("vfTok", [R, D], BF16), ("attT", [D, R], BF16),
    ("hidT", [HID, R], BF16), ("rds", [4, 512], F32),
]

PHASES = 99
DBG = ""


def run_phases(k, c, io, scr, upto=None):
    upto = PHASES if upto is None else upto
    with contextlib.ExitStack() as es1:
        rows = {"ig": k.sb(es1, "row_ig", [4, R], F32), "fg": k.sb(es1, "row_fg", [4, R], F32)}
        pers = {"GW": k.sb(es1, "GW", [4, 2, R], F32), "COLS": k.sb(es1, "COLS", [128, NCH, 3, 4], F32),
                "DECB": k.sb(es1, "DECB", [128, 4, NCH], F32)}
        phase1(k, c, io, scr, rows)
        if upto >= 2:
            phase2(k, c, io, scr, rows, pers)
        if upto >= 3:
            phase3(k, c, io, scr, pers)
    if upto >= 4:
        phase4(k, c, io, scr)
    if upto >= 6:
        with contextlib.ExitStack() as esw:
            Wg = ffn_w_alloc(k, esw, "p6")
            outproj_phase(k, c, io, scr, scr["catT"], io["w_out_ab"], io["xin"], scr["x1"], "p5",
                          prefetch=lambda: ffn_w_load(k, io, 0, Wg))
            ffn_in_phase(k, c, io, scr, 0, scr["x1"], "p6", W=Wg[0])
        ffn_out_phase(k, c, io, scr, 0, scr["x1"], scr["x2"], False, "p7")
    if upto >= 8:
        fox_phases(k, c, io, scr)
    if upto >= 9:
        with contextlib.ExitStack() as esw:
            Wg = ffn_w_alloc(k, esw, "pa")
            outproj_phase(k, c, io, scr, scr["attT"], io["w_out_fox"], scr["x2"], scr["x3"], "p9",
                          prefetch=lambda: ffn_w_load(k, io, 1, Wg))
            ffn_in_phase(k, c, io, scr, 1, scr["x3"], "pa", W=Wg[0])
        ffn_out_phase(k, c, io, scr, 1, scr["x3"], io["y"], True, "pb")


def build():
    nc = bass.Bass("TRN2", target_bir_lowering=False)
    io = {}
    for name, shape in IN_SPECS:
        io[name] = nc.dram_tensor(name, shape, F32, kind="ExternalInput").ap()
    for name, shape in OUT_SPECS:
        io[name] = nc.dram_tensor(name, shape, F32, kind="ExternalOutput").ap()
    scr = {}
    for name, shape, dt in SCR_SPECS:
        scr[name] = nc.dram_tensor("scr_" + name, shape, dt).ap()
    with contextlib.ExitStack() as es:
        k = KB(nc, es)
        c = make_consts(k, es)
        run_phases(k, c, io, scr)
        k.barrier()
    return nc


_NC_CACHE = {}


def kernel(**inp):
    f = lambda a: np.ascontiguousarray(np.asarray(a, dtype=np.float32))
    xp, xs = f(inp["x_prompt"]), f(inp["x_sample"])
    kk = np.arange(128)[:, None]
    qq = np.arange(640)[None, :]
    rel = qq - kk
    idx = np.clip(rel, -63, 128) + 63
    table = f(inp["rel_bias_table"])[0]
    bandbias = np.ascontiguousarray(table[:, idx])
    dch = (qq // 64) - (kk // 64)
    bandmask = np.where((dch >= 0) & (dch <= 8), 0.0, -BIG).astype(np.float32)
    common = {
        "norm_mix": f(inp["norm_mix"]), "norm_ffn": f(inp["norm_ffn"]), "norm_final": f(inp["norm_final"]),
        "w_in_ab": f(inp["w_in_ab"])[0], "b_gate_ab": f(inp["b_gate_ab"])[0],
        "mlstm_gain": f(inp["mlstm_gain"])[0], "bandbias": bandbias, "bandmask": bandmask,
        "w_out_ab": f(inp["w_out_ab"])[0], "w_in_fox": f(inp["w_in_fox"])[0],
        "b_fox_f": f(inp["b_fox_f"])[0], "w_out_fox": f(inp["w_out_fox"])[0],
        "w_ffn_in": f(inp["w_ffn_in"]), "w_ffn_out": f(inp["w_ffn_out"]),
    }
    in_maps = []
    for cid in range(8):
        b = cid % 4
        s0 = 4 * cid
        m = dict(common)
        m["xin"] = np.ascontiguousarray(np.concatenate([xp[b], xs[s0:s0 + 4].reshape(NS * TS, D)], axis=0))
        m["mC0"] = f(inp["state_mlstm_C"])[0, s0:s0 + 4]
        m["mn0"] = f(inp["state_mlstm_n"])[0, s0:s0 + 4]
        m["mm0"] = f(inp["state_mlstm_m"])[0, s0:s0 + 4]
        m["cbk"] = f(inp["cache_band_k"])[0, s0:s0 + 4]
        m["cbv"] = f(inp["cache_band_v"])[0, s0:s0 + 4]
        m["cfk"] = f(inp["cache_fox_k"])[0, s0:s0 + 4]
        m["cfv"] = f(inp["cache_fox_v"])[0, s0:s0 + 4]
        m["cflf"] = f(inp["cache_fox_logf"])[0, s0:s0 + 4]
        in_maps.append({kk_: np.ascontiguousarray(v) for kk_, v in m.items()})
    if "nc" not in _NC_CACHE:
        _NC_CACHE["nc"] = build()
    res = run_bass_kernel_spmd(_NC_CACHE["nc"], in_maps, core_ids=list(range(8)))
    rs = res.results
    P = lambda n: np.stack([rs[b][n] for b in range(4)], axis=0)
    S = lambda n: np.concatenate([rs[cid][n] for cid in range(8)], axis=0)
    y_prompt = np.stack([rs[b]["y"][:TP] for b in range(4)], axis=0)
    y_sample = np.concatenate([rs[cid]["y"][TP:].reshape(NS, TS, D) for cid in range(8)], axis=0)
    outs = (
        y_prompt, y_sample,
        P("pC")[None], P("pn")[None], P("pm")[None, :, :, 0],
        P("pbk")[None], P("pbv")[None], P("pfk")[None], P("pfv")[None], P("pflf")[None],
        S("sC")[None], S("sn")[None], S("sm")[None, :, :, 0],
        S("sbk")[None], S("sbv")[None], S("sfk")[None], S("sfv")[None], S("sflf")[None],
    )
    return tuple(np.ascontiguousarray(o, dtype=np.float32) for o in outs)


def chunks():
    cs = [(c * 128, 128) for c in range(32)]
    cs += [(TP + i * TS, TS) for i in range(NS)]
    return cs


NCH = 36


def phase2(k, c, io, scr, rows, pers):
    GW, COLS, DECB = pers["GW"], pers["COLS"], pers["DECB"]
    ig, fg = rows["ig"], rows["fg"]
    with contextlib.ExitStack() as es:
        negb = k.sb(es, "p2negb", [4, 1], F32)
        bigc = k.sb(es, "p2big", [4, 1], F32)
        m0 = k.sb(es, "p2m0", [4, NS], F32)
        NF = k.sb(es, "p2NF", [4, R], F32)
        WS = k.sb(es, "p2WS", [4, R], F32)
        EM = k.sb(es, "p2EM", [4, R], F32)
        DEC = k.sb(es, "p2DEC", [4, NCH], F32)
        PC = k.ps(es, "p2PC", [128, NCH, 3, 4])
        PD = k.ps(es, "p2PD", [128, 4, NCH])
        bg = io["b_gate_ab"]
        k.dma("sp", negb[:, :], bg[4:8].rearrange("(h o) -> h o", o=1), writes=[negb])
        k.dma("sp", bigc[:, :], bg[0:4].rearrange("(h o) -> h o", o=1), writes=[bigc])
        k.dma("sp", m0[:, :], io["mm0"].rearrange("s h -> h s"), writes=[m0], allow_slow_non_contiguous=True)
        k.op("dve", lambda h: h.tensor_scalar(out=negb[:, :], in0=negb[:, :], scalar1=-1.0, scalar2=None,
                                              op0=ALU.mult), writes=[negb])
        k.op("act", lambda h: h.activation(out=fg[:, :], in_=fg[:, :], func=AF.Exp, scale=-1.0,
                                           bias=negb[:, :]), reads=[negb], writes=[fg])
        k.op("act", lambda h: h.activation(out=fg[:, :], in_=fg[:, :], func=AF.Ln, bias=1.0), writes=[fg])
        segs = [(0, TP, None)] + [(TP + i * TS, TS, i) for i in range(NS)]
        for (s0, sn, si) in segs:
            k.op("dve", lambda h, s0=s0, sn=sn: h.tensor_tensor_scan(
                out=NF[:, s0:s0 + sn], data0=fg[:, s0:s0 + sn], data1=fg[:, s0:s0 + sn], initial=0.0,
                op0=ALU.add, op1=ALU.max), reads=[fg], writes=[NF])
        k.op("dve", lambda h: h.scalar_tensor_tensor(out=ig[:, :], in0=ig[:, :], scalar=bigc[:, :],
                                                     in1=NF[:, :], op0=ALU.add, op1=ALU.add),
             reads=[bigc, NF], writes=[ig])
        for (s0, sn, si) in segs:
            init = 0.0 if si is None else m0[:, si:si + 1]
            k.op("dve", lambda h, s0=s0, sn=sn, init=init: h.tensor_tensor_scan(
                out=GW[:, 0, s0:s0 + sn], data0=ig[:, s0:s0 + sn], data1=ig[:, s0:s0 + sn], initial=init,
                op0=ALU.max, op1=ALU.max), reads=[ig, m0], writes=[GW])
        Gp = GW[:, 0, 0:TP].rearrange("p (c t) -> p c t", t=128)
        k.op("pool", lambda h: h.memset(fg[:, 0:128], 0.0), writes=[fg])
        k.op("dve", lambda h: h.tensor_copy(
            out=fg[:, 128:TP].rearrange("p (c t) -> p c t", t=128),
            in_=Gp[:, 0:31, 127:128].to_broadcast([4, 31, 128])), reads=[GW], writes=[fg])
        for i in range(NS):
            s0 = TP + i * TS
            k.op("dve", lambda h, s0=s0, i=i: h.tensor_copy(out=fg[:, s0:s0 + TS],
                                                          in_=m0[:, i:i + 1].to_broadcast([4, TS])),
                 reads=[m0], writes=[fg])
        k.op("dve", lambda h: h.tensor_tensor(out=WS[:, :], in0=fg[:, :], in1=GW[:, 0, :], op=ALU.subtract),
             reads=[fg, GW], writes=[WS])
        k.op("act", lambda h: h.activation(out=GW[:, 1, :], in_=WS[:, :], func=AF.Exp), reads=[WS], writes=[GW])
        k.op("dve", lambda h: h.tensor_copy(
            out=WS[:, 0:TP].rearrange("p (c t) -> p c t", t=128),
            in_=Gp[:, :, 127:128].to_broadcast([4, 32, 128])), reads=[GW], writes=[WS])
        for i in range(NS):
            s0 = TP + i * TS
            k.op("dve", lambda h, s0=s0: h.tensor_copy(
                out=WS[:, s0:s0 + TS], in_=GW[:, 0, s0 + TS - 1:s0 + TS].to_broadcast([4, TS])),
                reads=[GW], writes=[WS])
        k.op("dve", lambda h: h.tensor_tensor(
            out=DEC[:, 0:32], in0=fg[:, 0:TP].rearrange("p (c t) -> p c t", t=128)[:, :, 0],
            in1=WS[:, 0:TP].rearrange("p (c t) -> p c t", t=128)[:, :, 0], op=ALU.subtract),
            reads=[fg, WS], writes=[DEC])
        for i in range(NS):
            s0 = TP + i * TS
            k.op("dve", lambda h, s0=s0, i=i: h.tensor_tensor(out=DEC[:, 32 + i:33 + i], in0=fg[:, s0:s0 + 1],
                                                            in1=WS[:, s0:s0 + 1], op=ALU.subtract),
                 reads=[fg, WS], writes=[DEC])
        k.op("act", lambda h: h.activation(out=DEC[:, :], in_=DEC[:, :], func=AF.Exp), writes=[DEC])
        k.op("dve", lambda h: h.tensor_tensor(out=WS[:, :], in0=ig[:, :], in1=WS[:, :], op=ALU.subtract),
             reads=[ig], writes=[WS])
        k.op("act", lambda h: h.activation(out=WS[:, :], in_=WS[:, :], func=AF.Exp), writes=[WS])
        k.op("dve", lambda h: h.tensor_tensor(out=EM[:, :], in0=GW[:, 0, :], in1=NF[:, :], op=ALU.subtract),
             reads=[GW, NF], writes=[EM])
        k.dma("sp", io["pm"][:, :], EM[:, TP - 1:TP], reads=[EM])
        for i in range(NS):
            e1 = TP + (i + 1) * TS
            k.dma("sp", io["sm"][i, :, :], EM[:, e1 - 1:e1], reads=[EM])
        k.op("act", lambda h: h.activation(out=EM[:, :], in_=EM[:, :], func=AF.Exp, scale=-1.0), writes=[EM])
        for ci, (r0, n) in enumerate(chunks()):
            for xi, X in enumerate((ig, WS, EM)):
                k.tr(PC[0:n, ci, xi, :], X[0:4, r0:r0 + n], c.ident[0:4, 0:4], reads=[X, c.ident], writes=[PC])
        k.op("dve", lambda h: h.tensor_copy(out=COLS[:, 0:32, :, :], in_=PC[:, 0:32, :, :]), reads=[PC], writes=[COLS])
        k.op("dve", lambda h: h.tensor_copy(out=COLS[0:64, 32:NCH, :, :], in_=PC[0:64, 32:NCH, :, :]), reads=[PC], par=[COLS])
        for hh in range(4):
            k.mm(PD[:, hh, :], c.sel[0:4, hh, :], DEC[0:4, :], True, True, reads=[c.sel, DEC], writes=[PD])
        k.op("dve", lambda h: h.tensor_copy(out=DECB[:, :, :], in_=PD[:, :, :]), reads=[PD], writes=[DECB])
    k.barrier()


def phase3(k, c, io, scr, pers):
    GW, COLS, DECB = pers["GW"], pers["COLS"], pers["DECB"]
    with contextlib.ExitStack() as es:
        gain = k.sb(es, "p3gain", [128, 512], F32)
        k.dma("sp", gain[:, :], io["mlstm_gain"].partition_broadcast(128), writes=[gain])
        qTs = [k.sb(es, f"p3q{i}", [128, 4, 128], BF16) for i in range(2)]
        kTs = [k.sb(es, f"p3k{i}", [128, 4, 128], BF16) for i in range(2)]
        kts = [k.sb(es, f"p3kt{i}", [128, 4, 128], BF16) for i in range(2)]
        vas = [k.sb(es, f"p3va{i}", [128, 4, 129], BF16) for i in range(2)]
        oss = [k.sb(es, f"p3os{i}", [128, 512], BF16) for i in range(3)]
        for i in range(2):
            k.op("pool", lambda h, i=i: h.memset(vas[i][:, :, 128:129], 1.0), writes=[vas[i]])
        Cf = [k.sb(es, f"p3Cf{h}", [128, 129], F32) for h in range(4)]
        Cb = [k.sb(es, f"p3Cb{h}", [128, 129], BF16) for h in range(4)]
        E = k.sb(es, "p3E", [128, 4, 128], F32)
        WT = k.sb(es, "p3WT", [128, 4, 128], BF16)
        QS = k.sb(es, "p3QS", [128, 4, 128], BF16)
        KW = k.sb(es, "p3KW", [128, 4, 128], BF16)
        ONs = [k.sb(es, f"p3ON{i}", [128, 4, 129], F32) for i in range(2)]
        H = k.sb(es, "p3H", [128, 4, 128], F32)
        SQ = k.sb(es, "p3SQ", [128, 4, 128], F32)
        OAs = [k.sb(es, f"p3OA{i}", [128, 512], BF16) for i in range(2)]
        G2s = [k.sb(es, f"p3G2{i}", [128, 512], F32) for i in range(2)]
        junk = k.sb(es, "p3junk", [128, 128], F32)
        sm2 = [k.sb(es, f"p3t{i}", [128, 4], F32) for i in range(4)]
        OT = k.sb(es, "p3OT", [128, 4, 128], BF16)
        sm = [k.sb(es, f"p3s{i}", [128, 4], F32) for i in range(6)]
        PS_S = k.ps(es, "p3PS", [128, 4, 128])
        PS_G = k.ps(es, "p3PG", [128, 4, 128])
        PS_W = k.ps(es, "p3PW", [128, 4, 128])
        PS_O = [k.ps(es, f"p3PO{i}", [128, 2, 129]) for i in range(2)]
        PS_D = [k.ps(es, f"p3PD{i}", [128, 2, 129]) for i in range(2)]
        PS_T = k.ps(es, "p3PT", [128, 4, 128], BF16)
        qv = scr["qTm"].rearrange("(h d) r -> d h r", h=4)
        kv = scr["kTm"].rearrange("(h d) r -> d h r", h=4)
        catv = scr["catT"][0:512, :].rearrange("(h d) r -> d h r", h=4)
        chs = chunks()

        def load(ci):
            r0, n = chs[ci]
            j = ci % 2
            k.dma("sp", qTs[j][:, :, 0:n], qv[:, :, r0:r0 + n], writes=[qTs[j]])
            k.dma("sp", kTs[j][:, :, 0:n], kv[:, :, r0:r0 + n], writes=[kTs[j]])
            k.dma("sp", kts[j][0:n, :, :], scr["kmTok"][r0:r0 + n, :].rearrange("t (h e) -> t h e", h=4),
                  writes=[kts[j]])
            k.dma("sp", vas[j][0:n, :, 0:128], scr["vmTok"][r0:r0 + n, :].rearrange("t (h e) -> t h e", h=4),
                  writes=[vas[j]])
            k.dma("sp", oss[ci % 3][0:n, :], scr["osTok"][r0:r0 + n, :], writes=[oss[ci % 3]])

        load(0)
        pendA, pendA_b, pendB = [], [], []
        for ci, (r0, n) in enumerate(chs):
            j = ci % 2
            qT, kT, kt, va, osg = qTs[j], kTs[j], kts[j], vas[j], oss[ci % 3]
            ON = ONs[ci % 2]
            if ci + 1 < NCH:
                load(ci + 1)
            if ci == 0:
                for hh in range(4):
                    k.op("pool", lambda h, hh=hh: h.memset(Cf[hh][:, :], 0.0), writes=[Cf[hh]])
                    k.op("pool", lambda h, hh=hh: h.memset(Cb[hh][:, :], 0.0), writes=[Cb[hh]])
            elif ci >= 32:
                si = ci - 32
                for hh in range(4):
                    k.dma("sp", Cf[hh][:, 0:128], io["mC0"][si, hh, :, :], writes=[Cf[hh]])
                    k.dma("sp", Cf[hh][:, 128:129], io["mn0"][si, hh, :].rearrange("(d o) -> d o", o=1),
                          par=[Cf[hh]])
                    k.op("act", lambda h, hh=hh: h.activation(out=Cb[hh][:, :], in_=Cf[hh][:, :], func=AF.Copy),
                         reads=[Cf[hh]], writes=[Cb[hh]])
            for hh in range(4):
                k.mm(PS_S[0:n, hh, 0:n], kT[:, hh, 0:n], qT[:, hh, 0:n], True, True, reads=[kT, qT], writes=[PS_S])
            for hh in range(4):
                k.mm(PS_G[0:n, hh, 0:n], c.sel[0:4, hh, 0:n], GW[0:4, 0, r0:r0 + n], True, True,
                     reads=[c.sel, GW], writes=[PS_G])
            for hh in range(4):
                k.mm(PS_W[:, hh, 0:n], c.sel[0:4, hh, :], GW[0:4, 1, r0:r0 + n], True, True,
                     reads=[c.sel, GW], writes=[PS_W])
            for hh in range(4):
                k.op("act", lambda h, hh=hh: h.activation(out=E[0:n, hh, 0:n], in_=PS_G[0:n, hh, 0:n], func=AF.Exp,
                                                        scale=-1.0, bias=COLS[0:n, ci, 0, hh:hh + 1]),
                     reads=[PS_G, COLS], writes=[E])
            for hh in range(4):
                k.op("pool", lambda h, hh=hh: h.tensor_tensor(out=E[0:n, hh, 0:n], in0=E[0:n, hh, 0:n],
                                                            in1=c.caus[0:n, 0:n], op=ALU.mult),
                     reads=[c.caus], writes=[E])
            k.op("dve", lambda h: h.tensor_tensor(out=WT[0:n, :, 0:n], in0=PS_S[0:n, :, 0:n], in1=E[0:n, :, 0:n],
                                                  op=ALU.mult), reads=[PS_S, E], writes=[WT])
            k.op("dve", lambda h: h.tensor_tensor(out=QS[:, :, 0:n], in0=PS_W[:, :, 0:n], in1=qT[:, :, 0:n],
                                                  op=ALU.mult), reads=[PS_W, qT], writes=[QS])
            for hh in range(4):
                po = PS_O[hh // 2]
                k.mm(po[0:n, hh % 2, :], QS[:, hh, 0:n], Cb[hh][:, :], True, False, reads=[QS, Cb[hh]], writes=[po])
                k.mm(po[0:n, hh % 2, :], WT[0:n, hh, 0:n], va[0:n, hh, :], False, True, reads=[WT, va], writes=[po])
            for i2 in range(2):
                k.op("dve", lambda h, i2=i2: h.tensor_copy(out=ON[0:n, 2 * i2:2 * i2 + 2, :], in_=PS_O[i2][0:n, :, :]),
                     reads=[PS_O[i2]], writes=[ON])
            for hh in range(4):
                k.op("act", lambda h, hh=hh: h.activation(out=KW[0:n, hh, :], in_=kt[0:n, hh, :], func=AF.Copy,
                                                        scale=COLS[0:n, ci, 1, hh:hh + 1]),
                     reads=[kt, COLS], writes=[KW])
            for hh in range(4):
                pd = PS_D[hh // 2]
                k.mm(pd[:, hh % 2, :], KW[0:n, hh, :], va[0:n, hh, :], True, True, reads=[KW, va], writes=[pd])
            for hh in range(4):
                pd = PS_D[hh // 2]
                k.op("dve", lambda h, hh=hh, pd=pd: h.scalar_tensor_tensor(
                    out=Cf[hh][:, :], in0=Cf[hh][:, :], scalar=DECB[:, hh, ci:ci + 1], in1=pd[:, hh % 2, :],
                    op0=ALU.mult, op1=ALU.add), reads=[DECB, pd], writes=[Cf[hh]])
                k.op("act", lambda h, hh=hh: h.activation(out=Cb[hh][:, :], in_=Cf[hh][:, :], func=AF.Copy),
                     reads=[Cf[hh]], writes=[Cb[hh]])
            if ci == 31 or ci >= 32:
                for hh in range(4):
                    if ci == 31:
                        oc, on = io["pC"][hh, :, :], io["pn"][hh, :]
                    else:
                        oc, on = io["sC"][ci - 32, hh, :, :], io["sn"][ci - 32, hh, :]
                    k.dma("sp", oc, Cf[hh][:, 0:128], reads=[Cf[hh]])
                    k.dma("sp", on.rearrange("(d o) -> d o", o=1), Cf[hh][:, 128:129], reads=[Cf[hh]])
            G2 = G2s[ci % 2]
            k.op("pool", lambda h, G2=G2, osg=osg: h.tensor_tensor(out=G2[0:n, :], in0=gain[0:n, :], in1=osg[0:n, :],
                                                                   op=ALU.mult), reads=[gain, osg], writes=[G2])

            def epiA(ci=ci, n=n, ON=ON, G2=G2, OAb=OAs[ci % 2]):
                den, dd, rr, mu, var, rs = sm
                s1, s2, aa, bb = sm2
                k.op("dve", lambda h: h.tensor_tensor(out=dd[0:n, :], in0=ON[0:n, :, 128], in1=COLS[0:n, ci, 2, :],
                                                      op=ALU.max), reads=[ON, COLS], writes=[dd])
                k.op("dve", lambda h: h.tensor_scalar(out=den[0:n, :], in0=ON[0:n, :, 128], scalar1=-1.0, scalar2=None,
                                                      op0=ALU.mult), reads=[ON], writes=[den])
                k.op("dve", lambda h: h.tensor_tensor(out=dd[0:n, :], in0=dd[0:n, :], in1=den[0:n, :],
                                                      op=ALU.max), reads=[den], writes=[dd])
                k.op("dve", lambda h: h.reciprocal(out=rr[0:n, :], in_=dd[0:n, :]), reads=[dd], writes=[rr])
                for hh in range(4):
                    k.op("act", lambda h, hh=hh: h.activation(out=junk[0:n, :], in_=ON[0:n, hh, 0:128], func=AF.Copy,
                                                            scale=rr[0:n, hh:hh + 1], accum_out=s1[0:n, hh:hh + 1]),
                         reads=[ON, rr], writes=[junk, s1])
                    k.op("act", lambda h, hh=hh: h.activation(out=junk[0:n, :], in_=ON[0:n, hh, 0:128], func=AF.Square,
                                                            scale=rr[0:n, hh:hh + 1], accum_out=s2[0:n, hh:hh + 1]),
                         reads=[ON, rr], writes=[junk, s2])
                k.op("dve", lambda h: h.tensor_scalar(out=mu[0:n, :], in0=s1[0:n, :], scalar1=1.0 / 128, scalar2=None,
                                                      op0=ALU.mult), reads=[s1], writes=[mu])
                k.op("dve", lambda h: h.tensor_tensor(out=var[0:n, :], in0=mu[0:n, :], in1=mu[0:n, :], op=ALU.mult),
                     reads=[mu], writes=[var])
                k.op("dve", lambda h: h.scalar_tensor_tensor(out=var[0:n, :], in0=s2[0:n, :], scalar=1.0 / 128,
                                                             in1=var[0:n, :], op0=ALU.mult, op1=ALU.subtract),
                     reads=[s2, var], writes=[var])
                k.op("dve", lambda h: h.tensor_scalar(out=var[0:n, :], in0=var[0:n, :], scalar1=0.0, scalar2=None,
                                                      op0=ALU.max), reads=[var], writes=[var])
                k.op("act", lambda h: h.activation(out=rs[0:n, :], in_=var[0:n, :], func=AF.Ln, bias=1e-6),
                     reads=[var], writes=[rs])
                k.op("act", lambda h: h.activation(out=rs[0:n, :], in_=rs[0:n, :], func=AF.Exp, scale=-0.5),
                     reads=[rs], writes=[rs])
                k.op("dve", lambda h: h.tensor_tensor(out=aa[0:n, :], in0=rr[0:n, :], in1=rs[0:n, :], op=ALU.mult),
                     reads=[rr, rs], writes=[aa])
                k.op("dve", lambda h: h.scalar_tensor_tensor(out=bb[0:n, :], in0=mu[0:n, :], scalar=-1.0, in1=rs[0:n, :],
                                                             op0=ALU.mult, op1=ALU.mult), reads=[mu, rs], writes=[bb])
                for hh in range(4):
                    k.op("pool", lambda h, hh=hh: h.tensor_scalar(out=H[0:n, hh, :], in0=ON[0:n, hh, 0:128],
                                                                scalar1=aa[0:n, hh:hh + 1], scalar2=bb[0:n, hh:hh + 1],
                                                                op0=ALU.mult, op1=ALU.add),
                         reads=[ON, aa, bb], writes=[H])
                k.op("dve", lambda h: h.tensor_tensor(out=OAb[0:n, :], in0=H[0:n, :, :].rearrange("p h e -> p (h e)"),
                                                      in1=G2[0:n, :], op=ALU.mult), reads=[H, G2], writes=[OAb])

            def epiB(r0=r0, n=n, OAb=OAs[ci % 2]):
                for hh in range(4):
                    k.tr(PS_T[:, hh, 0:n], OAb[0:n, hh * 128:(hh + 1) * 128], c.identb[0:n, 0:n],
                         reads=[OAb, c.identb], writes=[PS_T])
                k.op("act", lambda h: h.activation(out=OT[:, :, 0:n], in_=PS_T[:, :, 0:n], func=AF.Copy),
                     reads=[PS_T], writes=[OT])
                k.dma("sp", catv[:, :, r0:r0 + n], OT[:, :, 0:n], reads=[OT])
            if pendB:
                pendB.pop(0)()
            if pendA:
                pendA.pop(0)()
                pendB.append(pendA_b.pop(0))
            pendA.append(epiA)
            pendA_b.append(epiB)
        while pendA:
            pendA.pop(0)()
            pendB.append(pendA_b.pop(0))
        while pendB:
            pendB.pop(0)()
    k.barrier()


def attn_epilogue(k, c, PS_OT, nq, OS, RD, PS_BC, out_ap, out_t):
    k.op("dve", lambda h: h.tensor_copy(out=OS[0:65, 0:nq], in_=PS_OT[0:65, 0:nq]), reads=[PS_OT], writes=[OS])
    k.op("act", lambda h: h.activation(out=RD[64:65, 0:nq], in_=OS[64:65, 0:nq], func=AF.Ln), reads=[OS], writes=[RD])
    k.op("act", lambda h: h.activation(out=RD[64:65, 0:nq], in_=RD[64:65, 0:nq], func=AF.Exp, scale=-1.0),
         reads=[RD], writes=[RD])
    k.mm(PS_BC[0:64, 0:nq], c.ones[64:65, 0:64], RD[64:65, 0:nq], True, True, reads=[c.ones, RD], writes=[PS_BC])
    k.op("dve", lambda h: h.tensor_tensor(out=out_ap, in0=OS[0:64, 0:nq], in1=PS_BC[0:64, 0:nq], op=ALU.mult),
         reads=[OS, PS_BC], writes=[out_t])


def phase4(k, c, io, scr):
    with contextlib.ExitStack() as es:
        kTh = [k.sb(es, f"p4k{i}", [128, R], BF16) for i in range(2)]
        qTh = [k.sb(es, f"p4q{i}", [128, R], BF16) for i in range(2)]
        for t_ in kTh + qTh:
            k.op("pool", lambda h, t_=t_: h.memset(t_[64:128, :], 0.0), writes=[t_])
        vbh = [k.sb(es, f"p4v{i}", [128, NT, 65], BF16) for i in range(2)]
        bias = [k.sb(es, f"p4b{i}", [128, 640], F32) for i in range(2)]
        mask = k.sb(es, "p4mask", [128, 640], F32)
        ATT = [k.sb(es, f"p4att{i}", [64, R], BF16) for i in range(2)]
        TMP = [k.sb(es, f"p4tmp{i}", [128, 640], F32) for i in range(2)]
        P = [k.sb(es, f"p4P{i}", [128, 640], BF16) for i in range(2)]
        OS = k.sb(es, "p4OS", [65, 128], F32)
        RD = k.sb(es, "p4RD", [65, 128], F32)
        KC = k.sb(es, "p4KC", [128, 4, 64], BF16)
        KTC = k.sb(es, "p4KTC", [128, 512], BF16)
        k.op("pool", lambda h: h.memset(KTC[64:128, :], 0.0), writes=[KTC])
        VC = k.sb(es, "p4VC", [128, 4, 65], BF16)
        VN = k.sb(es, "p4VN", [64, 65], BF16)
        PS_A = k.ps(es, "p4PA", [128, 512])
        PS_B = k.ps(es, "p4PB", [128, 512])
        PS_OT = [k.ps(es, f"p4PO{i}", [128, 512]) for i in range(2)]
        PS_BC = k.ps(es, "p4PBC", [128, 512])
        PS_KT = k.ps(es, "p4PKT", [64, 4, 128], BF16)
        k.dma("sp", mask[:, :], io["bandmask"][:, :], writes=[mask])
        for i in range(2):
            k.op("pool", lambda h, i=i: h.memset(vbh[i][:, :, 64:65], 1.0), writes=[vbh[i]])
        k.op("pool", lambda h: h.memset(VC[:, :, 64:65], 1.0), writes=[VC])
        k.op("pool", lambda h: h.memset(VN[:, 64:65], 1.0), writes=[VN])
        vview = scr["vbTok"].rearrange("(t p) c -> p t c", p=128)

        def load(hh):
            j = hh % 2
            k.dma("sp", kTh[j][0:64, :], scr["kTb"][hh * 64:(hh + 1) * 64, :], writes=[kTh[j]])
            k.dma("sp", qTh[j][0:64, :], scr["qTb"][hh * 64:(hh + 1) * 64, :], writes=[qTh[j]])
            k.dma("sp", vbh[j][:, :, 0:64], vview[:, :, hh * 64:(hh + 1) * 64], writes=[vbh[j]])
            k.dma("sp", bias[j][:, :], io["bandbias"][hh, :, :], writes=[bias[j]])
            k.op("pool", lambda h: h.tensor_tensor(out=bias[j][:, :], in0=bias[j][:, :], in1=mask[:, :], op=ALU.add),
                 reads=[mask], writes=[bias[j]])

        load(0)
        it = 0
        for hh in range(8):
            j = hh % 2
            kT, qT, vb, bs, att = kTh[j], qTh[j], vbh[j], bias[j], ATT[j]
            if hh + 1 < 8:
                load(hh + 1)
            pendb = []
            for i in range(32):
                nj = min(i, 4) + 1
                tmp, p, pot = TMP[it % 2], P[it % 2], PS_OT[it % 2]
                it += 1
                for s in range(nj):
                    jj = i - s
                    dst = PS_A[:, s * 128:(s + 1) * 128] if s < 4 else PS_B[:, 0:128]
                    k.mm(dst, kT[:, jj * 128:(jj + 1) * 128], qT[:, i * 128:(i + 1) * 128], True, True,
                         reads=[kT, qT], writes=[PS_A if s < 4 else PS_B])
                na = min(nj, 4) * 128
                k.op("dve", lambda h, na=na, tmp=tmp: h.tensor_tensor(out=tmp[:, 0:na], in0=PS_A[:, 0:na],
                                                                     in1=bs[:, 0:na], op=ALU.add),
                     reads=[PS_A, bs], writes=[tmp])
                if nj == 5:
                    k.op("dve", lambda h, tmp=tmp: h.tensor_tensor(out=tmp[:, 512:640], in0=PS_B[:, 0:128],
                                                                  in1=bs[:, 512:640], op=ALU.add),
                         reads=[PS_B, bs], writes=[tmp])
                k.op("act", lambda h, tmp=tmp, p=p, nj=nj: h.activation(out=p[:, 0:nj * 128], in_=tmp[:, 0:nj * 128],
                                                                       func=AF.Exp), reads=[tmp], writes=[p])
                def fin(i=i, nj=nj, p=p, pot=pot):
                    for s in range(nj):
                        jj = i - s
                        k.mm(pot[0:65, 0:128], vb[:, jj, :], p[:, s * 128:(s + 1) * 128], s == 0, s == nj - 1,
                             reads=[vb, p], writes=[pot])
                    attn_epilogue(k, c, pot, 128, OS, RD, PS_BC, att[:, i * 128:(i + 1) * 128], att)
                if pendb:
                    pendb.pop(0)()
                pendb.append(fin)
            while pendb:
                pendb.pop(0)()
            for si in range(NS):
                r0 = TP + si * TS
                tmp, p, pot = TMP[it % 2], P[it % 2], PS_OT[it % 2]
                it += 1
                k.dma("pool", KC[:, :, :], io["cbk"][si, hh, :, :].rearrange("(t p) d -> p t d", p=128), writes=[KC])
                k.dma("pool", VC[:, :, 0:64], io["cbv"][si, hh, :, :].rearrange("(t p) d -> p t d", p=128), writes=[VC])
                k.dma("sp", VN[:, 0:64], scr["vbTok"][r0:r0 + TS, hh * 64:(hh + 1) * 64], writes=[VN])
                for m in range(4):
                    k.tr(PS_KT[:, m, :], KC[:, m, :], c.identb[:, :], reads=[KC, c.identb], writes=[PS_KT])
                k.op("act", lambda h: h.activation(out=KTC[0:64, :], in_=PS_KT[:, :, :].rearrange("p m n -> p (m n)"),
                                                   func=AF.Copy), reads=[PS_KT], writes=[KTC])
                for m in range(4):
                    k.mm(PS_A[:, m * 64:(m + 1) * 64], KTC[:, m * 128:(m + 1) * 128], qT[:, r0:r0 + TS], True, True,
                         reads=[KTC, qT], writes=[PS_A])
                k.mm(PS_A[0:64, 256:320], kT[:, r0:r0 + TS], qT[:, r0:r0 + TS], True, True, reads=[kT, qT], writes=[PS_A])
                for m in range(4):
                    b0 = 512 - 128 * m
                    k.op("dve", lambda h, m=m, b0=b0, tmp=tmp: h.tensor_tensor(
                        out=tmp[:, m * 64:(m + 1) * 64], in0=PS_A[:, m * 64:(m + 1) * 64], in1=bs[:, b0:b0 + 64],
                        op=ALU.add), reads=[PS_A, bs], writes=[tmp])
                k.op("dve", lambda h, tmp=tmp: h.tensor_tensor(out=tmp[0:64, 256:320], in0=PS_A[0:64, 256:320],
                                                              in1=bs[0:64, 0:64], op=ALU.add),
                     reads=[PS_A, bs], writes=[tmp])
                k.op("act", lambda h, tmp=tmp, p=p: h.activation(out=p[:, 0:256], in_=tmp[:, 0:256], func=AF.Exp),
                     reads=[tmp], writes=[p])
                k.op("act", lambda h, tmp=tmp, p=p: h.activation(out=p[0:64, 256:320], in_=tmp[0:64, 256:320],
                                                                func=AF.Exp), reads=[tmp], writes=[p])
                for m in range(4):
                    k.mm(pot[0:65, 0:64], VC[:, m, :], p[:, m * 64:(m + 1) * 64], m == 0, False, reads=[VC, p], writes=[pot])
                k.mm(pot[0:65, 0:64], VN[0:64, :], p[0:64, 256:320], False, True, reads=[VN, p], writes=[pot])
                attn_epilogue(k, c, pot, TS, OS, RD, PS_BC, att[:, r0:r0 + TS], att)
            k.dma("sp", scr["catT"][512 + hh * 64:512 + (hh + 1) * 64, :], att[:, :], reads=[att])
    k.barrier()


def outproj_phase(k, c, io, scr, cat_ap, w_ap, xin_ap, xout_ap, tag, prefetch=None):
    with contextlib.ExitStack() as es:
        W = k.sb(es, tag + "W", [128, 8, D], BF16)
        load_w(k, W, w_ap, 8, D)
        if prefetch is not None:
            prefetch()
        cats = [k.sb(es, f"{tag}c{i}", [128, 8, 512], BF16) for i in range(2)]
        xts = [k.sb(es, f"{tag}x{i}", [128, D], F32) for i in range(2)]
        xos = [k.sb(es, f"{tag}o{i}", [128, D], F32) for i in range(2)]
        pp = [k.ps(es, f"{tag}p{i}", [128, 512]) for i in range(4)]
        cv = cat_ap.rearrange("(k p) r -> p k r", p=128)
        gs = groups()
        k.dma("sp", cats[0][:, :, 0:gs[0][1]], cv[:, :, gs[0][0]:gs[0][0] + gs[0][1]], writes=[cats[0]])
        ti = 0
        pi = 0
        for gi, (tok0, n) in enumerate(gs):
            cat = cats[gi % 2]
            if gi + 1 < len(gs):
                t1, n1 = gs[gi + 1]
                k.dma("sp", cats[(gi + 1) % 2][:, :, 0:n1], cv[:, :, t1:t1 + n1], writes=[cats[(gi + 1) % 2]])
            for t in range(n // 128):
                r0 = tok0 + t * 128
                xt, xo = xts[ti % 2], xos[ti % 2]
                ti += 1
                k.dma("sp", xt[:, :], xin_ap[r0:r0 + 128, :], writes=[xt])
                for blk in range(2):
                    p = pp[pi % 4]
                    pi += 1
                    for kc in range(8):
                        k.mm(p[:, :], cat[:, kc, t * 128:(t + 1) * 128], W[:, kc, blk * 512:(blk + 1) * 512],
                             kc == 0, kc == 7, reads=[cat, W], writes=[p])
                    k.op("dve", lambda h, p=p, blk=blk, xt=xt, xo=xo: h.tensor_tensor(
                        out=xo[:, blk * 512:(blk + 1) * 512], in0=p[:, :], in1=xt[:, blk * 512:(blk + 1) * 512],
                        op=ALU.add), reads=[p, xt], writes=[xo])
                k.dma("pool", xout_ap[r0:r0 + 128, :], xo[:, :], reads=[xo])
    k.barrier()


def ffn_w_alloc(k, es, tag):
    return (k.sb(es, tag + "W", [128, 8, 2 * HID], BF16), k.sb(es, tag + "g", [128, 8], F32))


def ffn_w_load(k, io, layer, Wg):
    W, gcol = Wg
    load_w(k, W, io["w_ffn_in"][layer], 8, 2 * HID)
    scale_rows(k, None, W, io["norm_ffn"][layer, :], 8, None, gcol=gcol)
    return W


def ffn_in_phase(k, c, io, scr, layer, xin_ap, tag, W=None):
    with contextlib.ExitStack() as es:
        if W is None:
            W = ffn_w_load(k, io, layer, ffn_w_alloc(k, es, tag))
        st = NormState(k, es, tag)
        xts = [k.sb(es, f"{tag}x{i}", [128, D], F32) for i in range(2)]
        hTs = [k.sb(es, f"{tag}h{i}", [128, 8, 512], BF16) for i in range(2)]
        SG = [k.sb(es, f"{tag}sg{i}", [128, 512], F32) for i in range(2)]
        HS = [k.sb(es, f"{tag}hs{i}", [128, 512], BF16) for i in range(3)]
        PG = [k.ps(es, f"{tag}pg{i}", [128, 512]) for i in range(2)]
        PU = [k.ps(es, f"{tag}pu{i}", [128, 512]) for i in range(2)]
        ti = 0
        it = 0
        gl = groups()
        tis = [0]

        def norm_group(gi):
            tok0, n = gl[gi]
            for t in range(n // 128):
                xt = xts[tis[0] % 2]
                k.dma("sp", xt[:, :], xin_ap[tok0 + t * 128: tok0 + (t + 1) * 128, :], writes=[xt])
                rms_to_hT(k, c, xt, hTs[gi % 2], t * 128, st, tis[0])
                tis[0] += 1

        norm_group(0)
        for gi, (tok0, n) in enumerate(gl):
            hT = hTs[gi % 2]
            if gi + 1 < len(gl):
                norm_group(gi + 1)
            for cch in range(HID // 128):
                pg, pu, sg, hs = PG[it % 2], PU[it % 2], SG[it % 2], HS[it % 3]
                it += 1
                for kc in range(8):
                    k.mm(pg[:, 0:n], W[:, kc, cch * 128:(cch + 1) * 128], hT[:, kc, 0:n], kc == 0, kc == 7,
                         reads=[W, hT], writes=[pg])
                for kc in range(8):
                    k.mm(pu[:, 0:n], W[:, kc, HID + cch * 128:HID + (cch + 1) * 128], hT[:, kc, 0:n], kc == 0, kc == 7,
                         reads=[W, hT], writes=[pu])
                k.op("act", lambda h, pg=pg, sg=sg: h.activation(out=sg[:, 0:n], in_=pg[:, 0:n], func=AF.Silu),
                     reads=[pg], writes=[sg])
                k.op("dve", lambda h, pu=pu, sg=sg, hs=hs: h.tensor_tensor(out=hs[:, 0:n], in0=pu[:, 0:n], in1=sg[:, 0:n],
                                                                          op=ALU.mult), reads=[pu, sg], writes=[hs])
                k.dma("sp", scr["hidT"][cch * 128:(cch + 1) * 128, tok0:tok0 + n], hs[:, 0:n], reads=[hs])
    k.barrier()


def ffn_out_phase(k, c, io, scr, layer, xin_ap, xout_ap, final, tag):
    with contextlib.ExitStack() as es:
        KC = HID // 128
        W = k.sb(es, tag + "W", [128, KC, D], BF16)
        load_w(k, W, io["w_ffn_out"][layer], KC, D)
        hids = [k.sb(es, f"{tag}hd{i}", [128, KC, 512], BF16) for i in range(2)]
        xts = [k.sb(es, f"{tag}x{i}", [128, D], F32) for i in range(2)]
        xos = [k.sb(es, f"{tag}o{i}", [128, D], F32) for i in range(2)]
        pp = [k.ps(es, f"{tag}p{i}", [128, 512]) for i in range(4)]
        if final:
            gam = k.sb(es, tag + "gam", [128, D], F32)
            k.dma("sp", gam[:, :], io["norm_final"].partition_broadcast(128), writes=[gam])
            junk = k.sb(es, tag + "junk", [128, D], BF16)
            ss = [k.sb(es, f"{tag}ss{i}", [128, 1], F32) for i in range(2)]
            sq = [k.sb(es, f"{tag}sq{i}", [128, 1], F32) for i in range(2)]
            rstd = [k.sb(es, f"{tag}rs{i}", [128, 1], F32) for i in range(2)]
            ys = [k.sb(es, f"{tag}y{i}", [128, D], F32) for i in range(2)]
        hv = scr["hidT"].rearrange("(c p) r -> p c r", p=128)
        gs = groups()
        k.dma("sp", hids[0][:, :, 0:gs[0][1]], hv[:, :, gs[0][0]:gs[0][0] + gs[0][1]], writes=[hids[0]])
        ti = 0
        pi = 0
        for gi, (tok0, n) in enumerate(gs):
            hid = hids[gi % 2]
            if gi + 1 < len(gs):
                t1, n1 = gs[gi + 1]
                k.dma("sp", hids[(gi + 1) % 2][:, :, 0:n1], hv[:, :, t1:t1 + n1], writes=[hids[(gi + 1) % 2]])
            for t in range(n // 128):
                r0 = tok0 + t * 128
                j = ti % 2
                xt, xo = xts[j], xos[j]
                ti += 1
                k.dma("sp", xt[:, :], xin_ap[r0:r0 + 128, :], writes=[xt])
                for blk in range(2):
                    p = pp[pi % 4]
                    pi += 1
                    for kc in range(KC):
                        k.mm(p[:, :], hid[:, kc, t * 128:(t + 1) * 128], W[:, kc, blk * 512:(blk + 1) * 512],
                             kc == 0, kc == KC - 1, reads=[hid, W], writes=[p])
                    k.op("dve", lambda h, p=p, blk=blk, xt=xt, xo=xo: h.tensor_tensor(
                        out=xo[:, blk * 512:(blk + 1) * 512], in0=p[:, :], in1=xt[:, blk * 512:(blk + 1) * 512],
                        op=ALU.add), reads=[p, xt], writes=[xo])
                if not final:
                    k.dma("pool", xout_ap[r0:r0 + 128, :], xo[:, :], reads=[xo])
                else:
                    k.op("act", lambda h, xo=xo, j=j: h.activation(out=junk[:, :], in_=xo[:, :], func=AF.Square,
                                                                  accum_out=ss[j][:, :]),
                         reads=[xo], writes=[junk, ss[j]])
                    k.op("act", lambda h, j=j: h.activation(out=sq[j][:, :], in_=ss[j][:, :], func=AF.Sqrt,
                                                           scale=1.0 / D, bias=1e-6), reads=[ss[j]], writes=[sq[j]])
                    k.op("dve", lambda h, j=j: h.reciprocal(out=rstd[j][:, :], in_=sq[j][:, :]),
                         reads=[sq[j]], writes=[rstd[j]])
                    k.op("act", lambda h, xo=xo, j=j: h.activation(out=ys[j][:, :], in_=xo[:, :], func=AF.Copy,
                                                                  scale=rstd[j][:, :]),
                         reads=[xo, rstd[j]], writes=[ys[j]])
                    k.op("pool", lambda h, j=j: h.tensor_tensor(out=ys[j][:, :], in0=ys[j][:, :], in1=gam[:, :],
                                                               op=ALU.mult), reads=[gam], writes=[ys[j]])
                    k.dma("pool", xout_ap[r0:r0 + 128, :], ys[j][:, :], reads=[ys[j]])
    k.barrier()


def fox_phases(k, c, io, scr):
    with contextlib.ExitStack() as esf:
        frow = k.sb(esf, "fx_frow", [16, R], F32)
        fox_inproj(k, c, io, scr, frow)
        if "fx1" in DBG:
            return
        NFP = k.sb(esf, "fx_NFP", [16, TP], F32)
        FSN = k.sb(esf, "fx_FSN", [16, NS, TS], F32)
        NFcP = k.sb(esf, "fx_NFcP", [128, 32, 16], F32)
        NFcS = k.sb(esf, "fx_NFcS", [128, NS, 17, 16], F32)
        fox_prep(k, c, io, scr, frow, NFP, FSN, NFcP, NFcS)
        if "fx2" in DBG:
            return
        fox_attn(k, c, io, scr, NFP, FSN, NFcP, NFcS)


def fox_inproj(k, c, io, scr, frow):
    with contextlib.ExitStack() as es:
        stf = [k.sb(es, f"f1sf{i}", [128, 512], F32) for i in range(2)]
        W = k.sb(es, "f1W", [128, 8, 3088], BF16)
        load_w(k, W, io["w_in_fox"], 8, 3088)
        scale_rows(k, es, W, io["norm_mix"][1, :], 8, "f1g")
        st = NormState(k, es, "f1")
        xts = [k.sb(es, f"f1x{i}", [128, D], F32) for i in range(2)]
        hTs = [k.sb(es, f"f1h{i}", [128, 8, 512], BF16) for i in range(2)]
        pfm = [k.ps(es, f"f1pf{i}", [128, 512]) for i in range(2)]
        ptm = [k.ps(es, f"f1pt{i}", [128, 512]) for i in range(2)]
        stb = [k.sb(es, f"f1sb{i}", [128, 512], BF16) for i in range(4)]
        ev = 0
        sbi = 0
        sfi = 0
        ti = 0
        gl = groups()
        tis = [0]

        def norm_group(gi):
            tok0, n = gl[gi]
            for t in range(n // 128):
                xt = xts[tis[0] % 2]
                k.dma("sp", xt[:, :], scr["x2"][tok0 + t * 128: tok0 + (t + 1) * 128, :], writes=[xt])
                rms_to_hT(k, c, xt, hTs[gi % 2], t * 128, st, tis[0])
                tis[0] += 1

        norm_group(0)
        for gi, (tok0, n) in enumerate(gl):
            hT = hTs[gi % 2]
            if gi + 1 < len(gl):
                norm_group(gi + 1)
            for (c0, dst, scale) in ([] if "fxnofm" in DBG else ((0, scr["qTf"], 0.125), (1024, scr["kTf"], None))):
                for mc in range(8):
                    pp = pfm[ev % 2]
                    for kc in range(8):
                        k.mm(pp[:, 0:n], W[:, kc, c0 + mc * 128: c0 + (mc + 1) * 128], hT[:, kc, 0:n],
                             kc == 0, kc == 7, reads=[W, hT], writes=[pp])
                    sbt = stb[sbi % 4]
                    sbi += 1
                    evac(k, ev, sbt[:, 0:n], pp[:, 0:n], [pp], [sbt], scale=scale)
                    ev += 1
                    k.dma("sp", dst[mc * 128:(mc + 1) * 128, tok0:tok0 + n], sbt[:, 0:n], reads=[sbt])
            pp = pfm[ev % 2]
            for kc in range(0 if "fxnog" in DBG else 8):
                k.mm(pp[0:16, 0:n], W[:, kc, 3072:3088], hT[:, kc, 0:n], kc == 0, kc == 7, reads=[W, hT], writes=[pp])
            if "fxnog" not in DBG:
                evac(k, 1, frow[0:16, tok0:tok0 + n], pp[0:16, 0:n], [pp], [frow])
            ev += 1
            for t in range(0 if "fxnotm" in DBG else n // 128):
                r0 = tok0 + t * 128
                for (c0, outn, hb, tobf) in ((2048, "fv", 0, True), (2560, "fv", 1, True),
                                             (1024, "fk", 0, False), (1536, "fk", 1, False)):
                    pp = ptm[ev % 2]
                    for kc in range(8):
                        k.mm(pp[:, :], hT[:, kc, t * 128:(t + 1) * 128], W[:, kc, c0:c0 + 512],
                             kc == 0, kc == 7, reads=[W, hT], writes=[pp])
                    ev += 1
                    sft = stf[sfi % 2]
                    sfi += 1
                    evac(k, 1, sft[:, :], pp[:, :], [pp], [sft])
                    if tobf:
                        sbt = stb[sbi % 4]
                        sbi += 1
                        k.op("act", lambda h, sbt=sbt, sft=sft: h.activation(out=sbt[:, :], in_=sft[:, :], func=AF.Copy),
                             reads=[sft], writes=[sbt])
                        k.dma("sp", scr["vfTok"][r0:r0 + 128, hb * 512:(hb + 1) * 512], sbt[:, :], reads=[sbt])
                    for h8 in range(0 if "fxnoout" in DBG else 8):
                        src = sft[:, h8 * 64:(h8 + 1) * 64]
                        if r0 < TP:
                            k.dma("sp", io["p" + outn][hb * 8 + h8, r0:r0 + 128, :], src, reads=[sft])
                        else:
                            for s2 in range(2):
                                sq = (r0 - TP) // TS + s2
                                k.dma("sp", io["s" + outn][sq, hb * 8 + h8, :, :], src[s2 * 64:(s2 + 1) * 64], reads=[sft])
    k.barrier()


def fox_prep(k, c, io, scr, frow, NFP, FSN, NFcP, NFcS):
    with contextlib.ExitStack() as es:
        negb = k.sb(es, "f2negb", [16, 1], F32)
        LF = k.sb(es, "f2LF", [16, R], F32)
        CL = [k.sb(es, f"f2CL{i}", [16, 2048], F32) for i in range(2)]
        FC = [k.sb(es, f"f2FC{i}", [16, 2048], F32) for i in range(2)]
        PCp = k.ps(es, "f2PCp", [128, 32, 16])
        PCs = [k.ps(es, f"f2PCs{i}", [128, 17, 16]) for i in range(2)]
        k.dma("sp", negb[:, :], io["b_fox_f"].rearrange("(h o) -> h o", o=1), writes=[negb])
        k.op("dve", lambda h: h.tensor_scalar(out=negb[:, :], in0=negb[:, :], scalar1=-1.0, scalar2=None, op0=ALU.mult),
             writes=[negb])
        k.op("act", lambda h: h.activation(out=frow[:, :], in_=frow[:, :], func=AF.Exp, scale=-1.0, bias=negb[:, :]),
             reads=[negb], writes=[frow])
        k.op("act", lambda h: h.activation(out=frow[:, :], in_=frow[:, :], func=AF.Ln, bias=1.0), writes=[frow])
        k.op("dve", lambda h: h.tensor_scalar(out=LF[:, :], in0=frow[:, :], scalar1=-1.0, scalar2=None, op0=ALU.mult),
             reads=[frow], writes=[LF])
        k.dma("sp", io["pflf"][:, :], LF[:, 0:TP], reads=[LF])
        for i in range(NS):
            k.dma("sp", io["sflf"][i, :, :], LF[:, TP + i * TS:TP + (i + 1) * TS], reads=[LF])
        k.op("dve", lambda h: h.tensor_tensor_scan(out=NFP[:, :], data0=frow[:, 0:TP], data1=frow[:, 0:TP], initial=0.0,
                                                   op0=ALU.add, op1=ALU.max), reads=[frow], writes=[NFP])
        for t in range(32):
            k.tr(PCp[:, t, :], NFP[0:16, t * 128:(t + 1) * 128], c.ident[0:16, 0:16], reads=[NFP, c.ident], writes=[PCp])
        k.op("dve", lambda h: h.tensor_copy(out=NFcP[:, :, :], in_=PCp[:, :, :]), reads=[PCp], writes=[NFcP])
        for i in range(NS):
            cl, fc, pcs = CL[i % 2], FC[i % 2], PCs[i % 2]
            s0 = TP + i * TS
            k.dma("sp", cl[:, :], io["cflf"][i, :, :], writes=[cl])
            k.op("dve", lambda h, cl=cl: h.tensor_scalar(out=cl[:, :], in0=cl[:, :], scalar1=-1.0, scalar2=None,
                                                        op0=ALU.mult), writes=[cl])
            k.op("dve", lambda h, cl=cl, fc=fc: h.tensor_tensor_scan(out=fc[:, :], data0=cl[:, :], data1=cl[:, :],
                                                                    initial=0.0, op0=ALU.add, op1=ALU.max),
                 reads=[cl], writes=[fc])
            k.op("dve", lambda h, fc=fc, i=i, s0=s0: h.tensor_tensor_scan(
                out=FSN[:, i, :], data0=frow[:, s0:s0 + TS], data1=frow[:, s0:s0 + TS], initial=fc[:, 2047:2048],
                op0=ALU.add, op1=ALU.max), reads=[fc, frow], writes=[FSN])
            for m in range(16):
                k.tr(pcs[:, m, :], fc[0:16, m * 128:(m + 1) * 128], c.ident[0:16, 0:16], reads=[fc, c.ident], writes=[pcs])
            k.tr(pcs[0:64, 16, :], FSN[0:16, i, :], c.ident[0:16, 0:16], reads=[FSN, c.ident], writes=[pcs])
            k.op("dve", lambda h, pcs=pcs, i=i: h.tensor_copy(out=NFcS[:, i, 0:16, :], in_=pcs[:, 0:16, :]),
                 reads=[pcs], writes=[NFcS])
            k.op("dve", lambda h, pcs=pcs, i=i: h.tensor_copy(out=NFcS[0:64, i, 16, :], in_=pcs[0:64, 16, :]),
                 reads=[pcs], par=[NFcS])
    k.barrier()


def fox_attn(k, c, io, scr, NFP, FSN, NFcP, NFcS):
    with contextlib.ExitStack() as es:
        kTh = [k.sb(es, f"f3k{i}", [128, R], BF16) for i in range(2)]
        qTh = [k.sb(es, f"f3q{i}", [128, R], BF16) for i in range(2)]
        for t_ in kTh + qTh:
            k.op("pool", lambda h, t_=t_: h.memset(t_[64:128, :], 0.0), writes=[t_])
        vfh = [k.sb(es, f"f3v{i}", [128, NT, 65], BF16) for i in range(2)]
        ATT = k.sb(es, "f3att", [64, R], BF16)
        FQ = [k.sb(es, f"f3fq{i}", [128, 512], F32) for i in range(2)]
        BD = [[k.sb(es, f"f3bd{b}_{i}", [128, 512], F32) for i in range(4)] for b in range(2)]
        TMP = [k.sb(es, f"f3tmp{i}", [128, 512], F32) for i in range(4)]
        P = [k.sb(es, f"f3P{i}", [128, 512], BF16) for i in range(4)]
        OS = k.sb(es, "f3OS", [65, 512], F32)
        RD = k.sb(es, "f3RD", [65, 512], F32)
        KC = [k.sb(es, f"f3KC{i}", [128, 16, 64], BF16) for i in range(2)]
        KTC = k.sb(es, "f3KTC", [128, 2048], BF16)
        k.op("pool", lambda h: h.memset(KTC[64:128, :], 0.0), writes=[KTC])
        VC = [k.sb(es, f"f3VC{i}", [128, 16, 65], BF16) for i in range(2)]
        VN = [k.sb(es, f"f3VN{i}", [64, 65], BF16) for i in range(2)]
        FQs = k.sb(es, "f3FQs", [128, TS], F32)
        BDs = k.sb(es, "f3BDs", [64, TS], F32)
        TMPs = k.sb(es, "f3TMPs", [128, 1088], F32)
        Ps = k.sb(es, "f3Ps", [128, 1088], BF16)
        PS_S = [k.ps(es, f"f3PS{i}", [128, 512]) for i in range(3)]
        PS_O = [k.ps(es, f"f3PO{i}", [128, 512]) for i in range(2)]
        PS_F = k.ps(es, "f3PF", [128, 512])
        PS_BC = k.ps(es, "f3PBC", [128, 512])
        PS_KT = k.ps(es, "f3PKT", [64, 1024], BF16)
        LN = k.sb(es, "f3LN", [65, 512], F32)
        RDr = [k.sb(es, f"f3RDr{i}", [65, 512], F32) for i in range(2)]
        RB = [k.sb(es, f"f3RB{i}", [64, 512], F32) for i in range(2)]
        rdbuf = [TT(None) for _ in range(2)]
        for i in range(2):
            k.op("pool", lambda h, i=i: h.memset(vfh[i][:, :, 64:65], 1.0), writes=[vfh[i]])
            k.op("pool", lambda h, i=i: h.memset(VC[i][:, :, 64:65], 1.0), writes=[VC[i]])
            k.op("pool", lambda h, i=i: h.memset(VN[i][:, 64:65], 1.0), writes=[VN[i]])
        vview = scr["vfTok"].rearrange("(t p) c -> p t c", p=128)

        def load(hh):
            j = hh % 2
            k.dma("sp", kTh[j][0:64, :], scr["kTf"][hh * 64:(hh + 1) * 64, :], writes=[kTh[j]])
            k.dma("sp", qTh[j][0:64, :], scr["qTf"][hh * 64:(hh + 1) * 64, :], writes=[qTh[j]])
            k.dma("sp", vfh[j][:, :, 0:64], vview[:, :, hh * 64:(hh + 1) * 64], writes=[vfh[j]])

        def prep(hh, g, b):
            fq = FQ[b]
            k.mm(PS_F[:, 0:512], c.sel[0:16, hh, :], NFP[0:16, g * 512:(g + 1) * 512], True, True,
                 reads=[c.sel, NFP], writes=[PS_F])
            k.op("dve", lambda h: h.tensor_copy(out=fq[:, :], in_=PS_F[:, 0:512]), reads=[PS_F], writes=[fq])
            for r in range(4):
                b0 = 384 - 128 * r
                k.op("pool", lambda h, r=r, b0=b0: h.tensor_tensor(out=BD[b][r][:, :], in0=fq[:, :],
                                                                 in1=c.cpos[:, b0:b0 + 512], op=ALU.add),
                     reads=[fq, c.cpos], writes=[BD[b][r]])

        def sload(hh, si, b):
            r0 = TP + si * TS
            k.dma("pool", KC[b][:, :, :], io["cfk"][si, hh, :, :].rearrange("(t p) d -> p t d", p=128), writes=[KC[b]])
            k.dma("pool", VC[b][:, :, 0:64], io["cfv"][si, hh, :, :].rearrange("(t p) d -> p t d", p=128), writes=[VC[b]])
            k.dma("sp", VN[b][:, 0:64], scr["vfTok"][r0:r0 + TS, hh * 64:(hh + 1) * 64], writes=[VN[b]])

        load(0)
        it = 0
        gi = 0
        sli = 0
        pend_epiA = []
        pend_epiB = []
        sload(0, 0, 0)
        for hh in range(16):
            j = hh % 2
            kT, qT, vf = kTh[j], qTh[j], vfh[j]
            if hh + 1 < 16:
                load(hh + 1)
            prep(hh, 0, gi % 2)
            pend = []
            for g in range(8):
                b = gi % 2
                fq, bd, pso = FQ[b], BD[b], PS_O[b]
                gi += 1
                if g + 1 < 8:
                    prep(hh, g + 1, gi % 2)
                last = 4 * g + 3
                for jj in range(last + 1):
                    ps, tmp, p = PS_S[it % 3], TMP[it % 4], P[it % 4]
                    it += 1
                    k.mm(ps[:, 0:512], kT[:, jj * 128:(jj + 1) * 128], qT[:, g * 512:(g + 1) * 512], True, True,
                         reads=[kT, qT], writes=[ps])
                    bsrc = bd[jj - 4 * g] if jj >= 4 * g else fq
                    k.op("dve", lambda h, ps=ps, tmp=tmp, bsrc=bsrc: h.tensor_tensor(out=tmp[:, :], in0=ps[:, 0:512],
                                                                                    in1=bsrc[:, :], op=ALU.subtract),
                         reads=[ps, bsrc], writes=[tmp], skip_self=True, skip_war=("act",))
                    k.op("act", lambda h, tmp=tmp, p=p, jj=jj: h.activation(out=p[:, :], in_=tmp[:, :], func=AF.Exp,
                                                                           bias=NFcP[:, jj, hh:hh + 1]),
                         reads=[tmp, NFcP], writes=[p], skip_self=True, skip_war=("pe",))
                    pend.append(lambda jj=jj, p=p, pso=pso, last=last: k.mm(
                        pso[0:65, 0:512], vf[:, jj, :], p[:, :], jj == 0, jj == last, reads=[vf, p], writes=[pso]))
                    if len(pend) > 2:
                        pend.pop(0)()
                    if jj == 1 and pend_epiA:
                        pend_epiA.pop(0)()
                    if jj == last - 1 and pend_epiB:
                        pend_epiB.pop(0)()
                def epiA(pso=pso, b=b):
                    k.op("act", lambda h: h.activation(out=LN[64:65, :], in_=pso[64:65, 0:512], func=AF.Ln),
                         reads=[pso], writes=[LN])
                    k.op("act", lambda h: h.activation(out=RDr[b][64:65, :], in_=LN[64:65, :], func=AF.Exp, scale=-1.0),
                         reads=[LN], writes=[RDr[b]])
                    k.dma("sp", scr["rds"][b:b + 1, :], RDr[b][64:65, :], reads=[RDr[b]], writes=[rdbuf[b]])
                    k.dma("sp", RB[b][:, :], scr["rds"][b, :].partition_broadcast(64), reads=[rdbuf[b]], writes=[RB[b]])

                def epiB(pso=pso, b=b, g=g):
                    k.op("dve", lambda h: h.tensor_tensor(out=ATT[:, g * 512:(g + 1) * 512], in0=pso[0:64, 0:512],
                                                          in1=RB[b][:, :], op=ALU.mult),
                         reads=[pso, RB[b]], writes=[ATT])
                pend_epiA.append(epiA)
                pend_epiB.append(epiB)
            while pend:
                pend.pop(0)()
            while pend_epiA:
                pend_epiA.pop(0)()
            while pend_epiB:
                pend_epiB.pop(0)()
            for si in range(NS):
                r0 = TP + si * TS
                b = sli % 2
                sli += 1
                kc, vc, vn = KC[b], VC[b], VN[b]
                pso = PS_O[sli % 2]
                if si + 1 < NS:
                    sload(hh, si + 1, sli % 2)
                elif hh + 1 < 16:
                    sload(hh + 1, 0, sli % 2)
                for half in range(2):
                    for m in range(8):
                        k.tr(PS_KT[:, m * 128:(m + 1) * 128], kc[:, half * 8 + m, :], c.identb[:, :],
                             reads=[kc, c.identb], writes=[PS_KT])
                    k.op("act", lambda h, half=half: h.activation(out=KTC[0:64, half * 1024:(half + 1) * 1024],
                                                                in_=PS_KT[:, :], func=AF.Copy),
                         reads=[PS_KT], writes=[KTC])
                k.mm(PS_F[:, 0:TS], c.sel[0:16, hh, :], FSN[0:16, si, :], True, True, reads=[c.sel, FSN], writes=[PS_F])
                k.op("dve", lambda h: h.tensor_copy(out=FQs[:, :], in_=PS_F[:, 0:TS]), reads=[PS_F], writes=[FQs])
                k.op("pool", lambda h: h.tensor_tensor(out=BDs[:, :], in0=FQs[0:64, :], in1=c.cpos[0:64, 384:384 + TS],
                                                       op=ALU.add), reads=[FQs, c.cpos], writes=[BDs])
                for m in range(16):
                    ps = PS_S[m // 8]
                    k.mm(ps[:, (m % 8) * 64:(m % 8 + 1) * 64], KTC[:, m * 128:(m + 1) * 128], qT[:, r0:r0 + TS], True, True,
                         reads=[KTC, qT], writes=[ps])
                k.mm(PS_S[2][0:64, 0:TS], kT[:, r0:r0 + TS], qT[:, r0:r0 + TS], True, True, reads=[kT, qT], writes=[PS_S[2]])
                for m in range(16):
                    ps = PS_S[m // 8]
                    k.op("dve", lambda h, m=m, ps=ps: h.scalar_tensor_tensor(
                        out=TMPs[:, m * 64:(m + 1) * 64], in0=ps[:, (m % 8) * 64:(m % 8 + 1) * 64],
                        scalar=NFcS[:, si, m, hh:hh + 1], in1=FQs[:, :], op0=ALU.add, op1=ALU.subtract),
                        reads=[ps, NFcS, FQs], writes=[TMPs])
                k.op("dve", lambda h: h.scalar_tensor_tensor(
                    out=TMPs[0:64, 1024:1088], in0=PS_S[2][0:64, 0:TS], scalar=NFcS[0:64, si, 16, hh:hh + 1],
                    in1=BDs[:, :], op0=ALU.add, op1=ALU.subtract), reads=[PS_S[2], NFcS, BDs], writes=[TMPs])
                k.op("act", lambda h: h.activation(out=Ps[:, 0:1024], in_=TMPs[:, 0:1024], func=AF.Exp),
                     reads=[TMPs], writes=[Ps])
                k.op("act", lambda h: h.activation(out=Ps[0:64, 1024:1088], in_=TMPs[0:64, 1024:1088], func=AF.Exp),
                     reads=[TMPs], writes=[Ps])
                for m in range(16):
                    k.mm(pso[0:65, 0:TS], vc[:, m, :], Ps[:, m * 64:(m + 1) * 64], m == 0, False, reads=[vc, Ps], writes=[pso])
                k.mm(pso[0:65, 0:TS], vn[0:64, :], Ps[0:64, 1024:1088], False, True, reads=[vn, Ps], writes=[pso])
                attn_epilogue(k, c, pso, TS, OS, RD, PS_BC, ATT[:, r0:r0 + TS], ATT)
            k.dma("sp", scr["attT"][hh * 64:(hh + 1) * 64, :], ATT[:, :], reads=[ATT])
    k.barrier()
```

```python
import contextlib
import numpy as np
import concourse.bass as bass
import concourse.mybir as mybir
from concourse.bass_utils import run_bass_kernel_spmd

F32 = mybir.dt.float32
BF16 = mybir.dt.bfloat16
AF = mybir.ActivationFunctionType
ALU = mybir.AluOpType

D = 1024
TP = 4096
NS = 4
TS = 64
R = TP + NS * TS
NT = R // 128
HID = 2816
BIG = 30000.0


class Buf:
    __slots__ = ("w", "r")

    def __init__(self):
        self.w = []
        self.r = {}


class TT:
    def __init__(self, t):
        self.t = t
        self.b = Buf()

    def __getitem__(self, key):
        return self.t[key]


class Eng:
    def __init__(self, name, h, sem):
        self.name = name
        self.h = h
        self.sem = sem
        self.count = 0
        self.seen = {}


class DQ:
    def __init__(self, name, sems):
        self.sems = sems
        self.keys = [f"{name}{i}" for i in range(len(sems))]
        self.uses = [0] * len(sems)
        self.next = 0


class KB:
    def __init__(self, nc, es):
        self.nc = nc
        self.eng = {}
        for name, h in (("pe", nc.tensor), ("act", nc.scalar), ("dve", nc.vector),
                        ("pool", nc.gpsimd), ("sp", nc.sync)):
            self.eng[name] = Eng(name, h, es.enter_context(nc.semaphore("sem_" + name)))
        self.dq = {}
        for q in ("sp", "pool"):
            self.dq[q] = DQ("dq" + q, [es.enter_context(nc.semaphore(f"dq{q}{i}")) for i in range(8)])

    def sb(self, es, name, shape, dt):
        return TT(es.enter_context(self.nc.sbuf_tensor(name, shape, dt)))

    def ps(self, es, name, shape, dt=F32):
        return TT(es.enter_context(self.nc.psum_tensor(name, shape, dt)))

    def _wait(self, e, deps):
        best = {}
        for (key, sem, val) in deps:
            if key == "pe" and e.name == "pe":
                continue
            if e.seen.get(key, 0) >= val:
                continue
            if key not in best or best[key][1] < val:
                best[key] = (sem, val)
        for key, (sem, val) in best.items():
            e.h.wait_ge(sem, val)
            e.seen[key] = val

    def _deps(self, reads, writes, par=(), en=None, skip_war=()):
        deps = []
        for t in reads:
            deps.extend(t.b.w)
        for t in writes:
            deps.extend(d for d in t.b.w if d[0] != en)
            deps.extend(d for d in t.b.r.values() if d[0] != en and d[0] not in skip_war)
        for t in par:
            deps.extend(d for d in t.b.r.values() if d[0] != en)
        return deps

    def _mark(self, tok, reads, writes, par=()):
        for t in par:
            t.b.w.append(tok)
        for t in writes:
            t.b.w = [tok]
            t.b.r = {}
        for t in reads:
            if t in writes:
                continue
            old = t.b.r.get(tok[0])
            if old is None or old[2] < tok[2]:
                t.b.r[tok[0]] = tok

    def op(self, en, fn, reads=(), writes=(), par=(), skip_self=False, skip_war=()):
        e = self.eng[en]
        self._wait(e, self._deps(reads, writes, par, en if skip_self else None, skip_war))
        inst = fn(e.h)
        e.count += 1
        inst.then_inc(e.sem, 1)
        tok = (en, e.sem, e.count)
        self._mark(tok, reads, writes, par)
        return tok

    def dma(self, qn, out, in_, reads=(), writes=(), par=(), **kw):
        e = self.eng[qn]
        q = self.dq[qn]
        self._wait(e, self._deps(reads, writes, par))
        s = q.next % len(q.sems)
        q.next += 1
        if q.uses[s] > 0 and e.seen.get(q.keys[s], 0) < 16 * q.uses[s]:
            e.h.wait_ge(q.sems[s], 16 * q.uses[s])
            e.seen[q.keys[s]] = 16 * q.uses[s]
        inst = e.h.dma_start(out=out, in_=in_, **kw)
        q.uses[s] += 1
        inst.then_inc(q.sems[s], 16)
        tok = (q.keys[s], q.sems[s], 16 * q.uses[s])
        self._mark(tok, reads, writes, par)
        return tok

    def barrier(self):
        toks = [(n, e.sem, e.count) for n, e in self.eng.items() if e.count > 0]
        for q in self.dq.values():
            for i in range(len(q.sems)):
                if q.uses[i] > 0:
                    toks.append((q.keys[i], q.sems[i], 16 * q.uses[i]))
        for e in self.eng.values():
            self._wait(e, [t for t in toks if t[0] != e.name])

    def mm(self, out, lhsT, rhs, start, stop, reads, writes):
        return self.op("pe", lambda h: h.matmul(out, lhsT=lhsT, rhs=rhs, start=start, stop=stop),
                       reads=reads, writes=writes)

    def tr(self, out, in_, ident, reads, writes):
        return self.op("pe", lambda h: h.transpose(out=out, in_=in_, identity=ident),
                       reads=reads, writes=writes)


def load_w(k, dst, w_ap, kc_n, ncols, blk=2048):
    nb = -(-ncols // blk)
    blk = -(-ncols // nb)
    for kc in range(kc_n):
        for c0 in range(0, ncols, blk):
            c1 = min(ncols, c0 + blk)
            k.dma("pool", dst[:, kc, c0:c1], w_ap[kc * 128:(kc + 1) * 128, c0:c1], par=[dst])


def scale_rows(k, es, w, g_ap, kc_n, name, gcol=None):
    if gcol is None:
        gcol = k.sb(es, name, [128, kc_n], F32)
    k.dma("sp", gcol[:, :], g_ap.rearrange("(k p) -> p k", p=128), writes=[gcol],
          allow_slow_non_contiguous=True)
    for kc in range(kc_n):
        eng = "dve" if kc % 2 == 0 else "pool"
        k.op(eng, lambda h, kc=kc: h.tensor_scalar(out=w[:, kc, :], in0=w[:, kc, :],
                                                    scalar1=gcol[:, kc:kc + 1], scalar2=None,
                                                    op0=ALU.mult),
             reads=[gcol], writes=[w])


class Consts:
    pass


def make_consts(k, es):
    c = Consts()
    c.ones = k.sb(es, "c_ones", [128, 128], F32)
    c.ident = k.sb(es, "c_ident", [128, 128], F32)
    c.identb = k.sb(es, "c_identb", [128, 128], BF16)
    c.caus = k.sb(es, "c_caus", [128, 128], F32)
    c.cpos = k.sb(es, "c_cpos", [128, 896], F32)
    c.big = k.sb(es, "c_big", [128, 896], F32)
    c.sel = k.sb(es, "c_sel", [16, 16, 128], F32)
    k.op("pool", lambda h: h.memset(c.ones[:, :], 1.0), writes=[c.ones])
    k.op("pool", lambda h: h.memset(c.big[:, :], BIG), writes=[c.big])
    k.op("pool", lambda h: h.affine_select(out=c.ident[:, :], in_=c.ones[:, :], pattern=[[-1, 128]],
                                           compare_op=ALU.is_equal, fill=0.0, base=0,
                                           channel_multiplier=1),
         reads=[c.ones], writes=[c.ident])
    k.op("pool", lambda h: h.affine_select(out=c.caus[:, :], in_=c.ones[:, :], pattern=[[1, 128]],
                                           compare_op=ALU.is_ge, fill=0.0, base=0,
                                           channel_multiplier=-1),
         reads=[c.ones], writes=[c.caus])
    k.op("pool", lambda h: h.affine_select(out=c.cpos[:, :], in_=c.big[:, :], pattern=[[-1, 896]],
                                           compare_op=ALU.is_gt, fill=0.0, base=384,
                                           channel_multiplier=1),
         reads=[c.big], writes=[c.cpos])
    k.op("pool", lambda h: h.tensor_copy(out=c.identb[:, :], in_=c.ident[:, :]),
         reads=[c.ident], writes=[c.identb])
    for hh in range(16):
        k.op("dve", lambda h, hh=hh: h.tensor_copy(out=c.sel[:, hh, :],
                                                    in_=c.ident[0:16, hh:hh + 1].to_broadcast([16, 128])),
             reads=[c.ident], writes=[c.sel])
    return c


def rms_to_hT(k, c, x_t, hT, col0, st, ti):
    j = ti % 2
    k.op("act", lambda h: h.activation(out=st.junk[:, :], in_=x_t[:, :], func=AF.Square,
                                       accum_out=st.ss[j][:, :]),
         reads=[x_t], writes=[st.junk, st.ss[j]])
    k.op("act", lambda h: h.activation(out=st.sq[j][:, :], in_=st.ss[j][:, :], func=AF.Sqrt,
                                       scale=1.0 / D, bias=1e-6),
         reads=[st.ss[j]], writes=[st.sq[j]])
    k.op("dve", lambda h: h.reciprocal(out=st.rstd[j][:, :], in_=st.sq[j][:, :]),
         reads=[st.sq[j]], writes=[st.rstd[j]])
    k.op("act", lambda h: h.activation(out=st.xn[j][:, :], in_=x_t[:, :], func=AF.Copy,
                                       scale=st.rstd[j][:, :]),
         reads=[x_t, st.rstd[j]], writes=[st.xn[j]])
    for kc in range(8):
        k.tr(st.pT[j][:, kc * 128:(kc + 1) * 128], st.xn[j][:, kc * 128:(kc + 1) * 128],
             c.identb[:, :], reads=[st.xn[j], c.identb], writes=[st.pT[j]])
    k.op("dve", lambda h: h.tensor_copy(out=hT[:, :, col0:col0 + 128],
                                        in_=st.pT[j][:, :].rearrange("p (k n) -> p k n", k=8)),
         reads=[st.pT[j]], writes=[hT])


class NormState:
    def __init__(self, k, es, tag):
        self.junk = k.sb(es, tag + "junk", [128, D], BF16)
        self.ss = [k.sb(es, f"{tag}ss{i}", [128, 1], F32) for i in range(2)]
        self.sq = [k.sb(es, f"{tag}sq{i}", [128, 1], F32) for i in range(2)]
        self.rstd = [k.sb(es, f"{tag}rstd{i}", [128, 1], F32) for i in range(2)]
        self.xn = [k.sb(es, f"{tag}xn{i}", [128, D], BF16) for i in range(2)]
        self.pT = [k.ps(es, f"{tag}pT{i}", [128, D], BF16) for i in range(2)]


def evac(k, i, out, in_, reads, writes, scale=None, func=None):
    if func is not None or (i % 2 == 0):
        f = func if func is not None else AF.Copy
        sc = 1.0 if scale is None else scale
        return k.op("act", lambda h: h.activation(out=out, in_=in_, func=f, scale=sc),
                    reads=reads, writes=writes)
    if scale is None:
        scale = 1.0
    return k.op("dve", lambda h: h.tensor_scalar(out=out, in0=in_, scalar1=scale, scalar2=None,
                                                  op0=ALU.mult), reads=reads, writes=writes)


def groups():
    gs = [(g * 512, 512) for g in range(8)]
    gs.append((TP, NS * TS))
    return gs


def phase1(k, c, io, scr, rows):
    with contextlib.ExitStack() as es:
        stf = [k.sb(es, f"p1sf{i}", [128, 512], F32) for i in range(2)]
        W = k.sb(es, "p1W", [128, 8, 3592], BF16)
        load_w(k, W, io["w_in_ab"], 8, 3592)
        scale_rows(k, es, W, io["norm_mix"][0, :], 8, "p1g")
        st = NormState(k, es, "p1")
        xts = [k.sb(es, f"p1x{i}", [128, D], F32) for i in range(2)]
        hTs = [k.sb(es, f"p1h{i}", [128, 8, 512], BF16) for i in range(2)]
        pfm = [k.ps(es, f"p1pf{i}", [128, 512]) for i in range(2)]
        ptm = [k.ps(es, f"p1pt{i}", [128, 512]) for i in range(2)]
        stb = [k.sb(es, f"p1sb{i}", [128, 512], BF16) for i in range(4)]
        fm = [(0, scr["qTm"], None), (512, scr["kTm"], 128.0 ** -0.5),
              (2056, scr["qTb"], 0.125), (2568, scr["kTb"], None)]
        ev = 0
        sbi = 0
        sfi = 0
        ti = 0
        gl = groups()
        tis = [0]

        def norm_group(gi):
            tok0, n = gl[gi]
            for t in range(n // 128):
                xt = xts[tis[0] % 2]
                k.dma("sp", xt[:, :], io["xin"][tok0 + t * 128: tok0 + (t + 1) * 128, :], writes=[xt])
                rms_to_hT(k, c, xt, hTs[gi % 2], t * 128, st, tis[0])
                tis[0] += 1

        norm_group(0)
        for gi, (tok0, n) in enumerate(gl):
            hT = hTs[gi % 2]
            if gi + 1 < len(gl):
                norm_group(gi + 1)
            for (c0, dst, scale) in ([] if 'nofm' in DBG else fm):
                for mc in range(4):
                    pp = pfm[ev % 2]
                    for kc in range(8):
                        k.mm(pp[:, 0:n], W[:, kc, c0 + mc * 128: c0 + (mc + 1) * 128], hT[:, kc, 0:n],
                             kc == 0, kc == 7, reads=[W, hT], writes=[pp])
                    sbt = stb[sbi % 4]
                    sbi += 1
                    evac(k, ev, sbt[:, 0:n], pp[:, 0:n], [pp], [sbt], scale=scale)
                    ev += 1
                    k.dma("sp", dst[mc * 128:(mc + 1) * 128, tok0:tok0 + n], sbt[:, 0:n], reads=[sbt])
            for (c0, row) in ([] if 'nogate' in DBG else ((2048, rows["ig"]), (2052, rows["fg"]))):
                pp = pfm[ev % 2]
                for kc in range(8):
                    k.mm(pp[0:4, 0:n], W[:, kc, c0:c0 + 4], hT[:, kc, 0:n], kc == 0, kc == 7,
                         reads=[W, hT], writes=[pp])
                evac(k, 1, row[0:4, tok0:tok0 + n], pp[0:4, 0:n], [pp], [row])
                ev += 1
            for t in range(0 if 'notm' in DBG else n // 128):
                r0 = tok0 + t * 128
                is_out = (r0 >= TP - 512)
                blocks = [(512, scr["kmTok"], 128.0 ** -0.5, None, None),
                          (1024, scr["vmTok"], None, None, None),
                          (1536, scr["osTok"], None, (None if 'nosig' in DBG else AF.Sigmoid), None),
                          (3080, scr["vbTok"], None, None, "bv")]
                if is_out:
                    blocks.append((2568, None, None, None, "bk"))
                for (c0, dst, scale, func, outn) in blocks:
                    pp = ptm[ev % 2]
                    for kc in range(8):
                        k.mm(pp[:, :], hT[:, kc, t * 128:(t + 1) * 128], W[:, kc, c0:c0 + 512],
                             kc == 0, kc == 7, reads=[W, hT], writes=[pp])
                    both = (outn is not None and is_out)
                    if both:
                        sft = stf[sfi % 2]
                        sfi += 1
                        evac(k, 1, sft[:, :], pp[:, :], [pp], [sft])
                        ev += 1
                    if dst is not None:
                        sbt = stb[sbi % 4]
                        sbi += 1
                        if both:
                            k.op("act", lambda h, sbt=sbt, sft=sft: h.activation(out=sbt[:, :], in_=sft[:, :], func=AF.Copy),
                                 reads=[sft], writes=[sbt])
                        else:
                            evac(k, ev, sbt[:, :], pp[:, :], [pp], [sbt], scale=scale, func=func)
                        ev += 1
                        k.dma("sp", dst[r0:r0 + 128, :], sbt[:, :], reads=[sbt])
                    if both:
                        for hh in range(0 if ('noodma2' in DBG or 'whole' in DBG) else 8):
                            src = sft[:, hh * 64:(hh + 1) * 64]
                            if r0 < TP:
                                q0 = r0 - (TP - 512)
                                if 'toscr' in DBG:
                                    k.dma("sp", scr["x1"][q0:q0 + 128, hh * 64:(hh + 1) * 64], src, reads=[sft])
                                else:
                                    k.dma("pool" if 'opool' in DBG else "sp", io["p" + outn][hh, q0:q0 + 128, :], src, reads=[sft])
                            else:
                                for s2 in range(2):
                                    sq = (r0 - TP) // TS + s2
                                    k.dma("sp", io["s" + outn][sq, hh, :, :], src[s2 * 64:(s2 + 1) * 64],
                                          reads=[sft])
    k.barrier()


IN_SPECS = [
    ("xin", [R, D]), ("mC0", [NS, 4, 128, 128]), ("mn0", [NS, 4, 128]), ("mm0", [NS, 4]),
    ("cbk", [NS, 8, 512, 64]), ("cbv", [NS, 8, 512, 64]),
    ("cfk", [NS, 16, 2048, 64]), ("cfv", [NS, 16, 2048, 64]), ("cflf", [NS, 16, 2048]),
    ("norm_mix", [2, D]), ("norm_ffn", [2, D]), ("norm_final", [D]),
    ("w_in_ab", [D, 3592]), ("b_gate_ab", [8]), ("mlstm_gain", [512]), ("bandbias", [8, 128, 640]),
    ("bandmask", [128, 640]),
    ("w_out_ab", [D, D]), ("w_in_fox", [D, 3088]), ("b_fox_f", [16]), ("w_out_fox", [D, D]),
    ("w_ffn_in", [2, D, 2 * HID]), ("w_ffn_out", [2, HID, D]),
]
OUT_SPECS = [
    ("y", [R, D]), ("pC", [4, 128, 128]), ("pn", [4, 128]), ("pm", [4, 1]),
    ("pbk", [8, 512, 64]), ("pbv", [8, 512, 64]),
    ("pfk", [16, TP, 64]), ("pfv", [16, TP, 64]), ("pflf", [16, TP]),
    ("sC", [NS, 4, 128, 128]), ("sn", [NS, 4, 128]), ("sm", [NS, 4, 1]),
    ("sbk", [NS, 8, TS, 64]), ("sbv", [NS, 8, TS, 64]),
    ("sfk", [NS, 16, TS, 64]), ("sfv", [NS, 16, TS, 64]), ("sflf", [NS, 16, TS]),
]
SCR_SPECS = [
    ("qTm", [512, R], BF16), ("kTm", [512, R], BF16), ("qTb", [512, R], BF16), ("kTb", [512, R], BF16),
    ("kmTok", [R, 512], BF16), ("vmTok", [R, 512], BF16), ("osTok", [R, 512], BF16),
    ("vbTok", [R, 512], BF16), ("catT", [D, R], BF16),
    ("x1", [R, D], F32), ("x2", [R, D], F32), ("x3", [R, D], F32),
    ("qTf", [D, R], BF16), ("kTf", [D, R], BF16), ("vfTok", [R, D], BF16), ("attT", [D, R], BF16),
    ("hidT", [HID, R], BF16), ("rds", [4, 512], F32),
]

PHASES = 99
DBG = ""


def run_phases(k, c, io, scr, upto=None):
    upto = PHASES if upto is None else upto
    with contextlib.ExitStack() as es1:
        rows = {"ig": k.sb(es1, "row_ig", [4, R], F32), "fg": k.sb(es1, "row_fg", [4, R], F32)}
        pers = {"GW": k.sb(es1, "GW", [4, 2, R], F32), "COLS": k.sb(es1, "COLS", [128, NCH, 3, 4], F32),
                "DECB": k.sb(es1, "DECB", [128, 4, NCH], F32)}
        phase1(k, c, io, scr, rows)
        if upto >= 2:
            phase2(k, c, io, scr, rows, pers)
        if upto >= 3:
            phase3(k, c, io, scr, pers)
    if upto >= 4:
        phase4(k, c, io, scr)
    if upto >= 6:
        with contextlib.ExitStack() as esw:
            Wg = ffn_w_alloc(k, esw, "p6")
            outproj_phase(k, c, io, scr, scr["catT"], io["w_out_ab"], io["xin"], scr["x1"], "p5",
                          prefetch=lambda: ffn_w_load(k, io, 0, Wg))
            ffn_in_phase(k, c, io, scr, 0, scr["x1"], "p6", W=Wg[0])
        ffn_out_phase(k, c, io, scr, 0, scr["x1"], scr["x2"], False, "p7")
    if upto >= 8:
        fox_phases(k, c, io, scr)
    if upto >= 9:
        with contextlib.ExitStack() as esw:
            Wg = ffn_w_alloc(k, esw, "pa")
            outproj_phase(k, c, io, scr, scr["attT"], io["w_out_fox"], scr["x2"], scr["x3"], "p9",
                          prefetch=lambda: ffn_w_load(k, io, 1, Wg))
            ffn_in_phase(k, c, io, scr, 1, scr["x3"], "pa", W=Wg[0])
        ffn_out_phase(k, c, io, scr, 1, scr["x3"], io["y"], True, "pb")


def build():
    nc = bass.Bass("TRN2", target_bir_lowering=False)
    io = {}
    for name, shape in IN_SPECS:
        io[name] = nc.dram_tensor(name, shape, F32, kind="ExternalInput").ap()
    for name, shape in OUT_SPECS:
        io[name] = nc.dram_tensor(name, shape, F32, kind="ExternalOutput").ap()
    scr = {}
    for name, shape, dt in SCR_SPECS:
        scr[name] = nc.dram_tensor("scr_" + name, shape, dt).ap()
    with contextlib.ExitStack() as es:
        k = KB(nc, es)
        c = make_consts(k, es)
        run_phases(k, c, io, scr)
        k.barrier()
    return nc


_NC_CACHE = {}


def kernel(**inp):
    f = lambda a: np.ascontiguousarray(np.asarray(a, dtype=np.float32))
    xp, xs = f(inp["x_prompt"]), f(inp["x_sample"])
    kk = np.arange(128)[:, None]
    qq = np.arange(640)[None, :]
    rel = qq - kk
    idx = np.clip(rel, -63, 128) + 63
    table = f(inp["rel_bias_table"])[0]
    bandbias = np.ascontiguousarray(table[:, idx])
    dch = (qq // 64) - (kk // 64)
    bandmask = np.where((dch >= 0) & (dch <= 8), 0.0, -BIG).astype(np.float32)
    common = {
        "norm_mix": f(inp["norm_mix"]), "norm_ffn": f(inp["norm_ffn"]), "norm_final": f(inp["norm_final"]),
        "w_in_ab": f(inp["w_in_ab"])[0], "b_gate_ab": f(inp["b_gate_ab"])[0],
        "mlstm_gain": f(inp["mlstm_gain"])[0], "bandbias": bandbias, "bandmask": bandmask,
        "w_out_ab": f(inp["w_out_ab"])[0], "w_in_fox": f(inp["w_in_fox"])[0],
        "b_fox_f": f(inp["b_fox_f"])[0], "w_out_fox": f(inp["w_out_fox"])[0],
        "w_ffn_in": f(inp["w_ffn_in"]), "w_ffn_out": f(inp["w_ffn_out"]),
    }
    in_maps = []
    for cid in range(8):
        b = cid % 4
        s0 = 4 * cid
        m = dict(common)
        m["xin"] = np.ascontiguousarray(np.concatenate([xp[b], xs[s0:s0 + 4].reshape(NS * TS, D)], axis=0))
        m["mC0"] = f(inp["state_mlstm_C"])[0, s0:s0 + 4]
        m["mn0"] = f(inp["state_mlstm_n"])[0, s0:s0 + 4]
        m["mm0"] = f(inp["state_mlstm_m"])[0, s0:s0 + 4]
        m["cbk"] = f(inp["cache_band_k"])[0, s0:s0 + 4]
        m["cbv"] = f(inp["cache_band_v"])[0, s0:s0 + 4]
        m["cfk"] = f(inp["cache_fox_k"])[0, s0:s0 + 4]
        m["cfv"] = f(inp["cache_fox_v"])[0, s0:s0 + 4]
        m["cflf"] = f(inp["cache_fox_logf"])[0, s0:s0 + 4]
        in_maps.append({kk_: np.ascontiguousarray(v) for kk_, v in m.items()})
    if "nc" not in _NC_CACHE:
        _NC_CACHE["nc"] = build()
    res = run_bass_kernel_spmd(_NC_CACHE["nc"], in_maps, core_ids=list(range(8)))
    rs = res.results
    P = lambda n: np.stack([rs[b][n] for b in range(4)], axis=0)
    S = lambda n: np.concatenate([rs[cid][n] for cid in range(8)], axis=0)
    y_prompt = np.stack([rs[b]["y"][:TP] for b in range(4)], axis=0)
    y_sample = np.concatenate([rs[cid]["y"][TP:].reshape(NS, TS, D) for cid in range(8)], axis=0)
    outs = (
        y_prompt, y_sample,
        P("pC")[None], P("pn")[None], P("pm")[None, :, :, 0],
        P("pbk")[None], P("pbv")[None], P("pfk")[None], P("pfv")[None], P("pflf")[None],
        S("sC")[None], S("sn")[None], S("sm")[None, :, :, 0],
        S("sbk")[None], S("sbv")[None], S("sfk")[None], S("sfv")[None], S("sflf")[None],
    )
    return tuple(np.ascontiguousarray(o, dtype=np.float32) for o in outs)


def chunks():
    cs = [(c * 128, 128) for c in range(32)]
    cs += [(TP + i * TS, TS) for i in range(NS)]
    return cs


NCH = 36


def phase2(k, c, io, scr, rows, pers):
    GW, COLS, DECB = pers["GW"], pers["COLS"], pers["DECB"]
    ig, fg = rows["ig"], rows["fg"]
    with contextlib.ExitStack() as es:
        negb = k.sb(es, "p2negb", [4, 1], F32)
        bigc = k.sb(es, "p2big", [4, 1], F32)
        m0 = k.sb(es, "p2m0", [4, NS], F32)
        NF = k.sb(es, "p2NF", [4, R], F32)
        WS = k.sb(es, "p2WS", [4, R], F32)
        EM = k.sb(es, "p2EM", [4, R], F32)
        DEC = k.sb(es, "p2DEC", [4, NCH], F32)
        PC = k.ps(es, "p2PC", [128, NCH, 3, 4])
        PD = k.ps(es, "p2PD", [128, 4, NCH])
        bg = io["b_gate_ab"]
        k.dma("sp", negb[:, :], bg[4:8].rearrange("(h o) -> h o", o=1), writes=[negb])
        k.dma("sp", bigc[:, :], bg[0:4].rearrange("(h o) -> h o", o=1), writes=[bigc])
        k.dma("sp", m0[:, :], io["mm0"].rearrange("s h -> h s"), writes=[m0], allow_slow_non_contiguous=True)
        k.op("dve", lambda h: h.tensor_scalar(out=negb[:, :], in0=negb[:, :], scalar1=-1.0, scalar2=None,
                                              op0=ALU.mult), writes=[negb])
        k.op("act", lambda h: h.activation(out=fg[:, :], in_=fg[:, :], func=AF.Exp, scale=-1.0,
                                           bias=negb[:, :]), reads=[negb], writes=[fg])
        k.op("act", lambda h: h.activation(out=fg[:, :], in_=fg[:, :], func=AF.Ln, bias=1.0), writes=[fg])
        segs = [(0, TP, None)] + [(TP + i * TS, TS, i) for i in range(NS)]
        for (s0, sn, si) in segs:
            k.op("dve", lambda h, s0=s0, sn=sn: h.tensor_tensor_scan(
                out=NF[:, s0:s0 + sn], data0=fg[:, s0:s0 + sn], data1=fg[:, s0:s0 + sn], initial=0.0,
                op0=ALU.add, op1=ALU.max), reads=[fg], writes=[NF])
        k.op("dve", lambda h: h.scalar_tensor_tensor(out=ig[:, :], in0=ig[:, :], scalar=bigc[:, :],
                                                     in1=NF[:, :], op0=ALU.add, op1=ALU.add),
             reads=[bigc, NF], writes=[ig])
        for (s0, sn, si) in segs:
            init = 0.0 if si is None else m0[:, si:si + 1]
            k.op("dve", lambda h, s0=s0, sn=sn, init=init: h.tensor_tensor_scan(
                out=GW[:, 0, s0:s0 + sn], data0=ig[:, s0:s0 + sn], data1=ig[:, s0:s0 + sn], initial=init,
                op0=ALU.max, op1=ALU.max), reads=[ig, m0], writes=[GW])
        Gp = GW[:, 0, 0:TP].rearrange("p (c t) -> p c t", t=128)
        k.op("pool", lambda h: h.memset(fg[:, 0:128], 0.0), writes=[fg])
        k.op("dve", lambda h: h.tensor_copy(
            out=fg[:, 128:TP].rearrange("p (c t) -> p c t", t=128),
            in_=Gp[:, 0:31, 127:128].to_broadcast([4, 31, 128])), reads=[GW], writes=[fg])
        for i in range(NS):
            s0 = TP + i * TS
            k.op("dve", lambda h, s0=s0, i=i: h.tensor_copy(out=fg[:, s0:s0 + TS],
                                                          in_=m0[:, i:i + 1].to_broadcast([4, TS])),
                 reads=[m0], writes=[fg])
        k.op("dve", lambda h: h.tensor_tensor(out=WS[:, :], in0=fg[:, :], in1=GW[:, 0, :], op=ALU.subtract),
             reads=[fg, GW], writes=[WS])
        k.op("act", lambda h: h.activation(out=GW[:, 1, :], in_=WS[:, :], func=AF.Exp), reads=[WS], writes=[GW])
        k.op("dve", lambda h: h.tensor_copy(
            out=WS[:, 0:TP].rearrange("p (c t) -> p c t", t=128),
            in_=Gp[:, :, 127:128].to_broadcast([4, 32, 128])), reads=[GW], writes=[WS])
        for i in range(NS):
            s0 = TP + i * TS
            k.op("dve", lambda h, s0=s0: h.tensor_copy(
                out=WS[:, s0:s0 + TS], in_=GW[:, 0, s0 + TS - 1:s0 + TS].to_broadcast([4, TS])),
                reads=[GW], writes=[WS])
        k.op("dve", lambda h: h.tensor_tensor(
            out=DEC[:, 0:32], in0=fg[:, 0:TP].rearrange("p (c t) -> p c t", t=128)[:, :, 0],
            in1=WS[:, 0:TP].rearrange("p (c t) -> p c t", t=128)[:, :, 0], op=ALU.subtract),
            reads=[fg, WS], writes=[DEC])
        for i in range(NS):
            s0 = TP + i * TS
            k.op("dve", lambda h, s0=s0, i=i: h.tensor_tensor(out=DEC[:, 32 + i:33 + i], in0=fg[:, s0:s0 + 1],
                                                            in1=WS[:, s0:s0 + 1], op=ALU.subtract),
                 reads=[fg, WS], writes=[DEC])
        k.op("act", lambda h: h.activation(out=DEC[:, :], in_=DEC[:, :], func=AF.Exp), writes=[DEC])
        k.op("dve", lambda h: h.tensor_tensor(out=WS[:, :], in0=ig[:, :], in1=WS[:, :], op=ALU.subtract),
             reads=[ig], writes=[WS])
        k.op("act", lambda h: h.activation(out=WS[:, :], in_=WS[:, :], func=AF.Exp), writes=[WS])
        k.op("dve", lambda h: h.tensor_tensor(out=EM[:, :], in0=GW[:, 0, :], in1=NF[:, :], op=ALU.subtract),
             reads=[GW, NF], writes=[EM])
        k.dma("sp", io["pm"][:, :], EM[:, TP - 1:TP], reads=[EM])
        for i in range(NS):
            e1 = TP + (i + 1) * TS
            k.dma("sp", io["sm"][i, :, :], EM[:, e1 - 1:e1], reads=[EM])
        k.op("act", lambda h: h.activation(out=EM[:, :], in_=EM[:, :], func=AF.Exp, scale=-1.0), writes=[EM])
        for ci, (r0, n) in enumerate(chunks()):
            for xi, X in enumerate((ig, WS, EM)):
                k.tr(PC[0:n, ci, xi, :], X[0:4, r0:r0 + n], c.ident[0:4, 0:4], reads=[X, c.ident], writes=[PC])
        k.op("dve", lambda h: h.tensor_copy(out=COLS[:, 0:32, :, :], in_=PC[:, 0:32, :, :]), reads=[PC], writes=[COLS])
        k.op("dve", lambda h: h.tensor_copy(out=COLS[0:64, 32:NCH, :, :], in_=PC[0:64, 32:NCH, :, :]), reads=[PC], par=[COLS])
        for hh in range(4):
            k.mm(PD[:, hh, :], c.sel[0:4, hh, :], DEC[0:4, :], True, True, reads=[c.sel, DEC], writes=[PD])
        k.op("dve", lambda h: h.tensor_copy(out=DECB[:, :, :], in_=PD[:, :, :]), reads=[PD], writes=[DECB])
    k.barrier()


def phase3(k, c, io, scr, pers):
    GW, COLS, DECB = pers["GW"], pers["COLS"], pers["DECB"]
    with contextlib.ExitStack() as es:
        gain = k.sb(es, "p3gain", [128, 512], F32)
        k.dma("sp", gain[:, :], io["mlstm_gain"].partition_broadcast(128), writes=[gain])
        qTs = [k.sb(es, f"p3q{i}", [128, 4, 128], BF16) for i in range(2)]
        kTs = [k.sb(es, f"p3k{i}", [128, 4, 128], BF16) for i in range(2)]
        kts = [k.sb(es, f"p3kt{i}", [128, 4, 128], BF16) for i in range(2)]
        vas = [k.sb(es, f"p3va{i}", [128, 4, 129], BF16) for i in range(2)]
        oss = [k.sb(es, f"p3os{i}", [128, 512], BF16) for i in range(3)]
        for i in range(2):
            k.op("pool", lambda h, i=i: h.memset(vas[i][:, :, 128:129], 1.0), writes=[vas[i]])
        Cf = [k.sb(es, f"p3Cf{h}", [128, 129], F32) for h in range(4)]
        Cb = [k.sb(es, f"p3Cb{h}", [128, 129], BF16) for h in range(4)]
        E = k.sb(es, "p3E", [128, 4, 128], F32)
        WT = k.sb(es, "p3WT", [128, 4, 128], BF16)
        QS = k.sb(es, "p3QS", [128, 4, 128], BF16)
        KW = k.sb(es, "p3KW", [128, 4, 128], BF16)
        ONs = [k.sb(es, f"p3ON{i}", [128, 4, 129], F32) for i in range(2)]
        H = k.sb(es, "p3H", [128, 4, 128], F32)
        SQ = k.sb(es, "p3SQ", [128, 4, 128], F32)
        OAs = [k.sb(es, f"p3OA{i}", [128, 512], BF16) for i in range(2)]
        G2s = [k.sb(es, f"p3G2{i}", [128, 512], F32) for i in range(2)]
        junk = k.sb(es, "p3junk", [128, 128], F32)
        sm2 = [k.sb(es, f"p3t{i}", [128, 4], F32) for i in range(4)]
        OT = k.sb(es, "p3OT", [128, 4, 128], BF16)
        sm = [k.sb(es, f"p3s{i}", [128, 4], F32) for i in range(6)]
        PS_S = k.ps(es, "p3PS", [128, 4, 128])
        PS_G = k.ps(es, "p3PG", [128, 4, 128])
        PS_W = k.ps(es, "p3PW", [128, 4, 128])
        PS_O = [k.ps(es, f"p3PO{i}", [128, 2, 129]) for i in range(2)]
        PS_D = [k.ps(es, f"p3PD{i}", [128, 2, 129]) for i in range(2)]
        PS_T = k.ps(es, "p3PT", [128, 4, 128], BF16)
        qv = scr["qTm"].rearrange("(h d) r -> d h r", h=4)
        kv = scr["kTm"].rearrange("(h d) r -> d h r", h=4)
        catv = scr["catT"][0:512, :].rearrange("(h d) r -> d h r", h=4)
        chs = chunks()

        def load(ci):
            r0, n = chs[ci]
            j = ci % 2
            k.dma("sp", qTs[j][:, :, 0:n], qv[:, :, r0:r0 + n], writes=[qTs[j]])
            k.dma("sp", kTs[j][:, :, 0:n], kv[:, :, r0:r0 + n], writes=[kTs[j]])
            k.dma("sp", kts[j][0:n, :, :], scr["kmTok"][r0:r0 + n, :].rearrange("t (h e) -> t h e", h=4),
                  writes=[kts[j]])
            k.dma("sp", vas[j][0:n, :, 0:128], scr["vmTok"][r0:r0 + n, :].rearrange("t (h e) -> t h e", h=4),
                  writes=[vas[j]])
            k.dma("sp", oss[ci % 3][0:n, :], scr["osTok"][r0:r0 + n, :], writes=[oss[ci % 3]])

        load(0)
        pendA, pendA_b, pendB = [], [], []
        for ci, (r0, n) in enumerate(chs):
            j = ci % 2
            qT, kT, kt, va, osg = qTs[j], kTs[j], kts[j], vas[j], oss[ci % 3]
            ON = ONs[ci % 2]
            if ci + 1 < NCH:
                load(ci + 1)
            if ci == 0:
                for hh in range(4):
                    k.op("pool", lambda h, hh=hh: h.memset(Cf[hh][:, :], 0.0), writes=[Cf[hh]])
                    k.op("pool", lambda h, hh=hh: h.memset(Cb[hh][:, :], 0.0), writes=[Cb[hh]])
            elif ci >= 32:
                si = ci - 32
                for hh in range(4):
                    k.dma("sp", Cf[hh][:, 0:128], io["mC0"][si, hh, :, :], writes=[Cf[hh]])
                    k.dma("sp", Cf[hh][:, 128:129], io["mn0"][si, hh, :].rearrange("(d o) -> d o", o=1),
                          par=[Cf[hh]])
                    k.op("act", lambda h, hh=hh: h.activation(out=Cb[hh][:, :], in_=Cf[hh][:, :], func=AF.Copy),
                         reads=[Cf[hh]], writes=[Cb[hh]])
            for hh in range(4):
                k.mm(PS_S[0:n, hh, 0:n], kT[:, hh, 0:n], qT[:, hh, 0:n], True, True, reads=[kT, qT], writes=[PS_S])
            for hh in range(4):
                k.mm(PS_G[0:n, hh, 0:n], c.sel[0:4, hh, 0:n], GW[0:4, 0, r0:r0 + n], True, True,
                     reads=[c.sel, GW], writes=[PS_G])
            for hh in range(4):
                k.mm(PS_W[:, hh, 0:n], c.sel[0:4, hh, :], GW[0:4, 1, r0:r0 + n], True, True,
                     reads=[c.sel, GW], writes=[PS_W])
            for hh in range(4):
                k.op("act", lambda h, hh=hh: h.activation(out=E[0:n, hh, 0:n], in_=PS_G[0:n, hh, 0:n], func=AF.Exp,
                                                        scale=-1.0, bias=COLS[0:n, ci, 0, hh:hh + 1]),
                     reads=[PS_G, COLS], writes=[E])
            for hh in range(4):
                k.op("pool", lambda h, hh=hh: h.tensor_tensor(out=E[0:n, hh, 0:n], in0=E[0:n, hh, 0:n],
                                                            in1=c.caus[0:n, 0:n], op=ALU.mult),
                     reads=[c.caus], writes=[E])
            k.op("dve", lambda h: h.tensor_tensor(out=WT[0:n, :, 0:n], in0=PS_S[0:n, :, 0:n], in1=E[0:n, :, 0:n],
                                                  op=ALU.mult), reads=[PS_S, E], writes=[WT])
            k.op("dve", lambda h: h.tensor_tensor(out=QS[:, :, 0:n], in0=PS_W[:, :, 0:n], in1=qT[:, :, 0:n],
                                                  op=ALU.mult), reads=[PS_W, qT], writes=[QS])
            for hh in range(4):
                po = PS_O[hh // 2]
                k.mm(po[0:n, hh % 2, :], QS[:, hh, 0:n], Cb[hh][:, :], True, False, reads=[QS, Cb[hh]], writes=[po])
                k.mm(po[0:n, hh % 2, :], WT[0:n, hh, 0:n], va[0:n, hh, :], False, True, reads=[WT, va], writes=[po])
            for i2 in range(2):
                k.op("dve", lambda h, i2=i2: h.tensor_copy(out=ON[0:n, 2 * i2:2 * i2 + 2, :], in_=PS_O[i2][0:n, :, :]),
                     reads=[PS_O[i2]], writes=[ON])
            for hh in range(4):
                k.op("act", lambda h, hh=hh: h.activation(out=KW[0:n, hh, :], in_=kt[0:n, hh, :], func=AF.Copy,
                                                        scale=COLS[0:n, ci, 1, hh:hh + 1]),
                     reads=[kt, COLS], writes=[KW])
            for hh in range(4):
                pd = PS_D[hh // 2]
                k.mm(pd[:, hh % 2, :], KW[0:n, hh, :], va[0:n, hh, :], True, True, reads=[KW, va], writes=[pd])
            for hh in range(4):
                pd = PS_D[hh // 2]
                k.op("dve", lambda h, hh=hh, pd=pd: h.scalar_tensor_tensor(
                    out=Cf[hh][:, :], in0=Cf[hh][:, :], scalar=DECB[:, hh, ci:ci + 1], in1=pd[:, hh % 2, :],
                    op0=ALU.mult, op1=ALU.add), reads=[DECB, pd], writes=[Cf[hh]])
                k.op("act", lambda h, hh=hh: h.activation(out=Cb[hh][:, :], in_=Cf[hh][:, :], func=AF.Copy),
                     reads=[Cf[hh]], writes=[Cb[hh]])
            if ci == 31 or ci >= 32:
                for hh in range(4):
                    if ci == 31:
                        oc, on = io["pC"][hh, :, :], io["pn"][hh, :]
                    else:
                        oc, on = io["sC"][ci - 32, hh, :, :], io["sn"][ci - 32, hh, :]
                    k.dma("sp", oc, Cf[hh][:, 0:128], reads=[Cf[hh]])
                    k.dma("sp", on.rearrange("(d o) -> d o", o=1), Cf[hh][:, 128:129], reads=[Cf[hh]])
            G2 = G2s[ci % 2]
            k.op("pool", lambda h, G2=G2, osg=osg: h.tensor_tensor(out=G2[0:n, :], in0=gain[0:n, :], in1=osg[0:n, :],
                                                                   op=ALU.mult), reads=[gain, osg], writes=[G2])

            def epiA(ci=ci, n=n, ON=ON, G2=G2, OAb=OAs[ci % 2]):
                den, dd, rr, mu, var, rs = sm
                s1, s2, aa, bb = sm2
                k.op("dve", lambda h: h.tensor_tensor(out=dd[0:n, :], in0=ON[0:n, :, 128], in1=COLS[0:n, ci, 2, :],
                                                      op=ALU.max), reads=[ON, COLS], writes=[dd])
                k.op("dve", lambda h: h.tensor_scalar(out=den[0:n, :], in0=ON[0:n, :, 128], scalar1=-1.0, scalar2=None,
                                                      op0=ALU.mult), reads=[ON], writes=[den])
                k.op("dve", lambda h: h.tensor_tensor(out=dd[0:n, :], in0=dd[0:n, :], in1=den[0:n, :],
                                                      op=ALU.max), reads=[den], writes=[dd])
                k.op("dve", lambda h: h.reciprocal(out=rr[0:n, :], in_=dd[0:n, :]), reads=[dd], writes=[rr])
                for hh in range(4):
                    k.op("act", lambda h, hh=hh: h.activation(out=junk[0:n, :], in_=ON[0:n, hh, 0:128], func=AF.Copy,
                                                            scale=rr[0:n, hh:hh + 1], accum_out=s1[0:n, hh:hh + 1]),
                         reads=[ON, rr], writes=[junk, s1])
                    k.op("act", lambda h, hh=hh: h.activation(out=junk[0:n, :], in_=ON[0:n, hh, 0:128], func=AF.Square,
                                                            scale=rr[0:n, hh:hh + 1], accum_out=s2[0:n, hh:hh + 1]),
                         reads=[ON, rr], writes=[junk, s2])
                k.op("dve", lambda h: h.tensor_scalar(out=mu[0:n, :], in0=s1[0:n, :], scalar1=1.0 / 128, scalar2=None,
                                                      op0=ALU.mult), reads=[s1], writes=[mu])
                k.op("dve", lambda h: h.tensor_tensor(out=var[0:n, :], in0=mu[0:n, :], in1=mu[0:n, :], op=ALU.mult),
                     reads=[mu], writes=[var])
                k.op("dve", lambda h: h.scalar_tensor_tensor(out=var[0:n, :], in0=s2[0:n, :], scalar=1.0 / 128,
                                                             in1=var[0:n, :], op0=ALU.mult, op1=ALU.subtract),
                     reads=[s2, var], writes=[var])
                k.op("dve", lambda h: h.tensor_scalar(out=var[0:n, :], in0=var[0:n, :], scalar1=0.0, scalar2=None,
                                                      op0=ALU.max), reads=[var], writes=[var])
                k.op("act", lambda h: h.activation(out=rs[0:n, :], in_=var[0:n, :], func=AF.Ln, bias=1e-6),
                     reads=[var], writes=[rs])
                k.op("act", lambda h: h.activation(out=rs[0:n, :], in_=rs[0:n, :], func=AF.Exp, scale=-0.5),
                     reads=[rs], writes=[rs])
                k.op("dve", lambda h: h.tensor_tensor(out=aa[0:n, :], in0=rr[0:n, :], in1=rs[0:n, :], op=ALU.mult),
                     reads=[rr, rs], writes=[aa])
                k.op("dve", lambda h: h.scalar_tensor_tensor(out=bb[0:n, :], in0=mu[0:n, :], scalar=-1.0, in1=rs[0:n, :],
                                                             op0=ALU.mult, op1=ALU.mult), reads=[mu, rs], writes=[bb])
                for hh in range(4):
                    k.op("pool", lambda h, hh=hh: h.tensor_scalar(out=H[0:n, hh, :], in0=ON[0:n, hh, 0:128],
                                                                scalar1=aa[0:n, hh:hh + 1], scalar2=bb[0:n, hh:hh + 1],
                                                                op0=ALU.mult, op1=ALU.add),
                         reads=[ON, aa, bb], writes=[H])
                k.op("dve", lambda h: h.tensor_tensor(out=OAb[0:n, :], in0=H[0:n, :, :].rearrange("p h e -> p (h e)"),
                                                      in1=G2[0:n, :], op=ALU.mult), reads=[H, G2], writes=[OAb])

            def epiB(r0=r0, n=n, OAb=OAs[ci % 2]):
                for hh in range(4):
                    k.tr(PS_T[:, hh, 0:n], OAb[0:n, hh * 128:(hh + 1) * 128], c.identb[0:n, 0:n],
                         reads=[OAb, c.identb], writes=[PS_T])
                k.op("act", lambda h: h.activation(out=OT[:, :, 0:n], in_=PS_T[:, :, 0:n], func=AF.Copy),
                     reads=[PS_T], writes=[OT])
                k.dma("sp", catv[:, :, r0:r0 + n], OT[:, :, 0:n], reads=[OT])
            if pendB:
                pendB.pop(0)()
            if pendA:
                pendA.pop(0)()
                pendB.append(pendA_b.pop(0))
            pendA.append(epiA)
            pendA_b.append(epiB)
        while pendA:
            pendA.pop(0)()
            pendB.append(pendA_b.pop(0))
        while pendB:
            pendB.pop(0)()
    k.barrier()


def attn_epilogue(k, c, PS_OT, nq, OS, RD, PS_BC, out_ap, out_t):
    k.op("dve", lambda h: h.tensor_copy(out=OS[0:65, 0:nq], in_=PS_OT[0:65, 0:nq]), reads=[PS_OT], writes=[OS])
    k.op("act", lambda h: h.activation(out=RD[64:65, 0:nq], in_=OS[64:65, 0:nq], func=AF.Ln), reads=[OS], writes=[RD])
    k.op("act", lambda h: h.activation(out=RD[64:65, 0:nq], in_=RD[64:65, 0:nq], func=AF.Exp, scale=-1.0),
         reads=[RD], writes=[RD])
    k.mm(PS_BC[0:64, 0:nq], c.ones[64:65, 0:64], RD[64:65, 0:nq], True, True, reads=[c.ones, RD], writes=[PS_BC])
    k.op("dve", lambda h: h.tensor_tensor(out=out_ap, in0=OS[0:64, 0:nq], in1=PS_BC[0:64, 0:nq], op=ALU.mult),
         reads=[OS, PS_BC], writes=[out_t])


def phase4(k, c, io, scr):
    with contextlib.ExitStack() as es:
        kTh = [k.sb(es, f"p4k{i}", [128, R], BF16) for i in range(2)]
        qTh = [k.sb(es, f"p4q{i}", [128, R], BF16) for i in range(2)]
        for t_ in kTh + qTh:
            k.op("pool", lambda h, t_=t_: h.memset(t_[64:128, :], 0.0), writes=[t_])
        vbh = [k.sb(es, f"p4v{i}", [128, NT, 65], BF16) for i in range(2)]
        bias = [k.sb(es, f"p4b{i}", [128, 640], F32) for i in range(2)]
        mask = k.sb(es, "p4mask", [128, 640], F32)
        ATT = [k.sb(es, f"p4att{i}", [64, R], BF16) for i in range(2)]
        TMP = [k.sb(es, f"p4tmp{i}", [128, 640], F32) for i in range(2)]
        P = [k.sb(es, f"p4P{i}", [128, 640], BF16) for i in range(2)]
        OS = k.sb(es, "p4OS", [65, 128], F32)
        RD = k.sb(es, "p4RD", [65, 128], F32)
        KC = k.sb(es, "p4KC", [128, 4, 64], BF16)
        KTC = k.sb(es, "p4KTC", [128, 512], BF16)
        k.op("pool", lambda h: h.memset(KTC[64:128, :], 0.0), writes=[KTC])
        VC = k.sb(es, "p4VC", [128, 4, 65], BF16)
        VN = k.sb(es, "p4VN", [64, 65], BF16)
        PS_A = k.ps(es, "p4PA", [128, 512])
        PS_B = k.ps(es, "p4PB", [128, 512])
        PS_OT = [k.ps(es, f"p4PO{i}", [128, 512]) for i in range(2)]
        PS_BC = k.ps(es, "p4PBC", [128, 512])
        PS_KT = k.ps(es, "p4PKT", [64, 4, 128], BF16)
        k.dma("sp", mask[:, :], io["bandmask"][:, :], writes=[mask])
        for i in range(2):
            k.op("pool", lambda h, i=i: h.memset(vbh[i][:, :, 64:65], 1.0), writes=[vbh[i]])
        k.op("pool", lambda h: h.memset(VC[:, :, 64:65], 1.0), writes=[VC])
        k.op("pool", lambda h: h.memset(VN[:, 64:65], 1.0), writes=[VN])
        vview = scr["vbTok"].rearrange("(t p) c -> p t c", p=128)

        def load(hh):
            j = hh % 2
            k.dma("sp", kTh[j][0:64, :], scr["kTb"][hh * 64:(hh + 1) * 64, :], writes=[kTh[j]])
            k.dma("sp", qTh[j][0:64, :], scr["qTb"][hh * 64:(hh + 1) * 64, :], writes=[qTh[j]])
            k.dma("sp", vbh[j][:, :, 0:64], vview[:, :, hh * 64:(hh + 1) * 64], writes=[vbh[j]])
            k.dma("sp", bias[j][:, :], io["bandbias"][hh, :, :], writes=[bias[j]])
            k.op("pool", lambda h: h.tensor_tensor(out=bias[j][:, :], in0=bias[j][:, :], in1=mask[:, :], op=ALU.add),
                 reads=[mask], writes=[bias[j]])

        load(0)
        it = 0
        for hh in range(8):
            j = hh % 2
            kT, qT, vb, bs, att = kTh[j], qTh[j], vbh[j], bias[j], ATT[j]
            if hh + 1 < 8:
                load(hh + 1)
            pendb = []
            for i in range(32):
                nj = min(i, 4) + 1
                tmp, p, pot = TMP[it % 2], P[it % 2], PS_OT[it % 2]
                it += 1
                for s in range(nj):
                    jj = i - s
                    dst = PS_A[:, s * 128:(s + 1) * 128] if s < 4 else PS_B[:, 0:128]
                    k.mm(dst, kT[:, jj * 128:(jj + 1) * 128], qT[:, i * 128:(i + 1) * 128], True, True,
                         reads=[kT, qT], writes=[PS_A if s < 4 else PS_B])
                na = min(nj, 4) * 128
                k.op("dve", lambda h, na=na, tmp=tmp: h.tensor_tensor(out=tmp[:, 0:na], in0=PS_A[:, 0:na],
                                                                     in1=bs[:, 0:na], op=ALU.add),
                     reads=[PS_A, bs], writes=[tmp])
                if nj == 5:
                    k.op("dve", lambda h, tmp=tmp: h.tensor_tensor(out=tmp[:, 512:640], in0=PS_B[:, 0:128],
                                                                  in1=bs[:, 512:640], op=ALU.add),
                         reads=[PS_B, bs], writes=[tmp])
                k.op("act", lambda h, tmp=tmp, p=p, nj=nj: h.activation(out=p[:, 0:nj * 128], in_=tmp[:, 0:nj * 128],
                                                                       func=AF.Exp), reads=[tmp], writes=[p])
                def fin(i=i, nj=nj, p=p, pot=pot):
                    for s in range(nj):
                        jj = i - s
                        k.mm(pot[0:65, 0:128], vb[:, jj, :], p[:, s * 128:(s + 1) * 128], s == 0, s == nj - 1,
                             reads=[vb, p], writes=[pot])
                    attn_epilogue(k, c, pot, 128, OS, RD, PS_BC, att[:, i * 128:(i + 1) * 128], att)
                if pendb:
                    pendb.pop(0)()
                pendb.append(fin)
            while pendb:
                pendb.pop(0)()
            for si in range(NS):
                r0 = TP + si * TS
                tmp, p, pot = TMP[it % 2], P[it % 2], PS_OT[it % 2]
                it += 1
                k.dma("pool", KC[:, :, :], io["cbk"][si, hh, :, :].rearrange("(t p) d -> p t d", p=128), writes=[KC])
                k.dma("pool", VC[:, :, 0:64], io["cbv"][si, hh, :, :].rearrange("(t p) d -> p t d", p=128), writes=[VC])
                k.dma("sp", VN[:, 0:64], scr["vbTok"][r0:r0 + TS, hh * 64:(hh + 1) * 64], writes=[VN])
                for m in range(4):
                    k.tr(PS_KT[:, m, :], KC[:, m, :], c.identb[:, :], reads=[KC, c.identb], writes=[PS_KT])
                k.op("act", lambda h: h.activation(out=KTC[0:64, :], in_=PS_KT[:, :, :].rearrange("p m n -> p (m n)"),
                                                   func=AF.Copy), reads=[PS_KT], writes=[KTC])
                for m in range(4):
                    k.mm(PS_A[:, m * 64:(m + 1) * 64], KTC[:, m * 128:(m + 1) * 128], qT[:, r0:r0 + TS], True, True,
                         reads=[KTC, qT], writes=[PS_A])
                k.mm(PS_A[0:64, 256:320], kT[:, r0:r0 + TS], qT[:, r0:r0 + TS], True, True, reads=[kT, qT], writes=[PS_A])
                for m in range(4):
                    b0 = 512 - 128 * m
                    k.op("dve", lambda h, m=m, b0=b0, tmp=tmp: h.tensor_tensor(
                        out=tmp[:, m * 64:(m + 1) * 64], in0=PS_A[:, m * 64:(m + 1) * 64], in1=bs[:, b0:b0 + 64],
                        op=ALU.add), reads=[PS_A, bs], writes=[tmp])
                k.op("dve", lambda h, tmp=tmp: h.tensor_tensor(out=tmp[0:64, 256:320], in0=PS_A[0:64, 256:320],
                                                              in1=bs[0:64, 0:64], op=ALU.add),
                     reads=[PS_A, bs], writes=[tmp])
                k.op("act", lambda h, tmp=tmp, p=p: h.activation(out=p[:, 0:256], in_=tmp[:, 0:256], func=AF.Exp),
                     reads=[tmp], writes=[p])
                k.op("act", lambda h, tmp=tmp, p=p: h.activation(out=p[0:64, 256:320], in_=tmp[0:64, 256:320],
                                                                func=AF.Exp), reads=[tmp], writes=[p])
                for m in range(4):
                    k.mm(pot[0:65, 0:64], VC[:, m, :], p[:, m * 64:(m + 1) * 64], m == 0, False, reads=[VC, p], writes=[pot])
                k.mm(pot[0:65, 0:64], VN[0:64, :], p[0:64, 256:320], False, True, reads=[VN, p], writes=[pot])
                attn_epilogue(k, c, pot, TS, OS, RD, PS_BC, att[:, r0:r0 + TS], att)
            k.dma("sp", scr["catT"][512 + hh * 64:512 + (hh + 1) * 64, :], att[:, :], reads=[att])
    k.barrier()


def outproj_phase(k, c, io, scr, cat_ap, w_ap, xin_ap, xout_ap, tag, prefetch=None):
    with contextlib.ExitStack() as es:
        W = k.sb(es, tag + "W", [128, 8, D], BF16)
        load_w(k, W, w_ap, 8, D)
        if prefetch is not None:
            prefetch()
        cats = [k.sb(es, f"{tag}c{i}", [128, 8, 512], BF16) for i in range(2)]
        xts = [k.sb(es, f"{tag}x{i}", [128, D], F32) for i in range(2)]
        xos = [k.sb(es, f"{tag}o{i}", [128, D], F32) for i in range(2)]
        pp = [k.ps(es, f"{tag}p{i}", [128, 512]) for i in range(4)]
        cv = cat_ap.rearrange("(k p) r -> p k r", p=128)
        gs = groups()
        k.dma("sp", cats[0][:, :, 0:gs[0][1]], cv[:, :, gs[0][0]:gs[0][0] + gs[0][1]], writes=[cats[0]])
        ti = 0
        pi = 0
        for gi, (tok0, n) in enumerate(gs):
            cat = cats[gi % 2]
            if gi + 1 < len(gs):
                t1, n1 = gs[gi + 1]
                k.dma("sp", cats[(gi + 1) % 2][:, :, 0:n1], cv[:, :, t1:t1 + n1], writes=[cats[(gi + 1) % 2]])
            for t in range(n // 128):
                r0 = tok0 + t * 128
                xt, xo = xts[ti % 2], xos[ti % 2]
                ti += 1
                k.dma("sp", xt[:, :], xin_ap[r0:r0 + 128, :], writes=[xt])
                for blk in range(2):
                    p = pp[pi % 4]
                    pi += 1
                    for kc in range(8):
                        k.mm(p[:, :], cat[:, kc, t * 128:(t + 1) * 128], W[:, kc, blk * 512:(blk + 1) * 512],
                             kc == 0, kc == 7, reads=[cat, W], writes=[p])
                    k.op("dve", lambda h, p=p, blk=blk, xt=xt, xo=xo: h.tensor_tensor(
                        out=xo[:, blk * 512:(blk + 1) * 512], in0=p[:, :], in1=xt[:, blk * 512:(blk + 1) * 512],
                        op=ALU.add), reads=[p, xt], writes=[xo])
                k.dma("pool", xout_ap[r0:r0 + 128, :], xo[:, :], reads=[xo])
    k.barrier()


def ffn_w_alloc(k, es, tag):
    return (k.sb(es, tag + "W", [128, 8, 2 * HID], BF16), k.sb(es, tag + "g", [128, 8], F32))


def ffn_w_load(k, io, layer, Wg):
    W, gcol = Wg
    load_w(k, W, io["w_ffn_in"][layer], 8, 2 * HID)
    scale_rows(k, None, W, io["norm_ffn"][layer, :], 8, None, gcol=gcol)
    return W


def ffn_in_phase(k, c, io, scr, layer, xin_ap, tag, W=None):
    with contextlib.ExitStack() as es:
        if W is None:
            W = ffn_w_load(k, io, layer, ffn_w_alloc(k, es, tag))
        st = NormState(k, es, tag)
        xts = [k.sb(es, f"{tag}x{i}", [128, D], F32) for i in range(2)]
        hTs = [k.sb(es, f"{tag}h{i}", [128, 8, 512], BF16) for i in range(2)]
        SG = [k.sb(es, f"{tag}sg{i}", [128, 512], F32) for i in range(2)]
        HS = [k.sb(es, f"{tag}hs{i}", [128, 512], BF16) for i in range(3)]
        PG = [k.ps(es, f"{tag}pg{i}", [128, 512]) for i in range(2)]
        PU = [k.ps(es, f"{tag}pu{i}", [128, 512]) for i in range(2)]
        ti = 0
        it = 0
        gl = groups()
        tis = [0]

        def norm_group(gi):
            tok0, n = gl[gi]
            for t in range(n // 128):
                xt = xts[tis[0] % 2]
                k.dma("sp", xt[:, :], xin_ap[tok0 + t * 128: tok0 + (t + 1) * 128, :], writes=[xt])
                rms_to_hT(k, c, xt, hTs[gi % 2], t * 128, st, tis[0])
                tis[0] += 1

        norm_group(0)
        for gi, (tok0, n) in enumerate(gl):
            hT = hTs[gi % 2]
            if gi + 1 < len(gl):
                norm_group(gi + 1)
            for cch in range(HID // 128):
                pg, pu, sg, hs = PG[it % 2], PU[it % 2], SG[it % 2], HS[it % 3]
                it += 1
                for kc in range(8):
                    k.mm(pg[:, 0:n], W[:, kc, cch * 128:(cch + 1) * 128], hT[:, kc, 0:n], kc == 0, kc == 7,
                         reads=[W, hT], writes=[pg])
                for kc in range(8):
                    k.mm(pu[:, 0:n], W[:, kc, HID + cch * 128:HID + (cch + 1) * 128], hT[:, kc, 0:n], kc == 0, kc == 7,
                         reads=[W, hT], writes=[pu])
                k.op("act", lambda h, pg=pg, sg=sg: h.activation(out=sg[:, 0:n], in_=pg[:, 0:n], func=AF.Silu),
                     reads=[pg], writes=[sg])
                k.op("dve", lambda h, pu=pu, sg=sg, hs=hs: h.tensor_tensor(out=hs[:, 0:n], in0=pu[:, 0:n], in1=sg[:, 0:n],
                                                                          op=ALU.mult), reads=[pu, sg], writes=[hs])
                k.dma("pool", scr["hidT"][cch * 128:(cch + 1) * 128, tok0:tok0 + n], hs[:, 0:n], reads=[hs])
    k.barrier()


def ffn_out_phase(k, c, io, scr, layer, xin_ap, xout_ap, final, tag):
    with contextlib.ExitStack() as es:
        KC = HID // 128
        W = k.sb(es, tag + "W", [128, KC, D], BF16)
        load_w(k, W, io["w_ffn_out"][layer], KC, D)
        hids = [k.sb(es, f"{tag}hd{i}", [128, KC, 512], BF16) for i in range(2)]
        xts = [k.sb(es, f"{tag}x{i}", [128, D], F32) for i in range(2)]
        xos = [k.sb(es, f"{tag}o{i}", [128, D], F32) for i in range(2)]
        pp = [k.ps(es, f"{tag}p{i}", [128, 512]) for i in range(4)]
        if final:
            gam = k.sb(es, tag + "gam", [128, D], F32)
            k.dma("sp", gam[:, :], io["norm_final"].partition_broadcast(128), writes=[gam])
            junk = k.sb(es, tag + "junk", [128, D], BF16)
            ss = [k.sb(es, f"{tag}ss{i}", [128, 1], F32) for i in range(2)]
            sq = [k.sb(es, f"{tag}sq{i}", [128, 1], F32) for i in range(2)]
            rstd = [k.sb(es, f"{tag}rs{i}", [128, 1], F32) for i in range(2)]
            ys = [k.sb(es, f"{tag}y{i}", [128, D], F32) for i in range(2)]
        hv = scr["hidT"].rearrange("(c p) r -> p c r", p=128)
        gs = groups()
        k.dma("sp", hids[0][:, :, 0:gs[0][1]], hv[:, :, gs[0][0]:gs[0][0] + gs[0][1]], writes=[hids[0]])
        ti = 0
        pi = 0
        for gi, (tok0, n) in enumerate(gs):
            hid = hids[gi % 2]
            if gi + 1 < len(gs):
                t1, n1 = gs[gi + 1]
                k.dma("sp", hids[(gi + 1) % 2][:, :, 0:n1], hv[:, :, t1:t1 + n1], writes=[hids[(gi + 1) % 2]])
            for t in range(n // 128):
                r0 = tok0 + t * 128
                j = ti % 2
                xt, xo = xts[j], xos[j]
                ti += 1
                k.dma("sp", xt[:, :], xin_ap[r0:r0 + 128, :], writes=[xt])
                for blk in range(2):
                    p = pp[pi % 4]
                    pi += 1
                    for kc in range(KC):
                        k.mm(p[:, :], hid[:, kc, t * 128:(t + 1) * 128], W[:, kc, blk * 512:(blk + 1) * 512],
                             kc == 0, kc == KC - 1, reads=[hid, W], writes=[p])
                    k.op("dve", lambda h, p=p, blk=blk, xt=xt, xo=xo: h.tensor_tensor(
                        out=xo[:, blk * 512:(blk + 1) * 512], in0=p[:, :], in1=xt[:, blk * 512:(blk + 1) * 512],
                        op=ALU.add), reads=[p, xt], writes=[xo])
                if not final:
                    k.dma("pool", xout_ap[r0:r0 + 128, :], xo[:, :], reads=[xo])
                else:
                    k.op("act", lambda h, xo=xo, j=j: h.activation(out=junk[:, :], in_=xo[:, :], func=AF.Square,
                                                                  accum_out=ss[j][:, :]),
                         reads=[xo], writes=[junk, ss[j]])
                    k.op("act", lambda h, j=j: h.activation(out=sq[j][:, :], in_=ss[j][:, :], func=AF.Sqrt,
                                                           scale=1.0 / D, bias=1e-6), reads=[ss[j]], writes=[sq[j]])
                    k.op("dve", lambda h, j=j: h.reciprocal(out=rstd[j][:, :], in_=sq[j][:, :]),
                         reads=[sq[j]], writes=[rstd[j]])
                    k.op("act", lambda h, xo=xo, j=j: h.activation(out=ys[j][:, :], in_=xo[:, :], func=AF.Copy,
                                                                  scale=rstd[j][:, :]),
                         reads=[xo, rstd[j]], writes=[ys[j]])
                    k.op("pool", lambda h, j=j: h.tensor_tensor(out=ys[j][:, :], in0=ys[j][:, :], in1=gam[:, :],
                                                               op=ALU.mult), reads=[gam], writes=[ys[j]])
                    k.dma("pool", xout_ap[r0:r0 + 128, :], ys[j][:, :], reads=[ys[j]])
    k.barrier()


def fox_phases(k, c, io, scr):
    with contextlib.ExitStack() as esf:
        frow = k.sb(esf, "fx_frow", [16, R], F32)
        fox_inproj(k, c, io, scr, frow)
        if "fx1" in DBG:
            return
        NFP = k.sb(esf, "fx_NFP", [16, TP], F32)
        FSN = k.sb(esf, "fx_FSN", [16, NS, TS], F32)
        NFcP = k.sb(esf, "fx_NFcP", [128, 32, 16], F32)
        NFcS = k.sb(esf, "fx_NFcS", [128, NS, 17, 16], F32)
        fox_prep(k, c, io, scr, frow, NFP, FSN, NFcP, NFcS)
        if "fx2" in DBG:
            return
        fox_attn(k, c, io, scr, NFP, FSN, NFcP, NFcS)


def fox_inproj(k, c, io, scr, frow):
    with contextlib.ExitStack() as es:
        stf = [k.sb(es, f"f1sf{i}", [128, 512], F32) for i in range(2)]
        W = k.sb(es, "f1W", [128, 8, 3088], BF16)
        load_w(k, W, io["w_in_fox"], 8, 3088)
        scale_rows(k, es, W, io["norm_mix"][1, :], 8, "f1g")
        st = NormState(k, es, "f1")
        xts = [k.sb(es, f"f1x{i}", [128, D], F32) for i in range(2)]
        hTs = [k.sb(es, f"f1h{i}", [128, 8, 512], BF16) for i in range(2)]
        pfm = [k.ps(es, f"f1pf{i}", [128, 512]) for i in range(2)]
        ptm = [k.ps(es, f"f1pt{i}", [128, 512]) for i in range(2)]
        stb = [k.sb(es, f"f1sb{i}", [128, 512], BF16) for i in range(4)]
        ev = 0
        sbi = 0
        sfi = 0
        ti = 0
        gl = groups()
        tis = [0]

        def norm_group(gi):
            tok0, n = gl[gi]
            for t in range(n // 128):
                xt = xts[tis[0] % 2]
                k.dma("sp", xt[:, :], scr["x2"][tok0 + t * 128: tok0 + (t + 1) * 128, :], writes=[xt])
                rms_to_hT(k, c, xt, hTs[gi % 2], t * 128, st, tis[0])
                tis[0] += 1

        norm_group(0)
        for gi, (tok0, n) in enumerate(gl):
            hT = hTs[gi % 2]
            if gi + 1 < len(gl):
                norm_group(gi + 1)
            for (c0, dst, scale) in ([] if "fxnofm" in DBG else ((0, scr["qTf"], 0.125), (1024, scr["kTf"], None))):
                for mc in range(8):
                    pp = pfm[ev % 2]
                    for kc in range(8):
                        k.mm(pp[:, 0:n], W[:, kc, c0 + mc * 128: c0 + (mc + 1) * 128], hT[:, kc, 0:n],
                             kc == 0, kc == 7, reads=[W, hT], writes=[pp])
                    sbt = stb[sbi % 4]
                    sbi += 1
                    evac(k, ev, sbt[:, 0:n], pp[:, 0:n], [pp], [sbt], scale=scale)
                    ev += 1
                    k.dma("sp", dst[mc * 128:(mc + 1) * 128, tok0:tok0 + n], sbt[:, 0:n], reads=[sbt])
            pp = pfm[ev % 2]
            for kc in range(0 if "fxnog" in DBG else 8):
                k.mm(pp[0:16, 0:n], W[:, kc, 3072:3088], hT[:, kc, 0:n], kc == 0, kc == 7, reads=[W, hT], writes=[pp])
            if "fxnog" not in DBG:
                evac(k, 1, frow[0:16, tok0:tok0 + n], pp[0:16, 0:n], [pp], [frow])
            ev += 1
            for t in range(0 if "fxnotm" in DBG else n // 128):
                r0 = tok0 + t * 128
                for (c0, outn, hb, tobf) in ((2048, "fv", 0, True), (2560, "fv", 1, True),
                                             (1024, "fk", 0, False), (1536, "fk", 1, False)):
                    pp = ptm[ev % 2]
                    for kc in range(8):
                        k.mm(pp[:, :], hT[:, kc, t * 128:(t + 1) * 128], W[:, kc, c0:c0 + 512],
                             kc == 0, kc == 7, reads=[W, hT], writes=[pp])
                    ev += 1
                    sft = stf[sfi % 2]
                    sfi += 1
                    evac(k, 1, sft[:, :], pp[:, :], [pp], [sft])
                    if tobf:
                        sbt = stb[sbi % 4]
                        sbi += 1
                        k.op("act", lambda h, sbt=sbt, sft=sft: h.activation(out=sbt[:, :], in_=sft[:, :], func=AF.Copy),
                             reads=[sft], writes=[sbt])
                        k.dma("sp", scr["vfTok"][r0:r0 + 128, hb * 512:(hb + 1) * 512], sbt[:, :], reads=[sbt])
                    for h8 in range(0 if "fxnoout" in DBG else 8):
                        src = sft[:, h8 * 64:(h8 + 1) * 64]
                        if r0 < TP:
                            k.dma("sp", io["p" + outn][hb * 8 + h8, r0:r0 + 128, :], src, reads=[sft])
                        else:
                            for s2 in range(2):
                                sq = (r0 - TP) // TS + s2
                                k.dma("sp", io["s" + outn][sq, hb * 8 + h8, :, :], src[s2 * 64:(s2 + 1) * 64], reads=[sft])
    k.barrier()


def fox_prep(k, c, io, scr, frow, NFP, FSN, NFcP, NFcS):
    with contextlib.ExitStack() as es:
        negb = k.sb(es, "f2negb", [16, 1], F32)
        LF = k.sb(es, "f2LF", [16, R], F32)
        CL = [k.sb(es, f"f2CL{i}", [16, 2048], F32) for i in range(2)]
        FC = [k.sb(es, f"f2FC{i}", [16, 2048], F32) for i in range(2)]
        PCp = k.ps(es, "f2PCp", [128, 32, 16])
        PCs = [k.ps(es, f"f2PCs{i}", [128, 17, 16]) for i in range(2)]
        k.dma("sp", negb[:, :], io["b_fox_f"].rearrange("(h o) -> h o", o=1), writes=[negb])
        k.op("dve", lambda h: h.tensor_scalar(out=negb[:, :], in0=negb[:, :], scalar1=-1.0, scalar2=None, op0=ALU.mult),
             writes=[negb])
        k.op("act", lambda h: h.activation(out=frow[:, :], in_=frow[:, :], func=AF.Exp, scale=-1.0, bias=negb[:, :]),
             reads=[negb], writes=[frow])
        k.op("act", lambda h: h.activation(out=frow[:, :], in_=frow[:, :], func=AF.Ln, bias=1.0), writes=[frow])
        k.op("dve", lambda h: h.tensor_scalar(out=LF[:, :], in0=frow[:, :], scalar1=-1.0, scalar2=None, op0=ALU.mult),
             reads=[frow], writes=[LF])
        k.dma("sp", io["pflf"][:, :], LF[:, 0:TP], reads=[LF])
        for i in range(NS):
            k.dma("sp", io["sflf"][i, :, :], LF[:, TP + i * TS:TP + (i + 1) * TS], reads=[LF])
        k.op("dve", lambda h: h.tensor_tensor_scan(out=NFP[:, :], data0=frow[:, 0:TP], data1=frow[:, 0:TP], initial=0.0,
                                                   op0=ALU.add, op1=ALU.max), reads=[frow], writes=[NFP])
        for t in range(32):
            k.tr(PCp[:, t, :], NFP[0:16, t * 128:(t + 1) * 128], c.ident[0:16, 0:16], reads=[NFP, c.ident], writes=[PCp])
        k.op("dve", lambda h: h.tensor_copy(out=NFcP[:, :, :], in_=PCp[:, :, :]), reads=[PCp], writes=[NFcP])
        for i in range(NS):
            cl, fc, pcs = CL[i % 2], FC[i % 2], PCs[i % 2]
            s0 = TP + i * TS
            k.dma("sp", cl[:, :], io["cflf"][i, :, :], writes=[cl])
            k.op("dve", lambda h, cl=cl: h.tensor_scalar(out=cl[:, :], in0=cl[:, :], scalar1=-1.0, scalar2=None,
                                                        op0=ALU.mult), writes=[cl])
            k.op("dve", lambda h, cl=cl, fc=fc: h.tensor_tensor_scan(out=fc[:, :], data0=cl[:, :], data1=cl[:, :],
                                                                    initial=0.0, op0=ALU.add, op1=ALU.max),
                 reads=[cl], writes=[fc])
            k.op("dve", lambda h, fc=fc, i=i, s0=s0: h.tensor_tensor_scan(
                out=FSN[:, i, :], data0=frow[:, s0:s0 + TS], data1=frow[:, s0:s0 + TS], initial=fc[:, 2047:2048],
                op0=ALU.add, op1=ALU.max), reads=[fc, frow], writes=[FSN])
            for m in range(16):
                k.tr(pcs[:, m, :], fc[0:16, m * 128:(m + 1) * 128], c.ident[0:16, 0:16], reads=[fc, c.ident], writes=[pcs])
            k.tr(pcs[0:64, 16, :], FSN[0:16, i, :], c.ident[0:16, 0:16], reads=[FSN, c.ident], writes=[pcs])
            k.op("dve", lambda h, pcs=pcs, i=i: h.tensor_copy(out=NFcS[:, i, 0:16, :], in_=pcs[:, 0:16, :]),
                 reads=[pcs], writes=[NFcS])
            k.op("dve", lambda h, pcs=pcs, i=i: h.tensor_copy(out=NFcS[0:64, i, 16, :], in_=pcs[0:64, 16, :]),
                 reads=[pcs], par=[NFcS])
    k.barrier()


def fox_attn(k, c, io, scr, NFP, FSN, NFcP, NFcS):
    with contextlib.ExitStack() as es:
        kTh = [k.sb(es, f"f3k{i}", [128, R], BF16) for i in range(2)]
        qTh = [k.sb(es, f"f3q{i}", [128, R], BF16) for i in range(2)]
        for t_ in kTh + qTh:
            k.op("pool", lambda h, t_=t_: h.memset(t_[64:128, :], 0.0), writes=[t_])
        vfh = [k.sb(es, f"f3v{i}", [128, NT, 65], BF16) for i in range(2)]
        ATT = k.sb(es, "f3att", [64, R], BF16)
        FQ = [k.sb(es, f"f3fq{i}", [128, 512], F32) for i in range(2)]
        BD = [[k.sb(es, f"f3bd{b}_{i}", [128, 512], F32) for i in range(4)] for b in range(2)]
        TMP = [k.sb(es, f"f3tmp{i}", [128, 512], F32) for i in range(4)]
        P = [k.sb(es, f"f3P{i}", [128, 512], BF16) for i in range(4)]
        OS = k.sb(es, "f3OS", [65, 512], F32)
        RD = k.sb(es, "f3RD", [65, 512], F32)
        KC = [k.sb(es, f"f3KC{i}", [128, 16, 64], BF16) for i in range(2)]
        KTC = k.sb(es, "f3KTC", [128, 2048], BF16)
        k.op("pool", lambda h: h.memset(KTC[64:128, :], 0.0), writes=[KTC])
        VC = [k.sb(es, f"f3VC{i}", [128, 16, 65], BF16) for i in range(2)]
        VN = [k.sb(es, f"f3VN{i}", [64, 65], BF16) for i in range(2)]
        FQs = k.sb(es, "f3FQs", [128, TS], F32)
        BDs = k.sb(es, "f3BDs", [64, TS], F32)
        TMPs = k.sb(es, "f3TMPs", [128, 1088], F32)
        Ps = k.sb(es, "f3Ps", [128, 1088], BF16)
        PS_S = [k.ps(es, f"f3PS{i}", [128, 512]) for i in range(3)]
        PS_O = [k.ps(es, f"f3PO{i}", [128, 512]) for i in range(2)]
        PS_F = k.ps(es, "f3PF", [128, 512])
        PS_BC = k.ps(es, "f3PBC", [128, 512])
        PS_KT = k.ps(es, "f3PKT", [64, 1024], BF16)
        LN = k.sb(es, "f3LN", [65, 512], F32)
        RDr = [k.sb(es, f"f3RDr{i}", [65, 512], F32) for i in range(2)]
        RB = [k.sb(es, f"f3RB{i}", [64, 512], F32) for i in range(2)]
        rdbuf = [TT(None) for _ in range(2)]
        for i in range(2):
            k.op("pool", lambda h, i=i: h.memset(vfh[i][:, :, 64:65], 1.0), writes=[vfh[i]])
            k.op("pool", lambda h, i=i: h.memset(VC[i][:, :, 64:65], 1.0), writes=[VC[i]])
            k.op("pool", lambda h, i=i: h.memset(VN[i][:, 64:65], 1.0), writes=[VN[i]])
        vview = scr["vfTok"].rearrange("(t p) c -> p t c", p=128)

        def load(hh):
            j = hh % 2
            k.dma("sp", kTh[j][0:64, :], scr["kTf"][hh * 64:(hh + 1) * 64, :], writes=[kTh[j]])
            k.dma("sp", qTh[j][0:64, :], scr["qTf"][hh * 64:(hh + 1) * 64, :], writes=[qTh[j]])
            k.dma("sp", vfh[j][:, :, 0:64], vview[:, :, hh * 64:(hh + 1) * 64], writes=[vfh[j]])

        def prep(hh, g, b):
            fq = FQ[b]
            k.mm(PS_F[:, 0:512], c.sel[0:16, hh, :], NFP[0:16, g * 512:(g + 1) * 512], True, True,
                 reads=[c.sel, NFP], writes=[PS_F])
            k.op("dve", lambda h: h.tensor_copy(out=fq[:, :], in_=PS_F[:, 0:512]), reads=[PS_F], writes=[fq])
            for r in range(4):
                b0 = 384 - 128 * r
                k.op("pool", lambda h, r=r, b0=b0: h.tensor_tensor(out=BD[b][r][:, :], in0=fq[:, :],
                                                                 in1=c.cpos[:, b0:b0 + 512], op=ALU.add),
                     reads=[fq, c.cpos], writes=[BD[b][r]])

        def sload(hh, si, b):
            r0 = TP + si * TS
            k.dma("pool", KC[b][:, :, :], io["cfk"][si, hh, :, :].rearrange("(t p) d -> p t d", p=128), writes=[KC[b]])
            k.dma("pool", VC[b][:, :, 0:64], io["cfv"][si, hh, :, :].rearrange("(t p) d -> p t d", p=128), writes=[VC[b]])
            k.dma("sp", VN[b][:, 0:64], scr["vfTok"][r0:r0 + TS, hh * 64:(hh + 1) * 64], writes=[VN[b]])

        load(0)
        it = 0
        gi = 0
        sli = 0
        pend_epiA = []
        pend_epiB = []
        sload(0, 0, 0)
        for hh in range(16):
            j = hh % 2
            kT, qT, vf = kTh[j], qTh[j], vfh[j]
            if hh + 1 < 16:
                load(hh + 1)
            prep(hh, 0, gi % 2)
            pend = []
            for g in range(8):
                b = gi % 2
                fq, bd, pso = FQ[b], BD[b], PS_O[b]
                gi += 1
                if g + 1 < 8:
                    prep(hh, g + 1, gi % 2)
                last = 4 * g + 3
                for jj in range(last + 1):
                    ps, tmp, p = PS_S[it % 3], TMP[it % 4], P[it % 4]
                    it += 1
                    k.mm(ps[:, 0:512], kT[:, jj * 128:(jj + 1) * 128], qT[:, g * 512:(g + 1) * 512], True, True,
                         reads=[kT, qT], writes=[ps])
                    bsrc = bd[jj - 4 * g] if jj >= 4 * g else fq
                    k.op("dve", lambda h, ps=ps, tmp=tmp, bsrc=bsrc: h.tensor_tensor(out=tmp[:, :], in0=ps[:, 0:512],
                                                                                    in1=bsrc[:, :], op=ALU.subtract),
                         reads=[ps, bsrc], writes=[tmp], skip_self=True, skip_war=("act",))
                    k.op("act", lambda h, tmp=tmp, p=p, jj=jj: h.activation(out=p[:, :], in_=tmp[:, :], func=AF.Exp,
                                                                           bias=NFcP[:, jj, hh:hh + 1]),
                         reads=[tmp, NFcP], writes=[p], skip_self=True, skip_war=("pe",))
                    pend.append(lambda jj=jj, p=p, pso=pso, last=last: k.mm(
                        pso[0:65, 0:512], vf[:, jj, :], p[:, :], jj == 0, jj == last, reads=[vf, p], writes=[pso]))
                    if len(pend) > 2:
                        pend.pop(0)()
                    if jj == 1 and pend_epiA:
                        pend_epiA.pop(0)()
                    if jj == last - 1 and pend_epiB:
                        pend_epiB.pop(0)()
                def epiA(pso=pso, b=b):
                    k.op("act", lambda h: h.activation(out=LN[64:65, :], in_=pso[64:65, 0:512], func=AF.Ln),
                         reads=[pso], writes=[LN])
                    k.op("act", lambda h: h.activation(out=RDr[b][64:65, :], in_=LN[64:65, :], func=AF.Exp, scale=-1.0),
                         reads=[LN], writes=[RDr[b]])
                    k.dma("sp", scr["rds"][b:b + 1, :], RDr[b][64:65, :], reads=[RDr[b]], writes=[rdbuf[b]])
                    k.dma("sp", RB[b][:, :], scr["rds"][b, :].partition_broadcast(64), reads=[rdbuf[b]], writes=[RB[b]])

                def epiB(pso=pso, b=b, g=g):
                    k.op("dve", lambda h: h.tensor_tensor(out=ATT[:, g * 512:(g + 1) * 512], in0=pso[0:64, 0:512],
                                                          in1=RB[b][:, :], op=ALU.mult),
                         reads=[pso, RB[b]], writes=[ATT])
                pend_epiA.append(epiA)
                pend_epiB.append(epiB)
            while pend:
                pend.pop(0)()
            while pend_epiA:
                pend_epiA.pop(0)()
            while pend_epiB:
                pend_epiB.pop(0)()
            for si in range(NS):
                r0 = TP + si * TS
                b = sli % 2
                sli += 1
                kc, vc, vn = KC[b], VC[b], VN[b]
                pso = PS_O[sli % 2]
                if si + 1 < NS:
                    sload(hh, si + 1, sli % 2)
                elif hh + 1 < 16:
                    sload(hh + 1, 0, sli % 2)
                for half in range(2):
                    for m in range(8):
                        k.tr(PS_KT[:, m * 128:(m + 1) * 128], kc[:, half * 8 + m, :], c.identb[:, :],
                             reads=[kc, c.identb], writes=[PS_KT])
                    k.op("act", lambda h, half=half: h.activation(out=KTC[0:64, half * 1024:(half + 1) * 1024],
                                                                in_=PS_KT[:, :], func=AF.Copy),
                         reads=[PS_KT], writes=[KTC])
                k.mm(PS_F[:, 0:TS], c.sel[0:16, hh, :], FSN[0:16, si, :], True, True, reads=[c.sel, FSN], writes=[PS_F])
                k.op("dve", lambda h: h.tensor_copy(out=FQs[:, :], in_=PS_F[:, 0:TS]), reads=[PS_F], writes=[FQs])
                k.op("pool", lambda h: h.tensor_tensor(out=BDs[:, :], in0=FQs[0:64, :], in1=c.cpos[0:64, 384:384 + TS],
                                                       op=ALU.add), reads=[FQs, c.cpos], writes=[BDs])
                for m in range(16):
                    ps = PS_S[m // 8]
                    k.mm(ps[:, (m % 8) * 64:(m % 8 + 1) * 64], KTC[:, m * 128:(m + 1) * 128], qT[:, r0:r0 + TS], True, True,
                         reads=[KTC, qT], writes=[ps])
                k.mm(PS_S[2][0:64, 0:TS], kT[:, r0:r0 + TS], qT[:, r0:r0 + TS], True, True, reads=[kT, qT], writes=[PS_S[2]])
                for m in range(16):
                    ps = PS_S[m // 8]
                    k.op("dve", lambda h, m=m, ps=ps: h.scalar_tensor_tensor(
                        out=TMPs[:, m * 64:(m + 1) * 64], in0=ps[:, (m % 8) * 64:(m % 8 + 1) * 64],
                        scalar=NFcS[:, si, m, hh:hh + 1], in1=FQs[:, :], op0=ALU.add, op1=ALU.subtract),
                        reads=[ps, NFcS, FQs], writes=[TMPs])
                k.op("dve", lambda h: h.scalar_tensor_tensor(
                    out=TMPs[0:64, 1024:1088], in0=PS_S[2][0:64, 0:TS], scalar=NFcS[0:64, si, 16, hh:hh + 1],
                    in1=BDs[:, :], op0=ALU.add, op1=ALU.subtract), reads=[PS_S[2], NFcS, BDs], writes=[TMPs])
                k.op("act", lambda h: h.activation(out=Ps[:, 0:1024], in_=TMPs[:, 0:1024], func=AF.Exp),
                     reads=[TMPs], writes=[Ps])
                k.op("act", lambda h: h.activation(out=Ps[0:64, 1024:1088], in_=TMPs[0:64, 1024:1088], func=AF.Exp),
                     reads=[TMPs], writes=[Ps])
                for m in range(16):
                    k.mm(pso[0:65, 0:TS], vc[:, m, :], Ps[:, m * 64:(m + 1) * 64], m == 0, False, reads=[vc, Ps], writes=[pso])
                k.mm(pso[0:65, 0:TS], vn[0:64, :], Ps[0:64, 1024:1088], False, True, reads=[vn, Ps], writes=[pso])
                attn_epilogue(k, c, pso, TS, OS, RD, PS_BC, ATT[:, r0:r0 + TS], ATT)
            k.dma("sp", scr["attT"][hh * 64:(hh + 1) * 64, :], ATT[:, :], reads=[ATT])
    k.barrier()
```

```python
import contextlib
import numpy as np
import concourse.bass as bass
import concourse.mybir as mybir
from concourse.bass_utils import run_bass_kernel_spmd

F32 = mybir.dt.float32
BF16 = mybir.dt.bfloat16
AF = mybir.ActivationFunctionType
ALU = mybir.AluOpType

D = 1024
TP = 4096
NS = 4
TS = 64
R = TP + NS * TS
NT = R // 128
HID = 2816
BIG = 30000.0


class Buf:
    __slots__ = ("w", "r")

    def __init__(self):
        self.w = []
        self.r = {}


class TT:
    def __init__(self, t):
        self.t = t
        self.b = Buf()

    def __getitem__(self, key):
        return self.t[key]


class Eng:
    def __init__(self, name, h, sem):
        self.name = name
        self.h = h
        self.sem = sem
        self.count = 0
        self.seen = {}


class DQ:
    def __init__(self, name, sems):
        self.sems = sems
        self.keys = [f"{name}{i}" for i in range(len(sems))]
        self.uses = [0] * len(sems)
        self.next = 0


class KB:
    def __init__(self, nc, es):
        self.nc = nc
        self.eng = {}
        for name, h in (("pe", nc.tensor), ("act", nc.scalar), ("dve", nc.vector),
                        ("pool", nc.gpsimd), ("sp", nc.sync)):
            self.eng[name] = Eng(name, h, es.enter_context(nc.semaphore("sem_" + name)))
        self.dq = {}
        for q in ("sp", "pool"):
            self.dq[q] = DQ("dq" + q, [es.enter_context(nc.semaphore(f"dq{q}{i}")) for i in range(8)])

    def sb(self, es, name, shape, dt):
        return TT(es.enter_context(self.nc.sbuf_tensor(name, shape, dt)))

    def ps(self, es, name, shape, dt=F32):
        return TT(es.enter_context(self.nc.psum_tensor(name, shape, dt)))

    def _wait(self, e, deps):
        best = {}
        for (key, sem, val) in deps:
            if key == "pe" and e.name == "pe":
                continue
            if e.seen.get(key, 0) >= val:
                continue
            if key not in best or best[key][1] < val:
                best[key] = (sem, val)
        for key, (sem, val) in best.items():
            e.h.wait_ge(sem, val)
            e.seen[key] = val

    def _deps(self, reads, writes, par=(), en=None, skip_war=()):
        deps = []
        for t in reads:
            deps.extend(t.b.w)
        for t in writes:
            deps.extend(d for d in t.b.w if d[0] != en)
            deps.extend(d for d in t.b.r.values() if d[0] != en and d[0] not in skip_war)
        for t in par:
            deps.extend(d for d in t.b.r.values() if d[0] != en)
        return deps

    def _mark(self, tok, reads, writes, par=()):
        for t in par:
            t.b.w.append(tok)
        for t in writes:
            t.b.w = [tok]
            t.b.r = {}
        for t in reads:
            if t in writes:
                continue
            old = t.b.r.get(tok[0])
            if old is None or old[2] < tok[2]:
                t.b.r[tok[0]] = tok

    def op(self, en, fn, reads=(), writes=(), par=(), skip_self=False, skip_war=()):
        e = self.eng[en]
        self._wait(e, self._deps(reads, writes, par, en if skip_self else None, skip_war))
        inst = fn(e.h)
        e.count += 1
        inst.then_inc(e.sem, 1)
        tok = (en, e.sem, e.count)
        self._mark(tok, reads, writes, par)
        return tok

    def dma(self, qn, out, in_, reads=(), writes=(), par=(), **kw):
        e = self.eng[qn]
        q = self.dq[qn]
        self._wait(e, self._deps(reads, writes, par))
        s = q.next % len(q.sems)
        q.next += 1
        if q.uses[s] > 0 and e.seen.get(q.keys[s], 0) < 16 * q.uses[s]:
            e.h.wait_ge(q.sems[s], 16 * q.uses[s])
            e.seen[q.keys[s]] = 16 * q.uses[s]
        inst = e.h.dma_start(out=out, in_=in_, **kw)
        q.uses[s] += 1
        inst.then_inc(q.sems[s], 16)
        tok = (q.keys[s], q.sems[s], 16 * q.uses[s])
        self._mark(tok, reads, writes, par)
        return tok

    def barrier(self):
        toks = [(n, e.sem, e.count) for n, e in self.eng.items() if e.count > 0]
        for q in self.dq.values():
            for i in range(len(q.sems)):
                if q.uses[i] > 0:
                    toks.append((q.keys[i], q.sems[i], 16 * q.uses[i]))
        for e in self.eng.values():
            self._wait(e, [t for t in toks if t[0] != e.name])

    def mm(self, out, lhsT, rhs, start, stop, reads, writes):
        return self.op("pe", lambda h: h.matmul(out, lhsT=lhsT, rhs=rhs, start=start, stop=stop),
                       reads=reads, writes=writes)

    def tr(self, out, in_, ident, reads, writes):
        return self.op("pe", lambda h: h.transpose(out=out, in_=in_, identity=ident),
                       reads=reads, writes=writes)


def load_w(k, dst, w_ap, kc_n, ncols, blk=2048):
    nb = -(-ncols // blk)
    blk = -(-ncols // nb)
    for kc in range(kc_n):
        for c0 in range(0, ncols, blk):
            c1 = min(ncols, c0 + blk)
            k.dma("pool", dst[:, kc, c0:c1], w_ap[kc * 128:(kc + 1) * 128, c0:c1], par=[dst])


def scale_rows(k, es, w, g_ap, kc_n, name, gcol=None):
    if gcol is None:
        gcol = k.sb(es, name, [128, kc_n], F32)
    k.dma("sp", gcol[:, :], g_ap.rearrange("(k p) -> p k", p=128), writes=[gcol],
          allow_slow_non_contiguous=True)
    for kc in range(kc_n):
        eng = "dve" if kc % 2 == 0 else "pool"
        k.op(eng, lambda h, kc=kc: h.tensor_scalar(out=w[:, kc, :], in0=w[:, kc, :],
                                                    scalar1=gcol[:, kc:kc + 1], scalar2=None,
                                                    op0=ALU.mult),
             reads=[gcol], writes=[w])


class Consts:
    pass


def make_consts(k, es):
    c = Consts()
    c.ones = k.sb(es, "c_ones", [128, 128], F32)
    c.ident = k.sb(es, "c_ident", [128, 128], F32)
    c.identb = k.sb(es, "c_identb", [128, 128], BF16)
    c.caus = k.sb(es, "c_caus", [128, 128], F32)
    c.cpos = k.sb(es, "c_cpos", [128, 896], F32)
    c.big = k.sb(es, "c_big", [128, 896], F32)
    c.sel = k.sb(es, "c_sel", [16, 16, 128], F32)
    k.op("pool", lambda h: h.memset(c.ones[:, :], 1.0), writes=[c.ones])
    k.op("pool", lambda h: h.memset(c.big[:, :], BIG), writes=[c.big])
    k.op("pool", lambda h: h.affine_select(out=c.ident[:, :], in_=c.ones[:, :], pattern=[[-1, 128]],
                                           compare_op=ALU.is_equal, fill=0.0, base=0,
                                           channel_multiplier=1),
         reads=[c.ones], writes=[c.ident])
    k.op("pool", lambda h: h.affine_select(out=c.caus[:, :], in_=c.ones[:, :], pattern=[[1, 128]],
                                           compare_op=ALU.is_ge, fill=0.0, base=0,
                                           channel_multiplier=-1),
         reads=[c.ones], writes=[c.caus])
    k.op("pool", lambda h: h.affine_select(out=c.cpos[:, :], in_=c.big[:, :], pattern=[[-1, 896]],
                                           compare_op=ALU.is_gt, fill=0.0, base=384,
                                           channel_multiplier=1),
         reads=[c.big], writes=[c.cpos])
    k.op("pool", lambda h: h.tensor_copy(out=c.identb[:, :], in_=c.ident[:, :]),
         reads=[c.ident], writes=[c.identb])
    for hh in range(16):
        k.op("dve", lambda h, hh=hh: h.tensor_copy(out=c.sel[:, hh, :],
                                                    in_=c.ident[0:16, hh:hh + 1].to_broadcast([16, 128])),
             reads=[c.ident], writes=[c.sel])
    return c


def rms_to_hT(k, c, x_t, hT, col0, st, ti):
    j = ti % 2
    k.op("act", lambda h: h.activation(out=st.junk[:, :], in_=x_t[:, :], func=AF.Square,
                                       accum_out=st.ss[j][:, :]),
         reads=[x_t], writes=[st.junk, st.ss[j]])
    k.op("act", lambda h: h.activation(out=st.sq[j][:, :], in_=st.ss[j][:, :], func=AF.Sqrt,
                                       scale=1.0 / D, bias=1e-6),
         reads=[st.ss[j]], writes=[st.sq[j]])
    k.op("dve", lambda h: h.reciprocal(out=st.rstd[j][:, :], in_=st.sq[j][:, :]),
         reads=[st.sq[j]], writes=[st.rstd[j]])
    k.op("act", lambda h: h.activation(out=st.xn[j][:, :], in_=x_t[:, :], func=AF.Copy,
                                       scale=st.rstd[j][:, :]),
         reads=[x_t, st.rstd[j]], writes=[st.xn[j]])
    for kc in range(8):
        k.tr(st.pT[j][:, kc * 128:(kc + 1) * 128], st.xn[j][:, kc * 128:(kc + 1) * 128],
             c.identb[:, :], reads=[st.xn[j], c.identb], writes=[st.pT[j]])
    k.op("dve", lambda h: h.tensor_copy(out=hT[:, :, col0:col0 + 128],
                                        in_=st.pT[j][:, :].rearrange("p (k n) -> p k n", k=8)),
         reads=[st.pT[j]], writes=[hT])


class NormState:
    def __init__(self, k, es, tag):
        self.junk = k.sb(es, tag + "junk", [128, D], BF16)
        self.ss = [k.sb(es, f"{tag}ss{i}", [128, 1], F32) for i in range(2)]
        self.sq = [k.sb(es, f"{tag}sq{i}", [128, 1], F32) for i in range(2)]
        self.rstd = [k.sb(es, f"{tag}rstd{i}", [128, 1], F32) for i in range(2)]
        self.xn = [k.sb(es, f"{tag}xn{i}", [128, D], BF16) for i in range(2)]
        self.pT = [k.ps(es, f"{tag}pT{i}", [128, D], BF16) for i in range(2)]


def evac(k, i, out, in_, reads, writes, scale=None, func=None):
    if func is not None or (i % 2 == 0):
        f = func if func is not None else AF.Copy
        sc = 1.0 if scale is None else scale
        return k.op("act", lambda h: h.activation(out=out, in_=in_, func=f, scale=sc),
                    reads=reads, writes=writes)
    if scale is None:
        scale = 1.0
    return k.op("dve", lambda h: h.tensor_scalar(out=out, in0=in_, scalar1=scale, scalar2=None,
                                                  op0=ALU.mult), reads=reads, writes=writes)


def groups():
    gs = [(g * 512, 512) for g in range(8)]
    gs.append((TP, NS * TS))
    return gs


def phase1(k, c, io, scr, rows):
    with contextlib.ExitStack() as es:
        stf = [k.sb(es, f"p1sf{i}", [128, 512], F32) for i in range(2)]
        W = k.sb(es, "p1W", [128, 8, 3592], BF16)
        load_w(k, W, io["w_in_ab"], 8, 3592)
        scale_rows(k, es, W, io["norm_mix"][0, :], 8, "p1g")
        st = NormState(k, es, "p1")
        xts = [k.sb(es, f"p1x{i}", [128, D], F32) for i in range(2)]
        hTs = [k.sb(es, f"p1h{i}", [128, 8, 512], BF16) for i in range(2)]
        pfm = [k.ps(es, f"p1pf{i}", [128, 512]) for i in range(2)]
        ptm = [k.ps(es, f"p1pt{i}", [128, 512]) for i in range(2)]
        stb = [k.sb(es, f"p1sb{i}", [128, 512], BF16) for i in range(4)]
        fm = [(0, scr["qTm"], None), (512, scr["kTm"], 128.0 ** -0.5),
              (2056, scr["qTb"], 0.125), (2568, scr["kTb"], None)]
        ev = 0
        sbi = 0
        sfi = 0
        ti = 0
        gl = groups()
        tis = [0]

        def norm_group(gi):
            tok0, n = gl[gi]
            for t in range(n // 128):
                xt = xts[tis[0] % 2]
                k.dma("sp", xt[:, :], io["xin"][tok0 + t * 128: tok0 + (t + 1) * 128, :], writes=[xt])
                rms_to_hT(k, c, xt, hTs[gi % 2], t * 128, st, tis[0])
                tis[0] += 1

        norm_group(0)
        for gi, (tok0, n) in enumerate(gl):
            hT = hTs[gi % 2]
            if gi + 1 < len(gl):
                norm_group(gi + 1)
            for (c0, dst, scale) in ([] if 'nofm' in DBG else fm):
                for mc in range(4):
                    pp = pfm[ev % 2]
                    for kc in range(8):
                        k.mm(pp[:, 0:n], W[:, kc, c0 + mc * 128: c0 + (mc + 1) * 128], hT[:, kc, 0:n],
                             kc == 0, kc == 7, reads=[W, hT], writes=[pp])
                    sbt = stb[sbi % 4]
                    sbi += 1
                    evac(k, ev, sbt[:, 0:n], pp[:, 0:n], [pp], [sbt], scale=scale)
                    ev += 1
                    k.dma("pool", dst[mc * 128:(mc + 1) * 128, tok0:tok0 + n], sbt[:, 0:n], reads=[sbt])
            for (c0, row) in ([] if 'nogate' in DBG else ((2048, rows["ig"]), (2052, rows["fg"]))):
                pp = pfm[ev % 2]
                for kc in range(8):
                    k.mm(pp[0:4, 0:n], W[:, kc, c0:c0 + 4], hT[:, kc, 0:n], kc == 0, kc == 7,
                         reads=[W, hT], writes=[pp])
                evac(k, 1, row[0:4, tok0:tok0 + n], pp[0:4, 0:n], [pp], [row])
                ev += 1
            for t in range(0 if 'notm' in DBG else n // 128):
                r0 = tok0 + t * 128
                is_out = (r0 >= TP - 512)
                blocks = [(512, scr["kmTok"], 128.0 ** -0.5, None, None),
                          (1024, scr["vmTok"], None, None, None),
                          (1536, scr["osTok"], None, (None if 'nosig' in DBG else AF.Sigmoid), None),
                          (3080, scr["vbTok"], None, None, "bv")]
                if is_out:
                    blocks.append((2568, None, None, None, "bk"))
                for (c0, dst, scale, func, outn) in blocks:
                    pp = ptm[ev % 2]
                    for kc in range(8):
                        k.mm(pp[:, :], hT[:, kc, t * 128:(t + 1) * 128], W[:, kc, c0:c0 + 512],
                             kc == 0, kc == 7, reads=[W, hT], writes=[pp])
                    both = (outn is not None and is_out)
                    if both:
                        sft = stf[sfi % 2]
                        sfi += 1
                        evac(k, 1, sft[:, :], pp[:, :], [pp], [sft])
                        ev += 1
                    if dst is not None:
                        sbt = stb[sbi % 4]
                        sbi += 1
                        if both:
                            k.op("act", lambda h, sbt=sbt, sft=sft: h.activation(out=sbt[:, :], in_=sft[:, :], func=AF.Copy),
                                 reads=[sft], writes=[sbt])
                        else:
                            evac(k, ev, sbt[:, :], pp[:, :], [pp], [sbt], scale=scale, func=func)
                        ev += 1
                        k.dma("pool", dst[r0:r0 + 128, :], sbt[:, :], reads=[sbt])
                    if both:
                        for hh in range(0 if ('noodma2' in DBG or 'whole' in DBG) else 8):
                            src = sft[:, hh * 64:(hh + 1) * 64]
                            if r0 < TP:
                                q0 = r0 - (TP - 512)
                                if 'toscr' in DBG:
                                    k.dma("sp", scr["x1"][q0:q0 + 128, hh * 64:(hh + 1) * 64], src, reads=[sft])
                                else:
                                    k.dma("pool" if 'opool' in DBG else "sp", io["p" + outn][hh, q0:q0 + 128, :], src, reads=[sft])
                            else:
                                for s2 in range(2):
                                    sq = (r0 - TP) // TS + s2
                                    k.dma("sp", io["s" + outn][sq, hh, :, :], src[s2 * 64:(s2 + 1) * 64],
                                          reads=[sft])
    k.barrier()


IN_SPECS = [
    ("xin", [R, D]), ("mC0", [NS, 4, 128, 128]), ("mn0", [NS, 4, 128]), ("mm0", [NS, 4]),
    ("cbk", [NS, 8, 512, 64]), ("cbv", [NS, 8, 512, 64]),
    ("cfk", [NS, 16, 2048, 64]), ("cfv", [NS, 16, 2048, 64]), ("cflf", [NS, 16, 2048]),
    ("norm_mix", [2, D]), ("norm_ffn", [2, D]), ("norm_final", [D]),
    ("w_in_ab", [D, 3592]), ("b_gate_ab", [8]), ("mlstm_gain", [512]), ("bandbias", [8, 128, 640]),
    ("bandmask", [128, 640]),
    ("w_out_ab", [D, D]), ("w_in_fox", [D, 3088]), ("b_fox_f", [16]), ("w_out_fox", [D, D]),
    ("w_ffn_in", [2, D, 2 * HID]), ("w_ffn_out", [2, HID, D]),
]
OUT_SPECS = [
    ("y", [R, D]), ("pC", [4, 128, 128]), ("pn", [4, 128]), ("pm", [4, 1]),
    ("pbk", [8, 512, 64]), ("pbv", [8, 512, 64]),
    ("pfk", [16, TP, 64]), ("pfv", [16, TP, 64]), ("pflf", [16, TP]),
    ("sC", [NS, 4, 128, 128]), ("sn", [NS, 4, 128]), ("sm", [NS, 4, 1]),
    ("sbk", [NS, 8, TS, 64]), ("sbv", [NS, 8, TS, 64]),
    ("sfk", [NS, 16, TS, 64]), ("sfv", [NS, 16, TS, 64]), ("sflf", [NS, 16, TS]),
]
SCR_SPECS = [
    ("qTm", [512, R], BF16), ("kTm", [512, R], BF16), ("qTb", [512, R], BF16), ("kTb", [512, R], BF16),
    ("kmTok", [R, 512], BF16), ("vmTok", [R, 512], BF16), ("osTok", [R, 512], BF16),
    ("vbTok", [R, 512], BF16), ("catT", [D, R], BF16),
    ("x1", [R, D], F32), ("x2", [R, D], F32), ("x3", [R, D], F32),
    ("qTf", [D, R], BF16), ("kTf", [D, R], BF16), ("vfTok", [R, D], BF16), ("attT", [D, R], BF16),
    ("hidT", [HID, R], BF16), ("rds", [4, 512], F32),
]

PHASES = 99
DBG = ""


def run_phases(k, c, io, scr, upto=None):
    upto = PHASES if upto is None else upto
    with contextlib.ExitStack() as es1:
        rows = {"ig": k.sb(es1, "row_ig", [4, R], F32), "fg": k.sb(es1, "row_fg", [4, R], F32)}
        pers = {"GW": k.sb(es1, "GW", [4, 2, R], F32), "COLS": k.sb(es1, "COLS", [128, NCH, 3, 4], F32),
                "DECB": k.sb(es1, "DECB", [128, 4, NCH], F32)}
        phase1(k, c, io, scr, rows)
        if upto >= 2:
            phase2(k, c, io, scr, rows, pers)
        if upto >= 3:
            phase3(k, c, io, scr, pers)
    if upto >= 4:
        phase4(k, c, io, scr)
    if upto >= 6:
        with contextlib.ExitStack() as esw:
            Wg = ffn_w_alloc(k, esw, "p6")
            outproj_phase(k, c, io, scr, scr["catT"], io["w_out_ab"], io["xin"], scr["x1"], "p5",
                          prefetch=lambda: ffn_w_load(k, io, 0, Wg))
            ffn_in_phase(k, c, io, scr, 0, scr["x1"], "p6", W=Wg[0])
        ffn_out_phase(k, c, io, scr, 0, scr["x1"], scr["x2"], False, "p7")
    if upto >= 8:
        fox_phases(k, c, io, scr)
    if upto >= 9:
        with contextlib.ExitStack() as esw:
            Wg = ffn_w_alloc(k, esw, "pa")
            outproj_phase(k, c, io, scr, scr["attT"], io["w_out_fox"], scr["x2"], scr["x3"], "p9",
                          prefetch=lambda: ffn_w_load(k, io, 1, Wg))
            ffn_in_phase(k, c, io, scr, 1, scr["x3"], "pa", W=Wg[0])
        ffn_out_phase(k, c, io, scr, 1, scr["x3"], io["y"], True, "pb")


def build():
    nc = bass.Bass("TRN2", target_bir_lowering=False)
    io = {}
    for name, shape in IN_SPECS:
        io[name] = nc.dram_tensor(name, shape, F32, kind="ExternalInput").ap()
    for name, shape in OUT_SPECS:
        io[name] = nc.dram_tensor(name, shape, F32, kind="ExternalOutput").ap()
    scr = {}
    for name, shape, dt in SCR_SPECS:
        scr[name] = nc.dram_tensor("scr_" + name, shape, dt).ap()
    with contextlib.ExitStack() as es:
        k = KB(nc, es)
        c = make_consts(k, es)
        run_phases(k, c, io, scr)
        k.barrier()
    return nc


_NC_CACHE = {}


def kernel(**inp):
    f = lambda a: np.ascontiguousarray(np.asarray(a, dtype=np.float32))
    xp, xs = f(inp["x_prompt"]), f(inp["x_sample"])
    kk = np.arange(128)[:, None]
    qq = np.arange(640)[None, :]
    rel = qq - kk
    idx = np.clip(rel, -63, 128) + 63
    table = f(inp["rel_bias_table"])[0]
    bandbias = np.ascontiguousarray(table[:, idx])
    dch = (qq // 64) - (kk // 64)
    bandmask = np.where((dch >= 0) & (dch <= 8), 0.0, -BIG).astype(np.float32)
    common = {
        "norm_mix": f(inp["norm_mix"]), "norm_ffn": f(inp["norm_ffn"]), "norm_final": f(inp["norm_final"]),
        "w_in_ab": f(inp["w_in_ab"])[0], "b_gate_ab": f(inp["b_gate_ab"])[0],
        "mlstm_gain": f(inp["mlstm_gain"])[0], "bandbias": bandbias, "bandmask": bandmask,
        "w_out_ab": f(inp["w_out_ab"])[0], "w_in_fox": f(inp["w_in_fox"])[0],
        "b_fox_f": f(inp["b_fox_f"])[0], "w_out_fox": f(inp["w_out_fox"])[0],
        "w_ffn_in": f(inp["w_ffn_in"]), "w_ffn_out": f(inp["w_ffn_out"]),
    }
    in_maps = []
    for cid in range(8):
        b = cid % 4
        s0 = 4 * cid
        m = dict(common)
        m["xin"] = np.ascontiguousarray(np.concatenate([xp[b], xs[s0:s0 + 4].reshape(NS * TS, D)], axis=0))
        m["mC0"] = f(inp["state_mlstm_C"])[0, s0:s0 + 4]
        m["mn0"] = f(inp["state_mlstm_n"])[0, s0:s0 + 4]
        m["mm0"] = f(inp["state_mlstm_m"])[0, s0:s0 + 4]
        m["cbk"] = f(inp["cache_band_k"])[0, s0:s0 + 4]
        m["cbv"] = f(inp["cache_band_v"])[0, s0:s0 + 4]
        m["cfk"] = f(inp["cache_fox_k"])[0, s0:s0 + 4]
        m["cfv"] = f(inp["cache_fox_v"])[0, s0:s0 + 4]
        m["cflf"] = f(inp["cache_fox_logf"])[0, s0:s0 + 4]
        in_maps.append({kk_: np.ascontiguousarray(v) for kk_, v in m.items()})
    if "nc" not in _NC_CACHE:
        _NC_CACHE["nc"] = build()
    res = run_bass_kernel_spmd(_NC_CACHE["nc"], in_maps, core_ids=list(range(8)))
    rs = res.results
    P = lambda n: np.stack([rs[b][n] for b in range(4)], axis=0)
    S = lambda n: np.concatenate([rs[cid][n] for cid in range(8)], axis=0)
    y_prompt = np.stack([rs[b]["y"][:TP] for b in range(4)], axis=0)
    y_sample = np.concatenate([rs[cid]["y"][TP:].reshape(NS, TS, D) for cid in range(8)], axis=0)
    outs = (
        y_prompt, y_sample,
        P("pC")[None], P("pn")[None], P("pm")[None, :, :, 0],
        P("pbk")[None], P("pbv")[None], P("pfk")[None], P("pfv")[None], P("pflf")[None],
        S("sC")[None], S("sn")[None], S("sm")[None, :, :, 0],
        S("sbk")[None], S("sbv")[None], S("sfk")[None], S("sfv")[None], S("sflf")[None],
    )
    return tuple(np.ascontiguousarray(o, dtype=np.float32) for o in outs)


def chunks():
    cs = [(c * 128, 128) for c in range(32)]
    cs += [(TP + i * TS, TS) for i in range(NS)]
    return cs


NCH = 36


def phase2(k, c, io, scr, rows, pers):
    GW, COLS, DECB = pers["GW"], pers["COLS"], pers["DECB"]
    ig, fg = rows["ig"], rows["fg"]
    with contextlib.ExitStack() as es:
        negb = k.sb(es, "p2negb", [4, 1], F32)
        bigc = k.sb(es, "p2big", [4, 1], F32)
        m0 = k.sb(es, "p2m0", [4, NS], F32)
        NF = k.sb(es, "p2NF", [4, R], F32)
        WS = k.sb(es, "p2WS", [4, R], F32)
        EM = k.sb(es, "p2EM", [4, R], F32)
        DEC = k.sb(es, "p2DEC", [4, NCH], F32)
        PC = k.ps(es, "p2PC", [128, NCH, 3, 4])
        PD = k.ps(es, "p2PD", [128, 4, NCH])
        bg = io["b_gate_ab"]
        k.dma("sp", negb[:, :], bg[4:8].rearrange("(h o) -> h o", o=1), writes=[negb])
        k.dma("sp", bigc[:, :], bg[0:4].rearrange("(h o) -> h o", o=1), writes=[bigc])
        k.dma("sp", m0[:, :], io["mm0"].rearrange("s h -> h s"), writes=[m0], allow_slow_non_contiguous=True)
        k.op("dve", lambda h: h.tensor_scalar(out=negb[:, :], in0=negb[:, :], scalar1=-1.0, scalar2=None,
                                              op0=ALU.mult), writes=[negb])
        k.op("act", lambda h: h.activation(out=fg[:, :], in_=fg[:, :], func=AF.Exp, scale=-1.0,
                                           bias=negb[:, :]), reads=[negb], writes=[fg])
        k.op("act", lambda h: h.activation(out=fg[:, :], in_=fg[:, :], func=AF.Ln, bias=1.0), writes=[fg])
        segs = [(0, TP, None)] + [(TP + i * TS, TS, i) for i in range(NS)]
        for (s0, sn, si) in segs:
            k.op("dve", lambda h, s0=s0, sn=sn: h.tensor_tensor_scan(
                out=NF[:, s0:s0 + sn], data0=fg[:, s0:s0 + sn], data1=fg[:, s0:s0 + sn], initial=0.0,
                op0=ALU.add, op1=ALU.max), reads=[fg], writes=[NF])
        k.op("dve", lambda h: h.scalar_tensor_tensor(out=ig[:, :], in0=ig[:, :], scalar=bigc[:, :],
                                                     in1=NF[:, :], op0=ALU.add, op1=ALU.add),
             reads=[bigc, NF], writes=[ig])
        for (s0, sn, si) in segs:
            init = 0.0 if si is None else m0[:, si:si + 1]
            k.op("dve", lambda h, s0=s0, sn=sn, init=init: h.tensor_tensor_scan(
                out=GW[:, 0, s0:s0 + sn], data0=ig[:, s0:s0 + sn], data1=ig[:, s0:s0 + sn], initial=init,
                op0=ALU.max, op1=ALU.max), reads=[ig, m0], writes=[GW])
        Gp = GW[:, 0, 0:TP].rearrange("p (c t) -> p c t", t=128)
        k.op("pool", lambda h: h.memset(fg[:, 0:128], 0.0), writes=[fg])
        k.op("dve", lambda h: h.tensor_copy(
            out=fg[:, 128:TP].rearrange("p (c t) -> p c t", t=128),
            in_=Gp[:, 0:31, 127:128].to_broadcast([4, 31, 128])), reads=[GW], writes=[fg])
        for i in range(NS):
            s0 = TP + i * TS
            k.op("dve", lambda h, s0=s0, i=i: h.tensor_copy(out=fg[:, s0:s0 + TS],
                                                          in_=m0[:, i:i + 1].to_broadcast([4, TS])),
                 reads=[m0], writes=[fg])
        k.op("dve", lambda h: h.tensor_tensor(out=WS[:, :], in0=fg[:, :], in1=GW[:, 0, :], op=ALU.subtract),
             reads=[fg, GW], writes=[WS])
        k.op("act", lambda h: h.activation(out=GW[:, 1, :], in_=WS[:, :], func=AF.Exp), reads=[WS], writes=[GW])
        k.op("dve", lambda h: h.tensor_copy(
            out=WS[:, 0:TP].rearrange("p (c t) -> p c t", t=128),
            in_=Gp[:, :, 127:128].to_broadcast([4, 32, 128])), reads=[GW], writes=[WS])
        for i in range(NS):
            s0 = TP + i * TS
            k.op("dve", lambda h, s0=s0: h.tensor_copy(
                out=WS[:, s0:s0 + TS], in_=GW[:, 0, s0 + TS - 1:s0 + TS].to_broadcast([4, TS])),
                reads=[GW], writes=[WS])
        k.op("dve", lambda h: h.tensor_tensor(
            out=DEC[:, 0:32], in0=fg[:, 0:TP].rearrange("p (c t) -> p c t", t=128)[:, :, 0],
            in1=WS[:, 0:TP].rearrange("p (c t) -> p c t", t=128)[:, :, 0], op=ALU.subtract),
            reads=[fg, WS], writes=[DEC])
        for i in range(NS):
            s0 = TP + i * TS
            k.op("dve", lambda h, s0=s0, i=i: h.tensor_tensor(out=DEC[:, 32 + i:33 + i], in0=fg[:, s0:s0 + 1],
                                                            in1=WS[:, s0:s0 + 1], op=ALU.subtract),
                 reads=[fg, WS], writes=[DEC])
        k.op("act", lambda h: h.activation(out=DEC[:, :], in_=DEC[:, :], func=AF.Exp), writes=[DEC])
        k.op("dve", lambda h: h.tensor_tensor(out=WS[:, :], in0=ig[:, :], in1=WS[:, :], op=ALU.subtract),
             reads=[ig], writes=[WS])
        k.op("act", lambda h: h.activation(out=WS[:, :], in_=WS[:, :], func=AF.Exp), writes=[WS])
        k.op("dve", lambda h: h.tensor_tensor(out=EM[:, :], in0=GW[:, 0, :], in1=NF[:, :], op=ALU.subtract),
             reads=[GW, NF], writes=[EM])
        k.dma("sp", io["pm"][:, :], EM[:, TP - 1:TP], reads=[EM])
        for i in range(NS):
            e1 = TP + (i + 1) * TS
            k.dma("sp", io["sm"][i, :, :], EM[:, e1 - 1:e1], reads=[EM])
        k.op("act", lambda h: h.activation(out=EM[:, :], in_=EM[:, :], func=AF.Exp, scale=-1.0), writes=[EM])
        for ci, (r0, n) in enumerate(chunks()):
            for xi, X in enumerate((ig, WS, EM)):
                k.tr(PC[0:n, ci, xi, :], X[0:4, r0:r0 + n], c.ident[0:4, 0:4], reads=[X, c.ident], writes=[PC])
        k.op("dve", lambda h: h.tensor_copy(out=COLS[:, 0:32, :, :], in_=PC[:, 0:32, :, :]), reads=[PC], writes=[COLS])
        k.op("dve", lambda h: h.tensor_copy(out=COLS[0:64, 32:NCH, :, :], in_=PC[0:64, 32:NCH, :, :]), reads=[PC], par=[COLS])
        for hh in range(4):
            k.mm(PD[:, hh, :], c.sel[0:4, hh, :], DEC[0:4, :], True, True, reads=[c.sel, DEC], writes=[PD])
        k.op("dve", lambda h: h.tensor_copy(out=DECB[:, :, :], in_=PD[:, :, :]), reads=[PD], writes=[DECB])
    k.barrier()


def phase3(k, c, io, scr, pers):
    GW, COLS, DECB = pers["GW"], pers["COLS"], pers["DECB"]
    with contextlib.ExitStack() as es:
        gain = k.sb(es, "p3gain", [128, 512], F32)
        k.dma("sp", gain[:, :], io["mlstm_gain"].partition_broadcast(128), writes=[gain])
        qTs = [k.sb(es, f"p3q{i}", [128, 4, 128], BF16) for i in range(2)]
        kTs = [k.sb(es, f"p3k{i}", [128, 4, 128], BF16) for i in range(2)]
        kts = [k.sb(es, f"p3kt{i}", [128, 4, 128], BF16) for i in range(2)]
        vas = [k.sb(es, f"p3va{i}", [128, 4, 129], BF16) for i in range(2)]
        oss = [k.sb(es, f"p3os{i}", [128, 512], BF16) for i in range(3)]
        for i in range(2):
            k.op("pool", lambda h, i=i: h.memset(vas[i][:, :, 128:129], 1.0), writes=[vas[i]])
        Cf = [k.sb(es, f"p3Cf{h}", [128, 129], F32) for h in range(4)]
        Cb = [k.sb(es, f"p3Cb{h}", [128, 129], BF16) for h in range(4)]
        E = k.sb(es, "p3E", [128, 4, 128], F32)
        WT = k.sb(es, "p3WT", [128, 4, 128], BF16)
        QS = k.sb(es, "p3QS", [128, 4, 128], BF16)
        KW = k.sb(es, "p3KW", [128, 4, 128], BF16)
        ONs = [k.sb(es, f"p3ON{i}", [128, 4, 129], F32) for i in range(2)]
        H = k.sb(es, "p3H", [128, 4, 128], F32)
        SQ = k.sb(es, "p3SQ", [128, 4, 128], F32)
        OAs = [k.sb(es, f"p3OA{i}", [128, 512], BF16) for i in range(2)]
        G2s = [k.sb(es, f"p3G2{i}", [128, 512], F32) for i in range(2)]
        junk = k.sb(es, "p3junk", [128, 128], F32)
        sm2 = [k.sb(es, f"p3t{i}", [128, 4], F32) for i in range(4)]
        OT = k.sb(es, "p3OT", [128, 4, 128], BF16)
        sm = [k.sb(es, f"p3s{i}", [128, 4], F32) for i in range(6)]
        PS_S = k.ps(es, "p3PS", [128, 4, 128])
        PS_G = k.ps(es, "p3PG", [128, 4, 128])
        PS_W = k.ps(es, "p3PW", [128, 4, 128])
        PS_O = [k.ps(es, f"p3PO{i}", [128, 2, 129]) for i in range(2)]
        PS_D = [k.ps(es, f"p3PD{i}", [128, 2, 129]) for i in range(2)]
        PS_T = k.ps(es, "p3PT", [128, 4, 128], BF16)
        qv = scr["qTm"].rearrange("(h d) r -> d h r", h=4)
        kv = scr["kTm"].rearrange("(h d) r -> d h r", h=4)
        catv = scr["catT"][0:512, :].rearrange("(h d) r -> d h r", h=4)
        chs = chunks()

        def load(ci):
            r0, n = chs[ci]
            j = ci % 2
            k.dma("sp", qTs[j][:, :, 0:n], qv[:, :, r0:r0 + n], writes=[qTs[j]])
            k.dma("sp", kTs[j][:, :, 0:n], kv[:, :, r0:r0 + n], writes=[kTs[j]])
            k.dma("sp", kts[j][0:n, :, :], scr["kmTok"][r0:r0 + n, :].rearrange("t (h e) -> t h e", h=4),
                  writes=[kts[j]])
            k.dma("sp", vas[j][0:n, :, 0:128], scr["vmTok"][r0:r0 + n, :].rearrange("t (h e) -> t h e", h=4),
                  writes=[vas[j]])
            k.dma("sp", oss[ci % 3][0:n, :], scr["osTok"][r0:r0 + n, :], writes=[oss[ci % 3]])

        load(0)
        pendA, pendA_b, pendB = [], [], []
        for ci, (r0, n) in enumerate(chs):
            j = ci % 2
            qT, kT, kt, va, osg = qTs[j], kTs[j], kts[j], vas[j], oss[ci % 3]
            ON = ONs[ci % 2]
            if ci + 1 < NCH:
                load(ci + 1)
            if ci == 0:
                for hh in range(4):
                    k.op("pool", lambda h, hh=hh: h.memset(Cf[hh][:, :], 0.0), writes=[Cf[hh]])
                    k.op("pool", lambda h, hh=hh: h.memset(Cb[hh][:, :], 0.0), writes=[Cb[hh]])
            elif ci >= 32:
                si = ci - 32
                for hh in range(4):
                    k.dma("sp", Cf[hh][:, 0:128], io["mC0"][si, hh, :, :], writes=[Cf[hh]])
                    k.dma("sp", Cf[hh][:, 128:129], io["mn0"][si, hh, :].rearrange("(d o) -> d o", o=1),
                          par=[Cf[hh]])
                    k.op("act", lambda h, hh=hh: h.activation(out=Cb[hh][:, :], in_=Cf[hh][:, :], func=AF.Copy),
                         reads=[Cf[hh]], writes=[Cb[hh]])
            for hh in range(4):
                k.mm(PS_S[0:n, hh, 0:n], kT[:, hh, 0:n], qT[:, hh, 0:n], True, True, reads=[kT, qT], writes=[PS_S])
            for hh in range(4):
                k.mm(PS_G[0:n, hh, 0:n], c.sel[0:4, hh, 0:n], GW[0:4, 0, r0:r0 + n], True, True,
                     reads=[c.sel, GW], writes=[PS_G])
            for hh in range(4):
                k.mm(PS_W[:, hh, 0:n], c.sel[0:4, hh, :], GW[0:4, 1, r0:r0 + n], True, True,
                     reads=[c.sel, GW], writes=[PS_W])
            for hh in range(4):
                k.op("act", lambda h, hh=hh: h.activation(out=E[0:n, hh, 0:n], in_=PS_G[0:n, hh, 0:n], func=AF.Exp,
                                                        scale=-1.0, bias=COLS[0:n, ci, 0, hh:hh + 1]),
                     reads=[PS_G, COLS], writes=[E])
            for hh in range(4):
                k.op("pool", lambda h, hh=hh: h.tensor_tensor(out=E[0:n, hh, 0:n], in0=E[0:n, hh, 0:n],
                                                            in1=c.caus[0:n, 0:n], op=ALU.mult),
                     reads=[c.caus], writes=[E])
            k.op("dve", lambda h: h.tensor_tensor(out=WT[0:n, :, 0:n], in0=PS_S[0:n, :, 0:n], in1=E[0:n, :, 0:n],
                                                  op=ALU.mult), reads=[PS_S, E], writes=[WT])
            k.op("dve", lambda h: h.tensor_tensor(out=QS[:, :, 0:n], in0=PS_W[:, :, 0:n], in1=qT[:, :, 0:n],
                                                  op=ALU.mult), reads=[PS_W, qT], writes=[QS])
            for hh in range(4):
                po = PS_O[hh // 2]
                k.mm(po[0:n, hh % 2, :], QS[:, hh, 0:n], Cb[hh][:, :], True, False, reads=[QS, Cb[hh]], writes=[po])
                k.mm(po[0:n, hh % 2, :], WT[0:n, hh, 0:n], va[0:n, hh, :], False, True, reads=[WT, va], writes=[po])
            for i2 in range(2):
                k.op("dve", lambda h, i2=i2: h.tensor_copy(out=ON[0:n, 2 * i2:2 * i2 + 2, :], in_=PS_O[i2][0:n, :, :]),
                     reads=[PS_O[i2]], writes=[ON])
            for hh in range(4):
                k.op("act", lambda h, hh=hh: h.activation(out=KW[0:n, hh, :], in_=kt[0:n, hh, :], func=AF.Copy,
                                                        scale=COLS[0:n, ci, 1, hh:hh + 1]),
                     reads=[kt, COLS], writes=[KW])
            for hh in range(4):
                pd = PS_D[hh // 2]
                k.mm(pd[:, hh % 2, :], KW[0:n, hh, :], va[0:n, hh, :], True, True, reads=[KW, va], writes=[pd])
            for hh in range(4):
                pd = PS_D[hh // 2]
                k.op("dve", lambda h, hh=hh, pd=pd: h.scalar_tensor_tensor(
                    out=Cf[hh][:, :], in0=Cf[hh][:, :], scalar=DECB[:, hh, ci:ci + 1], in1=pd[:, hh % 2, :],
                    op0=ALU.mult, op1=ALU.add), reads=[DECB, pd], writes=[Cf[hh]])
                k.op("act", lambda h, hh=hh: h.activation(out=Cb[hh][:, :], in_=Cf[hh][:, :], func=AF.Copy),
                     reads=[Cf[hh]], writes=[Cb[hh]])
            if ci == 31 or ci >= 32:
                for hh in range(4):
                    if ci == 31:
                        oc, on = io["pC"][hh, :, :], io["pn"][hh, :]
                    else:
                        oc, on = io["sC"][ci - 32, hh, :, :], io["sn"][ci - 32, hh, :]
                    k.dma("sp", oc, Cf[hh][:, 0:128], reads=[Cf[hh]])
                    k.dma("sp", on.rearrange("(d o) -> d o", o=1), Cf[hh][:, 128:129], reads=[Cf[hh]])
            G2 = G2s[ci % 2]
            k.op("pool", lambda h, G2=G2, osg=osg: h.tensor_tensor(out=G2[0:n, :], in0=gain[0:n, :], in1=osg[0:n, :],
                                                                   op=ALU.mult), reads=[gain, osg], writes=[G2])

            def epiA(ci=ci, n=n, ON=ON, G2=G2, OAb=OAs[ci % 2]):
                den, dd, rr, mu, var, rs = sm
                s1, s2, aa, bb = sm2
                k.op("dve", lambda h: h.tensor_tensor(out=dd[0:n, :], in0=ON[0:n, :, 128], in1=COLS[0:n, ci, 2, :],
                                                      op=ALU.max), reads=[ON, COLS], writes=[dd])
                k.op("dve", lambda h: h.tensor_scalar(out=den[0:n, :], in0=ON[0:n, :, 128], scalar1=-1.0, scalar2=None,
                                                      op0=ALU.mult), reads=[ON], writes=[den])
                k.op("dve", lambda h: h.tensor_tensor(out=dd[0:n, :], in0=dd[0:n, :], in1=den[0:n, :],
                                                      op=ALU.max), reads=[den], writes=[dd])
                k.op("dve", lambda h: h.reciprocal(out=rr[0:n, :], in_=dd[0:n, :]), reads=[dd], writes=[rr])
                for hh in range(4):
                    k.op("act", lambda h, hh=hh: h.activation(out=junk[0:n, :], in_=ON[0:n, hh, 0:128], func=AF.Copy,
                                                            scale=rr[0:n, hh:hh + 1], accum_out=s1[0:n, hh:hh + 1]),
                         reads=[ON, rr], writes=[junk, s1])
                    k.op("act", lambda h, hh=hh: h.activation(out=junk[0:n, :], in_=ON[0:n, hh, 0:128], func=AF.Square,
                                                            scale=rr[0:n, hh:hh + 1], accum_out=s2[0:n, hh:hh + 1]),
                         reads=[ON, rr], writes=[junk, s2])
                k.op("dve", lambda h: h.tensor_scalar(out=mu[0:n, :], in0=s1[0:n, :], scalar1=1.0 / 128, scalar2=None,
                                                      op0=ALU.mult), reads=[s1], writes=[mu])
                k.op("dve", lambda h: h.tensor_tensor(out=var[0:n, :], in0=mu[0:n, :], in1=mu[0:n, :], op=ALU.mult),
                     reads=[mu], writes=[var])
                k.op("dve", lambda h: h.scalar_tensor_tensor(out=var[0:n, :], in0=s2[0:n, :], scalar=1.0 / 128,
                                                             in1=var[0:n, :], op0=ALU.mult, op1=ALU.subtract),
                     reads=[s2, var], writes=[var])
                k.op("dve", lambda h: h.tensor_scalar(out=var[0:n, :], in0=var[0:n, :], scalar1=0.0, scalar2=None,
                                                      op0=ALU.max), reads=[var], writes=[var])
                k.op("act", lambda h: h.activation(out=rs[0:n, :], in_=var[0:n, :], func=AF.Ln, bias=1e-6),
                     reads=[var], writes=[rs])
                k.op("act", lambda h: h.activation(out=rs[0:n, :], in_=rs[0:n, :], func=AF.Exp, scale=-0.5),
                     reads=[rs], writes=[rs])
                k.op("dve", lambda h: h.tensor_tensor(out=aa[0:n, :], in0=rr[0:n, :], in1=rs[0:n, :], op=ALU.mult),
                     reads=[rr, rs], writes=[aa])
                k.op("dve", lambda h: h.scalar_tensor_tensor(out=bb[0:n, :], in0=mu[0:n, :], scalar=-1.0, in1=rs[0:n, :],
                                                             op0=ALU.mult, op1=ALU.mult), reads=[mu, rs], writes=[bb])
                for hh in range(4):
                    k.op("pool", lambda h, hh=hh: h.tensor_scalar(out=H[0:n, hh, :], in0=ON[0:n, hh, 0:128],
                                                                scalar1=aa[0:n, hh:hh + 1], scalar2=bb[0:n, hh:hh + 1],
                                                                op0=ALU.mult, op1=ALU.add),
                         reads=[ON, aa, bb], writes=[H])
                k.op("dve", lambda h: h.tensor_tensor(out=OAb[0:n, :], in0=H[0:n, :, :].rearrange("p h e -> p (h e)"),
                                                      in1=G2[0:n, :], op=ALU.mult), reads=[H, G2], writes=[OAb])

            def epiB(r0=r0, n=n, OAb=OAs[ci % 2]):
                for hh in range(4):
                    k.tr(PS_T[:, hh, 0:n], OAb[0:n, hh * 128:(hh + 1) * 128], c.identb[0:n, 0:n],
                         reads=[OAb, c.identb], writes=[PS_T])
                k.op("act", lambda h: h.activation(out=OT[:, :, 0:n], in_=PS_T[:, :, 0:n], func=AF.Copy),
                     reads=[PS_T], writes=[OT])
                k.dma("sp", catv[:, :, r0:r0 + n], OT[:, :, 0:n], reads=[OT])
            if pendB:
                pendB.pop(0)()
            if pendA:
                pendA.pop(0)()
                pendB.append(pendA_b.pop(0))
            pendA.append(epiA)
            pendA_b.append(epiB)
        while pendA:
            pendA.pop(0)()
            pendB.append(pendA_b.pop(0))
        while pendB:
            pendB.pop(0)()
    k.barrier()


def attn_epilogue(k, c, PS_OT, nq, OS, RD, PS_BC, out_ap, out_t):
    k.op("dve", lambda h: h.tensor_copy(out=OS[0:65, 0:nq], in_=PS_OT[0:65, 0:nq]), reads=[PS_OT], writes=[OS])
    k.op("act", lambda h: h.activation(out=RD[64:65, 0:nq], in_=OS[64:65, 0:nq], func=AF.Ln), reads=[OS], writes=[RD])
    k.op("act", lambda h: h.activation(out=RD[64:65, 0:nq], in_=RD[64:65, 0:nq], func=AF.Exp, scale=-1.0),
         reads=[RD], writes=[RD])
    k.mm(PS_BC[0:64, 0:nq], c.ones[64:65, 0:64], RD[64:65, 0:nq], True, True, reads=[c.ones, RD], writes=[PS_BC])
    k.op("dve", lambda h: h.tensor_tensor(out=out_ap, in0=OS[0:64, 0:nq], in1=PS_BC[0:64, 0:nq], op=ALU.mult),
         reads=[OS, PS_BC], writes=[out_t])


def phase4(k, c, io, scr):
    with contextlib.ExitStack() as es:
        kTh = [k.sb(es, f"p4k{i}", [128, R], BF16) for i in range(2)]
        qTh = [k.sb(es, f"p4q{i}", [128, R], BF16) for i in range(2)]
        for t_ in kTh + qTh:
            k.op("pool", lambda h, t_=t_: h.memset(t_[64:128, :], 0.0), writes=[t_])
        vbh = [k.sb(es, f"p4v{i}", [128, NT, 65], BF16) for i in range(2)]
        bias = [k.sb(es, f"p4b{i}", [128, 640], F32) for i in range(2)]
        mask = k.sb(es, "p4mask", [128, 640], F32)
        ATT = [k.sb(es, f"p4att{i}", [64, R], BF16) for i in range(2)]
        TMP = [k.sb(es, f"p4tmp{i}", [128, 640], F32) for i in range(2)]
        P = [k.sb(es, f"p4P{i}", [128, 640], BF16) for i in range(2)]
        OS = k.sb(es, "p4OS", [65, 128], F32)
        RD = k.sb(es, "p4RD", [65, 128], F32)
        KC = k.sb(es, "p4KC", [128, 4, 64], BF16)
        KTC = k.sb(es, "p4KTC", [128, 512], BF16)
        k.op("pool", lambda h: h.memset(KTC[64:128, :], 0.0), writes=[KTC])
        VC = k.sb(es, "p4VC", [128, 4, 65], BF16)
        VN = k.sb(es, "p4VN", [64, 65], BF16)
        PS_A = k.ps(es, "p4PA", [128, 512])
        PS_B = k.ps(es, "p4PB", [128, 512])
        PS_OT = [k.ps(es, f"p4PO{i}", [128, 512]) for i in range(2)]
        PS_BC = k.ps(es, "p4PBC", [128, 512])
        PS_KT = k.ps(es, "p4PKT", [64, 4, 128], BF16)
        k.dma("sp", mask[:, :], io["bandmask"][:, :], writes=[mask])
        for i in range(2):
            k.op("pool", lambda h, i=i: h.memset(vbh[i][:, :, 64:65], 1.0), writes=[vbh[i]])
        k.op("pool", lambda h: h.memset(VC[:, :, 64:65], 1.0), writes=[VC])
        k.op("pool", lambda h: h.memset(VN[:, 64:65], 1.0), writes=[VN])
        vview = scr["vbTok"].rearrange("(t p) c -> p t c", p=128)

        def load(hh):
            j = hh % 2
            k.dma("sp", kTh[j][0:64, :], scr["kTb"][hh * 64:(hh + 1) * 64, :], writes=[kTh[j]])
            k.dma("sp", qTh[j][0:64, :], scr["qTb"][hh * 64:(hh + 1) * 64, :], writes=[qTh[j]])
            k.dma("sp", vbh[j][:, :, 0:64], vview[:, :, hh * 64:(hh + 1) * 64], writes=[vbh[j]])
            k.dma("sp", bias[j][:, :], io["bandbias"][hh, :, :], writes=[bias[j]])
            k.op("pool", lambda h: h.tensor_tensor(out=bias[j][:, :], in0=bias[j][:, :], in1=mask[:, :], op=ALU.add),
                 reads=[mask], writes=[bias[j]])

        load(0)
        it = 0
        for hh in range(8):
            j = hh % 2
            kT, qT, vb, bs, att = kTh[j], qTh[j], vbh[j], bias[j], ATT[j]
            if hh + 1 < 8:
                load(hh + 1)
            pendb = []
            for i in range(32):
                nj = min(i, 4) + 1
                tmp, p, pot = TMP[it % 2], P[it % 2], PS_OT[it % 2]
                it += 1
                for s in range(nj):
                    jj = i - s
                    dst = PS_A[:, s * 128:(s + 1) * 128] if s < 4 else PS_B[:, 0:128]
                    k.mm(dst, kT[:, jj * 128:(jj + 1) * 128], qT[:, i * 128:(i + 1) * 128], True, True,
                         reads=[kT, qT], writes=[PS_A if s < 4 else PS_B])
                na = min(nj, 4) * 128
                k.op("dve", lambda h, na=na, tmp=tmp: h.tensor_tensor(out=tmp[:, 0:na], in0=PS_A[:, 0:na],
                                                                     in1=bs[:, 0:na], op=ALU.add),
                     reads=[PS_A, bs], writes=[tmp])
                if nj == 5:
                    k.op("dve", lambda h, tmp=tmp: h.tensor_tensor(out=tmp[:, 512:640], in0=PS_B[:, 0:128],
                                                                  in1=bs[:, 512:640], op=ALU.add),
                         reads=[PS_B, bs], writes=[tmp])
                k.op("act", lambda h, tmp=tmp, p=p, nj=nj: h.activation(out=p[:, 0:nj * 128], in_=tmp[:, 0:nj * 128],
                                                                       func=AF.Exp), reads=[tmp], writes=[p])
                def fin(i=i, nj=nj, p=p, pot=pot):
                    for s in range(nj):
                        jj = i - s
                        k.mm(pot[0:65, 0:128], vb[:, jj, :], p[:, s * 128:(s + 1) * 128], s == 0, s == nj - 1,
                             reads=[vb, p], writes=[pot])
                    attn_epilogue(k, c, pot, 128, OS, RD, PS_BC, att[:, i * 128:(i + 1) * 128], att)
                if pendb:
                    pendb.pop(0)()
                pendb.append(fin)
            while pendb:
                pendb.pop(0)()
            for si in range(NS):
                r0 = TP + si * TS
                tmp, p, pot = TMP[it % 2], P[it % 2], PS_OT[it % 2]
                it += 1
                k.dma("pool", KC[:, :, :], io["cbk"][si, hh, :, :].rearrange("(t p) d -> p t d", p=128), writes=[KC])
                k.dma("pool", VC[:, :, 0:64], io["cbv"][si, hh, :, :].rearrange("(t p) d -> p t d", p=128), writes=[VC])
                k.dma("sp", VN[:, 0:64], scr["vbTok"][r0:r0 + TS, hh * 64:(hh + 1) * 64], writes=[VN])
                for m in range(4):
                    k.tr(PS_KT[:, m, :], KC[:, m, :], c.identb[:, :], reads=[KC, c.identb], writes=[PS_KT])
                k.op("act", lambda h: h.activation(out=KTC[0:64, :], in_=PS_KT[:, :, :].rearrange("p m n -> p (m n)"),
                                                   func=AF.Copy), reads=[PS_KT], writes=[KTC])
                for m in range(4):
                    k.mm(PS_A[:, m * 64:(m + 1) * 64], KTC[:, m * 128:(m + 1) * 128], qT[:, r0:r0 + TS], True, True,
                         reads=[KTC, qT], writes=[PS_A])
                k.mm(PS_A[0:64, 256:320], kT[:, r0:r0 + TS], qT[:, r0:r0 + TS], True, True, reads=[kT, qT], writes=[PS_A])
                for m in range(4):
                    b0 = 512 - 128 * m
                    k.op("dve", lambda h, m=m, b0=b0, tmp=tmp: h.tensor_tensor(
                        out=tmp[:, m * 64:(m + 1) * 64], in0=PS_A[:, m * 64:(m + 1) * 64], in1=bs[:, b0:b0 + 64],
                        op=ALU.add), reads=[PS_A, bs], writes=[tmp])
                k.op("dve", lambda h, tmp=tmp: h.tensor_tensor(out=tmp[0:64, 256:320], in0=PS_A[0:64, 256:320],
                                                              in1=bs[0:64, 0:64], op=ALU.add),
                     reads=[PS_A, bs], writes=[tmp])
                k.op("act", lambda h, tmp=tmp, p=p: h.activation(out=p[:, 0:256], in_=tmp[:, 0:256], func=AF.Exp),
                     reads=[tmp], writes=[p])
                k.op("act", lambda h, tmp=tmp, p=p: h.activation(out=p[0:64, 256:320], in_=tmp[0:64, 256:320],
                                                                func=AF.Exp), reads=[tmp], writes=[p])
                for m in range(4):
                    k.mm(pot[0:65, 0:64], VC[:, m, :], p[:, m * 64:(m + 1) * 64], m == 0, False, reads=[VC, p], writes=[pot])
                k.mm(pot[0:65, 0:64], VN[0:64, :], p[0:64, 256:320], False, True, reads=[VN, p], writes=[pot])
                attn_epilogue(k, c, pot, TS, OS, RD, PS_BC, att[:, r0:r0 + TS], att)
            k.dma("sp", scr["catT"][512 + hh * 64:512 + (hh + 1) * 64, :], att[:, :], reads=[att])
    k.barrier()


def outproj_phase(k, c, io, scr, cat_ap, w_ap, xin_ap, xout_ap, tag, prefetch=None):
    with contextlib.ExitStack() as es:
        W = k.sb(es, tag + "W", [128, 8, D], BF16)
        load_w(k, W, w_ap, 8, D)
        if prefetch is not None:
            prefetch()
        cats = [k.sb(es, f"{tag}c{i}", [128, 8, 512], BF16) for i in range(2)]
        xts = [k.sb(es, f"{tag}x{i}", [128, D], F32) for i in range(2)]
        xos = [k.sb(es, f"{tag}o{i}", [128, D], F32) for i in range(2)]
        pp = [k.ps(es, f"{tag}p{i}", [128, 512]) for i in range(4)]
        cv = cat_ap.rearrange("(k p) r -> p k r", p=128)
        gs = groups()
        k.dma("sp", cats[0][:, :, 0:gs[0][1]], cv[:, :, gs[0][0]:gs[0][0] + gs[0][1]], writes=[cats[0]])
        ti = 0
        pi = 0
        for gi, (tok0, n) in enumerate(gs):
            cat = cats[gi % 2]
            if gi + 1 < len(gs):
                t1, n1 = gs[gi + 1]
                k.dma("sp", cats[(gi + 1) % 2][:, :, 0:n1], cv[:, :, t1:t1 + n1], writes=[cats[(gi + 1) % 2]])
            for t in range(n // 128):
                r0 = tok0 + t * 128
                xt, xo = xts[ti % 2], xos[ti % 2]
                ti += 1
                k.dma("sp", xt[:, :], xin_ap[r0:r0 + 128, :], writes=[xt])
                for blk in range(2):
                    p = pp[pi % 4]
                    pi += 1
                    for kc in range(8):
                        k.mm(p[:, :], cat[:, kc, t * 128:(t + 1) * 128], W[:, kc, blk * 512:(blk + 1) * 512],
                             kc == 0, kc == 7, reads=[cat, W], writes=[p])
                    k.op("dve", lambda h, p=p, blk=blk, xt=xt, xo=xo: h.tensor_tensor(
                        out=xo[:, blk * 512:(blk + 1) * 512], in0=p[:, :], in1=xt[:, blk * 512:(blk + 1) * 512],
                        op=ALU.add), reads=[p, xt], writes=[xo])
                k.dma("pool", xout_ap[r0:r0 + 128, :], xo[:, :], reads=[xo])
    k.barrier()


def ffn_w_alloc(k, es, tag):
    return (k.sb(es, tag + "W", [128, 8, 2 * HID], BF16), k.sb(es, tag + "g", [128, 8], F32))


def ffn_w_load(k, io, layer, Wg):
    W, gcol = Wg
    load_w(k, W, io["w_ffn_in"][layer], 8, 2 * HID)
    scale_rows(k, None, W, io["norm_ffn"][layer, :], 8, None, gcol=gcol)
    return W


def ffn_in_phase(k, c, io, scr, layer, xin_ap, tag, W=None):
    with contextlib.ExitStack() as es:
        if W is None:
            W = ffn_w_load(k, io, layer, ffn_w_alloc(k, es, tag))
        st = NormState(k, es, tag)
        xts = [k.sb(es, f"{tag}x{i}", [128, D], F32) for i in range(2)]
        hTs = [k.sb(es, f"{tag}h{i}", [128, 8, 512], BF16) for i in range(2)]
        SG = [k.sb(es, f"{tag}sg{i}", [128, 512], F32) for i in range(2)]
        HS = [k.sb(es, f"{tag}hs{i}", [128, 512], BF16) for i in range(3)]
        PG = [k.ps(es, f"{tag}pg{i}", [128, 512]) for i in range(2)]
        PU = [k.ps(es, f"{tag}pu{i}", [128, 512]) for i in range(2)]
        ti = 0
        it = 0
        gl = groups()
        tis = [0]

        def norm_group(gi):
            tok0, n = gl[gi]
            for t in range(n // 128):
                xt = xts[tis[0] % 2]
                k.dma("sp", xt[:, :], xin_ap[tok0 + t * 128: tok0 + (t + 1) * 128, :], writes=[xt])
                rms_to_hT(k, c, xt, hTs[gi % 2], t * 128, st, tis[0])
                tis[0] += 1

        norm_group(0)
        for gi, (tok0, n) in enumerate(gl):
            hT = hTs[gi % 2]
            if gi + 1 < len(gl):
                norm_group(gi + 1)
            for cch in range(HID // 128):
                pg, pu, sg, hs = PG[it % 2], PU[it % 2], SG[it % 2], HS[it % 3]
                it += 1
                for kc in range(8):
                    k.mm(pg[:, 0:n], W[:, kc, cch * 128:(cch + 1) * 128], hT[:, kc, 0:n], kc == 0, kc == 7,
                         reads=[W, hT], writes=[pg])
                for kc in range(8):
                    k.mm(pu[:, 0:n], W[:, kc, HID + cch * 128:HID + (cch + 1) * 128], hT[:, kc, 0:n], kc == 0, kc == 7,
                         reads=[W, hT], writes=[pu])
                k.op("act", lambda h, pg=pg, sg=sg: h.activation(out=sg[:, 0:n], in_=pg[:, 0:n], func=AF.Silu),
                     reads=[pg], writes=[sg])
                k.op("dve", lambda h, pu=pu, sg=sg, hs=hs: h.tensor_tensor(out=hs[:, 0:n], in0=pu[:, 0:n], in1=sg[:, 0:n],
                                                                          op=ALU.mult), reads=[pu, sg], writes=[hs])
                k.dma("pool", scr["hidT"][cch * 128:(cch + 1) * 128, tok0:tok0 + n], hs[:, 0:n], reads=[hs])
    k.barrier()


def ffn_out_phase(k, c, io, scr, layer, xin_ap, xout_ap, final, tag):
    with contextlib.ExitStack() as es:
        KC = HID // 128
        W = k.sb(es, tag + "W", [128, KC, D], BF16)
        load_w(k, W, io["w_ffn_out"][layer], KC, D)
        hids = [k.sb(es, f"{tag}hd{i}", [128, KC, 512], BF16) for i in range(2)]
        xts = [k.sb(es, f"{tag}x{i}", [128, D], F32) for i in range(2)]
        xos = [k.sb(es, f"{tag}o{i}", [128, D], F32) for i in range(2)]
        pp = [k.ps(es, f"{tag}p{i}", [128, 512]) for i in range(4)]
        if final:
            gam = k.sb(es, tag + "gam", [128, D], F32)
            k.dma("sp", gam[:, :], io["norm_final"].partition_broadcast(128), writes=[gam])
            junk = k.sb(es, tag + "junk", [128, D], BF16)
            ss = [k.sb(es, f"{tag}ss{i}", [128, 1], F32) for i in range(2)]
            sq = [k.sb(es, f"{tag}sq{i}", [128, 1], F32) for i in range(2)]
            rstd = [k.sb(es, f"{tag}rs{i}", [128, 1], F32) for i in range(2)]
            ys = [k.sb(es, f"{tag}y{i}", [128, D], F32) for i in range(2)]
        hv = scr["hidT"].rearrange("(c p) r -> p c r", p=128)
        gs = groups()
        k.dma("sp", hids[0][:, :, 0:gs[0][1]], hv[:, :, gs[0][0]:gs[0][0] + gs[0][1]], writes=[hids[0]])
        ti = 0
        pi = 0
        for gi, (tok0, n) in enumerate(gs):
            hid = hids[gi % 2]
            if gi + 1 < len(gs):
                t1, n1 = gs[gi + 1]
                k.dma("sp", hids[(gi + 1) % 2][:, :, 0:n1], hv[:, :, t1:t1 + n1], writes=[hids[(gi + 1) % 2]])
            for t in range(n // 128):
                r0 = tok0 + t * 128
                j = ti % 2
                xt, xo = xts[j], xos[j]
                ti += 1
                k.dma("sp", xt[:, :], xin_ap[r0:r0 + 128, :], writes=[xt])
                for blk in range(2):
                    p = pp[pi % 4]
                    pi += 1
                    for kc in range(KC):
                        k.mm(p[:, :], hid[:, kc, t * 128:(t + 1) * 128], W[:, kc, blk * 512:(blk + 1) * 512],
                             kc == 0, kc == KC - 1, reads=[hid, W], writes=[p])
                    k.op("dve", lambda h, p=p, blk=blk, xt=xt, xo=xo: h.tensor_tensor(
                        out=xo[:, blk * 512:(blk + 1) * 512], in0=p[:, :], in1=xt[:, blk * 512:(blk + 1) * 512],
                        op=ALU.add), reads=[p, xt], writes=[xo])
                if not final:
                    k.dma("pool", xout_ap[r0:r0 + 128, :], xo[:, :], reads=[xo])
                else:
                    k.op("act", lambda h, xo=xo, j=j: h.activation(out=junk[:, :], in_=xo[:, :], func=AF.Square,
                                                                  accum_out=ss[j][:, :]),
                         reads=[xo], writes=[junk, ss[j]])
                    k.op("act", lambda h, j=j: h.activation(out=sq[j][:, :], in_=ss[j][:, :], func=AF.Sqrt,
                                                           scale=1.0 / D, bias=1e-6), reads=[ss[j]], writes=[sq[j]])
                    k.op("dve", lambda h, j=j: h.reciprocal(out=rstd[j][:, :], in_=sq[j][:, :]),
                         reads=[sq[j]], writes=[rstd[j]])
                    k.op("act", lambda h, xo=xo, j=j: h.activation(out=ys[j][:, :], in_=xo[:, :], func=AF.Copy,
                                                                  scale=rstd[j][:, :]),
                         reads=[xo, rstd[j]], writes=[ys[j]])
                    k.op("pool", lambda h, j=j: h.tensor_tensor(out=ys[j][:, :], in0=ys[j][:, :], in1=gam[:, :],
                                                               op=ALU.mult), reads=[gam], writes=[ys[j]])
                    k.dma("pool", xout_ap[r0:r0 + 128, :], ys[j][:, :], reads=[ys[j]])
    k.barrier()


def fox_phases(k, c, io, scr):
    with contextlib.ExitStack() as esf:
        frow = k.sb(esf, "fx_frow", [16, R], F32)
        fox_inproj(k, c, io, scr, frow)
        if "fx1" in DBG:
            return
        NFP = k.sb(esf, "fx_NFP", [16, TP], F32)
        FSN = k.sb(esf, "fx_FSN", [16, NS, TS], F32)
        NFcP = k.sb(esf, "fx_NFcP", [128, 32, 16], F32)
        NFcS = k.sb(esf, "fx_NFcS", [128, NS, 17, 16], F32)
        fox_prep(k, c, io, scr, frow, NFP, FSN, NFcP, NFcS)
        if "fx2" in DBG:
            return
        fox_attn(k, c, io, scr, NFP, FSN, NFcP, NFcS)


def fox_inproj(k, c, io, scr, frow):
    with contextlib.ExitStack() as es:
        stf = [k.sb(es, f"f1sf{i}", [128, 512], F32) for i in range(2)]
        W = k.sb(es, "f1W", [128, 8, 3088], BF16)
        load_w(k, W, io["w_in_fox"], 8, 3088)
        scale_rows(k, es, W, io["norm_mix"][1, :], 8, "f1g")
        st = NormState(k, es, "f1")
        xts = [k.sb(es, f"f1x{i}", [128, D], F32) for i in range(2)]
        hTs = [k.sb(es, f"f1h{i}", [128, 8, 512], BF16) for i in range(2)]
        pfm = [k.ps(es, f"f1pf{i}", [128, 512]) for i in range(2)]
        ptm = [k.ps(es, f"f1pt{i}", [128, 512]) for i in range(2)]
        stb = [k.sb(es, f"f1sb{i}", [128, 512], BF16) for i in range(4)]
        ev = 0
        sbi = 0
        sfi = 0
        ti = 0
        gl = groups()
        tis = [0]

        def norm_group(gi):
            tok0, n = gl[gi]
            for t in range(n // 128):
                xt = xts[tis[0] % 2]
                k.dma("sp", xt[:, :], scr["x2"][tok0 + t * 128: tok0 + (t + 1) * 128, :], writes=[xt])
                rms_to_hT(k, c, xt, hTs[gi % 2], t * 128, st, tis[0])
                tis[0] += 1

        norm_group(0)
        for gi, (tok0, n) in enumerate(gl):
            hT = hTs[gi % 2]
            if gi + 1 < len(gl):
                norm_group(gi + 1)
            for (c0, dst, scale) in ([] if "fxnofm" in DBG else ((0, scr["qTf"], 0.125), (1024, scr["kTf"], None))):
                for mc in range(8):
                    pp = pfm[ev % 2]
                    for kc in range(8):
                        k.mm(pp[:, 0:n], W[:, kc, c0 + mc * 128: c0 + (mc + 1) * 128], hT[:, kc, 0:n],
                             kc == 0, kc == 7, reads=[W, hT], writes=[pp])
                    sbt = stb[sbi % 4]
                    sbi += 1
                    evac(k, ev, sbt[:, 0:n], pp[:, 0:n], [pp], [sbt], scale=scale)
                    ev += 1
                    k.dma("pool", dst[mc * 128:(mc + 1) * 128, tok0:tok0 + n], sbt[:, 0:n], reads=[sbt])
            pp = pfm[ev % 2]
            for kc in range(0 if "fxnog" in DBG else 8):
                k.mm(pp[0:16, 0:n], W[:, kc, 3072:3088], hT[:, kc, 0:n], kc == 0, kc == 7, reads=[W, hT], writes=[pp])
            if "fxnog" not in DBG:
                evac(k, 1, frow[0:16, tok0:tok0 + n], pp[0:16, 0:n], [pp], [frow])
            ev += 1
            for t in range(0 if "fxnotm" in DBG else n // 128):
                r0 = tok0 + t * 128
                for (c0, outn, hb, tobf) in ((2048, "fv", 0, True), (2560, "fv", 1, True),
                                             (1024, "fk", 0, False), (1536, "fk", 1, False)):
                    pp = ptm[ev % 2]
                    for kc in range(8):
                        k.mm(pp[:, :], hT[:, kc, t * 128:(t + 1) * 128], W[:, kc, c0:c0 + 512],
                             kc == 0, kc == 7, reads=[W, hT], writes=[pp])
                    ev += 1
                    sft = stf[sfi % 2]
                    sfi += 1
                    evac(k, 1, sft[:, :], pp[:, :], [pp], [sft])
                    if tobf:
                        sbt = stb[sbi % 4]
                        sbi += 1
                        k.op("act", lambda h, sbt=sbt, sft=sft: h.activation(out=sbt[:, :], in_=sft[:, :], func=AF.Copy),
                             reads=[sft], writes=[sbt])
                        k.dma("pool", scr["vfTok"][r0:r0 + 128, hb * 512:(hb + 1) * 512], sbt[:, :], reads=[sbt])
                    for h8 in range(0 if "fxnoout" in DBG else 8):
                        src = sft[:, h8 * 64:(h8 + 1) * 64]
                        if r0 < TP:
                            k.dma("sp", io["p" + outn][hb * 8 + h8, r0:r0 + 128, :], src, reads=[sft])
                        else:
                            for s2 in range(2):
                                sq = (r0 - TP) // TS + s2
                                k.dma("sp", io["s" + outn][sq, hb * 8 + h8, :, :], src[s2 * 64:(s2 + 1) * 64], reads=[sft])
    k.barrier()


def fox_prep(k, c, io, scr, frow, NFP, FSN, NFcP, NFcS):
    with contextlib.ExitStack() as es:
        negb = k.sb(es, "f2negb", [16, 1], F32)
        LF = k.sb(es, "f2LF", [16, R], F32)
        CL = [k.sb(es, f"f2CL{i}", [16, 2048], F32) for i in range(2)]
        FC = [k.sb(es, f"f2FC{i}", [16, 2048], F32) for i in range(2)]
        PCp = k.ps(es, "f2PCp", [128, 32, 16])
        PCs = [k.ps(es, f"f2PCs{i}", [128, 17, 16]) for i in range(2)]
        k.dma("sp", negb[:, :], io["b_fox_f"].rearrange("(h o) -> h o", o=1), writes=[negb])
        k.op("dve", lambda h: h.tensor_scalar(out=negb[:, :], in0=negb[:, :], scalar1=-1.0, scalar2=None, op0=ALU.mult),
             writes=[negb])
        k.op("act", lambda h: h.activation(out=frow[:, :], in_=frow[:, :], func=AF.Exp, scale=-1.0, bias=negb[:, :]),
             reads=[negb], writes=[frow])
        k.op("act", lambda h: h.activation(out=frow[:, :], in_=frow[:, :], func=AF.Ln, bias=1.0), writes=[frow])
        k.op("dve", lambda h: h.tensor_scalar(out=LF[:, :], in0=frow[:, :], scalar1=-1.0, scalar2=None, op0=ALU.mult),
             reads=[frow], writes=[LF])
        k.dma("sp", io["pflf"][:, :], LF[:, 0:TP], reads=[LF])
        for i in range(NS):
            k.dma("sp", io["sflf"][i, :, :], LF[:, TP + i * TS:TP + (i + 1) * TS], reads=[LF])
        k.op("dve", lambda h: h.tensor_tensor_scan(out=NFP[:, :], data0=frow[:, 0:TP], data1=frow[:, 0:TP], initial=0.0,
                                                   op0=ALU.add, op1=ALU.max), reads=[frow], writes=[NFP])
        for t in range(32):
            k.tr(PCp[:, t, :], NFP[0:16, t * 128:(t + 1) * 128], c.ident[0:16, 0:16], reads=[NFP, c.ident], writes=[PCp])
        k.op("dve", lambda h: h.tensor_copy(out=NFcP[:, :, :], in_=PCp[:, :, :]), reads=[PCp], writes=[NFcP])
        for i in range(NS):
            cl, fc, pcs = CL[i % 2], FC[i % 2], PCs[i % 2]
            s0 = TP + i * TS
            k.dma("sp", cl[:, :], io["cflf"][i, :, :], writes=[cl])
            k.op("dve", lambda h, cl=cl: h.tensor_scalar(out=cl[:, :], in0=cl[:, :], scalar1=-1.0, scalar2=None,
                                                        op0=ALU.mult), writes=[cl])
            k.op("dve", lambda h, cl=cl, fc=fc: h.tensor_tensor_scan(out=fc[:, :], data0=cl[:, :], data1=cl[:, :],
                                                                    initial=0.0, op0=ALU.add, op1=ALU.max),
                 reads=[cl], writes=[fc])
            k.op("dve", lambda h, fc=fc, i=i, s0=s0: h.tensor_tensor_scan(
                out=FSN[:, i, :], data0=frow[:, s0:s0 + TS], data1=frow[:, s0:s0 + TS], initial=fc[:, 2047:2048],
                op0=ALU.add, op1=ALU.max), reads=[fc, frow], writes=[FSN])
            for m in range(16):
                k.tr(pcs[:, m, :], fc[0:16, m * 128:(m + 1) * 128], c.ident[0:16, 0:16], reads=[fc, c.ident], writes=[pcs])
            k.tr(pcs[0:64, 16, :], FSN[0:16, i, :], c.ident[0:16, 0:16], reads=[FSN, c.ident], writes=[pcs])
            k.op("dve", lambda h, pcs=pcs, i=i: h.tensor_copy(out=NFcS[:, i, 0:16, :], in_=pcs[:, 0:16, :]),
                 reads=[pcs], writes=[NFcS])
            k.op("dve", lambda h, pcs=pcs, i=i: h.tensor_copy(out=NFcS[0:64, i, 16, :], in_=pcs[0:64, 16, :]),
                 reads=[pcs], par=[NFcS])
    k.barrier()


def fox_attn(k, c, io, scr, NFP, FSN, NFcP, NFcS):
    with contextlib.ExitStack() as es:
        kTh = [k.sb(es, f"f3k{i}", [128, R], BF16) for i in range(2)]
        qTh = [k.sb(es, f"f3q{i}", [128, R], BF16) for i in range(2)]
        for t_ in kTh + qTh:
            k.op("pool", lambda h, t_=t_: h.memset(t_[64:128, :], 0.0), writes=[t_])
        vfh = [k.sb(es, f"f3v{i}", [128, NT, 65], BF16) for i in range(2)]
        ATT = k.sb(es, "f3att", [64, R], BF16)
        FQ = [k.sb(es, f"f3fq{i}", [128, 512], F32) for i in range(2)]
        BD = [[k.sb(es, f"f3bd{b}_{i}", [128, 512], F32) for i in range(4)] for b in range(2)]
        TMP = [k.sb(es, f"f3tmp{i}", [128, 512], F32) for i in range(4)]
        P = [k.sb(es, f"f3P{i}", [128, 512], BF16) for i in range(4)]
        OS = k.sb(es, "f3OS", [65, 512], F32)
        RD = k.sb(es, "f3RD", [65, 512], F32)
        KC = [k.sb(es, f"f3KC{i}", [128, 16, 64], BF16) for i in range(2)]
        KTC = k.sb(es, "f3KTC", [128, 2048], BF16)
        k.op("pool", lambda h: h.memset(KTC[64:128, :], 0.0), writes=[KTC])
        VC = [k.sb(es, f"f3VC{i}", [128, 16, 65], BF16) for i in range(2)]
        VN = [k.sb(es, f"f3VN{i}", [64, 65], BF16) for i in range(2)]
        FQs = k.sb(es, "f3FQs", [128, TS], F32)
        BDs = k.sb(es, "f3BDs", [64, TS], F32)
        TMPs = k.sb(es, "f3TMPs", [128, 1088], F32)
        Ps = k.sb(es, "f3Ps", [128, 1088], BF16)
        PS_S = [k.ps(es, f"f3PS{i}", [128, 512]) for i in range(3)]
        PS_O = [k.ps(es, f"f3PO{i}", [128, 512]) for i in range(2)]
        PS_F = k.ps(es, "f3PF", [128, 512])
        PS_BC = k.ps(es, "f3PBC", [128, 512])
        PS_KT = k.ps(es, "f3PKT", [64, 1024], BF16)
        LN = k.sb(es, "f3LN", [65, 512], F32)
        RDr = [k.sb(es, f"f3RDr{i}", [65, 512], F32) for i in range(2)]
        RB = [k.sb(es, f"f3RB{i}", [64, 512], F32) for i in range(2)]
        rdbuf = [TT(None) for _ in range(2)]
        for i in range(2):
            k.op("pool", lambda h, i=i: h.memset(vfh[i][:, :, 64:65], 1.0), writes=[vfh[i]])
            k.op("pool", lambda h, i=i: h.memset(VC[i][:, :, 64:65], 1.0), writes=[VC[i]])
            k.op("pool", lambda h, i=i: h.memset(VN[i][:, 64:65], 1.0), writes=[VN[i]])
        vview = scr["vfTok"].rearrange("(t p) c -> p t c", p=128)

        def load(hh):
            j = hh % 2
            k.dma("sp", kTh[j][0:64, :], scr["kTf"][hh * 64:(hh + 1) * 64, :], writes=[kTh[j]])
            k.dma("sp", qTh[j][0:64, :], scr["qTf"][hh * 64:(hh + 1) * 64, :], writes=[qTh[j]])
            k.dma("sp", vfh[j][:, :, 0:64], vview[:, :, hh * 64:(hh + 1) * 64], writes=[vfh[j]])

        def prep(hh, g, b):
            fq = FQ[b]
            k.mm(PS_F[:, 0:512], c.sel[0:16, hh, :], NFP[0:16, g * 512:(g + 1) * 512], True, True,
                 reads=[c.sel, NFP], writes=[PS_F])
            k.op("dve", lambda h: h.tensor_copy(out=fq[:, :], in_=PS_F[:, 0:512]), reads=[PS_F], writes=[fq])
            for r in range(4):
                b0 = 384 - 128 * r
                k.op("pool", lambda h, r=r, b0=b0: h.tensor_tensor(out=BD[b][r][:, :], in0=fq[:, :],
                                                                 in1=c.cpos[:, b0:b0 + 512], op=ALU.add),
                     reads=[fq, c.cpos], writes=[BD[b][r]])

        def sload(hh, si, b):
            r0 = TP + si * TS
            k.dma("pool", KC[b][:, :, :], io["cfk"][si, hh, :, :].rearrange("(t p) d -> p t d", p=128), writes=[KC[b]])
            k.dma("pool", VC[b][:, :, 0:64], io["cfv"][si, hh, :, :].rearrange("(t p) d -> p t d", p=128), writes=[VC[b]])
            k.dma("sp", VN[b][:, 0:64], scr["vfTok"][r0:r0 + TS, hh * 64:(hh + 1) * 64], writes=[VN[b]])

        load(0)
        it = 0
        gi = 0
        sli = 0
        pend_epiA = []
        pend_epiB = []
        sload(0, 0, 0)
        for hh in range(16):
            j = hh % 2
            kT, qT, vf = kTh[j], qTh[j], vfh[j]
            if hh + 1 < 16:
                load(hh + 1)
            prep(hh, 0, gi % 2)
            pend = []
            for g in range(8):
                b = gi % 2
                fq, bd, pso = FQ[b], BD[b], PS_O[b]
                gi += 1
                if g + 1 < 8:
                    prep(hh, g + 1, gi % 2)
                last = 4 * g + 3
                for jj in range(last + 1):
                    ps, tmp, p = PS_S[it % 3], TMP[it % 4], P[it % 4]
                    it += 1
                    k.mm(ps[:, 0:512], kT[:, jj * 128:(jj + 1) * 128], qT[:, g * 512:(g + 1) * 512], True, True,
                         reads=[kT, qT], writes=[ps])
                    bsrc = bd[jj - 4 * g] if jj >= 4 * g else fq
                    k.op("dve", lambda h, ps=ps, tmp=tmp, bsrc=bsrc: h.tensor_tensor(out=tmp[:, :], in0=ps[:, 0:512],
                                                                                    in1=bsrc[:, :], op=ALU.subtract),
                         reads=[ps, bsrc], writes=[tmp], skip_self=True, skip_war=("act",))
                    k.op("act", lambda h, tmp=tmp, p=p, jj=jj: h.activation(out=p[:, :], in_=tmp[:, :], func=AF.Exp,
                                                                           bias=NFcP[:, jj, hh:hh + 1]),
                         reads=[tmp, NFcP], writes=[p], skip_self=True, skip_war=("pe",))
                    pend.append(lambda jj=jj, p=p, pso=pso, last=last: k.mm(
                        pso[0:65, 0:512], vf[:, jj, :], p[:, :], jj == 0, jj == last, reads=[vf, p], writes=[pso]))
                    if len(pend) > 2:
                        pend.pop(0)()
                    if jj == 1 and pend_epiA:
                        pend_epiA.pop(0)()
                    if jj == last - 1 and pend_epiB:
                        pend_epiB.pop(0)()
                def epiA(pso=pso, b=b):
                    k.op("act", lambda h: h.activation(out=LN[64:65, :], in_=pso[64:65, 0:512], func=AF.Ln),
                         reads=[pso], writes=[LN])
                    k.op("act", lambda h: h.activation(out=RDr[b][64:65, :], in_=LN[64:65, :], func=AF.Exp, scale=-1.0),
                         reads=[LN], writes=[RDr[b]])
                    k.dma("sp", scr["rds"][b:b + 1, :], RDr[b][64:65, :], reads=[RDr[b]], writes=[rdbuf[b]])
                    k.dma("sp", RB[b][:, :], scr["rds"][b, :].partition_broadcast(64), reads=[rdbuf[b]], writes=[RB[b]])

                def epiB(pso=pso, b=b, g=g):
                    k.op("dve", lambda h: h.tensor_tensor(out=ATT[:, g * 512:(g + 1) * 512], in0=pso[0:64, 0:512],
                                                          in1=RB[b][:, :], op=ALU.mult),
                         reads=[pso, RB[b]], writes=[ATT])
                pend_epiA.append(epiA)
                pend_epiB.append(epiB)
            while pend:
                pend.pop(0)()
            while pend_epiA:
                pend_epiA.pop(0)()
            while pend_epiB:
                pend_epiB.pop(0)()
            for si in range(NS):
                r0 = TP + si * TS
                b = sli % 2
                sli += 1
                kc, vc, vn = KC[b], VC[b], VN[b]
                pso = PS_O[sli % 2]
                if si + 1 < NS:
                    sload(hh, si + 1, sli % 2)
                elif hh + 1 < 16:
                    sload(hh + 1, 0, sli % 2)
                for half in range(2):
                    for m in range(8):
                        k.tr(PS_KT[:, m * 128:(m + 1) * 128], kc[:, half * 8 + m, :], c.identb[:, :],
                             reads=[kc, c.identb], writes=[PS_KT])
                    k.op("act", lambda h, half=half: h.activation(out=KTC[0:64, half * 1024:(half + 1) * 1024],
                                                                in_=PS_KT[:, :], func=AF.Copy),
                         reads=[PS_KT], writes=[KTC])
                k.mm(PS_F[:, 0:TS], c.sel[0:16, hh, :], FSN[0:16, si, :], True, True, reads=[c.sel, FSN], writes=[PS_F])
                k.op("dve", lambda h: h.tensor_copy(out=FQs[:, :], in_=PS_F[:, 0:TS]), reads=[PS_F], writes=[FQs])
                k.op("pool", lambda h: h.tensor_tensor(out=BDs[:, :], in0=FQs[0:64, :], in1=c.cpos[0:64, 384:384 + TS],
                                                       op=ALU.add), reads=[FQs, c.cpos], writes=[BDs])
                for m in range(16):
                    ps = PS_S[m // 8]
                    k.mm(ps[:, (m % 8) * 64:(m % 8 + 1) * 64], KTC[:, m * 128:(m + 1) * 128], qT[:, r0:r0 + TS], True, True,
                         reads=[KTC, qT], writes=[ps])
                k.mm(PS_S[2][0:64, 0:TS], kT[:, r0:r0 + TS], qT[:, r0:r0 + TS], True, True, reads=[kT, qT], writes=[PS_S[2]])
                for m in range(16):
                    ps = PS_S[m // 8]
                    k.op("dve", lambda h, m=m, ps=ps: h.scalar_tensor_tensor(
                        out=TMPs[:, m * 64:(m + 1) * 64], in0=ps[:, (m % 8) * 64:(m % 8 + 1) * 64],
                        scalar=NFcS[:, si, m, hh:hh + 1], in1=FQs[:, :], op0=ALU.add, op1=ALU.subtract),
                        reads=[ps, NFcS, FQs], writes=[TMPs])
                k.op("dve", lambda h: h.scalar_tensor_tensor(
                    out=TMPs[0:64, 1024:1088], in0=PS_S[2][0:64, 0:TS], scalar=NFcS[0:64, si, 16, hh:hh + 1],
                    in1=BDs[:, :], op0=ALU.add, op1=ALU.subtract), reads=[PS_S[2], NFcS, BDs], writes=[TMPs])
                k.op("act", lambda h: h.activation(out=Ps[:, 0:1024], in_=TMPs[:, 0:1024], func=AF.Exp),
                     reads=[TMPs], writes=[Ps])
                k.op("act", lambda h: h.activation(out=Ps[0:64, 1024:1088], in_=TMPs[0:64, 1024:1088], func=AF.Exp),
                     reads=[TMPs], writes=[Ps])
                for m in range(16):
                    k.mm(pso[0:65, 0:TS], vc[:, m, :], Ps[:, m * 64:(m + 1) * 64], m == 0, False, reads=[vc, Ps], writes=[pso])
                k.mm(pso[0:65, 0:TS], vn[0:64, :], Ps[0:64, 1024:1088], False, True, reads=[vn, Ps], writes=[pso])
                attn_epilogue(k, c, pso, TS, OS, RD, PS_BC, ATT[:, r0:r0 + TS], ATT)
            k.dma("sp", scr["attT"][hh * 64:(hh + 1) * 64, :], ATT[:, :], reads=[ATT])
    k.barrier()
```
